# Optimizing a Trainium2 kernel written in Bass

```python
import jax, jax.numpy as jnp
from jax import lax
import numpy as np

D_MODEL = 1024
BATCH = 4
SEQ = 8192
DEPTH = 2

N_MIXERS = 2
N_META = 16
FOX_HEADS = 16
FOX_HEAD_DIM = D_MODEL // FOX_HEADS
FOX_Q_BLOCK = 128
HGRN_EXPAND = 128
HGRN_HEADS = D_MODEL // HGRN_EXPAND
HGRN_DK = HGRN_EXPAND
HGRN_DV = D_MODEL // HGRN_HEADS
HGRN_CHUNK = 64
FFN_HIDDEN = -(-8 * D_MODEL // (3 * 256)) * 256
N_FOX = (DEPTH + N_MIXERS - 1) // N_MIXERS
N_HGRN = DEPTH // N_MIXERS
EPS = 1e-6

kernel_name = "fox_hgrn2_interleaved_meta_trunk"


def rms_norm(x, gain):
    xf = x.astype(jnp.float32)
    y = xf * lax.rsqrt(jnp.mean(xf * xf, axis=-1, keepdims=True) + EPS)
    return (y * gain.astype(jnp.float32)).astype(x.dtype)


def _fox_attend(q_blk, c_q, pos_q, k, v, c_k, pos_k):
    s = jnp.einsum('bhqd,bhkd->bhqk', q_blk, k).astype(jnp.float32) * (FOX_HEAD_DIM ** -0.5)
    s = s + (c_q[..., :, None] - c_k[..., None, :])
    s = jnp.where(pos_k[None, :] <= pos_q[:, None], s, -jnp.inf)
    p = jax.nn.softmax(s, axis=-1)
    return jnp.einsum('bhqk,bhkd->bhqd', p, v.astype(jnp.float32))


def fox_mixer(h, w_in, b_f, q_gain, k_gain, w_out):
    B, L, D = h.shape
    n_blk = (L - N_META) // FOX_Q_BLOCK
    proj = h @ w_in
    q, k, v, gate, f_logit = jnp.split(proj, [D, 2 * D, 3 * D, 4 * D], axis=-1)
    q = rms_norm(q.reshape(B, L, FOX_HEADS, FOX_HEAD_DIM), q_gain)
    k = rms_norm(k.reshape(B, L, FOX_HEADS, FOX_HEAD_DIM), k_gain)
    v = v.reshape(B, L, FOX_HEADS, FOX_HEAD_DIM)
    log_f = jax.nn.log_sigmoid(f_logit.astype(jnp.float32) + b_f.astype(jnp.float32))
    c = jnp.cumsum(log_f, axis=1).transpose(0, 2, 1)
    q, k, v = (t.transpose(0, 2, 1, 3) for t in (q, k, v))
    pos = jnp.arange(L)
    o_meta = _fox_attend(q[:, :, :N_META], c[:, :, :N_META], pos[:N_META],
                         k[:, :, :N_META], v[:, :, :N_META], c[:, :, :N_META], pos[:N_META])
    qb = q[:, :, N_META:].reshape(B, FOX_HEADS, n_blk, FOX_Q_BLOCK, FOX_HEAD_DIM).transpose(2, 0, 1, 3, 4)
    cb = c[:, :, N_META:].reshape(B, FOX_HEADS, n_blk, FOX_Q_BLOCK).transpose(2, 0, 1, 3)
    pb = pos[N_META:].reshape(n_blk, FOX_Q_BLOCK)
    o_real = lax.map(lambda a: _fox_attend(a[0], a[1], a[2], k, v, c, pos), (qb, cb, pb))
    o_real = o_real.transpose(1, 2, 0, 3, 4).reshape(B, FOX_HEADS, L - N_META, FOX_HEAD_DIM)
    o = jnp.concatenate([o_meta, o_real], axis=2).transpose(0, 2, 1, 3).reshape(B, L, D)
    o = o * jax.nn.sigmoid(gate.astype(jnp.float32))
    return (o.astype(h.dtype) @ w_out).astype(h.dtype)


def _hgrn_chunk(S, inp):
    q, k, v, g = inp
    C = q.shape[2]
    b = jnp.cumsum(g, axis=2)
    causal = jnp.tril(jnp.ones((C, C), dtype=bool))
    diff = b[:, :, :, None, :] - b[:, :, None, :, :]
    decay = jnp.exp(jnp.where(causal[..., None], diff, -jnp.inf))
    attn = jnp.einsum('bhtd,bhsd,bhtsd->bhts', q, k, decay)
    o = jnp.einsum('bhts,bhsv->bhtv', attn, v) + jnp.einsum('bhtd,bhdv->bhtv', q * jnp.exp(b), S)
    b_last = b[:, :, -1:, :]
    S_new = jnp.exp(b_last[:, :, 0, :])[..., None] * S + jnp.einsum(
        'bhsd,bhsv->bhdv', k * jnp.exp(b_last - b), v)
    return S_new, o


def hgrn2_mixer(h, w_in, lb, g_gain, w_out):
    B, L, D = h.shape
    n_chunk = (L - N_META) // HGRN_CHUNK
    proj = h @ w_in
    q, f_logit, i, g_out = jnp.split(proj, 4, axis=-1)
    z = f_logit.astype(jnp.float32)
    lbf = lb.astype(jnp.float32)
    log_f = jnp.logaddexp(jnp.log(lbf), jnp.log1p(-lbf) + jax.nn.log_sigmoid(z))
    k = (1.0 - lbf) * jax.nn.sigmoid(-z)
    q = jax.nn.silu(q.astype(jnp.float32))
    v = i.astype(jnp.float32)
    heads = lambda t, d: t.reshape(B, L, HGRN_HEADS, d).transpose(0, 2, 1, 3)
    q, k, log_f, v = heads(q, HGRN_DK), heads(k, HGRN_DK), heads(log_f, HGRN_DK), heads(v, HGRN_DV)
    S0 = jnp.zeros((B, HGRN_HEADS, HGRN_DK, HGRN_DV), jnp.float32)
    S_meta, o_meta = _hgrn_chunk(S0, (q[:, :, :N_META], k[:, :, :N_META], v[:, :, :N_META], log_f[:, :, :N_META]))
    chunks = lambda t: t[:, :, N_META:].reshape(B, HGRN_HEADS, n_chunk, HGRN_CHUNK, t.shape[-1]).transpose(2, 0, 1, 3, 4)
    _, o_real = lax.scan(_hgrn_chunk, S_meta, (chunks(q), chunks(k), chunks(v), chunks(log_f)))
    o_real = o_real.transpose(1, 2, 0, 3, 4).reshape(B, HGRN_HEADS, L - N_META, HGRN_DV)
    o = jnp.concatenate([o_meta, o_real], axis=2).transpose(0, 2, 1, 3)
    o = rms_norm(o, g_gain) * jax.nn.silu(g_out.astype(jnp.float32).reshape(B, L, HGRN_HEADS, HGRN_DV))
    return (o.reshape(B, L, D).astype(h.dtype) @ w_out).astype(h.dtype)


def swiglu(h, w_in, w_out):
    gate, up = jnp.split(h @ w_in, 2, axis=-1)
    return ((jax.nn.silu(gate) * up) @ w_out).astype(h.dtype)


def setup_inputs(seed: int = 0) -> dict:
    key = jax.random.key(seed)
    ks = jax.random.split(key, 18)
    D = D_MODEL
    nrm = lambda k, shape, fan: jax.random.normal(k, shape, jnp.float32) * fan ** -0.5
    gain = lambda k, shape: 1.0 + 0.02 * jax.random.normal(k, shape, jnp.float32)
    return {
        "x": jax.random.normal(ks[0], (BATCH, SEQ, D), jnp.float32),
        "meta_tokens": jax.random.normal(ks[1], (N_META, D), jnp.float32),
        "attn_norm": gain(ks[2], (DEPTH, D)),
        "ffn_norm": gain(ks[3], (DEPTH, D)),
        "final_norm": gain(ks[4], (D,)),
        "fox_w_in": nrm(ks[5], (N_FOX, D, 4 * D + FOX_HEADS), D),
        "fox_b_f": jax.random.uniform(ks[6], (N_FOX, FOX_HEADS), jnp.float32, 1.0, 4.0),
        "fox_q_norm": gain(ks[7], (N_FOX, FOX_HEAD_DIM)),
        "fox_k_norm": gain(ks[8], (N_FOX, FOX_HEAD_DIM)),
        "fox_w_out": nrm(ks[9], (N_FOX, D, D), D),
        "hgrn_w_in": nrm(ks[10], (N_HGRN, D, 4 * D), D),
        "hgrn_lower_bounds": 0.1 * jax.random.normal(ks[11], (DEPTH, D), jnp.float32),
        "hgrn_g_norm": gain(ks[12], (N_HGRN, HGRN_DV)),
        "hgrn_w_out": nrm(ks[13], (N_HGRN, D, D), D),
        "ffn_w_in": nrm(ks[14], (DEPTH, D, 2 * FFN_HIDDEN), D),
        "ffn_w_out": nrm(ks[15], (DEPTH, FFN_HIDDEN, D), FFN_HIDDEN),
    }


def reference(x, meta_tokens, attn_norm, ffn_norm, final_norm, fox_w_in, fox_b_f, fox_q_norm,
              fox_k_norm, fox_w_out, hgrn_w_in, hgrn_lower_bounds, hgrn_g_norm, hgrn_w_out,
              ffn_w_in, ffn_w_out):
    B = x.shape[0]
    meta = jnp.broadcast_to(meta_tokens[None].astype(x.dtype), (B, N_META, D_MODEL))
    h = jnp.concatenate([meta, x], axis=1)
    lb_soft = jax.nn.softmax(hgrn_lower_bounds.astype(jnp.float32), axis=0)
    lower_bounds = jnp.cumsum(lb_soft, axis=0) - lb_soft[0]
    for i in range(DEPTH):
        hn = rms_norm(h, attn_norm[i])
        j = i // N_MIXERS
        if i % N_MIXERS == 0:
            h = h + fox_mixer(hn, fox_w_in[j], fox_b_f[j], fox_q_norm[j], fox_k_norm[j], fox_w_out[j])
        else:
            h = h + hgrn2_mixer(hn, hgrn_w_in[j], lower_bounds[i], hgrn_g_norm[j], hgrn_w_out[j])
        h = h + swiglu(rms_norm(h, ffn_norm[i]), ffn_w_in[i], ffn_w_out[i])
    h = rms_norm(h, final_norm)
    return h[:, N_META:]
```

```python
import numpy as np
import concourse.bass as bass
import concourse.mybir as mybir
from concourse.bass_utils import run_bass_kernel_spmd

F32 = mybir.dt.float32
BF16 = mybir.dt.bfloat16
AF = mybir.ActivationFunctionType
ALU = mybir.AluOpType
AX = mybir.AxisListType

PE, ACT, DVE, POOL, SP = "pe", "act", "dve", "pool", "sp"
ENGS = (PE, ACT, DVE, POOL, SP)
NDMA_SEM = 12


class Res:
    __slots__ = ("w", "r")

    def __init__(self):
        self.w = None
        self.r = []


class Buf:
    def __init__(self, t, name=""):
        self.t = t
        self.name = name
        self.res = Res()
        self.subs = {}

    def __getitem__(self, k):
        return self.t[k]

    def sub(self, key):
        r = self.subs.get(key)
        if r is None:
            r = self.subs[key] = Res()
        return r


def _res(x):
    return x.res if isinstance(x, Buf) else x


class Op:
    __slots__ = ("eng", "fn", "raw", "oth", "dma", "sigval", "need", "dsem", "dval", "dprev", "pos")


class Prog:
    def __init__(self, nc):
        self.nc = nc
        self.ops = {e: [] for e in ENGS}
        self.ndma = {e: 0 for e in ENGS}

    def add(self, eng, fn, reads=(), writes=(), dma=False):
        op = Op()
        op.eng, op.fn, op.dma = eng, fn, dma
        op.raw, op.oth = set(), set()
        op.need = False
        op.sigval = None
        for r in reads:
            r = _res(r)
            if r.w is not None:
                op.raw.add(r.w)
        for w in writes:
            w = _res(w)
            if w.w is not None:
                op.oth.add(w.w)
            for x in w.r:
                op.oth.add(x)
        for r in reads:
            _res(r).r.append(op)
        for w in writes:
            w = _res(w)
            w.w = op
            w.r = []
        if dma:
            i = self.ndma[eng]
            self.ndma[eng] += 1
            op.dsem = i % NDMA_SEM
            op.dval = 16 * (i // NDMA_SEM + 1)
        self.ops[eng].append(op)
        return op

    def cc(self, fn, reads=(), writes=()):
        op = self.add(POOL, fn, reads, writes, dma=True)
        self.ndma[POOL] -= 1
        self.ncc = getattr(self, "ncc", 0) + 1
        op.dsem = "cc"
        op.dval = self.ncc
        return op

    def barrier(self):
        last = {}
        for e in ENGS:
            lc = None
            for o in reversed(self.ops[e]):
                if isinstance(o, Op) and not o.dma:
                    lc = o
                    break
            if lc is not None:
                lc.need = True
            last[e] = lc
        mark = ("barrier", last, dict(self.ndma), getattr(self, "ncc", 0))
        for e in ENGS:
            self.ops[e].append(mark)

    def dma(self, out, in_, reads=(), writes=(), q=SP, **kw):
        return self.add(q, lambda e: e.dma_start(out=out, in_=in_, **kw), reads, writes, dma=True)

    def _needed(self, o, d):
        if d.dma:
            return True
        if d.eng != o.eng:
            return True
        if o.dma:
            return True
        if o.eng == PE:
            return False
        return d in o.raw

    def _deps(self, o):
        best = {}
        out = []
        for d in list(o.raw) + list(o.oth):
            if not self._needed(o, d):
                continue
            if d.dma:
                out.append(d)
            else:
                b = best.get(d.eng)
                if b is None or d.pos > b.pos:
                    best[d.eng] = d
        return out + list(best.values())

    def emit(self):
        nc = self.nc
        for e in ENGS:
            for k, o in enumerate(self.ops[e]):
                if isinstance(o, Op):
                    o.pos = k
        for e in ENGS:
            for o in self.ops[e]:
                if not isinstance(o, Op):
                    continue
                for d in self._deps(o):
                    if not d.dma:
                        d.need = True
        for e in ENGS:
            c = 0
            for o in self.ops[e]:
                if isinstance(o, Op) and (not o.dma) and o.need:
                    c += 1
                    o.sigval = c
        import contextlib
        with contextlib.ExitStack() as st:
            esem = {e: st.enter_context(nc.semaphore("s_" + e)) for e in ENGS}
            dsem = {e: [st.enter_context(nc.semaphore("d_%s%d" % (e, i))) for i in range(NDMA_SEM)]
                    for e in ENGS if self.ndma[e] > 0}
            ccsem = st.enter_context(nc.semaphore("s_cc"))
            block = st.enter_context(nc.Block())

            def run(ename, eng):
                seen = {}

                def wait(sem, val):
                    k = id(sem)
                    if seen.get(k, 0) >= val:
                        return
                    seen[k] = val
                    eng.wait_ge(sem, val)

                for o in self.ops[ename]:
                    if not isinstance(o, Op):
                        _, last, nd, ncc = o
                        if ncc > 0:
                            wait(ccsem, ncc)
                        for e2 in ENGS:
                            if last[e2] is not None:
                                wait(esem[e2], last[e2].sigval)
                            n = nd[e2]
                            for i in range(NDMA_SEM):
                                cnt = (n - i + NDMA_SEM - 1) // NDMA_SEM if n > i else 0
                                if cnt > 0:
                                    wait(dsem[e2][i], 16 * cnt)
                        continue
                    for d in self._deps(o):
                        if d.dma and d.dsem == "cc":
                            wait(ccsem, d.dval)
                        elif d.dma:
                            wait(dsem[d.eng][d.dsem], d.dval)
                        else:
                            wait(esem[d.eng], d.sigval)
                    if o.dma and o.dsem == "cc":
                        if o.dval > 1:
                            wait(ccsem, o.dval - 1)
                        o.fn(eng).then_inc(ccsem)
                    elif o.dma:
                        s = dsem[ename][o.dsem]
                        if o.dval > 16:
                            wait(s, o.dval - 16)
                        o.fn(eng).then_inc(s, 16)
                    else:
                        ins = o.fn(eng)
                        if o.need:
                            ins.then_inc(esem[ename], 1)
                if ename == POOL and getattr(self, "ncc", 0) > 0:
                    wait(ccsem, self.ncc)
                if self.ndma[ename] > 0:
                    n = self.ndma[ename]
                    for i in range(NDMA_SEM):
                        cnt = (n - i + NDMA_SEM - 1) // NDMA_SEM if n > i else 0
                        if cnt > 0:
                            wait(dsem[ename][i], 16 * cnt)

            @block.tensor
            def _(eng):
                run(PE, eng)

            @block.scalar
            def _(eng):
                run(ACT, eng)

            @block.vector
            def _(eng):
                run(DVE, eng)

            @block.gpsimd
            def _(eng):
                run(POOL, eng)

            @block.sync
            def _(eng):
                run(SP, eng)
import contextlib
import ml_dtypes

D = 1024
H = 16
DH = 64
NMETA = 16
SEQ = 8192
NPAD = 112
LP = NPAD + NMETA + SEQ
NT = LP // 128
FF = 2816
NJ = FF // 128
EPS = 1e-6
BLOCKS = [(0, 128)] + [(128 + 512 * i, 512) for i in range(16)]
NB = len(BLOCKS)


import os
DBG = os.environ.get('KDBG', '')


class Ctx:
    pass


def _mm(P, out_ap, pairs, reads, writes, start=True, stop=True):
    def fn(e):
        n = len(pairs)
        ins = None
        for i, (l, r) in enumerate(pairs):
            ins = e.matmul(out_ap, lhsT=l, rhs=r, start=(start and i == 0), stop=(stop and i == n - 1))
        return ins
    return P.add(PE, fn, reads, writes)


def _act(P, out, in_, func, reads, writes, **kw):
    return P.add(ACT, lambda e: e.activation(out=out, in_=in_, func=func, **kw), reads, writes)


def _tt(P, eng, out, in0, in1, op, reads, writes):
    return P.add(eng, lambda e: e.tensor_tensor(out=out, in0=in0, in1=in1, op=op), reads, writes)


def _ts(P, eng, out, in0, s1, s2, op0, op1, reads, writes):
    if op1 is None:
        return P.add(eng, lambda e: e.tensor_scalar(out=out, in0=in0, scalar1=s1, scalar2=None, op0=op0), reads, writes)
    return P.add(eng, lambda e: e.tensor_scalar(out=out, in0=in0, scalar1=s1, scalar2=s2, op0=op0, op1=op1), reads, writes)


def _stt(P, out, in0, scalar, in1, op0, op1, reads, writes):
    return P.add(DVE, lambda e: e.scalar_tensor_tensor(out=out, in0=in0, scalar=scalar, in1=in1, op0=op0, op1=op1), reads, writes)


def _copy(P, eng, out, in_, reads, writes):
    if eng == ACT:
        return P.add(ACT, lambda e: e.copy(out=out, in_=in_), reads, writes)
    return P.add(eng, lambda e: e.tensor_copy(out=out, in_=in_), reads, writes)


def _rmsnorm_rows(P, C, ht, hn, gbc, tmp):
    junk, ssq, lnv, rstd = tmp
    _act(P, junk[:], ht[:], AF.Square, [ht], [junk, ssq], accum_out=ssq[:])
    _act(P, lnv[:], ssq[:], AF.Ln, [ssq], [lnv], scale=1.0 / D, bias=EPS)
    _act(P, rstd[:], lnv[:], AF.Exp, [lnv], [rstd], scale=-0.5)
    _stt(P, hn[:], ht[:], rstd[:, 0:1], gbc[:], ALU.mult, ALU.mult, [ht, rstd, gbc], [hn])


def _transpose_block(P, C, hns, nt, hnT, pTs, cnt0):
    n = nt * 128
    for c in range(8):
        pT = pTs[(cnt0 + c) % len(pTs)]

        def fn(e, c=c, pT=pT):
            ins = None
            for tl in range(nt):
                ins = e.transpose(out=pT[:, tl * 128:(tl + 1) * 128], in_=hns[tl][:, c * 128:(c + 1) * 128],
                                  identity=C.identb[:])
            return ins
        P.add(PE, fn, list(hns[:nt]) + [C.identb], [pT])
        _copy(P, ACT if c % 2 == 0 else DVE, hnT[:, c, 0:n], pT[:, 0:n], [pT], [hnT.sub(c)])


def stage1(P, nc, C, sb, ps, bounce=None):
    NHL = C.nheads
    NQ = NHL // 2
    WC = 4 * NHL * DH + NHL
    W = sb("s1_W", [128, 8, WC], BF16)
    wst = [sb("s1_wst%d" % i, [128, WC], F32) for i in range(2)]
    Wr = [W.sub(k) for k in range(8)]
    gbc = sb("s1_gbc", [128, D], F32)
    P.dma(gbc[:], C.attn_norm[0].partition_broadcast(128), writes=[gbc])
    gcol = sb("s1_gcol", [128, 2], F32)
    for hh in range(2):
        P.dma(gcol[64 * hh:64 * hh + 64, 0:1], C.fox_q_norm[0].rearrange("(p o) -> p o", o=1), writes=[gcol])
        P.dma(gcol[64 * hh:64 * hh + 64, 1:2], C.fox_k_norm[0].rearrange("(p o) -> p o", o=1), writes=[gcol])
    _ts(P, DVE, gcol[:, 0:1], gcol[:, 0:1], 0.125, None, ALU.mult, None, [gcol], [gcol])
    negbf = sb("s1_negbf", [NHL, 1], F32)
    P.dma(negbf[:], C.fox_b_f[0].rearrange("(p o) -> p o", o=1), writes=[negbf])
    _ts(P, DVE, negbf[:], negbf[:], -1.0, None, ALU.mult, None, [negbf], [negbf])
    ones16 = sb("s1_ones16", [NHL, 512], F32)
    P.add(DVE, lambda e: e.memset(ones16[:], 1.0), [], [ones16])
    zero16 = sb("s1_zero16", [NHL, 1], F32)
    P.add(DVE, lambda e: e.memset(zero16[:], 0.0), [], [zero16])

    hts = [sb("s1_ht%d" % i, [128, D], F32) for i in range(2)]
    hns = [sb("s1_hn%d" % i, [128, D], BF16) for i in range(8)]
    junk = sb("s1_junk", [128, D], BF16)
    ssqs = [[sb("s1_ssq%d_%d" % (i, j), [128, 1], F32) for j in range(3)] for i in range(2)]
    hnTs = [sb("s1_hnT%d" % i, [128, 8, 512], BF16) for i in range(2)]
    pTs = [ps("s1_pT%d" % i, [128, 512], BF16) for i in range(2)]
    psA = [ps("s1_psA%d" % i, [128, 512], F32) for i in range(4)]
    psSs = [ps("s1_psS%d" % i, [128, 512], F32) for i in range(2)]
    sqs = [sb("s1_sq%d" % i, [128, 512], BF16) for i in range(2)]
    lnvs = [sb("s1_lnv%d" % i, [128, 512], F32) for i in range(2)]
    rss = [sb("s1_rs%d" % i, [128, 512], F32) for i in range(2)]
    outs = [sb("s1_out%d" % i, [128, 512], BF16) for i in range(6)]
    vsts = [sb("s1_vst%d" % i, [128, NHL, 4, 65], BF16) for i in range(2)]
    for v in vsts:
        P.add(POOL, lambda e, v=v: e.memset(v[:, :, :, 64:65], 1.0), [], [v])
    ef = sb("s1_ef", [NHL, 512], F32)
    lf = sb("s1_lf", [NHL, 512], F32)
    cps = [sb("s1_cp%d" % i, [NHL, 512], F32) for i in range(2)]
    hsp = [[sb("s1_hsp%d_%d" % (i, j), [NHL, 512], BF16) for j in range(6)] for i in range(2)]
    r1 = sb("s1_r1", [NHL, 512], F32)
    r2 = sb("s1_r2", [NHL, 512], F32)
    onesb = sb("s1_onesb", [3, LP // 4], BF16)
    P.add(DVE, lambda e: e.memset(onesb[:], 1.0), [], [onesb])
    for h in range(NHL):
        for qd in range(4):
            cs_ = slice(qd * (LP // 4), (qd + 1) * (LP // 4))
            P.dma(C.KT_d[h, 67:70, cs_], onesb[0:3, :], reads=[onesb], q=POOL)
            P.dma(C.QT_d[h, 64:67, cs_], onesb[0:3, :], reads=[onesb], q=POOL)

    cA = 0
    cO = 0
    cT = 0
    cTh = [0]

    def fa_load(bi, tl):
        t0, n = BLOCKS[bi]
        if bi >= NB or tl >= n // 128:
            return
        ht = hts[tl % 2]
        P.dma(ht[:], C.h0[t0 + tl * 128:t0 + (tl + 1) * 128, :], writes=[ht])

    def fa_norm(bi, tl):
        t0, n = BLOCKS[bi]
        if bi >= NB or tl >= n // 128:
            return
        myhn = hns[(bi % 2) * 4:(bi % 2) * 4 + 4]
        tmp = [junk] + ssqs[cTh[0] % 2]
        cTh[0] += 1
        _rmsnorm_rows(P, C, hts[tl % 2], myhn[tl], gbc, tmp)

    def s1_front_a(bi):
        for tl in range(4):
            fa_load(bi, tl)
            fa_norm(bi, tl)

    def s1_front_b(bi):
        t0, n = BLOCKS[bi]
        myhn = hns[(bi % 2) * 4:(bi % 2) * 4 + 4]
        _transpose_block(P, C, myhn, n // 128, hnTs[bi % 2], pTs, 0)

    s1_front_a(0)
    for kc in range(8):
        P.dma(wst[kc % 2][:], C.fox_w_in[kc * 128:(kc + 1) * 128, :], writes=[wst[kc % 2]])
        _copy(P, ACT if kc % 2 == 0 else DVE, W[:, kc, :], wst[kc % 2][:], [wst[kc % 2]], [W.sub(kc)])
    s1_front_b(0)
    for bi, (t0, n) in enumerate(BLOCKS):
        nt = n // 128
        hnT = hnTs[bi % 2]
        if bi + 1 < NB:
            fa_load(bi + 1, 0)
            fa_load(bi + 1, 1)
        hr = [hnT.sub(c) for c in range(8)]
        pqs = {}

        def qk_a(c):
            nonlocal cA
            cols = c * 128
            pq = psA[cA % 4]
            cA += 1
            pqs[c] = pq
            _mm(P, pq[:, 0:n], [(W[:, kc, cols:cols + 128], hnT[:, kc, 0:n]) for kc in range(8)], Wr + hr, [pq])
            _act(P, sqs[c % 2][:, 0:n], pq[:, 0:n], AF.Square, [pq], [sqs[c % 2]])

        def qk_b(c):
            nonlocal cO
            pq = pqs[c]
            sq, lnv, rs = sqs[c % 2], lnvs[c % 2], rss[c % 2]
            psS = psSs[c % 2]
            _mm(P, psS[:, 0:n], [(C.bdones[:], sq[:, 0:n])], [C.bdones, sq], [psS])
            _act(P, lnv[:, 0:n], psS[:, 0:n], AF.Ln, [psS], [lnv], scale=1.0 / DH, bias=EPS)
            _act(P, rs[:, 0:n], lnv[:, 0:n], AF.Exp, [lnv], [rs], scale=-0.5)
            ob = outs[cO % 6]
            cO += 1
            gi = 0 if c < NQ else 1
            _stt(P, ob[:, 0:n], pq[:, 0:n], gcol[:, gi:gi + 1], rs[:, 0:n], ALU.mult, ALU.mult, [pq, gcol, rs], [ob])
            dst = C.QT_d if c < NQ else C.KT_d
            for hh in range(2):
                h = (c % NQ) * 2 + hh
                P.dma(dst[h, 0:64, t0:t0 + n], ob[64 * hh:64 * hh + 64, 0:n], reads=[ob], q=POOL)

        qk_a(0)
        for c in range(2 * NQ):
            if c + 1 < 2 * NQ:
                qk_a(c + 1)
            qk_b(c)
        if bi + 1 < NB:
            fa_norm(bi + 1, 0)
            fa_norm(bi + 1, 1)
            fa_load(bi + 1, 2)
            fa_load(bi + 1, 3)
        for c in range(NQ):
            cols = 3 * NHL * DH + c * 128
            pg = psA[cA % 4]
            cA += 1
            _mm(P, pg[:, 0:n], [(W[:, kc, cols:cols + 128], hnT[:, kc, 0:n]) for kc in range(8)], Wr + hr, [pg])
            lnv, rs = lnvs[c % 2], rss[c % 2]
            _act(P, rs[:, 0:n], pg[:, 0:n], AF.Exp, [pg], [rs], scale=-1.0)
            _ts(P, DVE, lnv[:, 0:n], rs[:, 0:n], 1.0, None, ALU.add, None, [rs], [lnv])
            ob = outs[cO % 6]
            cO += 1
            P.add(DVE, lambda e, rs=rs, lnv=lnv, n=n: e.reciprocal(out=rs[:, 0:n], in_=lnv[:, 0:n]), [lnv], [rs])
            _copy(P, POOL, ob[:, 0:n], rs[:, 0:n], [rs], [ob])
            P.dma(C.SG_d[c * 128:(c + 1) * 128, t0:t0 + n], ob[:, 0:n], reads=[ob], q=POOL)
        pf = psA[cA % 4]
        cA += 1
        _mm(P, pf[0:NHL, 0:n], [(W[:, kc, 4 * NHL * DH:WC], hnT[:, kc, 0:n]) for kc in range(8)], Wr + hr, [pf])
        _act(P, ef[:, 0:n], pf[0:NHL, 0:n], AF.Exp, [pf], [ef], scale=-1.0, bias=negbf[:, 0:1])
        _act(P, lf[:, 0:n], ef[:, 0:n], AF.Ln, [ef], [lf], bias=1.0)
        cp = cps[bi % 2]
        if bi == 0:
            ref_ap, ref_b = zero16[:, 0:1], zero16
        else:
            pn = BLOCKS[bi - 1][1]
            ref_ap, ref_b = cps[(bi - 1) % 2][:, pn - 1:pn], cps[(bi - 1) % 2]
        P.add(DVE, lambda e, cp=cp, ref_ap=ref_ap, n=n: e.tensor_tensor_scan(
            out=cp[:, 0:n], data0=ones16[:, 0:n], data1=lf[:, 0:n], initial=ref_ap, op0=ALU.mult, op1=ALU.add),
            [ones16, lf, ref_b], [cp])
        hh = hsp[bi % 2]
        _copy(P, DVE, hh[0][:, 0:n], cp[:, 0:n], [cp], [hh[0]])
        _tt(P, DVE, r1[:, 0:n], cp[:, 0:n], hh[0][:, 0:n], ALU.subtract, [cp, hh[0]], [r1])
        _copy(P, DVE, hh[1][:, 0:n], r1[:, 0:n], [r1], [hh[1]])
        _tt(P, DVE, r2[:, 0:n], r1[:, 0:n], hh[1][:, 0:n], ALU.subtract, [r1, hh[1]], [r2])
        _copy(P, DVE, hh[2][:, 0:n], r2[:, 0:n], [r2], [hh[2]])
        for j in range(3):
            _ts(P, DVE, hh[3 + j][:, 0:n], hh[j][:, 0:n], -1.0, None, ALU.mult, None, [hh[j]], [hh[3 + j]])
            P.dma(C.KT_d[:, 64 + j, t0:t0 + n], hh[j][:, 0:n], reads=[hh[j]])
            P.dma(C.QT_d[:, 67 + j, t0:t0 + n], hh[3 + j][:, 0:n], reads=[hh[3 + j]])
        if bi + 1 < NB:
            fa_norm(bi + 1, 2)
            fa_norm(bi + 1, 3)
        vst = vsts[bi % 2]
        if bi == 0:
            P.add(POOL, lambda e, v=vst: e.memset(v[0:NPAD, :, 0:1, 64:65], 0.0), [], [vst])
        for tl in range(nt):
            for half in range(NHL // 8):
                pv = psA[cA % 4]
                cA += 1
                cols = 2 * NHL * DH + half * 512
                _mm(P, pv[:, :], [(hnT[:, kc, tl * 128:(tl + 1) * 128], W[:, kc, cols:cols + 512]) for kc in range(8)],
                    Wr + hr, [pv])
                _copy(P, DVE if half == 0 else ACT, vst[:, 8 * half:8 * half + 8, tl, 0:64],
                      pv[:, :].rearrange("p (h d) -> p h d", h=8), [pv], [vst])
        for h in range(NHL):
            P.dma(C.VA_d[h, :, t0 // 128:t0 // 128 + nt, :], vst[:, h, 0:nt, :], reads=[vst])
        if bi == 0:
            P.add(POOL, lambda e, v=vst: e.memset(v[0:NPAD, :, 0:1, 64:65], 1.0), [], [vst])
        if bi + 1 < NB:
            s1_front_b(bi + 1)


def stage2(P, nc, C, sb, ps, bounce=None):
    for kc in range(8):
        P.dma(C.Wo[0][:, kc, :], C.fox_w_out[kc * 128:(kc + 1) * 128, :], writes=[C.Wo[0]], q=POOL)
        P.dma(C.Wo[1][:, kc, :], C.hgrn_w_out[kc * 128:(kc + 1) * 128, :], writes=[C.Wo[1]], q=POOL)
    gate_t = sb("s2_gate", [128, 1], F32)
    gate = [Res()]
    conv = convert_weights(P, nc, C, bounce, gate) if bounce is not None else iter(())
    KTb = [sb("s2_KT%d" % i, [70, LP], BF16) for i in range(2)]
    VAb = [sb("s2_VA%d" % i, [128, NT, 65], BF16) for i in range(2)]
    Qbs = [sb("s2_Q%d" % i, [70, 512], BF16) for i in range(3)]
    sgs = [sb("s2_sg%d" % i, [64, 512], BF16) for i in range(3)]
    biases = [sb("s2_bias%d" % i, [128, NT], F32) for i in range(3)]
    Pts = [sb("s2_Pt%d" % i, [128, 512], BF16) for i in range(6)]
    Sb = [ps("s2_S%d" % i, [128, 512], F32) for i in range(5)]
    Ob = [ps("s2_O%d" % i, [128, 512], F32) for i in range(2)]
    Rps = ps("s2_R", [128, 512], F32)
    rds = [sb("s2_rd%d" % i, [128, 512], F32) for i in range(2)]
    rd2s = [sb("s2_rd2%d" % i, [128, 512], F32) for i in range(2)]
    pending = []
    onesf = sb("s2_onesf", [128, 64], F32)
    P.add(DVE, lambda e: e.memset(onesf[:], 1.0), [], [onesf])
    zt2 = sb("s2_zt", [64, 128], BF16)
    P.add(DVE, lambda e: e.memset(zt2[:], 0.0), [], [zt2])
    for h in range(C.nheads):
        P.dma(C.OGi[h][:, LP:LP + 128], zt2[:], reads=[zt2], writes=[C.OGi_res[h]])
    Osbs = [sb("s2_Osb%d" % i, [64, 512], F32) for i in range(2)]
    og1s = [sb("s2_og1%d" % i, [64, 512], F32) for i in range(2)]
    ogbs = [sb("s2_ogb%d" % i, [64, 512], BF16) for i in range(2)]

    items = []
    for h in range(C.nheads):
        for bi, (t0, n) in enumerate(BLOCKS):
            nkt = (t0 + n) // 128
            for kt in range(nkt):
                items.append((h, bi, kt, nkt))
    state = {}

    def prologue(h, bi):
        t0, n = BLOCKS[bi]
        g = h * NB + bi
        if g in state or g >= C.nheads * NB:
            return
        state[g] = True
        if g % 3 == 0:
            gate[0] = Res()
            P.add(DVE, lambda e: e.memset(gate_t[:], 0.0), [], [gate[0]])
            next(conv, None)
        if bi == 0:
            KTh, VAh = KTb[h % 2], VAb[h % 2]
            P.dma(KTh[:, :], C.KT_d[h], writes=[KTh])
            P.dma(VAh[:], C.VA_d[h], writes=[VAh])
        Qb, sg, bias = Qbs[g % 3], sgs[g % 3], biases[g % 3]
        P.dma(Qb[0:70, 0:n], C.QT_d[h, :, t0:t0 + n], writes=[Qb])
        P.dma(sg[0:64, 0:n], C.SG_d[64 * h:64 * h + 64, t0:t0 + n], writes=[sg])

    def qk(i):
        h, bi, kt, nkt = items[i]
        t0, n = BLOCKS[bi]
        g = h * NB + bi
        if kt == 0:
            prologue(h, bi)
        j = kt - t0 // 128
        c0 = 128 * j if j >= 0 else 0
        S = Sb[i % 5]
        _mm(P, S[:, c0:n], [(KTb[h % 2][0:70, kt * 128:(kt + 1) * 128], Qbs[g % 3][0:70, c0:n])],
            [KTb[h % 2], Qbs[g % 3]], [S])

    def rest(i):
        h, bi, kt, nkt = items[i]
        t0, n = BLOCKS[bi]
        g = h * NB + bi
        j = kt - t0 // 128
        c0 = 128 * j if j >= 0 else 0
        S, Pt, O = Sb[i % 5], Pts[i % 6], Ob[g % 2]
        bias = biases[g % 3]
        if kt == 0:
            while pending and pending[0][0] <= g - 2:
                epilogue2(pending.pop(0)[0])
            prologue((g + 1) // NB, (g + 1) % NB)
        _act(P, Pt[:, c0:n], S[:, c0:n], AF.Exp, [S], [Pt])
        if j >= 0:
            _tt(P, DVE, Pt[:, c0:c0 + 128], Pt[:, c0:c0 + 128], C.tri[:], ALU.mult, [Pt, C.tri], [Pt])
        _mm(P, O[0:65, c0:n], [(VAb[h % 2][:, kt, 0:65], Pt[:, c0:n])], [VAb[h % 2], Pt], [O],
            start=(kt == 0), stop=(kt == nkt - 1))
        if kt == nkt - 1:
            Osb = Osbs[g % 2]
            rd, rd2 = rds[g % 2], rd2s[g % 2]
            _ts(P, DVE, rd[64:65, 0:n], O[64:65, 0:n], 1e-30, None, ALU.max, None, [O], [rd])
            P.add(DVE, lambda e: e.reciprocal(out=rd2[64:65, 0:n], in_=rd[64:65, 0:n]), [rd], [rd2])
            _copy(P, ACT, Osb[:, 0:n], O[0:64, 0:n], [O], [Osb])
            pending.append((g, i))
        while pending and (i - pending[0][1] >= 2 or i == len(items) - 1):
            epilogue2(pending.pop(0)[0])

    def epilogue2(g):
        h, bi = g // NB, g % NB
        t0, n = BLOCKS[bi]
        Osb, og1, ogb, sg = Osbs[g % 2], og1s[g % 2], ogbs[g % 2], sgs[g % 3]
        rd2 = rd2s[g % 2]
        _mm(P, Rps[0:64, 0:n], [(onesf[64:65, 0:64], rd2[64:65, 0:n])], [onesf, rd2], [Rps])
        _tt(P, DVE, og1[:, 0:n], Osb[:, 0:n], Rps[0:64, 0:n], ALU.mult, [Osb, Rps], [og1])
        _tt(P, DVE, ogb[:, 0:n], og1[:, 0:n], sg[0:64, 0:n], ALU.mult, [og1, sg], [ogb])
        P.dma(C.OGi[h][:, t0:t0 + n], ogb[:, 0:n], reads=[ogb], writes=[C.OGi_res[h]])
        if bi == NB - 1:
            P.cc(lambda e, h=h: e.collective_compute("AllGather", ALU.bypass, replica_groups=C.RG,
                                                     ins=[C.OGi_t[h].ap().opt()], outs=[C.OGo_t[h].ap().opt()]),
                 reads=[C.OGi_res[h]], writes=[C.OGo_res[h]])

    LA = 3
    N = len(items)
    for i in range(N + LA):
        if i < N:
            qk(i)
        if i - LA >= 0:
            rest(i - LA)
    for _ in conv:
        pass


def build(stages=3, debug=False, nheads=H // 2, conv=None, s3parts=99, nblk=NB):
    nc = bass.Bass("TRN2", target_bir_lowering=False)
    C = Ctx()
    C.nheads = nheads
    C.s3parts = s3parts
    C.nblk = nblk
    if conv is None:
        conv = stages >= 3

    def din(name, shape, dt=F32):
        return nc.dram_tensor(name, list(shape), dt, kind="ExternalInput").ap()

    def dscr(name, shape, dt, out=False):
        return nc.dram_tensor(name, list(shape), dt, kind="ExternalOutput" if out else "Internal").ap()

    C.h0 = din("h0", [LP, D])
    C.attn_norm = din("attn_norm", [2, D])
    C.ffn_norm = din("ffn_norm", [2, D])
    C.final_norm = din("final_norm", [D])
    NHL = nheads
    C.fox_w_in = din("fox_w_in", [D, 4 * NHL * DH + NHL])
    C.fox_b_f = din("fox_b_f", [1, NHL])
    C.fox_q_norm = din("fox_q_norm", [1, DH])
    C.fox_k_norm = din("fox_k_norm", [1, DH])
    C.fox_w_out = din("fox_w_out", [D, D])
    C.hgrn_w_in = din("hgrn_w_in", [D, 4 * NHG * 128])
    C.lb = din("hgrn_lower_bounds", [2, NHG * 128])
    C.h0h = din("h0h", [HT * 128, D])
    C.rmask = din("rmask", [128, 8 * 512], mybir.dt.uint16)
    C.rmaskf = din("rmaskf", [128, 1])
    C.g_norm = din("hgrn_g_norm", [1, 128])
    C.hgrn_w_out = din("hgrn_w_out", [D, D])
    C.ffn_w_in = din("ffn_w_in", [2, D, 2 * FF])
    C.ffn_w_out = din("ffn_w_out", [2, FF, D])
    c_identb = din("c_identb", [128, 128], BF16)
    c_identf = din("c_identf", [128, 128], F32)
    c_bdones = din("c_bdones", [128, 128], BF16)
    c_tri = din("c_tri", [128, 128], BF16)
    d1 = debug and stages == 1
    C.QT_d = dscr("QT_d", [NHL, 70, LP], BF16, d1)
    C.KT_d = dscr("KT_d", [NHL, 70, LP], BF16, d1)
    C.VA_d = dscr("VA_d", [NHL, 128, NT, 65], BF16, d1)
    C.SG_d = dscr("SG_d", [NHL * DH, LP], BF16, d1)
    C.RG = [[0, 1], [2, 3], [4, 5], [6, 7]]
    C.OGi_t = [nc.dram_tensor("OGi%d" % h, [DH, LP + 128], BF16) for h in range(NHL)]
    C.OGo_t = [nc.dram_tensor("OGo%d" % h, [2 * DH, LP + 128], BF16) for h in range(NHL)]
    C.HNi_t = [nc.dram_tensor("HNi%d" % i, [128, 8, n], BF16) for i, (t0, n) in enumerate(LBLK)]
    C.HNo_t = [nc.dram_tensor("HNo%d" % i, [256, 8, n], BF16) for i, (t0, n) in enumerate(LBLK)]
    C.HNi = [t.ap() for t in C.HNi_t]
    C.HNo = [t.ap().rearrange("(r p) k n -> r p k n", r=2) for t in C.HNo_t]
    C.HNi_res = [Res() for _ in LBLK]
    C.HNo_res = [Res() for _ in LBLK]
    C.OG1i_t = [nc.dram_tensor("OG1i%d" % g, [128, NHG, LBLK[g % NLB][1]], BF16) for g in range(2 * NLB)]
    C.OG1o_t = [nc.dram_tensor("OG1o%d" % g, [256, NHG, LBLK[g % NLB][1]], BF16) for g in range(2 * NLB)]
    C.OG1i = [t.ap() for t in C.OG1i_t]
    C.OG1o = [t.ap().rearrange("(r p) k n -> r p k n", r=2) for t in C.OG1o_t]
    C.OG1i_res = [Res() for _ in range(2 * NLB)]
    C.OG1o_res = [Res() for _ in range(2 * NLB)]
    C.H1 = dscr("H1", [HT * 128, D], F32)
    C.H1_res = [Res() for _ in LBLK]
    C.OGi = [t.ap() for t in C.OGi_t]
    C.OGo = [t.ap() for t in C.OGo_t]
    C.OGi_res = [Res() for _ in range(NHL)]
    C.OGo_res = [Res() for _ in range(NHL)]
    C.W1S = dscr("W1S", [2, NJ, 128, 8, 256], BF16)
    C.W2S = dscr("W2S", [2, NJ, 128, D], BF16)
    C.WHS = dscr("WHS", [NHG, 128, 8, 384], BF16)
    C.WHV = dscr("WHV", [1, 128, 8, 512], BF16)
    C.out = nc.dram_tensor("out", [HT * 128, D], F32, kind="ExternalOutput").ap()
    C.dbg_x1 = nc.dram_tensor("dbg_x1", [nblk, 128, 8, 512], BF16, kind="ExternalOutput").ap() if (debug and stages == 3) else None

    with contextlib.ExitStack() as st0:
        P = Prog(nc)

        def mk(st):
            def sb(name, shape, dt):
                return Buf(st.enter_context(nc.sbuf_tensor(name, list(shape), dt)), name)

            def ps(name, shape, dt):
                return Buf(st.enter_context(nc.psum_tensor(name, list(shape), dt)), name)
            return sb, ps
        sb0, ps0 = mk(st0)
        C.identb = sb0("identb", [128, 128], BF16)
        C.identf = sb0("identf", [128, 128], F32)
        C.bdones = sb0("bdones", [128, 128], BF16)
        C.tri = sb0("tri", [128, 128], BF16)
        P.dma(C.identb[:], c_identb, writes=[C.identb])
        P.dma(C.identf[:], c_identf, writes=[C.identf])
        P.dma(C.bdones[:], c_bdones, writes=[C.bdones])
        P.dma(C.tri[:], c_tri, writes=[C.tri])
        C.Wo = [sb0("Wo0", [128, 8, D], BF16), sb0("Wo1", [128, 8, D], BF16)]
        if debug:
            cpc_o = nc.dram_tensor("CPC_o", [128, NT, NHL], F32, kind="ExternalOutput").ap()
            refb_o = nc.dram_tensor("REFB_o", [128, NB, NHL], F32, kind="ExternalOutput").ap()
        with contextlib.ExitStack() as stm:
            sbm, psm = mk(stm)
            C.CPC = sbm("CPC", [128, NT, NHL], F32)
            C.REFB = sbm("REFB", [128, NB, NHL], F32)
            bounce = [sbm("bounce%d" % i, [128, 2 * FF], BF16) for i in range(2)]
            with contextlib.ExitStack() as st:
                sb, ps = mk(st)
                stage1(P, nc, C, sb, ps, bounce if conv else None)
                if debug:
                    P.dma(cpc_o, C.CPC[:], reads=[C.CPC])
                    P.dma(refb_o, C.REFB[:], reads=[C.REFB])
                P.barrier()
            if stages >= 2:
                with contextlib.ExitStack() as st:
                    sb, ps = mk(st)
                    stage2(P, nc, C, sb, ps, bounce if conv else None)
                    P.barrier()
        if stages >= 3:
            with contextlib.ExitStack() as st:
                sb, ps = mk(st)
                stage3(P, nc, C, sb, ps)
                P.barrier()
        P.emit()
    return nc


def host_consts():
    bf = ml_dtypes.bfloat16
    idx = np.arange(128)
    return {
        "c_identb": np.eye(128, dtype=np.float32).astype(bf),
        "c_identf": np.eye(128, dtype=np.float32),
        "c_bdones": (idx[:, None] // 64 == idx[None, :] // 64).astype(np.float32).astype(bf),
        "c_tri": (idx[:, None] <= idx[None, :]).astype(np.float32).astype(bf),
    }


WNAMES = ["attn_norm", "ffn_norm", "final_norm", "fox_w_in", "fox_b_f", "fox_q_norm", "fox_k_norm", "fox_w_out",
          "hgrn_w_in", "hgrn_lower_bounds", "hgrn_g_norm", "hgrn_w_out", "ffn_w_in", "ffn_w_out"]


def make_in_maps(inputs, ncores=8):
    x = np.asarray(inputs["x"], dtype=np.float32)
    meta = np.asarray(inputs["meta_tokens"], dtype=np.float32)
    consts = host_consts()
    maps = []
    NHL = H // 2
    for c in range(ncores):
        b, r = c // 2, c % 2
        h0 = np.zeros((LP, D), np.float32)
        h0[NPAD:NPAD + NMETA] = meta
        h0[NPAD + NMETA:] = x[b]
        h0h = np.zeros((HT * 128, D), np.float32)
        seg = h0[r * HT * 128:(r + 1) * HT * 128]
        h0h[:seg.shape[0]] = seg
        m = {"h0": h0, "h0h": h0h, "rmask": np.full((128, 8 * 512), 0xFFFF if r else 0, np.uint16),
             "rmaskf": np.full((128, 1), float(r), np.float32)}
        for k in WNAMES:
            a = np.asarray(inputs[k], dtype=np.float32)
            if k in ("fox_w_in", "fox_w_out", "hgrn_w_in", "hgrn_w_out"):
                a = a.reshape(a.shape[-2], a.shape[-1])
            if k == "fox_w_in":
                w = NHL * DH
                a = np.concatenate([a[:, s0 + r * w:s0 + (r + 1) * w] for s0 in (0, D, 2 * D, 3 * D)]
                                   + [a[:, 4 * D + r * NHL:4 * D + (r + 1) * NHL]], axis=1)
            if k == "fox_b_f":
                a = a[:, r * NHL:(r + 1) * NHL]
            if k == "hgrn_w_in":
                w = NHG * 128
                a = np.concatenate([a[:, s0 + r * w:s0 + (r + 1) * w] for s0 in (0, D, 2 * D, 3 * D)], axis=1)
            if k == "hgrn_lower_bounds":
                a = a[:, r * NHG * 128:(r + 1) * NHG * 128]
            m[k] = np.ascontiguousarray(a)
        m.update(consts)
        maps.append(m)
    return maps


def kernel(**inputs):
    nc = build(stages=3)
    maps = make_in_maps(inputs, 8)
    res = run_bass_kernel_spmd(nc, maps, core_ids=list(range(8)))
    outs = []
    for b in range(4):
        o0 = np.asarray(res.results[2 * b]["out"], dtype=np.float32)
        o1 = np.asarray(res.results[2 * b + 1]["out"], dtype=np.float32)
        outs.append(np.concatenate([o0[NPAD + NMETA:], o1[:SEQ - (HT * 128 - NPAD - NMETA)]], axis=0))
    return np.stack(outs, axis=0)


class Stream:
    def __init__(self, P, bufs, reqs, q=SP, keep=0):
        self.P, self.bufs, self.reqs, self.q, self.keep = P, bufs, reqs, q, keep
        self.issued = 0

    def get(self, i):
        lim = min(len(self.reqs), i + len(self.bufs) - self.keep)
        while self.issued < lim:
            k = self.issued
            b = self.bufs[k % len(self.bufs)]
            dst, src = self.reqs[k]
            self.P.dma(dst(b), src, writes=[b], q=self.q)
            self.issued += 1
        return self.bufs[i % len(self.bufs)]


def convert_weights(P, nc, C, bounce, gate):
    k = [0]

    def ld(src, width):
        b = bounce[k[0] % 2]
        k[0] += 1
        P.dma(b[:, 0:width], src, reads=[gate[0]], writes=[b], q=POOL)
        return b
    for l in range(2):
        for kc in range(8):
            b = ld(C.ffn_w_in[l, kc * 128:(kc + 1) * 128, :], 2 * FF)
            for hf in range(2):
                P.dma(C.W1S[l, :, :, kc, hf * 128:(hf + 1) * 128].rearrange("j p c -> p j c"),
                      b[:, hf * FF:(hf + 1) * FF].rearrange("p (j c) -> p j c", c=128), reads=[b], q=POOL)
            yield
        wo = C.ffn_w_out[l].rearrange("(j p) c -> p j c", p=128)
        for j0 in range(0, NJ, 5):
            j1 = min(NJ, j0 + 5)
            b = ld(wo[:, j0:j1, :], (j1 - j0) * D)
            P.dma(C.W2S[l, j0:j1].rearrange("j p c -> p j c"),
                  b[:, 0:(j1 - j0) * D].rearrange("p (j c) -> p j c", c=D), reads=[b], q=POOL)
            yield
    GW = NHG * 128
    for kc in range(8):
        b = ld(C.hgrn_w_in[kc * 128:(kc + 1) * 128, :], 4 * GW)
        for gi, base in enumerate((0, GW, 3 * GW)):
            P.dma(C.WHS[:, :, kc, gi * 128:(gi + 1) * 128].rearrange("h p c -> p h c"),
                  b[:, base:base + GW].rearrange("p (h c) -> p h c", c=128), reads=[b], q=POOL)
        P.dma(C.WHV[0, :, kc, :], b[:, 2 * GW:3 * GW], reads=[b], q=POOL)
        yield


NHG = 4
HT = 33
LBLK = [(0, 128)] + [(128 + 512 * i, 512) for i in range(8)]
NLB = len(LBLK)


def stage3(P, nc, C, sb, ps):
    B = [ps("s3_B%d" % i, [128, 512], F32) for i in range(6)]
    Tt = [ps("s3_T%d" % i, [128, 1024], BF16) for i in range(2)]
    for i in range(2):
        v = Buf(Tt[i].t[:, :].bitcast(F32), "B%df" % (6 + i))
        v.res = Tt[i].res
        B.append(v)

    def bres(i):
        return [B[i].res]

    prow = sb("s3_prow", [32, 128], F32)
    P.dma(prow[0:8, :], C.lb.rearrange("r (h p) -> (r h) p", p=128), writes=[prow])
    P.dma(prow[8:16, :], C.attn_norm[1].rearrange("(c p) -> c p", p=128), writes=[prow])
    P.dma(prow[16:24, :], C.ffn_norm[0].rearrange("(c p) -> c p", p=128), writes=[prow])
    P.dma(prow[24:32, :], C.ffn_norm[1].rearrange("(c p) -> c p", p=128), writes=[prow])
    pcol = sb("s3_pcol", [128, 32], F32)
    P.add(PE, lambda e: e.transpose(out=B[5][:, 0:32], in_=prow[0:32, :], identity=C.identf[0:32, 0:32]),
          [prow, C.identf], bres(5))
    _copy(P, DVE, pcol[:], B[5][:, 0:32], bres(5), [pcol])
    omlb = sb("s3_omlb", [128, NHG], F32)
    _tt(P, DVE, omlb[:], pcol[:, 4:8], pcol[:, 0:4], ALU.subtract, [pcol], [omlb])
    _act(P, omlb[:], omlb[:], AF.Exp, [omlb], [omlb])
    _ts(P, DVE, omlb[:], omlb[:], 1.0, None, ALU.add, None, [omlb], [omlb])
    P.add(DVE, lambda e: e.reciprocal(out=omlb[:], in_=omlb[:]), [omlb], [omlb])
    lnomlb = sb("s3_lnomlb", [128, NHG], F32)
    _act(P, lnomlb[:], omlb[:], AF.Ln, [omlb], [lnomlb])
    gcols = {"attn1": pcol[:, 8:16], "ffn0": pcol[:, 16:24], "ffn1": pcol[:, 24:32]}
    gn = sb("s3_gn", [128, 1], F32)
    P.dma(gn[:], C.g_norm[0].rearrange("(p o) -> p o", o=1), writes=[gn])
    gfin = sb("s3_gfin", [128, D], F32)
    P.dma(gfin[:], C.final_norm.partition_broadcast(128), writes=[gfin])
    mh = sb("s3_mh", [128, 1], F32)
    P.add(POOL, lambda e: e.memset(mh[:], -0.5), [], [mh])
    ones128b = sb("s3_ones128b", [128, 128], BF16)
    P.add(DVE, lambda e: e.memset(ones128b[:], 1.0), [], [ones128b])
    onesf = sb("s3_onesf", [128, 128], F32)
    P.add(DVE, lambda e: e.memset(onesf[:], 1.0), [], [onesf])
    rmk = sb("s3_rmk", [128, 512], mybir.dt.uint16)
    P.dma(rmk[:], C.rmask[:, 0:512], writes=[rmk])
    rmf = sb("s3_rmf", [128, 1], F32)
    P.dma(rmf[:], C.rmaskf, writes=[rmf])
    junk = sb("s3_junk", [128, D], BF16)
    nrm = [[sb("s3_nrm%d_%d" % (i, j), [128, 1], F32) for j in range(3)] for i in range(4)]
    cnt = {"y": 0, "f": 0, "T": 0, "n": 0, "h": 0}

    def ybank():
        cnt["y"] += 1
        return B[cnt["y"] % 4]

    def fbank():
        cnt["f"] += 1
        return B[cnt["f"] % 6]

    def hbank():
        cnt["h"] += 1
        return B[cnt["h"] % 3]

    def norm_rows(ht):
        ssq, tt, rstd = nrm[cnt["n"] % 4]
        cnt["n"] += 1
        _act(P, junk[:], ht[:], AF.Square, [ht], [junk, ssq], accum_out=ssq[:])
        _ts(P, DVE, tt[:], ssq[:], 1.0 / D, EPS, ALU.mult, ALU.add, [ssq], [tt])
        _tt(P, POOL, rstd[:], tt[:], mh[:], ALU.pow, [tt, mh], [rstd])
        return rstd

    def norm_A(hs, hns, nt):
        for tl in range(nt):
            rstd = norm_rows(hs[tl])
            _ts(P, DVE, hns[tl][:], hs[tl][:], rstd[:, 0:1], None, ALU.mult, None, [hs[tl], rstd], [hns[tl]])

    def norm_B(hns, hnT, nt, gcol):
        n = nt * 128
        for c in range(8):
            half = cnt["T"] % 2
            cnt["T"] += 1
            pT = Tt[half][:, 0:512]
            res = Tt[half].res

            def fn(e, c=c, pT=pT):
                ins = None
                for tl in range(nt):
                    ins = e.transpose(out=pT[:, tl * 128:(tl + 1) * 128], in_=hns[tl][:, c * 128:(c + 1) * 128],
                                      identity=C.identb[:])
                return ins
            P.add(PE, fn, list(hns[:nt]) + [C.identb], [res])
            if c % 2 == 0:
                _ts(P, DVE, hnT[:, c, 0:n], pT[:, 0:n], gcol[:, c:c + 1], None, ALU.mult, None, [res, pcol], [hnT.sub(c)])
            else:
                P.add(ACT, lambda e, c=c, pT=pT: e.mul(out=hnT[:, c, 0:n], in_=pT[:, 0:n], mul=gcol[:, c:c + 1]),
                      [res, pcol], [hnT.sub(c)])

    def ag(in_t, out_t, rin, rout):
        P.cc(lambda e: e.collective_compute("AllGather", ALU.bypass, replica_groups=C.RG,
                                            ins=[in_t.ap().opt()], outs=[out_t.ap().opt()]),
             reads=[rin], writes=[rout])

    def scope():
        st = contextlib.ExitStack()
        return st, (lambda name, shape, dt: Buf(st.enter_context(nc.sbuf_tensor(name, list(shape), dt)), name))

    def ffn_phase(l):
        st, sbp = scope()
        with st:
            pre = "p%d_" % (1 + 2 * l)
            Wo = C.Wo[l]
            hs2 = [[sbp(pre + "h%d_%d" % (j, i), [128, D], F32) for i in range(4)] for j in range(2)]
            X0s = [sbp(pre + "X0_%d" % j, [128, 8, 512], BF16) for j in range(2)]
            Xcs = [sbp(pre + "Xc%d" % i, [128, 512], BF16) for i in range(2)]
            hns = [sbp(pre + "hn%d" % i, [128, D], BF16) for i in range(4)]
            hnT = sbp(pre + "hnT", [128, 8, 512], BF16)
            hnT1 = sbp(pre + "hnT1", [128, 8, 512], BF16) if l == 0 else None
            actT = sbp(pre + "actT", [128, NJ, 512], BF16)
            sils = [sbp(pre + "sil%d" % i, [128, 512], F32) for i in range(2)]
            W1b = [sbp(pre + "W1b%d" % i, [128, 2, 8, 256], BF16) for i in range(3)]
            W2b = [sbp(pre + "W2b%d" % i, [128, 2, D], BF16) for i in range(3)]
            orow = [sbp(pre + "orow%d" % i, [128, D], F32) for i in range(2)] if l == 1 else None
            NJP = NJ // 2
            S1 = Stream(P, W1b, [(lambda b: b[:], C.W1S[l, 2 * jp:2 * jp + 2].rearrange("j p k c -> p j k c"))
                                 for lb in range(NLB) for jp in range(NJP)])
            S2 = Stream(P, W2b, [(lambda b: b[:], C.W2S[l, 2 * jp:2 * jp + 2].rearrange("j p c -> p j c"))
                                 for lb in range(NLB) for jp in range(NJP)])
            S1.get(0)
            S2.get(0)
            hnTr = [hnT.sub(c) for c in range(8)]
            blocks = LBLK[:C.nblk]
            oc = [0]

            def stA(i):
                t0, n = blocks[i]
                nt = n // 128
                hs, X0 = hs2[i % 2], X0s[i % 2]
                for tl in range(nt):
                    if l == 0:
                        P.dma(hs[tl][:], C.h0h[t0 + tl * 128:t0 + (tl + 1) * 128, :], writes=[hs[tl]])
                    else:
                        P.dma(hs[tl][:], C.H1[t0 + tl * 128:t0 + (tl + 1) * 128, :], reads=[C.H1_res[i]], writes=[hs[tl]])
                for kc in range(8):
                    Xc = Xcs[kc % 2]
                    if l == 0:
                        rk, hl = kc // 4, 2 * (kc % 4)
                        for hh in range(2):
                            src = C.OGo[hl + hh][64 * rk:64 * rk + 64, :]
                            P.dma(X0[64 * hh:64 * hh + 64, kc, 0:n], src[:, t0:t0 + n], reads=[C.OGo_res[hl + hh]],
                                  writes=[X0.sub(kc)])
                            P.dma(Xc[64 * hh:64 * hh + 64, 0:n], src[:, HT * 128 + t0:HT * 128 + t0 + n],
                                  reads=[C.OGo_res[hl + hh]], writes=[Xc])
                    else:
                        rk, hd = kc // 4, kc % 4
                        P.dma(X0[:, kc, 0:n], C.OG1o[i][rk, :, hd, :], reads=[C.OG1o_res[i]], writes=[X0.sub(kc)])
                        P.dma(Xc[:, 0:n], C.OG1o[NLB + i][rk, :, hd, :], reads=[C.OG1o_res[NLB + i]], writes=[Xc])
                    P.add(DVE, lambda e, n=n, kc=kc, Xc=Xc, X0=X0: e.copy_predicated(
                        out=X0[:, kc, 0:n], mask=rmk[:, 0:n], data=Xc[:, 0:n]), [Xc, rmk, X0.sub(kc)], [X0.sub(kc)])

            def stW(i):
                t0, n = blocks[i]
                hs, X = hs2[i % 2], X0s[i % 2]
                for tl in range(n // 128):
                    for hf in range(2):
                        y = ybank()
                        _mm(P, y[:, :], [(X[:, kc, tl * 128:(tl + 1) * 128], Wo[:, kc, hf * 512:(hf + 1) * 512])
                                         for kc in range(8)], [X.sub(kc) for kc in range(8)] + [Wo], [y])
                        _tt(P, DVE, hs[tl][:, hf * 512:(hf + 1) * 512], hs[tl][:, hf * 512:(hf + 1) * 512], y[:, :], ALU.add,
                            [hs[tl], y], [hs[tl]])

            hns1 = [sbp(pre + "hn1_%d" % i, [128, D], BF16) for i in range(4)] if l == 0 else None

            def stNa(i):
                t0, n = blocks[i]
                norm_A(hs2[i % 2], hns, n // 128)

            def stNb(i):
                t0, n = blocks[i]
                norm_B(hns, hnT, n // 128, gcols["ffn%d" % l])

            def stCin(i):
                t0, n = blocks[i]
                for j in range(NJ):
                    w = S1.get(i * NJP + j // 2)
                    jj = j % 2
                    g, u = fbank(), fbank()
                    _mm(P, g[:, 0:n], [(w[:, jj, kc, 0:128], hnT[:, kc, 0:n]) for kc in range(8)], [w] + hnTr, [g])
                    _mm(P, u[:, 0:n], [(w[:, jj, kc, 128:256], hnT[:, kc, 0:n]) for kc in range(8)], [w] + hnTr, [u])
                    sl = sils[j % 2]
                    _act(P, sl[:, 0:n], g[:, 0:n], AF.Silu, [g], [sl])
                    _tt(P, DVE, actT[:, j, 0:n], sl[:, 0:n], u[:, 0:n], ALU.mult, [sl, u], [actT.sub(j)])

            def stCout(i):
                t0, n = blocks[i]
                nt = n // 128
                hs = hs2[i % 2]
                for j in range(NJ):
                    w = S2.get(i * NJP + j // 2)

                    def fn(e, j=j, w=w):
                        ins = None
                        for tl in range(nt):
                            for hf in range(2):
                                ins = e.matmul(B[tl * 2 + hf][:, :], lhsT=actT[:, j, tl * 128:(tl + 1) * 128],
                                               rhs=w[:, j % 2, hf * 512:(hf + 1) * 512], start=(j == 0), stop=(j == NJ - 1))
                        return ins
                    P.add(PE, fn, [w, actT.sub(j)], sum([bres(k) for k in range(2 * nt)], []))
                for k in sorted(range(2 * nt), key=lambda k: -k):
                    tl, hf = k // 2, k % 2
                    _tt(P, DVE, hs[tl][:, hf * 512:(hf + 1) * 512], hs[tl][:, hf * 512:(hf + 1) * 512],
                        B[k][:, :], ALU.add, [hs[tl]] + bres(k), [hs[tl]])

            def stDa(i):
                t0, n = blocks[i]
                nt = n // 128
                hs = hs2[i % 2]
                if l == 0:
                    for tl in range(nt):
                        P.dma(C.H1[t0 + tl * 128:t0 + (tl + 1) * 128, :], hs[tl][:], reads=[hs[tl]], writes=[C.H1_res[i]])
                    norm_A(hs, hns1, nt)
                else:
                    for tl in range(nt):
                        rstd = norm_rows(hs[tl])
                        ob = orow[oc[0] % 2]
                        oc[0] += 1
                        _stt(P, ob[:], hs[tl][:], rstd[:, 0:1], gfin[:], ALU.mult, ALU.mult, [hs[tl], rstd, gfin], [ob])
                        P.dma(C.out[t0 + tl * 128:t0 + (tl + 1) * 128, :], ob[:], reads=[ob])

            def stDb(i):
                if l != 0:
                    return
                t0, n = blocks[i]
                nt = n // 128
                norm_B(hns1, hnT1, nt, gcols["attn1"])
                h1r = [hnT1.sub(c) for c in range(8)]
                if i == 0:
                    _ts(P, DVE, hnT1[:, :, 0:NPAD], hnT1[:, :, 0:NPAD], rmf[:, 0:1], None, ALU.mult, None, h1r + [rmf], h1r)
                P.dma(C.HNi[i], hnT1[:, :, 0:n], reads=h1r, writes=[C.HNi_res[i]])
                ag(C.HNi_t[i], C.HNo_t[i], C.HNi_res[i], C.HNo_res[i])

            nb = len(blocks)
            stA(0)
            stW(0)
            stNa(0)
            stNb(0)
            for i in range(nb):
                if i + 1 < nb:
                    stA(i + 1)
                stCin(i)
                if i > 0:
                    stDb(i - 1)
                if i + 1 < nb:
                    stW(i + 1)
                    stNa(i + 1)
                stCout(i)
                if i + 1 < nb:
                    stNb(i + 1)
                stDa(i)
            stDb(nb - 1)
            P.barrier()

    def hgrn_phase():
        st, sbp = scope()
        with st:
            hnTs = [sbp("p2_hnT%d" % i, [128, 8, 512], BF16) for i in range(2)]
            X1s = [sbp("p2_X1_%d" % i, [128, NHG, 512], BF16) for i in range(2)]
            WHb = [sbp("p2_WHb%d" % i, [128, 8, 512], BF16) for i in range(4)]
            silq = sbp("p2_silq", [128, NHG, 512], BF16)
            sgT = sbp("p2_sgT", [128, NHG, 512], BF16)
            Vsb2 = [[sbp("p2_V%d_%d" % (j, i), [128, NHG * 128], BF16) for i in range(4)] for j in range(2)]
            NBUF = 4
            mkb = lambda nm, dt: [sbp("p2_%s%d" % (nm, i), [128, 512], dt) for i in range(NBUF)]
            qts, kts, khs = mkb("qt", BF16), mkb("kt", BF16), mkb("kh", BF16)
            khTs = [sbp("p2_khT%d" % i, [128, 4, 128], BF16) for i in range(NBUF)]
            ezs, kks, ebs = mkb("ez", F32), mkb("kk", F32), mkb("eb", F32)
            mk2 = lambda nm, dt: [sbp("p2_%s%d" % (nm, i), [128, 512], dt) for i in range(2)]
            bbs, enbs, lnfs, ebcs = mk2("bb", F32), mk2("enb", F32), mk2("lnf", F32), mk2("ebc", F32)
            Asb4s = [sbp("p2_Asb4%d" % i, [128, 4, 128], BF16) for i in range(NBUF)]
            Usbs = mkb("Usb", F32)
            Sst = sbp("p2_S", [128, NHG, 128], F32)
            P.add(DVE, lambda e: e.memset(Sst[:], 0.0), [], [Sst])
            osqs = [sbp("p2_osq%d" % i, [128, 512], BF16) for i in range(2)]
            ons = [sbp("p2_on%d" % i, [128, 512], F32) for i in range(2)]
            rh = []
            for gb in range(2 * NLB):
                rh.append((lambda b: b[:], C.WHV[0]))
                for hd in range(NHG):
                    rh.append((lambda b: b[:, :, 0:384], C.WHS[hd]))
            SH = Stream(P, WHb, rh, keep=1)
            items = [(gb, hd) for gb in range(2 * NLB) for hd in range(NHG)]

            def geo(gb):
                rk, lb = gb // NLB, gb % NLB
                t0, n = LBLK[lb]
                return rk, lb, n, n // 128

            def h_front(i):
                gb, hd = items[i]
                rk, lb, n, nt = geo(gb)
                hnT = hnTs[gb % 2]
                hr = [hnT.sub(c) for c in range(8)]
                Vs = Vsb2[gb % 2]
                if hd == 0:
                    P.dma(hnT[:, :, 0:n], C.HNo[lb][rk], reads=[C.HNo_res[lb]], writes=hr, q=POOL)
                    w = SH.get(gb * (NHG + 1))
                    for tl in range(nt):
                        y = ybank()
                        _mm(P, y[:, :], [(hnT[:, kc, tl * 128:(tl + 1) * 128], w[:, kc, 0:512]) for kc in range(8)], [w] + hr, [y])
                        _copy(P, ACT if tl % 2 == 0 else DVE, Vs[tl][:, :], y[:, :], [y], [Vs[tl]])
                w = SH.get(gb * (NHG + 1) + 1 + hd)
                k3 = i % NBUF
                ez, kk, eb = ezs[k3], kks[k3], ebs[k3]
                lnf, bb, enb, ebc = [x[i % 2] for x in (lnfs, bbs, enbs, ebcs)]
                qt, kt, kh, khT, Asb4, Usb = qts[k3], kts[k3], khs[k3], khTs[k3], Asb4s[k3], Usbs[k3]
                qp, gp = hbank(), hbank()
                _mm(P, qp[:, 0:n], [(w[:, kc, 0:128], hnT[:, kc, 0:n]) for kc in range(8)], [w] + hr, [qp])
                _mm(P, gp[:, 0:n], [(w[:, kc, 256:384], hnT[:, kc, 0:n]) for kc in range(8)], [w] + hr, [gp])
                _act(P, silq[:, hd, 0:n], qp[:, 0:n], AF.Silu, [qp], [silq.sub(hd)])
                yield
                _act(P, sgT[:, hd, 0:n], gp[:, 0:n], AF.Silu, [gp], [sgT.sub(hd)])
                yield
                zp = hbank()
                _mm(P, zp[:, 0:n], [(w[:, kc, 128:256], hnT[:, kc, 0:n]) for kc in range(8)], [w] + hr, [zp])
                _act(P, ez[:, 0:n], zp[:, 0:n], AF.Exp, [zp], [ez])
                yield
                _act(P, ez[:, 0:n], ez[:, 0:n], AF.Ln, [ez], [ez], bias=1.0)
                yield
                _act(P, kk[:, 0:n], ez[:, 0:n], AF.Exp, [ez, lnomlb], [kk], scale=-1.0, bias=lnomlb[:, hd:hd + 1])
                yield
                _act(P, lnf[:, 0:n], kk[:, 0:n], AF.Ln, [kk], [lnf], scale=-1.0, bias=1.0)
                yield
                for c in range(nt):
                    P.add(DVE, lambda e, c=c: e.tensor_tensor_scan(
                        out=bb[:, c * 128:(c + 1) * 128], data0=onesf[:, 0:128], data1=lnf[:, c * 128:(c + 1) * 128],
                        initial=0.0, op0=ALU.mult, op1=ALU.add), [onesf, lnf], [bb])
                _act(P, eb[:, 0:n], bb[:, 0:n], AF.Exp, [bb], [eb])
                yield
                _act(P, enb[:, 0:n], bb[:, 0:n], AF.Exp, [bb], [enb], scale=-1.0)
                yield
                for c in range(nt):
                    _act(P, ebc[:, c * 128:(c + 1) * 128], bb[:, c * 128:(c + 1) * 128], AF.Exp, [bb], [ebc], scale=-1.0,
                         bias=bb[:, c * 128 + 127:c * 128 + 128])
                _tt(P, DVE, qt[:, 0:n], silq[:, hd, 0:n], eb[:, 0:n], ALU.mult, [silq.sub(hd), eb], [qt])
                yield
                _tt(P, DVE, kt[:, 0:n], kk[:, 0:n], enb[:, 0:n], ALU.mult, [kk, enb], [kt])
                yield
                _tt(P, DVE, kh[:, 0:n], kk[:, 0:n], ebc[:, 0:n], ALU.mult, [kk, ebc], [kh])
                yield
                pT = Tt[0][:, 0:512]
                res = Tt[0].res

                def fnT(e):
                    ins = None
                    for c in range(nt):
                        ins = e.transpose(out=pT[:, c * 128:(c + 1) * 128], in_=kh[:, c * 128:(c + 1) * 128], identity=C.identb[:])
                    return ins
                P.add(PE, fnT, [kh, C.identb], [res])
                _copy(P, ACT, khT[:, 0:nt, :], pT[:, 0:n].rearrange("p (c s) -> p c s", s=128), [res], [khT])
                yield
                Ab, Ub = B[5], B[7]

                def fnA(e):
                    ins = None
                    for c in range(nt):
                        cs = slice(c * 128, (c + 1) * 128)
                        ins = e.matmul(Ab[:, cs], lhsT=kt[:, cs], rhs=qt[:, cs], start=True, stop=True)
                    return ins
                P.add(PE, fnA, [kt, qt], [Ab])
                P.add(DVE, lambda e: e.tensor_tensor(
                    out=Asb4[:, 0:nt, :], in0=Ab[:, 0:n].rearrange("p (c s) -> p c s", s=128),
                    in1=C.tri[:, :].unsqueeze(1).to_broadcast([128, nt, 128]), op=ALU.mult), [Ab, C.tri], [Asb4])

                def fnU(e):
                    ins = None
                    for c in range(nt):
                        cs = slice(c * 128, (c + 1) * 128)
                        ins = e.matmul(Ub[:, cs], lhsT=khT[:, c, :], rhs=Vs[c][:, hd * 128:(hd + 1) * 128], start=True, stop=True)
                    return ins
                P.add(PE, fnU, [khT] + list(Vs[:nt]), [Ub])
                _copy(P, ACT, Usb[:, 0:n], Ub[:, 0:n], [Ub], [Usb])
                yield

            Sbf4s = [sbp("p2_Sbf4%d" % i, [128, 4, 128], BF16) for i in range(NBUF)]

            def h_state(i):
                gb, hd = items[i]
                rk, lb, n, nt = geo(gb)
                k3 = i % NBUF
                eb, Usb, Sbf4 = ebs[k3], Usbs[k3], Sbf4s[k3]
                for c in range(nt):
                    cs = slice(c * 128, (c + 1) * 128)
                    _copy(P, DVE, Sbf4[:, c, :], Sst[:, hd, :], [Sst.sub(hd)], [Sbf4])
                    _stt(P, Sst[:, hd, :], Sst[:, hd, :], eb[:, c * 128 + 127:c * 128 + 128], Usb[:, cs], ALU.mult, ALU.add,
                         [Sst.sub(hd), eb, Usb], [Sst.sub(hd)])

            def h_back(i):
                gb, hd = items[i]
                rk, lb, n, nt = geo(gb)
                Vs = Vsb2[gb % 2]
                k3 = i % NBUF
                qt, Asb4, Sbf4 = qts[k3], Asb4s[k3], Sbf4s[k3]
                X1 = X1s[gb % 2]
                osq, on = osqs[i % 2], ons[i % 2]
                op = B[3 + hd % 2]

                def fnO(e):
                    ins = None
                    for c in range(nt):
                        cs = slice(c * 128, (c + 1) * 128)
                        e.matmul(op[:, cs], lhsT=Vs[c][:, hd * 128:(hd + 1) * 128], rhs=Asb4[:, c, :], start=True, stop=False)
                        ins = e.matmul(op[:, cs], lhsT=Sbf4[:, c, :], rhs=qt[:, cs], start=False, stop=True)
                    return ins
                P.add(PE, fnO, [Sbf4, qt, Asb4] + list(Vs[:nt]), [op])
                yield
                _act(P, osq[:, 0:n], op[:, 0:n], AF.Square, [op], [osq])
                yield
                sp = hbank()
                _mm(P, sp[:, 0:n], [(ones128b[:], osq[:, 0:n])], [ones128b, osq], [sp])
                lnv, rs = ezs[k3], kks[k3]
                _act(P, lnv[:, 0:n], sp[:, 0:n], AF.Ln, [sp], [lnv], scale=1.0 / 128, bias=EPS)
                yield
                _act(P, rs[:, 0:n], lnv[:, 0:n], AF.Exp, [lnv], [rs], scale=-0.5)
                yield
                _stt(P, on[:, 0:n], op[:, 0:n], gn[:, 0:1], rs[:, 0:n], ALU.mult, ALU.mult, [op, gn, rs], [on])
                yield
                _tt(P, DVE, X1[:, hd, 0:n], on[:, 0:n], sgT[:, hd, 0:n], ALU.mult, [on, sgT.sub(hd)], [X1])
                yield
                if hd == NHG - 1:
                    P.dma(C.OG1i[gb], X1[:, :, 0:n], reads=[X1], writes=[C.OG1i_res[gb]])
                    ag(C.OG1i_t[gb], C.OG1o_t[gb], C.OG1i_res[gb], C.OG1o_res[gb])

            def zipped(gens):
                gens = list(gens)
                if DBG == "seq":
                    for g in gens:
                        for _ in g:
                            pass
                    return
                while gens:
                    for g in list(gens):
                        try:
                            next(g)
                        except StopIteration:
                            gens.remove(g)

            N = len(items)
            npair = N // 2
            zipped([h_front(0), h_front(1)])
            h_state(0)
            h_state(1)
            for t in range(npair):
                gens = [h_back(2 * t), h_back(2 * t + 1)]
                if t + 1 < npair:
                    gens = [h_front(2 * t + 2), h_front(2 * t + 3)] + gens
                zipped(gens)
                if t + 1 < npair:
                    h_state(2 * t + 2)
                    h_state(2 * t + 3)
            P.barrier()

    ffn_phase(0)
    if C.s3parts >= 2:
        hgrn_phase()
    if C.s3parts >= 3:
        ffn_phase(1)
```

```python
import numpy as np
import concourse.bass as bass
import concourse.mybir as mybir
from concourse.bass_utils import run_bass_kernel_spmd

F32 = mybir.dt.float32
BF16 = mybir.dt.bfloat16
AF = mybir.ActivationFunctionType
ALU = mybir.AluOpType
AX = mybir.AxisListType

PE, ACT, DVE, POOL, SP = "pe", "act", "dve", "pool", "sp"
ENGS = (PE, ACT, DVE, POOL, SP)
NDMA_SEM = 12


class Res:
    __slots__ = ("w", "r")

    def __init__(self):
        self.w = None
        self.r = []


class Buf:
    def __init__(self, t, name=""):
        self.t = t
        self.name = name
        self.res = Res()
        self.subs = {}

    def __getitem__(self, k):
        return self.t[k]

    def sub(self, key):
        r = self.subs.get(key)
        if r is None:
            r = self.subs[key] = Res()
        return r


def _res(x):
    return x.res if isinstance(x, Buf) else x


class Op:
    __slots__ = ("eng", "fn", "raw", "oth", "dma", "sigval", "need", "dsem", "dval", "dprev", "pos")


class Prog:
    def __init__(self, nc):
        self.nc = nc
        self.ops = {e: [] for e in ENGS}
        self.ndma = {e: 0 for e in ENGS}

    def add(self, eng, fn, reads=(), writes=(), dma=False):
        op = Op()
        op.eng, op.fn, op.dma = eng, fn, dma
        op.raw, op.oth = set(), set()
        op.need = False
        op.sigval = None
        for r in reads:
            r = _res(r)
            if r.w is not None:
                op.raw.add(r.w)
        for w in writes:
            w = _res(w)
            if w.w is not None:
                op.oth.add(w.w)
            for x in w.r:
                op.oth.add(x)
        for r in reads:
            _res(r).r.append(op)
        for w in writes:
            w = _res(w)
            w.w = op
            w.r = []
        if dma:
            i = self.ndma[eng]
            self.ndma[eng] += 1
            op.dsem = i % NDMA_SEM
            op.dval = 16 * (i // NDMA_SEM + 1)
        self.ops[eng].append(op)
        return op

    def cc(self, fn, reads=(), writes=()):
        op = self.add(POOL, fn, reads, writes, dma=True)
        self.ndma[POOL] -= 1
        self.ncc = getattr(self, "ncc", 0) + 1
        op.dsem = "cc"
        op.dval = self.ncc
        return op

    def barrier(self):
        last = {}
        for e in ENGS:
            lc = None
            for o in reversed(self.ops[e]):
                if isinstance(o, Op) and not o.dma:
                    lc = o
                    break
            if lc is not None:
                lc.need = True
            last[e] = lc
        mark = ("barrier", last, dict(self.ndma), getattr(self, "ncc", 0))
        for e in ENGS:
            self.ops[e].append(mark)

    def dma(self, out, in_, reads=(), writes=(), q=SP, **kw):
        return self.add(q, lambda e: e.dma_start(out=out, in_=in_, **kw), reads, writes, dma=True)

    def _needed(self, o, d):
        if d.dma:
            return True
        if d.eng != o.eng:
            return True
        if o.dma:
            return True
        if o.eng == PE:
            return False
        return d in o.raw

    def _deps(self, o):
        best = {}
        out = []
        for d in list(o.raw) + list(o.oth):
            if not self._needed(o, d):
                continue
            if d.dma:
                out.append(d)
            else:
                b = best.get(d.eng)
                if b is None or d.pos > b.pos:
                    best[d.eng] = d
        return out + list(best.values())

    def emit(self):
        nc = self.nc
        for e in ENGS:
            for k, o in enumerate(self.ops[e]):
                if isinstance(o, Op):
                    o.pos = k
        for e in ENGS:
            for o in self.ops[e]:
                if not isinstance(o, Op):
                    continue
                for d in self._deps(o):
                    if not d.dma:
                        d.need = True
        for e in ENGS:
            c = 0
            for o in self.ops[e]:
                if isinstance(o, Op) and (not o.dma) and o.need:
                    c += 1
                    o.sigval = c
        import contextlib
        with contextlib.ExitStack() as st:
            esem = {e: st.enter_context(nc.semaphore("s_" + e)) for e in ENGS}
            dsem = {e: [st.enter_context(nc.semaphore("d_%s%d" % (e, i))) for i in range(NDMA_SEM)]
                    for e in ENGS if self.ndma[e] > 0}
            ccsem = st.enter_context(nc.semaphore("s_cc"))
            block = st.enter_context(nc.Block())

            def run(ename, eng):
                seen = {}

                def wait(sem, val):
                    k = id(sem)
                    if seen.get(k, 0) >= val:
                        return
                    seen[k] = val
                    eng.wait_ge(sem, val)

                for o in self.ops[ename]:
                    if not isinstance(o, Op):
                        _, last, nd, ncc = o
                        if ncc > 0:
                            wait(ccsem, ncc)
                        for e2 in ENGS:
                            if last[e2] is not None:
                                wait(esem[e2], last[e2].sigval)
                            n = nd[e2]
                            for i in range(NDMA_SEM):
                                cnt = (n - i + NDMA_SEM - 1) // NDMA_SEM if n > i else 0
                                if cnt > 0:
                                    wait(dsem[e2][i], 16 * cnt)
                        continue
                    for d in self._deps(o):
                        if d.dma and d.dsem == "cc":
                            wait(ccsem, d.dval)
                        elif d.dma:
                            wait(dsem[d.eng][d.dsem], d.dval)
                        else:
                            wait(esem[d.eng], d.sigval)
                    if o.dma and o.dsem == "cc":
                        if o.dval > 1:
                            wait(ccsem, o.dval - 1)
                        o.fn(eng).then_inc(ccsem)
                    elif o.dma:
                        s = dsem[ename][o.dsem]
                        if o.dval > 16:
                            wait(s, o.dval - 16)
                        o.fn(eng).then_inc(s, 16)
                    else:
                        ins = o.fn(eng)
                        if o.need:
                            ins.then_inc(esem[ename], 1)
                if ename == POOL and getattr(self, "ncc", 0) > 0:
                    wait(ccsem, self.ncc)
                if self.ndma[ename] > 0:
                    n = self.ndma[ename]
                    for i in range(NDMA_SEM):
                        cnt = (n - i + NDMA_SEM - 1) // NDMA_SEM if n > i else 0
                        if cnt > 0:
                            wait(dsem[ename][i], 16 * cnt)

            @block.tensor
            def _(eng):
                run(PE, eng)

            @block.scalar
            def _(eng):
                run(ACT, eng)

            @block.vector
            def _(eng):
                run(DVE, eng)

            @block.gpsimd
            def _(eng):
                run(POOL, eng)

            @block.sync
            def _(eng):
                run(SP, eng)
import contextlib
import ml_dtypes

D = 1024
H = 16
DH = 64
NMETA = 16
SEQ = 8192
NPAD = 112
LP = NPAD + NMETA + SEQ
NT = LP // 128
FF = 2816
NJ = FF // 128
EPS = 1e-6
BLOCKS = [(0, 128)] + [(128 + 512 * i, 512) for i in range(16)]
NB = len(BLOCKS)


import os
DBG = os.environ.get('KDBG', '')


class Ctx:
    pass


def _mm(P, out_ap, pairs, reads, writes, start=True, stop=True):
    def fn(e):
        n = len(pairs)
        ins = None
        for i, (l, r) in enumerate(pairs):
            ins = e.matmul(out_ap, lhsT=l, rhs=r, start=(start and i == 0), stop=(stop and i == n - 1))
        return ins
    return P.add(PE, fn, reads, writes)


def _act(P, out, in_, func, reads, writes, **kw):
    return P.add(ACT, lambda e: e.activation(out=out, in_=in_, func=func, **kw), reads, writes)


def _tt(P, eng, out, in0, in1, op, reads, writes):
    return P.add(eng, lambda e: e.tensor_tensor(out=out, in0=in0, in1=in1, op=op), reads, writes)


def _ts(P, eng, out, in0, s1, s2, op0, op1, reads, writes):
    if op1 is None:
        return P.add(eng, lambda e: e.tensor_scalar(out=out, in0=in0, scalar1=s1, scalar2=None, op0=op0), reads, writes)
    return P.add(eng, lambda e: e.tensor_scalar(out=out, in0=in0, scalar1=s1, scalar2=s2, op0=op0, op1=op1), reads, writes)


def _stt(P, out, in0, scalar, in1, op0, op1, reads, writes):
    return P.add(DVE, lambda e: e.scalar_tensor_tensor(out=out, in0=in0, scalar=scalar, in1=in1, op0=op0, op1=op1), reads, writes)


def _copy(P, eng, out, in_, reads, writes):
    if eng == ACT:
        return P.add(ACT, lambda e: e.copy(out=out, in_=in_), reads, writes)
    return P.add(eng, lambda e: e.tensor_copy(out=out, in_=in_), reads, writes)


def _rmsnorm_rows(P, C, ht, hn, gbc, tmp):
    junk, ssq, lnv, rstd = tmp
    _act(P, junk[:], ht[:], AF.Square, [ht], [junk, ssq], accum_out=ssq[:])
    _act(P, lnv[:], ssq[:], AF.Ln, [ssq], [lnv], scale=1.0 / D, bias=EPS)
    _act(P, rstd[:], lnv[:], AF.Exp, [lnv], [rstd], scale=-0.5)
    _stt(P, hn[:], ht[:], rstd[:, 0:1], gbc[:], ALU.mult, ALU.mult, [ht, rstd, gbc], [hn])


def _transpose_block(P, C, hns, nt, hnT, pTs, cnt0):
    n = nt * 128
    for c in range(8):
        pT = pTs[(cnt0 + c) % len(pTs)]

        def fn(e, c=c, pT=pT):
            ins = None
            for tl in range(nt):
                ins = e.transpose(out=pT[:, tl * 128:(tl + 1) * 128], in_=hns[tl][:, c * 128:(c + 1) * 128],
                                  identity=C.identb[:])
            return ins
        P.add(PE, fn, list(hns[:nt]) + [C.identb], [pT])
        _copy(P, ACT if c % 2 == 0 else DVE, hnT[:, c, 0:n], pT[:, 0:n], [pT], [hnT.sub(c)])


def stage1(P, nc, C, sb, ps, bounce=None):
    NHL = C.nheads
    NQ = NHL // 2
    WC = 4 * NHL * DH + NHL
    W = sb("s1_W", [128, 8, WC], BF16)
    wst = [sb("s1_wst%d" % i, [128, WC], F32) for i in range(2)]
    Wr = [W.sub(k) for k in range(8)]
    gbc = sb("s1_gbc", [128, D], F32)
    P.dma(gbc[:], C.attn_norm[0].partition_broadcast(128), writes=[gbc])
    gcol = sb("s1_gcol", [128, 2], F32)
    for hh in range(2):
        P.dma(gcol[64 * hh:64 * hh + 64, 0:1], C.fox_q_norm[0].rearrange("(p o) -> p o", o=1), writes=[gcol])
        P.dma(gcol[64 * hh:64 * hh + 64, 1:2], C.fox_k_norm[0].rearrange("(p o) -> p o", o=1), writes=[gcol])
    _ts(P, DVE, gcol[:, 0:1], gcol[:, 0:1], 0.125, None, ALU.mult, None, [gcol], [gcol])
    negbf = sb("s1_negbf", [NHL, 1], F32)
    P.dma(negbf[:], C.fox_b_f[0].rearrange("(p o) -> p o", o=1), writes=[negbf])
    _ts(P, DVE, negbf[:], negbf[:], -1.0, None, ALU.mult, None, [negbf], [negbf])
    ones16 = sb("s1_ones16", [NHL, 512], F32)
    P.add(DVE, lambda e: e.memset(ones16[:], 1.0), [], [ones16])
    zero16 = sb("s1_zero16", [NHL, 1], F32)
    P.add(DVE, lambda e: e.memset(zero16[:], 0.0), [], [zero16])

    hts = [sb("s1_ht%d" % i, [128, D], F32) for i in range(2)]
    hns = [sb("s1_hn%d" % i, [128, D], BF16) for i in range(8)]
    junk = sb("s1_junk", [128, D], BF16)
    ssqs = [[sb("s1_ssq%d_%d" % (i, j), [128, 1], F32) for j in range(3)] for i in range(2)]
    hnTs = [sb("s1_hnT%d" % i, [128, 8, 512], BF16) for i in range(2)]
    pTs = [ps("s1_pT%d" % i, [128, 512], BF16) for i in range(2)]
    psA = [ps("s1_psA%d" % i, [128, 512], F32) for i in range(4)]
    psSs = [ps("s1_psS%d" % i, [128, 512], F32) for i in range(2)]
    sqs = [sb("s1_sq%d" % i, [128, 512], BF16) for i in range(2)]
    lnvs = [sb("s1_lnv%d" % i, [128, 512], F32) for i in range(2)]
    rss = [sb("s1_rs%d" % i, [128, 512], F32) for i in range(2)]
    outs = [sb("s1_out%d" % i, [128, 512], BF16) for i in range(6)]
    vsts = [sb("s1_vst%d" % i, [128, NHL, 4, 65], BF16) for i in range(2)]
    for v in vsts:
        P.add(POOL, lambda e, v=v: e.memset(v[:, :, :, 64:65], 1.0), [], [v])
    ef = sb("s1_ef", [NHL, 512], F32)
    lf = sb("s1_lf", [NHL, 512], F32)
    cps = [sb("s1_cp%d" % i, [NHL, 512], F32) for i in range(2)]
    hsp = [[sb("s1_hsp%d_%d" % (i, j), [NHL, 512], BF16) for j in range(6)] for i in range(2)]
    r1 = sb("s1_r1", [NHL, 512], F32)
    r2 = sb("s1_r2", [NHL, 512], F32)
    onesb = sb("s1_onesb", [3, LP // 4], BF16)
    P.add(DVE, lambda e: e.memset(onesb[:], 1.0), [], [onesb])
    for h in range(NHL):
        for qd in range(4):
            cs_ = slice(qd * (LP // 4), (qd + 1) * (LP // 4))
            P.dma(C.KT_d[h, 67:70, cs_], onesb[0:3, :], reads=[onesb], q=POOL)
            P.dma(C.QT_d[h, 64:67, cs_], onesb[0:3, :], reads=[onesb], q=POOL)

    cA = 0
    cO = 0
    cT = 0
    cTh = [0]

    def fa_load(bi, tl):
        t0, n = BLOCKS[bi]
        if bi >= NB or tl >= n // 128:
            return
        ht = hts[tl % 2]
        P.dma(ht[:], C.h0[t0 + tl * 128:t0 + (tl + 1) * 128, :], writes=[ht])

    def fa_norm(bi, tl):
        t0, n = BLOCKS[bi]
        if bi >= NB or tl >= n // 128:
            return
        myhn = hns[(bi % 2) * 4:(bi % 2) * 4 + 4]
        tmp = [junk] + ssqs[cTh[0] % 2]
        cTh[0] += 1
        _rmsnorm_rows(P, C, hts[tl % 2], myhn[tl], gbc, tmp)

    def s1_front_a(bi):
        for tl in range(4):
            fa_load(bi, tl)
            fa_norm(bi, tl)

    def s1_front_b(bi):
        t0, n = BLOCKS[bi]
        myhn = hns[(bi % 2) * 4:(bi % 2) * 4 + 4]
        _transpose_block(P, C, myhn, n // 128, hnTs[bi % 2], pTs, 0)

    s1_front_a(0)
    for kc in range(8):
        P.dma(wst[kc % 2][:], C.fox_w_in[kc * 128:(kc + 1) * 128, :], writes=[wst[kc % 2]])
        _copy(P, ACT if kc % 2 == 0 else DVE, W[:, kc, :], wst[kc % 2][:], [wst[kc % 2]], [W.sub(kc)])
    s1_front_b(0)
    for bi, (t0, n) in enumerate(BLOCKS):
        nt = n // 128
        hnT = hnTs[bi % 2]
        if bi + 1 < NB:
            fa_load(bi + 1, 0)
            fa_load(bi + 1, 1)
        hr = [hnT.sub(c) for c in range(8)]
        pqs = {}

        def qk_a(c):
            nonlocal cA
            cols = c * 128
            pq = psA[cA % 4]
            cA += 1
            pqs[c] = pq
            _mm(P, pq[:, 0:n], [(W[:, kc, cols:cols + 128], hnT[:, kc, 0:n]) for kc in range(8)], Wr + hr, [pq])
            _act(P, sqs[c % 2][:, 0:n], pq[:, 0:n], AF.Square, [pq], [sqs[c % 2]])

        def qk_b(c):
            nonlocal cO
            pq = pqs[c]
            sq, lnv, rs = sqs[c % 2], lnvs[c % 2], rss[c % 2]
            psS = psSs[c % 2]
            _mm(P, psS[:, 0:n], [(C.bdones[:], sq[:, 0:n])], [C.bdones, sq], [psS])
            _act(P, lnv[:, 0:n], psS[:, 0:n], AF.Ln, [psS], [lnv], scale=1.0 / DH, bias=EPS)
            _act(P, rs[:, 0:n], lnv[:, 0:n], AF.Exp, [lnv], [rs], scale=-0.5)
            ob = outs[cO % 6]
            cO += 1
            gi = 0 if c < NQ else 1
            _stt(P, ob[:, 0:n], pq[:, 0:n], gcol[:, gi:gi + 1], rs[:, 0:n], ALU.mult, ALU.mult, [pq, gcol, rs], [ob])
            dst = C.QT_d if c < NQ else C.KT_d
            for hh in range(2):
                h = (c % NQ) * 2 + hh
                P.dma(dst[h, 0:64, t0:t0 + n], ob[64 * hh:64 * hh + 64, 0:n], reads=[ob], q=POOL)

        qk_a(0)
        for c in range(2 * NQ):
            if c + 1 < 2 * NQ:
                qk_a(c + 1)
            qk_b(c)
        if bi + 1 < NB:
            fa_norm(bi + 1, 0)
            fa_norm(bi + 1, 1)
            fa_load(bi + 1, 2)
            fa_load(bi + 1, 3)
        for c in range(NQ):
            cols = 3 * NHL * DH + c * 128
            pg = psA[cA % 4]
            cA += 1
            _mm(P, pg[:, 0:n], [(W[:, kc, cols:cols + 128], hnT[:, kc, 0:n]) for kc in range(8)], Wr + hr, [pg])
            lnv, rs = lnvs[c % 2], rss[c % 2]
            _act(P, rs[:, 0:n], pg[:, 0:n], AF.Exp, [pg], [rs], scale=-1.0)
            _act(P, lnv[:, 0:n], rs[:, 0:n], AF.Ln, [rs], [lnv], bias=1.0)
            ob = outs[cO % 6]
            cO += 1
            _act(P, ob[:, 0:n], lnv[:, 0:n], AF.Exp, [lnv], [ob], scale=-1.0)
            P.dma(C.SG_d[c * 128:(c + 1) * 128, t0:t0 + n], ob[:, 0:n], reads=[ob], q=POOL)
        if bi + 1 < NB:
            fa_norm(bi + 1, 2)
            fa_norm(bi + 1, 3)
        pf = psA[cA % 4]
        cA += 1
        _mm(P, pf[0:NHL, 0:n], [(W[:, kc, 4 * NHL * DH:WC], hnT[:, kc, 0:n]) for kc in range(8)], Wr + hr, [pf])
        _act(P, ef[:, 0:n], pf[0:NHL, 0:n], AF.Exp, [pf], [ef], scale=-1.0, bias=negbf[:, 0:1])
        _act(P, lf[:, 0:n], ef[:, 0:n], AF.Ln, [ef], [lf], bias=1.0)
        cp = cps[bi % 2]
        if bi == 0:
            ref_ap, ref_b = zero16[:, 0:1], zero16
        else:
            pn = BLOCKS[bi - 1][1]
            ref_ap, ref_b = cps[(bi - 1) % 2][:, pn - 1:pn], cps[(bi - 1) % 2]
        P.add(DVE, lambda e, cp=cp, ref_ap=ref_ap, n=n: e.tensor_tensor_scan(
            out=cp[:, 0:n], data0=ones16[:, 0:n], data1=lf[:, 0:n], initial=ref_ap, op0=ALU.mult, op1=ALU.add),
            [ones16, lf, ref_b], [cp])
        hh = hsp[bi % 2]
        _copy(P, DVE, hh[0][:, 0:n], cp[:, 0:n], [cp], [hh[0]])
        _tt(P, DVE, r1[:, 0:n], cp[:, 0:n], hh[0][:, 0:n], ALU.subtract, [cp, hh[0]], [r1])
        _copy(P, DVE, hh[1][:, 0:n], r1[:, 0:n], [r1], [hh[1]])
        _tt(P, DVE, r2[:, 0:n], r1[:, 0:n], hh[1][:, 0:n], ALU.subtract, [r1, hh[1]], [r2])
        _copy(P, DVE, hh[2][:, 0:n], r2[:, 0:n], [r2], [hh[2]])
        for j in range(3):
            _ts(P, DVE, hh[3 + j][:, 0:n], hh[j][:, 0:n], -1.0, None, ALU.mult, None, [hh[j]], [hh[3 + j]])
            P.dma(C.KT_d[:, 64 + j, t0:t0 + n], hh[j][:, 0:n], reads=[hh[j]])
            P.dma(C.QT_d[:, 67 + j, t0:t0 + n], hh[3 + j][:, 0:n], reads=[hh[3 + j]])
        vst = vsts[bi % 2]
        if bi == 0:
            P.add(POOL, lambda e, v=vst: e.memset(v[0:NPAD, :, 0:1, 64:65], 0.0), [], [vst])
        for tl in range(nt):
            for half in range(NHL // 8):
                pv = psA[cA % 4]
                cA += 1
                cols = 2 * NHL * DH + half * 512
                _mm(P, pv[:, :], [(hnT[:, kc, tl * 128:(tl + 1) * 128], W[:, kc, cols:cols + 512]) for kc in range(8)],
                    Wr + hr, [pv])
                _copy(P, DVE if half == 0 else ACT, vst[:, 8 * half:8 * half + 8, tl, 0:64],
                      pv[:, :].rearrange("p (h d) -> p h d", h=8), [pv], [vst])
        for h in range(NHL):
            P.dma(C.VA_d[h, :, t0 // 128:t0 // 128 + nt, :], vst[:, h, 0:nt, :], reads=[vst])
        if bi == 0:
            P.add(POOL, lambda e, v=vst: e.memset(v[0:NPAD, :, 0:1, 64:65], 1.0), [], [vst])
        if bi + 1 < NB:
            s1_front_b(bi + 1)


def stage2(P, nc, C, sb, ps, bounce=None):
    for kc in range(8):
        P.dma(C.Wo[0][:, kc, :], C.fox_w_out[kc * 128:(kc + 1) * 128, :], writes=[C.Wo[0]], q=POOL)
        P.dma(C.Wo[1][:, kc, :], C.hgrn_w_out[kc * 128:(kc + 1) * 128, :], writes=[C.Wo[1]], q=POOL)
    gate_t = sb("s2_gate", [128, 1], F32)
    gate = [Res()]
    conv = convert_weights(P, nc, C, bounce, gate) if bounce is not None else iter(())
    KTb = [sb("s2_KT%d" % i, [70, LP], BF16) for i in range(2)]
    VAb = [sb("s2_VA%d" % i, [128, NT, 65], BF16) for i in range(2)]
    Qbs = [sb("s2_Q%d" % i, [70, 512], BF16) for i in range(3)]
    sgs = [sb("s2_sg%d" % i, [64, 512], BF16) for i in range(3)]
    biases = [sb("s2_bias%d" % i, [128, NT], F32) for i in range(3)]
    Pts = [sb("s2_Pt%d" % i, [128, 512], BF16) for i in range(6)]
    Sb = [ps("s2_S%d" % i, [128, 512], F32) for i in range(5)]
    Ob = [ps("s2_O%d" % i, [128, 512], F32) for i in range(2)]
    Rps = ps("s2_R", [128, 512], F32)
    rds = [sb("s2_rd%d" % i, [128, 512], F32) for i in range(2)]
    rd2s = [sb("s2_rd2%d" % i, [128, 512], F32) for i in range(2)]
    pending = []
    onesf = sb("s2_onesf", [128, 64], F32)
    P.add(DVE, lambda e: e.memset(onesf[:], 1.0), [], [onesf])
    zt2 = sb("s2_zt", [64, 128], BF16)
    P.add(DVE, lambda e: e.memset(zt2[:], 0.0), [], [zt2])
    for h in range(C.nheads):
        P.dma(C.OGi[h][:, LP:LP + 128], zt2[:], reads=[zt2], writes=[C.OGi_res[h]])
    Osbs = [sb("s2_Osb%d" % i, [64, 512], F32) for i in range(2)]
    og1s = [sb("s2_og1%d" % i, [64, 512], F32) for i in range(2)]
    ogbs = [sb("s2_ogb%d" % i, [64, 512], BF16) for i in range(2)]

    items = []
    for h in range(C.nheads):
        for bi, (t0, n) in enumerate(BLOCKS):
            nkt = (t0 + n) // 128
            for kt in range(nkt):
                items.append((h, bi, kt, nkt))
    state = {}

    def prologue(h, bi):
        t0, n = BLOCKS[bi]
        g = h * NB + bi
        if g in state or g >= C.nheads * NB:
            return
        state[g] = True
        if g % 3 == 0:
            gate[0] = Res()
            P.add(DVE, lambda e: e.memset(gate_t[:], 0.0), [], [gate[0]])
            next(conv, None)
        if bi == 0:
            KTh, VAh = KTb[h % 2], VAb[h % 2]
            P.dma(KTh[:, :], C.KT_d[h], writes=[KTh])
            P.dma(VAh[:], C.VA_d[h], writes=[VAh])
        Qb, sg, bias = Qbs[g % 3], sgs[g % 3], biases[g % 3]
        P.dma(Qb[0:70, 0:n], C.QT_d[h, :, t0:t0 + n], writes=[Qb])
        P.dma(sg[0:64, 0:n], C.SG_d[64 * h:64 * h + 64, t0:t0 + n], writes=[sg])

    def qk(i):
        h, bi, kt, nkt = items[i]
        t0, n = BLOCKS[bi]
        g = h * NB + bi
        if kt == 0:
            prologue(h, bi)
        j = kt - t0 // 128
        c0 = 128 * j if j >= 0 else 0
        S = Sb[i % 5]
        _mm(P, S[:, c0:n], [(KTb[h % 2][0:70, kt * 128:(kt + 1) * 128], Qbs[g % 3][0:70, c0:n])],
            [KTb[h % 2], Qbs[g % 3]], [S])

    def rest(i):
        h, bi, kt, nkt = items[i]
        t0, n = BLOCKS[bi]
        g = h * NB + bi
        j = kt - t0 // 128
        c0 = 128 * j if j >= 0 else 0
        S, Pt, O = Sb[i % 5], Pts[i % 6], Ob[g % 2]
        bias = biases[g % 3]
        if kt == 0:
            while pending and pending[0][0] <= g - 2:
                epilogue2(pending.pop(0)[0])
            prologue((g + 1) // NB, (g + 1) % NB)
        _act(P, Pt[:, c0:n], S[:, c0:n], AF.Exp, [S], [Pt])
        if j >= 0:
            _tt(P, DVE, Pt[:, c0:c0 + 128], Pt[:, c0:c0 + 128], C.tri[:], ALU.mult, [Pt, C.tri], [Pt])
        _mm(P, O[0:65, c0:n], [(VAb[h % 2][:, kt, 0:65], Pt[:, c0:n])], [VAb[h % 2], Pt], [O],
            start=(kt == 0), stop=(kt == nkt - 1))
        if kt == nkt - 1:
            Osb = Osbs[g % 2]
            rd, rd2 = rds[g % 2], rd2s[g % 2]
            _ts(P, DVE, rd[64:65, 0:n], O[64:65, 0:n], 1e-30, None, ALU.max, None, [O], [rd])
            P.add(DVE, lambda e: e.reciprocal(out=rd2[64:65, 0:n], in_=rd[64:65, 0:n]), [rd], [rd2])
            _copy(P, ACT, Osb[:, 0:n], O[0:64, 0:n], [O], [Osb])
            pending.append((g, i))
        while pending and (i - pending[0][1] >= 2 or i == len(items) - 1):
            epilogue2(pending.pop(0)[0])

    def epilogue2(g):
        h, bi = g // NB, g % NB
        t0, n = BLOCKS[bi]
        Osb, og1, ogb, sg = Osbs[g % 2], og1s[g % 2], ogbs[g % 2], sgs[g % 3]
        rd2 = rd2s[g % 2]
        _mm(P, Rps[0:64, 0:n], [(onesf[64:65, 0:64], rd2[64:65, 0:n])], [onesf, rd2], [Rps])
        _tt(P, DVE, og1[:, 0:n], Osb[:, 0:n], Rps[0:64, 0:n], ALU.mult, [Osb, Rps], [og1])
        _tt(P, DVE, ogb[:, 0:n], og1[:, 0:n], sg[0:64, 0:n], ALU.mult, [og1, sg], [ogb])
        P.dma(C.OGi[h][:, t0:t0 + n], ogb[:, 0:n], reads=[ogb], writes=[C.OGi_res[h]])
        if bi == NB - 1:
            P.cc(lambda e, h=h: e.collective_compute("AllGather", ALU.bypass, replica_groups=C.RG,
                                                     ins=[C.OGi_t[h].ap().opt()], outs=[C.OGo_t[h].ap().opt()]),
                 reads=[C.OGi_res[h]], writes=[C.OGo_res[h]])

    LA = 3
    N = len(items)
    for i in range(N + LA):
        if i < N:
            qk(i)
        if i - LA >= 0:
            rest(i - LA)
    for _ in conv:
        pass


def build(stages=3, debug=False, nheads=H // 2, conv=None, s3parts=99, nblk=NB):
    nc = bass.Bass("TRN2", target_bir_lowering=False)
    C = Ctx()
    C.nheads = nheads
    C.s3parts = s3parts
    C.nblk = nblk
    if conv is None:
        conv = stages >= 3

    def din(name, shape, dt=F32):
        return nc.dram_tensor(name, list(shape), dt, kind="ExternalInput").ap()

    def dscr(name, shape, dt, out=False):
        return nc.dram_tensor(name, list(shape), dt, kind="ExternalOutput" if out else "Internal").ap()

    C.h0 = din("h0", [LP, D])
    C.attn_norm = din("attn_norm", [2, D])
    C.ffn_norm = din("ffn_norm", [2, D])
    C.final_norm = din("final_norm", [D])
    NHL = nheads
    C.fox_w_in = din("fox_w_in", [D, 4 * NHL * DH + NHL])
    C.fox_b_f = din("fox_b_f", [1, NHL])
    C.fox_q_norm = din("fox_q_norm", [1, DH])
    C.fox_k_norm = din("fox_k_norm", [1, DH])
    C.fox_w_out = din("fox_w_out", [D, D])
    C.hgrn_w_in = din("hgrn_w_in", [D, 4 * NHG * 128])
    C.lb = din("hgrn_lower_bounds", [2, NHG * 128])
    C.h0h = din("h0h", [HT * 128, D])
    C.rmask = din("rmask", [128, 8 * 512], mybir.dt.uint16)
    C.rmaskf = din("rmaskf", [128, 1])
    C.g_norm = din("hgrn_g_norm", [1, 128])
    C.hgrn_w_out = din("hgrn_w_out", [D, D])
    C.ffn_w_in = din("ffn_w_in", [2, D, 2 * FF])
    C.ffn_w_out = din("ffn_w_out", [2, FF, D])
    c_identb = din("c_identb", [128, 128], BF16)
    c_identf = din("c_identf", [128, 128], F32)
    c_bdones = din("c_bdones", [128, 128], BF16)
    c_tri = din("c_tri", [128, 128], BF16)
    d1 = debug and stages == 1
    C.QT_d = dscr("QT_d", [NHL, 70, LP], BF16, d1)
    C.KT_d = dscr("KT_d", [NHL, 70, LP], BF16, d1)
    C.VA_d = dscr("VA_d", [NHL, 128, NT, 65], BF16, d1)
    C.SG_d = dscr("SG_d", [NHL * DH, LP], BF16, d1)
    C.RG = [[0, 1], [2, 3], [4, 5], [6, 7]]
    C.OGi_t = [nc.dram_tensor("OGi%d" % h, [DH, LP + 128], BF16) for h in range(NHL)]
    C.OGo_t = [nc.dram_tensor("OGo%d" % h, [2 * DH, LP + 128], BF16) for h in range(NHL)]
    C.HNi_t = [nc.dram_tensor("HNi%d" % i, [128, 8, n], BF16) for i, (t0, n) in enumerate(LBLK)]
    C.HNo_t = [nc.dram_tensor("HNo%d" % i, [256, 8, n], BF16) for i, (t0, n) in enumerate(LBLK)]
    C.HNi = [t.ap() for t in C.HNi_t]
    C.HNo = [t.ap().rearrange("(r p) k n -> r p k n", r=2) for t in C.HNo_t]
    C.HNi_res = [Res() for _ in LBLK]
    C.HNo_res = [Res() for _ in LBLK]
    C.OG1i_t = [nc.dram_tensor("OG1i%d" % g, [128, NHG, LBLK[g % NLB][1]], BF16) for g in range(2 * NLB)]
    C.OG1o_t = [nc.dram_tensor("OG1o%d" % g, [256, NHG, LBLK[g % NLB][1]], BF16) for g in range(2 * NLB)]
    C.OG1i = [t.ap() for t in C.OG1i_t]
    C.OG1o = [t.ap().rearrange("(r p) k n -> r p k n", r=2) for t in C.OG1o_t]
    C.OG1i_res = [Res() for _ in range(2 * NLB)]
    C.OG1o_res = [Res() for _ in range(2 * NLB)]
    C.H1 = dscr("H1", [HT * 128, D], F32)
    C.H1_res = [Res() for _ in LBLK]
    C.OGi = [t.ap() for t in C.OGi_t]
    C.OGo = [t.ap() for t in C.OGo_t]
    C.OGi_res = [Res() for _ in range(NHL)]
    C.OGo_res = [Res() for _ in range(NHL)]
    C.W1S = dscr("W1S", [2, NJ, 128, 8, 256], BF16)
    C.W2S = dscr("W2S", [2, NJ, 128, D], BF16)
    C.WHS = dscr("WHS", [NHG, 128, 8, 384], BF16)
    C.WHV = dscr("WHV", [1, 128, 8, 512], BF16)
    C.out = nc.dram_tensor("out", [HT * 128, D], F32, kind="ExternalOutput").ap()
    C.dbg_x1 = nc.dram_tensor("dbg_x1", [nblk, 128, 8, 512], BF16, kind="ExternalOutput").ap() if (debug and stages == 3) else None

    with contextlib.ExitStack() as st0:
        P = Prog(nc)

        def mk(st):
            def sb(name, shape, dt):
                return Buf(st.enter_context(nc.sbuf_tensor(name, list(shape), dt)), name)

            def ps(name, shape, dt):
                return Buf(st.enter_context(nc.psum_tensor(name, list(shape), dt)), name)
            return sb, ps
        sb0, ps0 = mk(st0)
        C.identb = sb0("identb", [128, 128], BF16)
        C.identf = sb0("identf", [128, 128], F32)
        C.bdones = sb0("bdones", [128, 128], BF16)
        C.tri = sb0("tri", [128, 128], BF16)
        P.dma(C.identb[:], c_identb, writes=[C.identb])
        P.dma(C.identf[:], c_identf, writes=[C.identf])
        P.dma(C.bdones[:], c_bdones, writes=[C.bdones])
        P.dma(C.tri[:], c_tri, writes=[C.tri])
        C.Wo = [sb0("Wo0", [128, 8, D], BF16), sb0("Wo1", [128, 8, D], BF16)]
        if debug:
            cpc_o = nc.dram_tensor("CPC_o", [128, NT, NHL], F32, kind="ExternalOutput").ap()
            refb_o = nc.dram_tensor("REFB_o", [128, NB, NHL], F32, kind="ExternalOutput").ap()
        with contextlib.ExitStack() as stm:
            sbm, psm = mk(stm)
            C.CPC = sbm("CPC", [128, NT, NHL], F32)
            C.REFB = sbm("REFB", [128, NB, NHL], F32)
            bounce = [sbm("bounce%d" % i, [128, 2 * FF], BF16) for i in range(2)]
            with contextlib.ExitStack() as st:
                sb, ps = mk(st)
                stage1(P, nc, C, sb, ps, bounce if conv else None)
                if debug:
                    P.dma(cpc_o, C.CPC[:], reads=[C.CPC])
                    P.dma(refb_o, C.REFB[:], reads=[C.REFB])
                P.barrier()
            if stages >= 2:
                with contextlib.ExitStack() as st:
                    sb, ps = mk(st)
                    stage2(P, nc, C, sb, ps, bounce if conv else None)
                    P.barrier()
        if stages >= 3:
            with contextlib.ExitStack() as st:
                sb, ps = mk(st)
                stage3(P, nc, C, sb, ps)
                P.barrier()
        P.emit()
    return nc


def host_consts():
    bf = ml_dtypes.bfloat16
    idx = np.arange(128)
    return {
        "c_identb": np.eye(128, dtype=np.float32).astype(bf),
        "c_identf": np.eye(128, dtype=np.float32),
        "c_bdones": (idx[:, None] // 64 == idx[None, :] // 64).astype(np.float32).astype(bf),
        "c_tri": (idx[:, None] <= idx[None, :]).astype(np.float32).astype(bf),
    }


WNAMES = ["attn_norm", "ffn_norm", "final_norm", "fox_w_in", "fox_b_f", "fox_q_norm", "fox_k_norm", "fox_w_out",
          "hgrn_w_in", "hgrn_lower_bounds", "hgrn_g_norm", "hgrn_w_out", "ffn_w_in", "ffn_w_out"]


def make_in_maps(inputs, ncores=8):
    x = np.asarray(inputs["x"], dtype=np.float32)
    meta = np.asarray(inputs["meta_tokens"], dtype=np.float32)
    consts = host_consts()
    maps = []
    NHL = H // 2
    for c in range(ncores):
        b, r = c // 2, c % 2
        h0 = np.zeros((LP, D), np.float32)
        h0[NPAD:NPAD + NMETA] = meta
        h0[NPAD + NMETA:] = x[b]
        h0h = np.zeros((HT * 128, D), np.float32)
        seg = h0[r * HT * 128:(r + 1) * HT * 128]
        h0h[:seg.shape[0]] = seg
        m = {"h0": h0, "h0h": h0h, "rmask": np.full((128, 8 * 512), 0xFFFF if r else 0, np.uint16),
             "rmaskf": np.full((128, 1), float(r), np.float32)}
        for k in WNAMES:
            a = np.asarray(inputs[k], dtype=np.float32)
            if k in ("fox_w_in", "fox_w_out", "hgrn_w_in", "hgrn_w_out"):
                a = a.reshape(a.shape[-2], a.shape[-1])
            if k == "fox_w_in":
                w = NHL * DH
                a = np.concatenate([a[:, s0 + r * w:s0 + (r + 1) * w] for s0 in (0, D, 2 * D, 3 * D)]
                                   + [a[:, 4 * D + r * NHL:4 * D + (r + 1) * NHL]], axis=1)
            if k == "fox_b_f":
                a = a[:, r * NHL:(r + 1) * NHL]
            if k == "hgrn_w_in":
                w = NHG * 128
                a = np.concatenate([a[:, s0 + r * w:s0 + (r + 1) * w] for s0 in (0, D, 2 * D, 3 * D)], axis=1)
            if k == "hgrn_lower_bounds":
                a = a[:, r * NHG * 128:(r + 1) * NHG * 128]
            m[k] = np.ascontiguousarray(a)
        m.update(consts)
        maps.append(m)
    return maps


def kernel(**inputs):
    nc = build(stages=3)
    maps = make_in_maps(inputs, 8)
    res = run_bass_kernel_spmd(nc, maps, core_ids=list(range(8)))
    outs = []
    for b in range(4):
        o0 = np.asarray(res.results[2 * b]["out"], dtype=np.float32)
        o1 = np.asarray(res.results[2 * b + 1]["out"], dtype=np.float32)
        outs.append(np.concatenate([o0[NPAD + NMETA:], o1[:SEQ - (HT * 128 - NPAD - NMETA)]], axis=0))
    return np.stack(outs, axis=0)


class Stream:
    def __init__(self, P, bufs, reqs, q=SP, keep=0):
        self.P, self.bufs, self.reqs, self.q, self.keep = P, bufs, reqs, q, keep
        self.issued = 0

    def get(self, i):
        lim = min(len(self.reqs), i + len(self.bufs) - self.keep)
        while self.issued < lim:
            k = self.issued
            b = self.bufs[k % len(self.bufs)]
            dst, src = self.reqs[k]
            self.P.dma(dst(b), src, writes=[b], q=self.q)
            self.issued += 1
        return self.bufs[i % len(self.bufs)]


def convert_weights(P, nc, C, bounce, gate):
    k = [0]

    def ld(src, width):
        b = bounce[k[0] % 2]
        k[0] += 1
        P.dma(b[:, 0:width], src, reads=[gate[0]], writes=[b], q=POOL)
        return b
    for l in range(2):
        for kc in range(8):
            b = ld(C.ffn_w_in[l, kc * 128:(kc + 1) * 128, :], 2 * FF)
            for hf in range(2):
                P.dma(C.W1S[l, :, :, kc, hf * 128:(hf + 1) * 128].rearrange("j p c -> p j c"),
                      b[:, hf * FF:(hf + 1) * FF].rearrange("p (j c) -> p j c", c=128), reads=[b], q=POOL)
            yield
        wo = C.ffn_w_out[l].rearrange("(j p) c -> p j c", p=128)
        for j0 in range(0, NJ, 5):
            j1 = min(NJ, j0 + 5)
            b = ld(wo[:, j0:j1, :], (j1 - j0) * D)
            P.dma(C.W2S[l, j0:j1].rearrange("j p c -> p j c"),
                  b[:, 0:(j1 - j0) * D].rearrange("p (j c) -> p j c", c=D), reads=[b], q=POOL)
            yield
    GW = NHG * 128
    for kc in range(8):
        b = ld(C.hgrn_w_in[kc * 128:(kc + 1) * 128, :], 4 * GW)
        for gi, base in enumerate((0, GW, 3 * GW)):
            P.dma(C.WHS[:, :, kc, gi * 128:(gi + 1) * 128].rearrange("h p c -> p h c"),
                  b[:, base:base + GW].rearrange("p (h c) -> p h c", c=128), reads=[b], q=POOL)
        P.dma(C.WHV[0, :, kc, :], b[:, 2 * GW:3 * GW], reads=[b], q=POOL)
        yield


NHG = 4
HT = 33
LBLK = [(0, 128)] + [(128 + 512 * i, 512) for i in range(8)]
NLB = len(LBLK)


def stage3(P, nc, C, sb, ps):
    B = [ps("s3_B%d" % i, [128, 512], F32) for i in range(6)]
    Tt = [ps("s3_T%d" % i, [128, 1024], BF16) for i in range(2)]
    for i in range(2):
        v = Buf(Tt[i].t[:, :].bitcast(F32), "B%df" % (6 + i))
        v.res = Tt[i].res
        B.append(v)

    def bres(i):
        return [B[i].res]

    prow = sb("s3_prow", [32, 128], F32)
    P.dma(prow[0:8, :], C.lb.rearrange("r (h p) -> (r h) p", p=128), writes=[prow])
    P.dma(prow[8:16, :], C.attn_norm[1].rearrange("(c p) -> c p", p=128), writes=[prow])
    P.dma(prow[16:24, :], C.ffn_norm[0].rearrange("(c p) -> c p", p=128), writes=[prow])
    P.dma(prow[24:32, :], C.ffn_norm[1].rearrange("(c p) -> c p", p=128), writes=[prow])
    pcol = sb("s3_pcol", [128, 32], F32)
    P.add(PE, lambda e: e.transpose(out=B[5][:, 0:32], in_=prow[0:32, :], identity=C.identf[0:32, 0:32]),
          [prow, C.identf], bres(5))
    _copy(P, DVE, pcol[:], B[5][:, 0:32], bres(5), [pcol])
    omlb = sb("s3_omlb", [128, NHG], F32)
    _tt(P, DVE, omlb[:], pcol[:, 4:8], pcol[:, 0:4], ALU.subtract, [pcol], [omlb])
    _act(P, omlb[:], omlb[:], AF.Exp, [omlb], [omlb])
    _ts(P, DVE, omlb[:], omlb[:], 1.0, None, ALU.add, None, [omlb], [omlb])
    P.add(DVE, lambda e: e.reciprocal(out=omlb[:], in_=omlb[:]), [omlb], [omlb])
    lnomlb = sb("s3_lnomlb", [128, NHG], F32)
    _act(P, lnomlb[:], omlb[:], AF.Ln, [omlb], [lnomlb])
    gcols = {"attn1": pcol[:, 8:16], "ffn0": pcol[:, 16:24], "ffn1": pcol[:, 24:32]}
    gn = sb("s3_gn", [128, 1], F32)
    P.dma(gn[:], C.g_norm[0].rearrange("(p o) -> p o", o=1), writes=[gn])
    gfin = sb("s3_gfin", [128, D], F32)
    P.dma(gfin[:], C.final_norm.partition_broadcast(128), writes=[gfin])
    mh = sb("s3_mh", [128, 1], F32)
    P.add(POOL, lambda e: e.memset(mh[:], -0.5), [], [mh])
    ones128b = sb("s3_ones128b", [128, 128], BF16)
    P.add(DVE, lambda e: e.memset(ones128b[:], 1.0), [], [ones128b])
    onesf = sb("s3_onesf", [128, 128], F32)
    P.add(DVE, lambda e: e.memset(onesf[:], 1.0), [], [onesf])
    rmk = sb("s3_rmk", [128, 512], mybir.dt.uint16)
    P.dma(rmk[:], C.rmask[:, 0:512], writes=[rmk])
    rmf = sb("s3_rmf", [128, 1], F32)
    P.dma(rmf[:], C.rmaskf, writes=[rmf])
    junk = sb("s3_junk", [128, D], BF16)
    nrm = [[sb("s3_nrm%d_%d" % (i, j), [128, 1], F32) for j in range(3)] for i in range(4)]
    cnt = {"y": 0, "f": 0, "T": 0, "n": 0, "h": 0}

    def ybank():
        cnt["y"] += 1
        return B[cnt["y"] % 4]

    def fbank():
        cnt["f"] += 1
        return B[cnt["f"] % 6]

    def hbank():
        cnt["h"] += 1
        return B[cnt["h"] % 3]

    def norm_rows(ht):
        ssq, tt, rstd = nrm[cnt["n"] % 4]
        cnt["n"] += 1
        _act(P, junk[:], ht[:], AF.Square, [ht], [junk, ssq], accum_out=ssq[:])
        _ts(P, DVE, tt[:], ssq[:], 1.0 / D, EPS, ALU.mult, ALU.add, [ssq], [tt])
        _tt(P, POOL, rstd[:], tt[:], mh[:], ALU.pow, [tt, mh], [rstd])
        return rstd

    def norm_A(hs, hns, nt):
        for tl in range(nt):
            rstd = norm_rows(hs[tl])
            _ts(P, DVE, hns[tl][:], hs[tl][:], rstd[:, 0:1], None, ALU.mult, None, [hs[tl], rstd], [hns[tl]])

    def norm_B(hns, hnT, nt, gcol):
        n = nt * 128
        for c in range(8):
            half = cnt["T"] % 2
            cnt["T"] += 1
            pT = Tt[half][:, 0:512]
            res = Tt[half].res

            def fn(e, c=c, pT=pT):
                ins = None
                for tl in range(nt):
                    ins = e.transpose(out=pT[:, tl * 128:(tl + 1) * 128], in_=hns[tl][:, c * 128:(c + 1) * 128],
                                      identity=C.identb[:])
                return ins
            P.add(PE, fn, list(hns[:nt]) + [C.identb], [res])
            if c % 2 == 0:
                _ts(P, DVE, hnT[:, c, 0:n], pT[:, 0:n], gcol[:, c:c + 1], None, ALU.mult, None, [res, pcol], [hnT.sub(c)])
            else:
                P.add(ACT, lambda e, c=c, pT=pT: e.mul(out=hnT[:, c, 0:n], in_=pT[:, 0:n], mul=gcol[:, c:c + 1]),
                      [res, pcol], [hnT.sub(c)])

    def ag(in_t, out_t, rin, rout):
        P.cc(lambda e: e.collective_compute("AllGather", ALU.bypass, replica_groups=C.RG,
                                            ins=[in_t.ap().opt()], outs=[out_t.ap().opt()]),
             reads=[rin], writes=[rout])

    def scope():
        st = contextlib.ExitStack()
        return st, (lambda name, shape, dt: Buf(st.enter_context(nc.sbuf_tensor(name, list(shape), dt)), name))

    def ffn_phase(l):
        st, sbp = scope()
        with st:
            pre = "p%d_" % (1 + 2 * l)
            Wo = C.Wo[l]
            hs2 = [[sbp(pre + "h%d_%d" % (j, i), [128, D], F32) for i in range(4)] for j in range(2)]
            X0s = [sbp(pre + "X0_%d" % j, [128, 8, 512], BF16) for j in range(2)]
            Xcs = [sbp(pre + "Xc%d" % i, [128, 512], BF16) for i in range(2)]
            hns = [sbp(pre + "hn%d" % i, [128, D], BF16) for i in range(4)]
            hnT = sbp(pre + "hnT", [128, 8, 512], BF16)
            hnT1 = sbp(pre + "hnT1", [128, 8, 512], BF16) if l == 0 else None
            actT = sbp(pre + "actT", [128, NJ, 512], BF16)
            sils = [sbp(pre + "sil%d" % i, [128, 512], F32) for i in range(2)]
            W1b = [sbp(pre + "W1b%d" % i, [128, 2, 8, 256], BF16) for i in range(3)]
            W2b = [sbp(pre + "W2b%d" % i, [128, 2, D], BF16) for i in range(3)]
            orow = [sbp(pre + "orow%d" % i, [128, D], F32) for i in range(2)] if l == 1 else None
            NJP = NJ // 2
            S1 = Stream(P, W1b, [(lambda b: b[:], C.W1S[l, 2 * jp:2 * jp + 2].rearrange("j p k c -> p j k c"))
                                 for lb in range(NLB) for jp in range(NJP)])
            S2 = Stream(P, W2b, [(lambda b: b[:], C.W2S[l, 2 * jp:2 * jp + 2].rearrange("j p c -> p j c"))
                                 for lb in range(NLB) for jp in range(NJP)])
            S1.get(0)
            S2.get(0)
            hnTr = [hnT.sub(c) for c in range(8)]
            blocks = LBLK[:C.nblk]
            oc = [0]

            def stA(i):
                t0, n = blocks[i]
                nt = n // 128
                hs, X0 = hs2[i % 2], X0s[i % 2]
                for tl in range(nt):
                    if l == 0:
                        P.dma(hs[tl][:], C.h0h[t0 + tl * 128:t0 + (tl + 1) * 128, :], writes=[hs[tl]])
                    else:
                        P.dma(hs[tl][:], C.H1[t0 + tl * 128:t0 + (tl + 1) * 128, :], reads=[C.H1_res[i]], writes=[hs[tl]])
                for kc in range(8):
                    Xc = Xcs[kc % 2]
                    if l == 0:
                        rk, hl = kc // 4, 2 * (kc % 4)
                        for hh in range(2):
                            src = C.OGo[hl + hh][64 * rk:64 * rk + 64, :]
                            P.dma(X0[64 * hh:64 * hh + 64, kc, 0:n], src[:, t0:t0 + n], reads=[C.OGo_res[hl + hh]],
                                  writes=[X0.sub(kc)])
                            P.dma(Xc[64 * hh:64 * hh + 64, 0:n], src[:, HT * 128 + t0:HT * 128 + t0 + n],
                                  reads=[C.OGo_res[hl + hh]], writes=[Xc])
                    else:
                        rk, hd = kc // 4, kc % 4
                        P.dma(X0[:, kc, 0:n], C.OG1o[i][rk, :, hd, :], reads=[C.OG1o_res[i]], writes=[X0.sub(kc)])
                        P.dma(Xc[:, 0:n], C.OG1o[NLB + i][rk, :, hd, :], reads=[C.OG1o_res[NLB + i]], writes=[Xc])
                    P.add(DVE, lambda e, n=n, kc=kc, Xc=Xc, X0=X0: e.copy_predicated(
                        out=X0[:, kc, 0:n], mask=rmk[:, 0:n], data=Xc[:, 0:n]), [Xc, rmk, X0.sub(kc)], [X0.sub(kc)])

            def stW(i):
                t0, n = blocks[i]
                hs, X = hs2[i % 2], X0s[i % 2]
                for tl in range(n // 128):
                    for hf in range(2):
                        y = ybank()
                        _mm(P, y[:, :], [(X[:, kc, tl * 128:(tl + 1) * 128], Wo[:, kc, hf * 512:(hf + 1) * 512])
                                         for kc in range(8)], [X.sub(kc) for kc in range(8)] + [Wo], [y])
                        _tt(P, DVE, hs[tl][:, hf * 512:(hf + 1) * 512], hs[tl][:, hf * 512:(hf + 1) * 512], y[:, :], ALU.add,
                            [hs[tl], y], [hs[tl]])

            hns1 = [sbp(pre + "hn1_%d" % i, [128, D], BF16) for i in range(4)] if l == 0 else None

            def stNa(i):
                t0, n = blocks[i]
                norm_A(hs2[i % 2], hns, n // 128)

            def stNb(i):
                t0, n = blocks[i]
                norm_B(hns, hnT, n // 128, gcols["ffn%d" % l])

            def stCin(i):
                t0, n = blocks[i]
                for j in range(NJ):
                    w = S1.get(i * NJP + j // 2)
                    jj = j % 2
                    g, u = fbank(), fbank()
                    _mm(P, g[:, 0:n], [(w[:, jj, kc, 0:128], hnT[:, kc, 0:n]) for kc in range(8)], [w] + hnTr, [g])
                    _mm(P, u[:, 0:n], [(w[:, jj, kc, 128:256], hnT[:, kc, 0:n]) for kc in range(8)], [w] + hnTr, [u])
                    sl = sils[j % 2]
                    _act(P, sl[:, 0:n], g[:, 0:n], AF.Silu, [g], [sl])
                    _tt(P, DVE, actT[:, j, 0:n], sl[:, 0:n], u[:, 0:n], ALU.mult, [sl, u], [actT.sub(j)])

            def stCout(i):
                t0, n = blocks[i]
                nt = n // 128
                hs = hs2[i % 2]
                for j in range(NJ):
                    w = S2.get(i * NJP + j // 2)

                    def fn(e, j=j, w=w):
                        ins = None
                        for tl in range(nt):
                            for hf in range(2):
                                ins = e.matmul(B[tl * 2 + hf][:, :], lhsT=actT[:, j, tl * 128:(tl + 1) * 128],
                                               rhs=w[:, j % 2, hf * 512:(hf + 1) * 512], start=(j == 0), stop=(j == NJ - 1))
                        return ins
                    P.add(PE, fn, [w, actT.sub(j)], sum([bres(k) for k in range(2 * nt)], []))
                for k in sorted(range(2 * nt), key=lambda k: -k):
                    tl, hf = k // 2, k % 2
                    _tt(P, DVE, hs[tl][:, hf * 512:(hf + 1) * 512], hs[tl][:, hf * 512:(hf + 1) * 512],
                        B[k][:, :], ALU.add, [hs[tl]] + bres(k), [hs[tl]])

            def stDa(i):
                t0, n = blocks[i]
                nt = n // 128
                hs = hs2[i % 2]
                if l == 0:
                    for tl in range(nt):
                        P.dma(C.H1[t0 + tl * 128:t0 + (tl + 1) * 128, :], hs[tl][:], reads=[hs[tl]], writes=[C.H1_res[i]])
                    norm_A(hs, hns1, nt)
                else:
                    for tl in range(nt):
                        rstd = norm_rows(hs[tl])
                        ob = orow[oc[0] % 2]
                        oc[0] += 1
                        _stt(P, ob[:], hs[tl][:], rstd[:, 0:1], gfin[:], ALU.mult, ALU.mult, [hs[tl], rstd, gfin], [ob])
                        P.dma(C.out[t0 + tl * 128:t0 + (tl + 1) * 128, :], ob[:], reads=[ob])

            def stDb(i):
                if l != 0:
                    return
                t0, n = blocks[i]
                nt = n // 128
                norm_B(hns1, hnT1, nt, gcols["attn1"])
                h1r = [hnT1.sub(c) for c in range(8)]
                if i == 0:
                    _ts(P, DVE, hnT1[:, :, 0:NPAD], hnT1[:, :, 0:NPAD], rmf[:, 0:1], None, ALU.mult, None, h1r + [rmf], h1r)
                P.dma(C.HNi[i], hnT1[:, :, 0:n], reads=h1r, writes=[C.HNi_res[i]])
                ag(C.HNi_t[i], C.HNo_t[i], C.HNi_res[i], C.HNo_res[i])

            nb = len(blocks)
            stA(0)
            stW(0)
            stNa(0)
            stNb(0)
            for i in range(nb):
                if i + 1 < nb:
                    stA(i + 1)
                stCin(i)
                if i > 0:
                    stDb(i - 1)
                if i + 1 < nb:
                    stW(i + 1)
                    stNa(i + 1)
                stCout(i)
                if i + 1 < nb:
                    stNb(i + 1)
                stDa(i)
            stDb(nb - 1)
            P.barrier()

    def hgrn_phase():
        st, sbp = scope()
        with st:
            hnTs = [sbp("p2_hnT%d" % i, [128, 8, 512], BF16) for i in range(2)]
            X1s = [sbp("p2_X1_%d" % i, [128, NHG, 512], BF16) for i in range(2)]
            WHb = [sbp("p2_WHb%d" % i, [128, 8, 512], BF16) for i in range(4)]
            silq = sbp("p2_silq", [128, NHG, 512], BF16)
            sgT = sbp("p2_sgT", [128, NHG, 512], BF16)
            Vsb2 = [[sbp("p2_V%d_%d" % (j, i), [128, NHG * 128], BF16) for i in range(4)] for j in range(2)]
            NBUF = 4
            mkb = lambda nm, dt: [sbp("p2_%s%d" % (nm, i), [128, 512], dt) for i in range(NBUF)]
            qts, kts, khs = mkb("qt", BF16), mkb("kt", BF16), mkb("kh", BF16)
            khTs = [sbp("p2_khT%d" % i, [128, 4, 128], BF16) for i in range(NBUF)]
            ezs, kks, ebs = mkb("ez", F32), mkb("kk", F32), mkb("eb", F32)
            mk2 = lambda nm, dt: [sbp("p2_%s%d" % (nm, i), [128, 512], dt) for i in range(2)]
            bbs, enbs, lnfs, ebcs = mk2("bb", F32), mk2("enb", F32), mk2("lnf", F32), mk2("ebc", F32)
            Asb4s = [sbp("p2_Asb4%d" % i, [128, 4, 128], BF16) for i in range(NBUF)]
            Usbs = mkb("Usb", F32)
            Sst = sbp("p2_S", [128, NHG, 128], F32)
            P.add(DVE, lambda e: e.memset(Sst[:], 0.0), [], [Sst])
            osqs = [sbp("p2_osq%d" % i, [128, 512], BF16) for i in range(2)]
            ons = [sbp("p2_on%d" % i, [128, 512], F32) for i in range(2)]
            rh = []
            for gb in range(2 * NLB):
                rh.append((lambda b: b[:], C.WHV[0]))
                for hd in range(NHG):
                    rh.append((lambda b: b[:, :, 0:384], C.WHS[hd]))
            SH = Stream(P, WHb, rh, keep=1)
            items = [(gb, hd) for gb in range(2 * NLB) for hd in range(NHG)]

            def geo(gb):
                rk, lb = gb // NLB, gb % NLB
                t0, n = LBLK[lb]
                return rk, lb, n, n // 128

            def h_front(i):
                gb, hd = items[i]
                rk, lb, n, nt = geo(gb)
                hnT = hnTs[gb % 2]
                hr = [hnT.sub(c) for c in range(8)]
                Vs = Vsb2[gb % 2]
                if hd == 0:
                    P.dma(hnT[:, :, 0:n], C.HNo[lb][rk], reads=[C.HNo_res[lb]], writes=hr, q=POOL)
                    w = SH.get(gb * (NHG + 1))
                    for tl in range(nt):
                        y = ybank()
                        _mm(P, y[:, :], [(hnT[:, kc, tl * 128:(tl + 1) * 128], w[:, kc, 0:512]) for kc in range(8)], [w] + hr, [y])
                        _copy(P, ACT if tl % 2 == 0 else DVE, Vs[tl][:, :], y[:, :], [y], [Vs[tl]])
                w = SH.get(gb * (NHG + 1) + 1 + hd)
                k3 = i % NBUF
                ez, kk, eb = ezs[k3], kks[k3], ebs[k3]
                lnf, bb, enb, ebc = [x[i % 2] for x in (lnfs, bbs, enbs, ebcs)]
                qt, kt, kh, khT, Asb4, Usb = qts[k3], kts[k3], khs[k3], khTs[k3], Asb4s[k3], Usbs[k3]
                qp, gp = hbank(), hbank()
                _mm(P, qp[:, 0:n], [(w[:, kc, 0:128], hnT[:, kc, 0:n]) for kc in range(8)], [w] + hr, [qp])
                _mm(P, gp[:, 0:n], [(w[:, kc, 256:384], hnT[:, kc, 0:n]) for kc in range(8)], [w] + hr, [gp])
                _act(P, silq[:, hd, 0:n], qp[:, 0:n], AF.Silu, [qp], [silq.sub(hd)])
                yield
                _act(P, sgT[:, hd, 0:n], gp[:, 0:n], AF.Silu, [gp], [sgT.sub(hd)])
                yield
                zp = hbank()
                _mm(P, zp[:, 0:n], [(w[:, kc, 128:256], hnT[:, kc, 0:n]) for kc in range(8)], [w] + hr, [zp])
                _act(P, ez[:, 0:n], zp[:, 0:n], AF.Exp, [zp], [ez])
                yield
                _act(P, ez[:, 0:n], ez[:, 0:n], AF.Ln, [ez], [ez], bias=1.0)
                yield
                _act(P, kk[:, 0:n], ez[:, 0:n], AF.Exp, [ez, lnomlb], [kk], scale=-1.0, bias=lnomlb[:, hd:hd + 1])
                yield
                _act(P, lnf[:, 0:n], kk[:, 0:n], AF.Ln, [kk], [lnf], scale=-1.0, bias=1.0)
                yield
                for c in range(nt):
                    P.add(DVE, lambda e, c=c: e.tensor_tensor_scan(
                        out=bb[:, c * 128:(c + 1) * 128], data0=onesf[:, 0:128], data1=lnf[:, c * 128:(c + 1) * 128],
                        initial=0.0, op0=ALU.mult, op1=ALU.add), [onesf, lnf], [bb])
                _act(P, eb[:, 0:n], bb[:, 0:n], AF.Exp, [bb], [eb])
                yield
                _act(P, enb[:, 0:n], bb[:, 0:n], AF.Exp, [bb], [enb], scale=-1.0)
                yield
                for c in range(nt):
                    _act(P, ebc[:, c * 128:(c + 1) * 128], bb[:, c * 128:(c + 1) * 128], AF.Exp, [bb], [ebc], scale=-1.0,
                         bias=bb[:, c * 128 + 127:c * 128 + 128])
                _tt(P, DVE, qt[:, 0:n], silq[:, hd, 0:n], eb[:, 0:n], ALU.mult, [silq.sub(hd), eb], [qt])
                yield
                _tt(P, DVE, kt[:, 0:n], kk[:, 0:n], enb[:, 0:n], ALU.mult, [kk, enb], [kt])
                yield
                _tt(P, DVE, kh[:, 0:n], kk[:, 0:n], ebc[:, 0:n], ALU.mult, [kk, ebc], [kh])
                yield
                pT = Tt[0][:, 0:512]
                res = Tt[0].res

                def fnT(e):
                    ins = None
                    for c in range(nt):
                        ins = e.transpose(out=pT[:, c * 128:(c + 1) * 128], in_=kh[:, c * 128:(c + 1) * 128], identity=C.identb[:])
                    return ins
                P.add(PE, fnT, [kh, C.identb], [res])
                _copy(P, ACT, khT[:, 0:nt, :], pT[:, 0:n].rearrange("p (c s) -> p c s", s=128), [res], [khT])
                yield
                Ab, Ub = B[5], B[7]

                def fnA(e):
                    ins = None
                    for c in range(nt):
                        cs = slice(c * 128, (c + 1) * 128)
                        ins = e.matmul(Ab[:, cs], lhsT=kt[:, cs], rhs=qt[:, cs], start=True, stop=True)
                    return ins
                P.add(PE, fnA, [kt, qt], [Ab])
                P.add(DVE, lambda e: e.tensor_tensor(
                    out=Asb4[:, 0:nt, :], in0=Ab[:, 0:n].rearrange("p (c s) -> p c s", s=128),
                    in1=C.tri[:, :].unsqueeze(1).to_broadcast([128, nt, 128]), op=ALU.mult), [Ab, C.tri], [Asb4])

                def fnU(e):
                    ins = None
                    for c in range(nt):
                        cs = slice(c * 128, (c + 1) * 128)
                        ins = e.matmul(Ub[:, cs], lhsT=khT[:, c, :], rhs=Vs[c][:, hd * 128:(hd + 1) * 128], start=True, stop=True)
                    return ins
                P.add(PE, fnU, [khT] + list(Vs[:nt]), [Ub])
                _copy(P, ACT, Usb[:, 0:n], Ub[:, 0:n], [Ub], [Usb])
                yield

            Sbf4s = [sbp("p2_Sbf4%d" % i, [128, 4, 128], BF16) for i in range(NBUF)]

            def h_state(i):
                gb, hd = items[i]
                rk, lb, n, nt = geo(gb)
                k3 = i % NBUF
                eb, Usb, Sbf4 = ebs[k3], Usbs[k3], Sbf4s[k3]
                for c in range(nt):
                    cs = slice(c * 128, (c + 1) * 128)
                    _copy(P, DVE, Sbf4[:, c, :], Sst[:, hd, :], [Sst.sub(hd)], [Sbf4])
                    _stt(P, Sst[:, hd, :], Sst[:, hd, :], eb[:, c * 128 + 127:c * 128 + 128], Usb[:, cs], ALU.mult, ALU.add,
                         [Sst.sub(hd), eb, Usb], [Sst.sub(hd)])

            def h_back(i):
                gb, hd = items[i]
                rk, lb, n, nt = geo(gb)
                Vs = Vsb2[gb % 2]
                k3 = i % NBUF
                qt, Asb4, Sbf4 = qts[k3], Asb4s[k3], Sbf4s[k3]
                X1 = X1s[gb % 2]
                osq, on = osqs[i % 2], ons[i % 2]
                op = B[3 + hd % 2]

                def fnO(e):
                    ins = None
                    for c in range(nt):
                        cs = slice(c * 128, (c + 1) * 128)
                        e.matmul(op[:, cs], lhsT=Vs[c][:, hd * 128:(hd + 1) * 128], rhs=Asb4[:, c, :], start=True, stop=False)
                        ins = e.matmul(op[:, cs], lhsT=Sbf4[:, c, :], rhs=qt[:, cs], start=False, stop=True)
                    return ins
                P.add(PE, fnO, [Sbf4, qt, Asb4] + list(Vs[:nt]), [op])
                yield
                _act(P, osq[:, 0:n], op[:, 0:n], AF.Square, [op], [osq])
                yield
                sp = hbank()
                _mm(P, sp[:, 0:n], [(ones128b[:], osq[:, 0:n])], [ones128b, osq], [sp])
                lnv, rs = ezs[k3], kks[k3]
                _act(P, lnv[:, 0:n], sp[:, 0:n], AF.Ln, [sp], [lnv], scale=1.0 / 128, bias=EPS)
                yield
                _act(P, rs[:, 0:n], lnv[:, 0:n], AF.Exp, [lnv], [rs], scale=-0.5)
                yield
                _stt(P, on[:, 0:n], op[:, 0:n], gn[:, 0:1], rs[:, 0:n], ALU.mult, ALU.mult, [op, gn, rs], [on])
                yield
                _tt(P, DVE, X1[:, hd, 0:n], on[:, 0:n], sgT[:, hd, 0:n], ALU.mult, [on, sgT.sub(hd)], [X1])
                yield
                if hd == NHG - 1:
                    P.dma(C.OG1i[gb], X1[:, :, 0:n], reads=[X1], writes=[C.OG1i_res[gb]])
                    ag(C.OG1i_t[gb], C.OG1o_t[gb], C.OG1i_res[gb], C.OG1o_res[gb])

            def zipped(gens):
                gens = list(gens)
                if DBG == "seq":
                    for g in gens:
                        for _ in g:
                            pass
                    return
                while gens:
                    for g in list(gens):
                        try:
                            next(g)
                        except StopIteration:
                            gens.remove(g)

            N = len(items)
            npair = N // 2
            zipped([h_front(0), h_front(1)])
            h_state(0)
            h_state(1)
            for t in range(npair):
                gens = [h_back(2 * t), h_back(2 * t + 1)]
                if t + 1 < npair:
                    gens = [h_front(2 * t + 2), h_front(2 * t + 3)] + gens
                zipped(gens)
                if t + 1 < npair:
                    h_state(2 * t + 2)
                    h_state(2 * t + 3)
            P.barrier()

    ffn_phase(0)
    if C.s3parts >= 2:
        hgrn_phase()
    if C.s3parts >= 3:
        ffn_phase(1)
```

```python
import numpy as np
import concourse.bass as bass
import concourse.mybir as mybir
from concourse.bass_utils import run_bass_kernel_spmd

F32 = mybir.dt.float32
BF16 = mybir.dt.bfloat16
AF = mybir.ActivationFunctionType
ALU = mybir.AluOpType
AX = mybir.AxisListType

PE, ACT, DVE, POOL, SP = "pe", "act", "dve", "pool", "sp"
ENGS = (PE, ACT, DVE, POOL, SP)
NDMA_SEM = 12


class Res:
    __slots__ = ("w", "r")

    def __init__(self):
        self.w = None
        self.r = []


class Buf:
    def __init__(self, t, name=""):
        self.t = t
        self.name = name
        self.res = Res()
        self.subs = {}

    def __getitem__(self, k):
        return self.t[k]

    def sub(self, key):
        r = self.subs.get(key)
        if r is None:
            r = self.subs[key] = Res()
        return r


def _res(x):
    return x.res if isinstance(x, Buf) else x


class Op:
    __slots__ = ("eng", "fn", "raw", "oth", "dma", "sigval", "need", "dsem", "dval", "dprev", "pos")


class Prog:
    def __init__(self, nc):
        self.nc = nc
        self.ops = {e: [] for e in ENGS}
        self.ndma = {e: 0 for e in ENGS}

    def add(self, eng, fn, reads=(), writes=(), dma=False):
        op = Op()
        op.eng, op.fn, op.dma = eng, fn, dma
        op.raw, op.oth = set(), set()
        op.need = False
        op.sigval = None
        for r in reads:
            r = _res(r)
            if r.w is not None:
                op.raw.add(r.w)
        for w in writes:
            w = _res(w)
            if w.w is not None:
                op.oth.add(w.w)
            for x in w.r:
                op.oth.add(x)
        for r in reads:
            _res(r).r.append(op)
        for w in writes:
            w = _res(w)
            w.w = op
            w.r = []
        if dma:
            i = self.ndma[eng]
            self.ndma[eng] += 1
            op.dsem = i % NDMA_SEM
            op.dval = 16 * (i // NDMA_SEM + 1)
        self.ops[eng].append(op)
        return op

    def cc(self, fn, reads=(), writes=()):
        op = self.add(POOL, fn, reads, writes, dma=True)
        self.ndma[POOL] -= 1
        self.ncc = getattr(self, "ncc", 0) + 1
        op.dsem = "cc"
        op.dval = self.ncc
        return op

    def barrier(self):
        last = {}
        for e in ENGS:
            lc = None
            for o in reversed(self.ops[e]):
                if isinstance(o, Op) and not o.dma:
                    lc = o
                    break
            if lc is not None:
                lc.need = True
            last[e] = lc
        mark = ("barrier", last, dict(self.ndma), getattr(self, "ncc", 0))
        for e in ENGS:
            self.ops[e].append(mark)

    def dma(self, out, in_, reads=(), writes=(), q=SP, **kw):
        return self.add(q, lambda e: e.dma_start(out=out, in_=in_, **kw), reads, writes, dma=True)

    def _needed(self, o, d):
        if d.dma:
            return True
        if d.eng != o.eng:
            return True
        if o.dma:
            return True
        if o.eng == PE:
            return False
        return d in o.raw

    def _deps(self, o):
        best = {}
        out = []
        for d in list(o.raw) + list(o.oth):
            if not self._needed(o, d):
                continue
            if d.dma:
                out.append(d)
            else:
                b = best.get(d.eng)
                if b is None or d.pos > b.pos:
                    best[d.eng] = d
        return out + list(best.values())

    def emit(self):
        nc = self.nc
        for e in ENGS:
            for k, o in enumerate(self.ops[e]):
                if isinstance(o, Op):
                    o.pos = k
        for e in ENGS:
            for o in self.ops[e]:
                if not isinstance(o, Op):
                    continue
                for d in self._deps(o):
                    if not d.dma:
                        d.need = True
        for e in ENGS:
            c = 0
            for o in self.ops[e]:
                if isinstance(o, Op) and (not o.dma) and o.need:
                    c += 1
                    o.sigval = c
        import contextlib
        with contextlib.ExitStack() as st:
            esem = {e: st.enter_context(nc.semaphore("s_" + e)) for e in ENGS}
            dsem = {e: [st.enter_context(nc.semaphore("d_%s%d" % (e, i))) for i in range(NDMA_SEM)]
                    for e in ENGS if self.ndma[e] > 0}
            ccsem = st.enter_context(nc.semaphore("s_cc"))
            block = st.enter_context(nc.Block())

            def run(ename, eng):
                seen = {}

                def wait(sem, val):
                    k = id(sem)
                    if seen.get(k, 0) >= val:
                        return
                    seen[k] = val
                    eng.wait_ge(sem, val)

                for o in self.ops[ename]:
                    if not isinstance(o, Op):
                        _, last, nd, ncc = o
                        if ncc > 0:
                            wait(ccsem, ncc)
                        for e2 in ENGS:
                            if last[e2] is not None:
                                wait(esem[e2], last[e2].sigval)
                            n = nd[e2]
                            for i in range(NDMA_SEM):
                                cnt = (n - i + NDMA_SEM - 1) // NDMA_SEM if n > i else 0
                                if cnt > 0:
                                    wait(dsem[e2][i], 16 * cnt)
                        continue
                    for d in self._deps(o):
                        if d.dma and d.dsem == "cc":
                            wait(ccsem, d.dval)
                        elif d.dma:
                            wait(dsem[d.eng][d.dsem], d.dval)
                        else:
                            wait(esem[d.eng], d.sigval)
                    if o.dma and o.dsem == "cc":
                        if o.dval > 1:
                            wait(ccsem, o.dval - 1)
                        o.fn(eng).then_inc(ccsem)
                    elif o.dma:
                        s = dsem[ename][o.dsem]
                        if o.dval > 16:
                            wait(s, o.dval - 16)
                        o.fn(eng).then_inc(s, 16)
                    else:
                        ins = o.fn(eng)
                        if o.need:
                            ins.then_inc(esem[ename], 1)
                if ename == POOL and getattr(self, "ncc", 0) > 0:
                    wait(ccsem, self.ncc)
                if self.ndma[ename] > 0:
                    n = self.ndma[ename]
                    for i in range(NDMA_SEM):
                        cnt = (n - i + NDMA_SEM - 1) // NDMA_SEM if n > i else 0
                        if cnt > 0:
                            wait(dsem[ename][i], 16 * cnt)

            @block.tensor
            def _(eng):
                run(PE, eng)

            @block.scalar
            def _(eng):
                run(ACT, eng)

            @block.vector
            def _(eng):
                run(DVE, eng)

            @block.gpsimd
            def _(eng):
                run(POOL, eng)

            @block.sync
            def _(eng):
                run(SP, eng)
import contextlib
import ml_dtypes

D = 1024
H = 16
DH = 64
NMETA = 16
SEQ = 8192
NPAD = 112
LP = NPAD + NMETA + SEQ
NT = LP // 128
FF = 2816
NJ = FF // 128
EPS = 1e-6
BLOCKS = [(0, 128)] + [(128 + 512 * i, 512) for i in range(16)]
NB = len(BLOCKS)


import os
DBG = os.environ.get('KDBG', '')


class Ctx:
    pass


def _mm(P, out_ap, pairs, reads, writes, start=True, stop=True):
    def fn(e):
        n = len(pairs)
        ins = None
        for i, (l, r) in enumerate(pairs):
            ins = e.matmul(out_ap, lhsT=l, rhs=r, start=(start and i == 0), stop=(stop and i == n - 1))
        return ins
    return P.add(PE, fn, reads, writes)


def _act(P, out, in_, func, reads, writes, **kw):
    return P.add(ACT, lambda e: e.activation(out=out, in_=in_, func=func, **kw), reads, writes)


def _tt(P, eng, out, in0, in1, op, reads, writes):
    return P.add(eng, lambda e: e.tensor_tensor(out=out, in0=in0, in1=in1, op=op), reads, writes)


def _ts(P, eng, out, in0, s1, s2, op0, op1, reads, writes):
    if op1 is None:
        return P.add(eng, lambda e: e.tensor_scalar(out=out, in0=in0, scalar1=s1, scalar2=None, op0=op0), reads, writes)
    return P.add(eng, lambda e: e.tensor_scalar(out=out, in0=in0, scalar1=s1, scalar2=s2, op0=op0, op1=op1), reads, writes)


def _stt(P, out, in0, scalar, in1, op0, op1, reads, writes):
    return P.add(DVE, lambda e: e.scalar_tensor_tensor(out=out, in0=in0, scalar=scalar, in1=in1, op0=op0, op1=op1), reads, writes)


def _copy(P, eng, out, in_, reads, writes):
    if eng == ACT:
        return P.add(ACT, lambda e: e.copy(out=out, in_=in_), reads, writes)
    return P.add(eng, lambda e: e.tensor_copy(out=out, in_=in_), reads, writes)


def _rmsnorm_rows(P, C, ht, hn, gbc, tmp):
    junk, ssq, lnv, rstd = tmp
    _act(P, junk[:], ht[:], AF.Square, [ht], [junk, ssq], accum_out=ssq[:])
    _act(P, lnv[:], ssq[:], AF.Ln, [ssq], [lnv], scale=1.0 / D, bias=EPS)
    _act(P, rstd[:], lnv[:], AF.Exp, [lnv], [rstd], scale=-0.5)
    _stt(P, hn[:], ht[:], rstd[:, 0:1], gbc[:], ALU.mult, ALU.mult, [ht, rstd, gbc], [hn])


def _transpose_block(P, C, hns, nt, hnT, pTs, cnt0):
    n = nt * 128
    for c in range(8):
        pT = pTs[(cnt0 + c) % len(pTs)]

        def fn(e, c=c, pT=pT):
            ins = None
            for tl in range(nt):
                ins = e.transpose(out=pT[:, tl * 128:(tl + 1) * 128], in_=hns[tl][:, c * 128:(c + 1) * 128],
                                  identity=C.identb[:])
            return ins
        P.add(PE, fn, list(hns[:nt]) + [C.identb], [pT])
        _copy(P, ACT if c % 2 == 0 else DVE, hnT[:, c, 0:n], pT[:, 0:n], [pT], [hnT.sub(c)])


def stage1(P, nc, C, sb, ps, bounce=None):
    NHL = C.nheads
    NQ = NHL // 2
    WC = 4 * NHL * DH + NHL
    W = sb("s1_W", [128, 8, WC], BF16)
    wst = [sb("s1_wst%d" % i, [128, WC], F32) for i in range(2)]
    Wr = [W.sub(k) for k in range(8)]
    gbc = sb("s1_gbc", [128, D], F32)
    P.dma(gbc[:], C.attn_norm[0].partition_broadcast(128), writes=[gbc])
    gcol = sb("s1_gcol", [128, 2], F32)
    for hh in range(2):
        P.dma(gcol[64 * hh:64 * hh + 64, 0:1], C.fox_q_norm[0].rearrange("(p o) -> p o", o=1), writes=[gcol])
        P.dma(gcol[64 * hh:64 * hh + 64, 1:2], C.fox_k_norm[0].rearrange("(p o) -> p o", o=1), writes=[gcol])
    _ts(P, DVE, gcol[:, 0:1], gcol[:, 0:1], 0.125, None, ALU.mult, None, [gcol], [gcol])
    negbf = sb("s1_negbf", [NHL, 1], F32)
    P.dma(negbf[:], C.fox_b_f[0].rearrange("(p o) -> p o", o=1), writes=[negbf])
    _ts(P, DVE, negbf[:], negbf[:], -1.0, None, ALU.mult, None, [negbf], [negbf])
    ones16 = sb("s1_ones16", [NHL, 512], F32)
    P.add(DVE, lambda e: e.memset(ones16[:], 1.0), [], [ones16])
    zero16 = sb("s1_zero16", [NHL, 1], F32)
    P.add(DVE, lambda e: e.memset(zero16[:], 0.0), [], [zero16])

    hts = [sb("s1_ht%d" % i, [128, D], F32) for i in range(2)]
    hns = [sb("s1_hn%d" % i, [128, D], BF16) for i in range(8)]
    junk = sb("s1_junk", [128, D], BF16)
    ssqs = [[sb("s1_ssq%d_%d" % (i, j), [128, 1], F32) for j in range(3)] for i in range(2)]
    hnTs = [sb("s1_hnT%d" % i, [128, 8, 512], BF16) for i in range(2)]
    pTs = [ps("s1_pT%d" % i, [128, 512], BF16) for i in range(2)]
    psA = [ps("s1_psA%d" % i, [128, 512], F32) for i in range(4)]
    psSs = [ps("s1_psS%d" % i, [128, 512], F32) for i in range(2)]
    sqs = [sb("s1_sq%d" % i, [128, 512], BF16) for i in range(2)]
    lnvs = [sb("s1_lnv%d" % i, [128, 512], F32) for i in range(2)]
    rss = [sb("s1_rs%d" % i, [128, 512], F32) for i in range(2)]
    outs = [sb("s1_out%d" % i, [128, 512], BF16) for i in range(6)]
    vsts = [sb("s1_vst%d" % i, [128, NHL, 4, 65], BF16) for i in range(2)]
    for v in vsts:
        P.add(POOL, lambda e, v=v: e.memset(v[:, :, :, 64:65], 1.0), [], [v])
    ef = sb("s1_ef", [NHL, 512], F32)
    lf = sb("s1_lf", [NHL, 512], F32)
    cps = [sb("s1_cp%d" % i, [NHL, 512], F32) for i in range(2)]
    hsp = [[sb("s1_hsp%d_%d" % (i, j), [NHL, 512], BF16) for j in range(6)] for i in range(2)]
    r1 = sb("s1_r1", [NHL, 512], F32)
    r2 = sb("s1_r2", [NHL, 512], F32)
    onesb = sb("s1_onesb", [3, LP // 4], BF16)
    P.add(DVE, lambda e: e.memset(onesb[:], 1.0), [], [onesb])
    for h in range(NHL):
        for qd in range(4):
            cs_ = slice(qd * (LP // 4), (qd + 1) * (LP // 4))
            P.dma(C.KT_d[h, 67:70, cs_], onesb[0:3, :], reads=[onesb], q=POOL)
            P.dma(C.QT_d[h, 64:67, cs_], onesb[0:3, :], reads=[onesb], q=POOL)

    cA = 0
    cO = 0
    cT = 0
    cTh = [0]

    def fa_load(bi, tl):
        t0, n = BLOCKS[bi]
        if bi >= NB or tl >= n // 128:
            return
        ht = hts[tl % 2]
        P.dma(ht[:], C.h0[t0 + tl * 128:t0 + (tl + 1) * 128, :], writes=[ht])

    def fa_norm(bi, tl):
        t0, n = BLOCKS[bi]
        if bi >= NB or tl >= n // 128:
            return
        myhn = hns[(bi % 2) * 4:(bi % 2) * 4 + 4]
        tmp = [junk] + ssqs[cTh[0] % 2]
        cTh[0] += 1
        _rmsnorm_rows(P, C, hts[tl % 2], myhn[tl], gbc, tmp)

    def s1_front_a(bi):
        for tl in range(4):
            fa_load(bi, tl)
            fa_norm(bi, tl)

    def s1_front_b(bi):
        t0, n = BLOCKS[bi]
        myhn = hns[(bi % 2) * 4:(bi % 2) * 4 + 4]
        _transpose_block(P, C, myhn, n // 128, hnTs[bi % 2], pTs, 0)

    s1_front_a(0)
    for kc in range(8):
        P.dma(wst[kc % 2][:], C.fox_w_in[kc * 128:(kc + 1) * 128, :], writes=[wst[kc % 2]])
        _copy(P, ACT if kc % 2 == 0 else DVE, W[:, kc, :], wst[kc % 2][:], [wst[kc % 2]], [W.sub(kc)])
    s1_front_b(0)
    for bi, (t0, n) in enumerate(BLOCKS):
        nt = n // 128
        hnT = hnTs[bi % 2]
        if bi + 1 < NB:
            fa_load(bi + 1, 0)
            fa_load(bi + 1, 1)
        hr = [hnT.sub(c) for c in range(8)]
        pqs = {}

        def qk_a(c):
            nonlocal cA
            cols = c * 128
            pq = psA[cA % 4]
            cA += 1
            pqs[c] = pq
            _mm(P, pq[:, 0:n], [(W[:, kc, cols:cols + 128], hnT[:, kc, 0:n]) for kc in range(8)], Wr + hr, [pq])
            _act(P, sqs[c % 2][:, 0:n], pq[:, 0:n], AF.Square, [pq], [sqs[c % 2]])

        def qk_b(c):
            nonlocal cO
            pq = pqs[c]
            sq, lnv, rs = sqs[c % 2], lnvs[c % 2], rss[c % 2]
            psS = psSs[c % 2]
            _mm(P, psS[:, 0:n], [(C.bdones[:], sq[:, 0:n])], [C.bdones, sq], [psS])
            _act(P, lnv[:, 0:n], psS[:, 0:n], AF.Ln, [psS], [lnv], scale=1.0 / DH, bias=EPS)
            _act(P, rs[:, 0:n], lnv[:, 0:n], AF.Exp, [lnv], [rs], scale=-0.5)
            ob = outs[cO % 6]
            cO += 1
            gi = 0 if c < NQ else 1
            _stt(P, ob[:, 0:n], pq[:, 0:n], gcol[:, gi:gi + 1], rs[:, 0:n], ALU.mult, ALU.mult, [pq, gcol, rs], [ob])
            dst = C.QT_d if c < NQ else C.KT_d
            for hh in range(2):
                h = (c % NQ) * 2 + hh
                P.dma(dst[h, 0:64, t0:t0 + n], ob[64 * hh:64 * hh + 64, 0:n], reads=[ob], q=POOL)

        qk_a(0)
        for c in range(2 * NQ):
            if c + 1 < 2 * NQ:
                qk_a(c + 1)
            qk_b(c)
        if bi + 1 < NB:
            fa_norm(bi + 1, 0)
            fa_norm(bi + 1, 1)
            fa_load(bi + 1, 2)
            fa_load(bi + 1, 3)
        for c in range(NQ):
            cols = 3 * NHL * DH + c * 128
            pg = psA[cA % 4]
            cA += 1
            _mm(P, pg[:, 0:n], [(W[:, kc, cols:cols + 128], hnT[:, kc, 0:n]) for kc in range(8)], Wr + hr, [pg])
            lnv, rs = lnvs[c % 2], rss[c % 2]
            _act(P, rs[:, 0:n], pg[:, 0:n], AF.Exp, [pg], [rs], scale=-1.0)
            _act(P, lnv[:, 0:n], rs[:, 0:n], AF.Ln, [rs], [lnv], bias=1.0)
            ob = outs[cO % 6]
            cO += 1
            _act(P, ob[:, 0:n], lnv[:, 0:n], AF.Exp, [lnv], [ob], scale=-1.0)
            P.dma(C.SG_d[c * 128:(c + 1) * 128, t0:t0 + n], ob[:, 0:n], reads=[ob], q=POOL)
        pf = psA[cA % 4]
        cA += 1
        _mm(P, pf[0:NHL, 0:n], [(W[:, kc, 4 * NHL * DH:WC], hnT[:, kc, 0:n]) for kc in range(8)], Wr + hr, [pf])
        _act(P, ef[:, 0:n], pf[0:NHL, 0:n], AF.Exp, [pf], [ef], scale=-1.0, bias=negbf[:, 0:1])
        _act(P, lf[:, 0:n], ef[:, 0:n], AF.Ln, [ef], [lf], bias=1.0)
        cp = cps[bi % 2]
        if bi == 0:
            ref_ap, ref_b = zero16[:, 0:1], zero16
        else:
            pn = BLOCKS[bi - 1][1]
            ref_ap, ref_b = cps[(bi - 1) % 2][:, pn - 1:pn], cps[(bi - 1) % 2]
        P.add(DVE, lambda e, cp=cp, ref_ap=ref_ap, n=n: e.tensor_tensor_scan(
            out=cp[:, 0:n], data0=ones16[:, 0:n], data1=lf[:, 0:n], initial=ref_ap, op0=ALU.mult, op1=ALU.add),
            [ones16, lf, ref_b], [cp])
        hh = hsp[bi % 2]
        _copy(P, DVE, hh[0][:, 0:n], cp[:, 0:n], [cp], [hh[0]])
        _tt(P, DVE, r1[:, 0:n], cp[:, 0:n], hh[0][:, 0:n], ALU.subtract, [cp, hh[0]], [r1])
        _copy(P, DVE, hh[1][:, 0:n], r1[:, 0:n], [r1], [hh[1]])
        _tt(P, DVE, r2[:, 0:n], r1[:, 0:n], hh[1][:, 0:n], ALU.subtract, [r1, hh[1]], [r2])
        _copy(P, DVE, hh[2][:, 0:n], r2[:, 0:n], [r2], [hh[2]])
        for j in range(3):
            _ts(P, DVE, hh[3 + j][:, 0:n], hh[j][:, 0:n], -1.0, None, ALU.mult, None, [hh[j]], [hh[3 + j]])
            P.dma(C.KT_d[:, 64 + j, t0:t0 + n], hh[j][:, 0:n], reads=[hh[j]])
            P.dma(C.QT_d[:, 67 + j, t0:t0 + n], hh[3 + j][:, 0:n], reads=[hh[3 + j]])
        if bi + 1 < NB:
            fa_norm(bi + 1, 2)
            fa_norm(bi + 1, 3)
        vst = vsts[bi % 2]
        if bi == 0:
            P.add(POOL, lambda e, v=vst: e.memset(v[0:NPAD, :, 0:1, 64:65], 0.0), [], [vst])
        for tl in range(nt):
            for half in range(NHL // 8):
                pv = psA[cA % 4]
                cA += 1
                cols = 2 * NHL * DH + half * 512
                _mm(P, pv[:, :], [(hnT[:, kc, tl * 128:(tl + 1) * 128], W[:, kc, cols:cols + 512]) for kc in range(8)],
                    Wr + hr, [pv])
                _copy(P, DVE if half == 0 else ACT, vst[:, 8 * half:8 * half + 8, tl, 0:64],
                      pv[:, :].rearrange("p (h d) -> p h d", h=8), [pv], [vst])
        for h in range(NHL):
            P.dma(C.VA_d[h, :, t0 // 128:t0 // 128 + nt, :], vst[:, h, 0:nt, :], reads=[vst])
        if bi == 0:
            P.add(POOL, lambda e, v=vst: e.memset(v[0:NPAD, :, 0:1, 64:65], 1.0), [], [vst])
        if bi + 1 < NB:
            s1_front_b(bi + 1)


def stage2(P, nc, C, sb, ps, bounce=None):
    for kc in range(8):
        P.dma(C.Wo[0][:, kc, :], C.fox_w_out[kc * 128:(kc + 1) * 128, :], writes=[C.Wo[0]], q=POOL)
        P.dma(C.Wo[1][:, kc, :], C.hgrn_w_out[kc * 128:(kc + 1) * 128, :], writes=[C.Wo[1]], q=POOL)
    gate_t = sb("s2_gate", [128, 1], F32)
    gate = [Res()]
    conv = convert_weights(P, nc, C, bounce, gate) if bounce is not None else iter(())
    KTb = [sb("s2_KT%d" % i, [70, LP], BF16) for i in range(2)]
    VAb = [sb("s2_VA%d" % i, [128, NT, 65], BF16) for i in range(2)]
    Qbs = [sb("s2_Q%d" % i, [70, 512], BF16) for i in range(3)]
    sgs = [sb("s2_sg%d" % i, [64, 512], BF16) for i in range(3)]
    biases = [sb("s2_bias%d" % i, [128, NT], F32) for i in range(3)]
    Pts = [sb("s2_Pt%d" % i, [128, 512], BF16) for i in range(6)]
    Sb = [ps("s2_S%d" % i, [128, 512], F32) for i in range(5)]
    Ob = [ps("s2_O%d" % i, [128, 512], F32) for i in range(2)]
    Rps = ps("s2_R", [128, 512], F32)
    rds = [sb("s2_rd%d" % i, [128, 512], F32) for i in range(2)]
    rd2s = [sb("s2_rd2%d" % i, [128, 512], F32) for i in range(2)]
    pending = []
    onesf = sb("s2_onesf", [128, 64], F32)
    P.add(DVE, lambda e: e.memset(onesf[:], 1.0), [], [onesf])
    zt2 = sb("s2_zt", [64, 128], BF16)
    P.add(DVE, lambda e: e.memset(zt2[:], 0.0), [], [zt2])
    for h in range(C.nheads):
        P.dma(C.OGi[h][:, LP:LP + 128], zt2[:], reads=[zt2], writes=[C.OGi_res[h]])
    Osbs = [sb("s2_Osb%d" % i, [64, 512], F32) for i in range(2)]
    og1s = [sb("s2_og1%d" % i, [64, 512], F32) for i in range(2)]
    ogbs = [sb("s2_ogb%d" % i, [64, 512], BF16) for i in range(2)]

    items = []
    for h in range(C.nheads):
        for bi, (t0, n) in enumerate(BLOCKS):
            nkt = (t0 + n) // 128
            for kt in range(nkt):
                items.append((h, bi, kt, nkt))
    state = {}

    def prologue(h, bi):
        t0, n = BLOCKS[bi]
        g = h * NB + bi
        if g in state or g >= C.nheads * NB:
            return
        state[g] = True
        if g % 3 == 0:
            gate[0] = Res()
            P.add(DVE, lambda e: e.memset(gate_t[:], 0.0), [], [gate[0]])
            next(conv, None)
        if bi == 0:
            KTh, VAh = KTb[h % 2], VAb[h % 2]
            P.dma(KTh[:, :], C.KT_d[h], writes=[KTh])
            P.dma(VAh[:], C.VA_d[h], writes=[VAh])
        Qb, sg, bias = Qbs[g % 3], sgs[g % 3], biases[g % 3]
        P.dma(Qb[0:70, 0:n], C.QT_d[h, :, t0:t0 + n], writes=[Qb])
        P.dma(sg[0:64, 0:n], C.SG_d[64 * h:64 * h + 64, t0:t0 + n], writes=[sg])

    def qk(i):
        h, bi, kt, nkt = items[i]
        t0, n = BLOCKS[bi]
        g = h * NB + bi
        if kt == 0:
            prologue(h, bi)
        j = kt - t0 // 128
        c0 = 128 * j if j >= 0 else 0
        S = Sb[i % 5]
        _mm(P, S[:, c0:n], [(KTb[h % 2][0:70, kt * 128:(kt + 1) * 128], Qbs[g % 3][0:70, c0:n])],
            [KTb[h % 2], Qbs[g % 3]], [S])

    def rest(i):
        h, bi, kt, nkt = items[i]
        t0, n = BLOCKS[bi]
        g = h * NB + bi
        j = kt - t0 // 128
        c0 = 128 * j if j >= 0 else 0
        S, Pt, O = Sb[i % 5], Pts[i % 6], Ob[g % 2]
        bias = biases[g % 3]
        if kt == 0:
            while pending and pending[0][0] <= g - 2:
                epilogue2(pending.pop(0)[0])
            prologue((g + 1) // NB, (g + 1) % NB)
        _act(P, Pt[:, c0:n], S[:, c0:n], AF.Exp, [S], [Pt])
        if j >= 0:
            _tt(P, DVE, Pt[:, c0:c0 + 128], Pt[:, c0:c0 + 128], C.tri[:], ALU.mult, [Pt, C.tri], [Pt])
        _mm(P, O[0:65, c0:n], [(VAb[h % 2][:, kt, 0:65], Pt[:, c0:n])], [VAb[h % 2], Pt], [O],
            start=(kt == 0), stop=(kt == nkt - 1))
        if kt == nkt - 1:
            Osb = Osbs[g % 2]
            rd, rd2 = rds[g % 2], rd2s[g % 2]
            _ts(P, DVE, rd[64:65, 0:n], O[64:65, 0:n], 1e-30, None, ALU.max, None, [O], [rd])
            P.add(DVE, lambda e: e.reciprocal(out=rd2[64:65, 0:n], in_=rd[64:65, 0:n]), [rd], [rd2])
            _copy(P, ACT, Osb[:, 0:n], O[0:64, 0:n], [O], [Osb])
            pending.append((g, i))
        while pending and (i - pending[0][1] >= 2 or i == len(items) - 1):
            epilogue2(pending.pop(0)[0])

    def epilogue2(g):
        h, bi = g // NB, g % NB
        t0, n = BLOCKS[bi]
        Osb, og1, ogb, sg = Osbs[g % 2], og1s[g % 2], ogbs[g % 2], sgs[g % 3]
        rd2 = rd2s[g % 2]
        _mm(P, Rps[0:64, 0:n], [(onesf[64:65, 0:64], rd2[64:65, 0:n])], [onesf, rd2], [Rps])
        _tt(P, DVE, og1[:, 0:n], Osb[:, 0:n], Rps[0:64, 0:n], ALU.mult, [Osb, Rps], [og1])
        _tt(P, DVE, ogb[:, 0:n], og1[:, 0:n], sg[0:64, 0:n], ALU.mult, [og1, sg], [ogb])
        P.dma(C.OGi[h][:, t0:t0 + n], ogb[:, 0:n], reads=[ogb], writes=[C.OGi_res[h]])
        if bi == NB - 1:
            P.cc(lambda e, h=h: e.collective_compute("AllGather", ALU.bypass, replica_groups=C.RG,
                                                     ins=[C.OGi_t[h].ap().opt()], outs=[C.OGo_t[h].ap().opt()]),
                 reads=[C.OGi_res[h]], writes=[C.OGo_res[h]])

    LA = 3
    N = len(items)
    for i in range(N + LA):
        if i < N:
            qk(i)
        if i - LA >= 0:
            rest(i - LA)
    for _ in conv:
        pass


def build(stages=3, debug=False, nheads=H // 2, conv=None, s3parts=99, nblk=NB):
    nc = bass.Bass("TRN2", target_bir_lowering=False)
    C = Ctx()
    C.nheads = nheads
    C.s3parts = s3parts
    C.nblk = nblk
    if conv is None:
        conv = stages >= 3

    def din(name, shape, dt=F32):
        return nc.dram_tensor(name, list(shape), dt, kind="ExternalInput").ap()

    def dscr(name, shape, dt, out=False):
        return nc.dram_tensor(name, list(shape), dt, kind="ExternalOutput" if out else "Internal").ap()

    C.h0 = din("h0", [LP, D])
    C.attn_norm = din("attn_norm", [2, D])
    C.ffn_norm = din("ffn_norm", [2, D])
    C.final_norm = din("final_norm", [D])
    NHL = nheads
    C.fox_w_in = din("fox_w_in", [D, 4 * NHL * DH + NHL])
    C.fox_b_f = din("fox_b_f", [1, NHL])
    C.fox_q_norm = din("fox_q_norm", [1, DH])
    C.fox_k_norm = din("fox_k_norm", [1, DH])
    C.fox_w_out = din("fox_w_out", [D, D])
    C.hgrn_w_in = din("hgrn_w_in", [D, 4 * NHG * 128])
    C.lb = din("hgrn_lower_bounds", [2, NHG * 128])
    C.h0h = din("h0h", [HT * 128, D])
    C.rmask = din("rmask", [128, 8 * 512], mybir.dt.uint16)
    C.rmaskf = din("rmaskf", [128, 1])
    C.g_norm = din("hgrn_g_norm", [1, 128])
    C.hgrn_w_out = din("hgrn_w_out", [D, D])
    C.ffn_w_in = din("ffn_w_in", [2, D, 2 * FF])
    C.ffn_w_out = din("ffn_w_out", [2, FF, D])
    c_identb = din("c_identb", [128, 128], BF16)
    c_identf = din("c_identf", [128, 128], F32)
    c_bdones = din("c_bdones", [128, 128], BF16)
    c_tri = din("c_tri", [128, 128], BF16)
    d1 = debug and stages == 1
    C.QT_d = dscr("QT_d", [NHL, 70, LP], BF16, d1)
    C.KT_d = dscr("KT_d", [NHL, 70, LP], BF16, d1)
    C.VA_d = dscr("VA_d", [NHL, 128, NT, 65], BF16, d1)
    C.SG_d = dscr("SG_d", [NHL * DH, LP], BF16, d1)
    C.RG = [[0, 1], [2, 3], [4, 5], [6, 7]]
    C.OGi_t = [nc.dram_tensor("OGi%d" % h, [DH, LP + 128], BF16) for h in range(NHL)]
    C.OGo_t = [nc.dram_tensor("OGo%d" % h, [2 * DH, LP + 128], BF16) for h in range(NHL)]
    C.HNi_t = [nc.dram_tensor("HNi%d" % i, [128, 8, n], BF16) for i, (t0, n) in enumerate(LBLK)]
    C.HNo_t = [nc.dram_tensor("HNo%d" % i, [256, 8, n], BF16) for i, (t0, n) in enumerate(LBLK)]
    C.HNi = [t.ap() for t in C.HNi_t]
    C.HNo = [t.ap().rearrange("(r p) k n -> r p k n", r=2) for t in C.HNo_t]
    C.HNi_res = [Res() for _ in LBLK]
    C.HNo_res = [Res() for _ in LBLK]
    C.OG1i_t = [nc.dram_tensor("OG1i%d" % g, [128, NHG, LBLK[g % NLB][1]], BF16) for g in range(2 * NLB)]
    C.OG1o_t = [nc.dram_tensor("OG1o%d" % g, [256, NHG, LBLK[g % NLB][1]], BF16) for g in range(2 * NLB)]
    C.OG1i = [t.ap() for t in C.OG1i_t]
    C.OG1o = [t.ap().rearrange("(r p) k n -> r p k n", r=2) for t in C.OG1o_t]
    C.OG1i_res = [Res() for _ in range(2 * NLB)]
    C.OG1o_res = [Res() for _ in range(2 * NLB)]
    C.H1 = dscr("H1", [HT * 128, D], F32)
    C.H1_res = [Res() for _ in LBLK]
    C.OGi = [t.ap() for t in C.OGi_t]
    C.OGo = [t.ap() for t in C.OGo_t]
    C.OGi_res = [Res() for _ in range(NHL)]
    C.OGo_res = [Res() for _ in range(NHL)]
    C.W1S = dscr("W1S", [2, NJ, 128, 8, 256], BF16)
    C.W2S = dscr("W2S", [2, NJ, 128, D], BF16)
    C.WHS = dscr("WHS", [NHG, 128, 8, 384], BF16)
    C.WHV = dscr("WHV", [1, 128, 8, 512], BF16)
    C.out = nc.dram_tensor("out", [HT * 128, D], F32, kind="ExternalOutput").ap()
    C.dbg_x1 = nc.dram_tensor("dbg_x1", [nblk, 128, 8, 512], BF16, kind="ExternalOutput").ap() if (debug and stages == 3) else None

    with contextlib.ExitStack() as st0:
        P = Prog(nc)

        def mk(st):
            def sb(name, shape, dt):
                return Buf(st.enter_context(nc.sbuf_tensor(name, list(shape), dt)), name)

            def ps(name, shape, dt):
                return Buf(st.enter_context(nc.psum_tensor(name, list(shape), dt)), name)
            return sb, ps
        sb0, ps0 = mk(st0)
        C.identb = sb0("identb", [128, 128], BF16)
        C.identf = sb0("identf", [128, 128], F32)
        C.bdones = sb0("bdones", [128, 128], BF16)
        C.tri = sb0("tri", [128, 128], BF16)
        P.dma(C.identb[:], c_identb, writes=[C.identb])
        P.dma(C.identf[:], c_identf, writes=[C.identf])
        P.dma(C.bdones[:], c_bdones, writes=[C.bdones])
        P.dma(C.tri[:], c_tri, writes=[C.tri])
        C.Wo = [sb0("Wo0", [128, 8, D], BF16), sb0("Wo1", [128, 8, D], BF16)]
        if debug:
            cpc_o = nc.dram_tensor("CPC_o", [128, NT, NHL], F32, kind="ExternalOutput").ap()
            refb_o = nc.dram_tensor("REFB_o", [128, NB, NHL], F32, kind="ExternalOutput").ap()
        with contextlib.ExitStack() as stm:
            sbm, psm = mk(stm)
            C.CPC = sbm("CPC", [128, NT, NHL], F32)
            C.REFB = sbm("REFB", [128, NB, NHL], F32)
            bounce = [sbm("bounce%d" % i, [128, 2 * FF], BF16) for i in range(2)]
            with contextlib.ExitStack() as st:
                sb, ps = mk(st)
                stage1(P, nc, C, sb, ps, bounce if conv else None)
                if debug:
                    P.dma(cpc_o, C.CPC[:], reads=[C.CPC])
                    P.dma(refb_o, C.REFB[:], reads=[C.REFB])
                P.barrier()
            if stages >= 2:
                with contextlib.ExitStack() as st:
                    sb, ps = mk(st)
                    stage2(P, nc, C, sb, ps, bounce if conv else None)
                    P.barrier()
        if stages >= 3:
            with contextlib.ExitStack() as st:
                sb, ps = mk(st)
                stage3(P, nc, C, sb, ps)
                P.barrier()
        P.emit()
    return nc


def host_consts():
    bf = ml_dtypes.bfloat16
    idx = np.arange(128)
    return {
        "c_identb": np.eye(128, dtype=np.float32).astype(bf),
        "c_identf": np.eye(128, dtype=np.float32),
        "c_bdones": (idx[:, None] // 64 == idx[None, :] // 64).astype(np.float32).astype(bf),
        "c_tri": (idx[:, None] <= idx[None, :]).astype(np.float32).astype(bf),
    }


WNAMES = ["attn_norm", "ffn_norm", "final_norm", "fox_w_in", "fox_b_f", "fox_q_norm", "fox_k_norm", "fox_w_out",
          "hgrn_w_in", "hgrn_lower_bounds", "hgrn_g_norm", "hgrn_w_out", "ffn_w_in", "ffn_w_out"]


def make_in_maps(inputs, ncores=8):
    x = np.asarray(inputs["x"], dtype=np.float32)
    meta = np.asarray(inputs["meta_tokens"], dtype=np.float32)
    consts = host_consts()
    maps = []
    NHL = H // 2
    for c in range(ncores):
        b, r = c // 2, c % 2
        h0 = np.zeros((LP, D), np.float32)
        h0[NPAD:NPAD + NMETA] = meta
        h0[NPAD + NMETA:] = x[b]
        h0h = np.zeros((HT * 128, D), np.float32)
        seg = h0[r * HT * 128:(r + 1) * HT * 128]
        h0h[:seg.shape[0]] = seg
        m = {"h0": h0, "h0h": h0h, "rmask": np.full((128, 8 * 512), 0xFFFF if r else 0, np.uint16),
             "rmaskf": np.full((128, 1), float(r), np.float32)}
        for k in WNAMES:
            a = np.asarray(inputs[k], dtype=np.float32)
            if k in ("fox_w_in", "fox_w_out", "hgrn_w_in", "hgrn_w_out"):
                a = a.reshape(a.shape[-2], a.shape[-1])
            if k == "fox_w_in":
                w = NHL * DH
                a = np.concatenate([a[:, s0 + r * w:s0 + (r + 1) * w] for s0 in (0, D, 2 * D, 3 * D)]
                                   + [a[:, 4 * D + r * NHL:4 * D + (r + 1) * NHL]], axis=1)
            if k == "fox_b_f":
                a = a[:, r * NHL:(r + 1) * NHL]
            if k == "hgrn_w_in":
                w = NHG * 128
                a = np.concatenate([a[:, s0 + r * w:s0 + (r + 1) * w] for s0 in (0, D, 2 * D, 3 * D)], axis=1)
            if k == "hgrn_lower_bounds":
                a = a[:, r * NHG * 128:(r + 1) * NHG * 128]
            m[k] = np.ascontiguousarray(a)
        m.update(consts)
        maps.append(m)
    return maps


def kernel(**inputs):
    nc = build(stages=3)
    maps = make_in_maps(inputs, 8)
    res = run_bass_kernel_spmd(nc, maps, core_ids=list(range(8)))
    outs = []
    for b in range(4):
        o0 = np.asarray(res.results[2 * b]["out"], dtype=np.float32)
        o1 = np.asarray(res.results[2 * b + 1]["out"], dtype=np.float32)
        outs.append(np.concatenate([o0[NPAD + NMETA:], o1[:SEQ - (HT * 128 - NPAD - NMETA)]], axis=0))
    return np.stack(outs, axis=0)


class Stream:
    def __init__(self, P, bufs, reqs, q=SP, keep=0):
        self.P, self.bufs, self.reqs, self.q, self.keep = P, bufs, reqs, q, keep
        self.issued = 0

    def get(self, i):
        lim = min(len(self.reqs), i + len(self.bufs) - self.keep)
        while self.issued < lim:
            k = self.issued
            b = self.bufs[k % len(self.bufs)]
            dst, src = self.reqs[k]
            self.P.dma(dst(b), src, writes=[b], q=self.q)
            self.issued += 1
        return self.bufs[i % len(self.bufs)]


def convert_weights(P, nc, C, bounce, gate):
    k = [0]

    def ld(src, width):
        b = bounce[k[0] % 2]
        k[0] += 1
        P.dma(b[:, 0:width], src, reads=[gate[0]], writes=[b], q=POOL)
        return b
    for l in range(2):
        for kc in range(8):
            b = ld(C.ffn_w_in[l, kc * 128:(kc + 1) * 128, :], 2 * FF)
            for hf in range(2):
                P.dma(C.W1S[l, :, :, kc, hf * 128:(hf + 1) * 128].rearrange("j p c -> p j c"),
                      b[:, hf * FF:(hf + 1) * FF].rearrange("p (j c) -> p j c", c=128), reads=[b], q=POOL)
            yield
        wo = C.ffn_w_out[l].rearrange("(j p) c -> p j c", p=128)
        for j0 in range(0, NJ, 5):
            j1 = min(NJ, j0 + 5)
            b = ld(wo[:, j0:j1, :], (j1 - j0) * D)
            P.dma(C.W2S[l, j0:j1].rearrange("j p c -> p j c"),
                  b[:, 0:(j1 - j0) * D].rearrange("p (j c) -> p j c", c=D), reads=[b], q=POOL)
            yield
    GW = NHG * 128
    for kc in range(8):
        b = ld(C.hgrn_w_in[kc * 128:(kc + 1) * 128, :], 4 * GW)
        for gi, base in enumerate((0, GW, 3 * GW)):
            P.dma(C.WHS[:, :, kc, gi * 128:(gi + 1) * 128].rearrange("h p c -> p h c"),
                  b[:, base:base + GW].rearrange("p (h c) -> p h c", c=128), reads=[b], q=POOL)
        P.dma(C.WHV[0, :, kc, :], b[:, 2 * GW:3 * GW], reads=[b], q=POOL)
        yield


NHG = 4
HT = 33
LBLK = [(0, 128)] + [(128 + 512 * i, 512) for i in range(8)]
NLB = len(LBLK)


def stage3(P, nc, C, sb, ps):
    B = [ps("s3_B%d" % i, [128, 512], F32) for i in range(6)]
    Tt = [ps("s3_T%d" % i, [128, 1024], BF16) for i in range(2)]
    for i in range(2):
        v = Buf(Tt[i].t[:, :].bitcast(F32), "B%df" % (6 + i))
        v.res = Tt[i].res
        B.append(v)

    def bres(i):
        return [B[i].res]

    prow = sb("s3_prow", [32, 128], F32)
    P.dma(prow[0:8, :], C.lb.rearrange("r (h p) -> (r h) p", p=128), writes=[prow])
    P.dma(prow[8:16, :], C.attn_norm[1].rearrange("(c p) -> c p", p=128), writes=[prow])
    P.dma(prow[16:24, :], C.ffn_norm[0].rearrange("(c p) -> c p", p=128), writes=[prow])
    P.dma(prow[24:32, :], C.ffn_norm[1].rearrange("(c p) -> c p", p=128), writes=[prow])
    pcol = sb("s3_pcol", [128, 32], F32)
    P.add(PE, lambda e: e.transpose(out=B[5][:, 0:32], in_=prow[0:32, :], identity=C.identf[0:32, 0:32]),
          [prow, C.identf], bres(5))
    _copy(P, DVE, pcol[:], B[5][:, 0:32], bres(5), [pcol])
    omlb = sb("s3_omlb", [128, NHG], F32)
    _tt(P, DVE, omlb[:], pcol[:, 4:8], pcol[:, 0:4], ALU.subtract, [pcol], [omlb])
    _act(P, omlb[:], omlb[:], AF.Exp, [omlb], [omlb])
    _ts(P, DVE, omlb[:], omlb[:], 1.0, None, ALU.add, None, [omlb], [omlb])
    P.add(DVE, lambda e: e.reciprocal(out=omlb[:], in_=omlb[:]), [omlb], [omlb])
    lnomlb = sb("s3_lnomlb", [128, NHG], F32)
    _act(P, lnomlb[:], omlb[:], AF.Ln, [omlb], [lnomlb])
    gcols = {"attn1": pcol[:, 8:16], "ffn0": pcol[:, 16:24], "ffn1": pcol[:, 24:32]}
    gn = sb("s3_gn", [128, 1], F32)
    P.dma(gn[:], C.g_norm[0].rearrange("(p o) -> p o", o=1), writes=[gn])
    gfin = sb("s3_gfin", [128, D], F32)
    P.dma(gfin[:], C.final_norm.partition_broadcast(128), writes=[gfin])
    mh = sb("s3_mh", [128, 1], F32)
    P.add(POOL, lambda e: e.memset(mh[:], -0.5), [], [mh])
    ones128b = sb("s3_ones128b", [128, 128], BF16)
    P.add(DVE, lambda e: e.memset(ones128b[:], 1.0), [], [ones128b])
    onesf = sb("s3_onesf", [128, 128], F32)
    P.add(DVE, lambda e: e.memset(onesf[:], 1.0), [], [onesf])
    rmk = sb("s3_rmk", [128, 512], mybir.dt.uint16)
    P.dma(rmk[:], C.rmask[:, 0:512], writes=[rmk])
    rmf = sb("s3_rmf", [128, 1], F32)
    P.dma(rmf[:], C.rmaskf, writes=[rmf])
    junk = sb("s3_junk", [128, D], BF16)
    nrm = [[sb("s3_nrm%d_%d" % (i, j), [128, 1], F32) for j in range(3)] for i in range(4)]
    cnt = {"y": 0, "f": 0, "T": 0, "n": 0, "h": 0}

    def ybank():
        cnt["y"] += 1
        return B[cnt["y"] % 4]

    def fbank():
        cnt["f"] += 1
        return B[cnt["f"] % 6]

    def hbank():
        cnt["h"] += 1
        return B[cnt["h"] % 3]

    def norm_rows(ht):
        ssq, tt, rstd = nrm[cnt["n"] % 4]
        cnt["n"] += 1
        _act(P, junk[:], ht[:], AF.Square, [ht], [junk, ssq], accum_out=ssq[:])
        _ts(P, DVE, tt[:], ssq[:], 1.0 / D, EPS, ALU.mult, ALU.add, [ssq], [tt])
        _tt(P, POOL, rstd[:], tt[:], mh[:], ALU.pow, [tt, mh], [rstd])
        return rstd

    def norm_A(hs, hns, nt):
        for tl in range(nt):
            rstd = norm_rows(hs[tl])
            _ts(P, DVE, hns[tl][:], hs[tl][:], rstd[:, 0:1], None, ALU.mult, None, [hs[tl], rstd], [hns[tl]])

    def norm_B(hns, hnT, nt, gcol):
        n = nt * 128
        for c in range(8):
            half = cnt["T"] % 2
            cnt["T"] += 1
            pT = Tt[half][:, 0:512]
            res = Tt[half].res

            def fn(e, c=c, pT=pT):
                ins = None
                for tl in range(nt):
                    ins = e.transpose(out=pT[:, tl * 128:(tl + 1) * 128], in_=hns[tl][:, c * 128:(c + 1) * 128],
                                      identity=C.identb[:])
                return ins
            P.add(PE, fn, list(hns[:nt]) + [C.identb], [res])
            if c % 2 == 0:
                _ts(P, DVE, hnT[:, c, 0:n], pT[:, 0:n], gcol[:, c:c + 1], None, ALU.mult, None, [res, pcol], [hnT.sub(c)])
            else:
                P.add(ACT, lambda e, c=c, pT=pT: e.mul(out=hnT[:, c, 0:n], in_=pT[:, 0:n], mul=gcol[:, c:c + 1]),
                      [res, pcol], [hnT.sub(c)])

    def ag(in_t, out_t, rin, rout):
        P.cc(lambda e: e.collective_compute("AllGather", ALU.bypass, replica_groups=C.RG,
                                            ins=[in_t.ap().opt()], outs=[out_t.ap().opt()]),
             reads=[rin], writes=[rout])

    def scope():
        st = contextlib.ExitStack()
        return st, (lambda name, shape, dt: Buf(st.enter_context(nc.sbuf_tensor(name, list(shape), dt)), name))

    def ffn_phase(l):
        st, sbp = scope()
        with st:
            pre = "p%d_" % (1 + 2 * l)
            Wo = C.Wo[l]
            hs2 = [[sbp(pre + "h%d_%d" % (j, i), [128, D], F32) for i in range(4)] for j in range(2)]
            X0s = [sbp(pre + "X0_%d" % j, [128, 8, 512], BF16) for j in range(2)]
            Xcs = [sbp(pre + "Xc%d" % i, [128, 512], BF16) for i in range(2)]
            hns = [sbp(pre + "hn%d" % i, [128, D], BF16) for i in range(4)]
            hnT = sbp(pre + "hnT", [128, 8, 512], BF16)
            hnT1 = sbp(pre + "hnT1", [128, 8, 512], BF16) if l == 0 else None
            actT = sbp(pre + "actT", [128, NJ, 512], BF16)
            sils = [sbp(pre + "sil%d" % i, [128, 512], F32) for i in range(2)]
            W1b = [sbp(pre + "W1b%d" % i, [128, 2, 8, 256], BF16) for i in range(3)]
            W2b = [sbp(pre + "W2b%d" % i, [128, 2, D], BF16) for i in range(3)]
            orow = [sbp(pre + "orow%d" % i, [128, D], F32) for i in range(2)] if l == 1 else None
            NJP = NJ // 2
            S1 = Stream(P, W1b, [(lambda b: b[:], C.W1S[l, 2 * jp:2 * jp + 2].rearrange("j p k c -> p j k c"))
                                 for lb in range(NLB) for jp in range(NJP)])
            S2 = Stream(P, W2b, [(lambda b: b[:], C.W2S[l, 2 * jp:2 * jp + 2].rearrange("j p c -> p j c"))
                                 for lb in range(NLB) for jp in range(NJP)])
            S1.get(0)
            S2.get(0)
            hnTr = [hnT.sub(c) for c in range(8)]
            blocks = LBLK[:C.nblk]
            oc = [0]

            def stA(i):
                t0, n = blocks[i]
                nt = n // 128
                hs, X0 = hs2[i % 2], X0s[i % 2]
                for tl in range(nt):
                    if l == 0:
                        P.dma(hs[tl][:], C.h0h[t0 + tl * 128:t0 + (tl + 1) * 128, :], writes=[hs[tl]])
                    else:
                        P.dma(hs[tl][:], C.H1[t0 + tl * 128:t0 + (tl + 1) * 128, :], reads=[C.H1_res[i]], writes=[hs[tl]])
                yield
                for kc in range(8):
                    Xc = Xcs[kc % 2]
                    if l == 0:
                        rk, hl = kc // 4, 2 * (kc % 4)
                        for hh in range(2):
                            src = C.OGo[hl + hh][64 * rk:64 * rk + 64, :]
                            P.dma(X0[64 * hh:64 * hh + 64, kc, 0:n], src[:, t0:t0 + n], reads=[C.OGo_res[hl + hh]],
                                  writes=[X0.sub(kc)])
                            P.dma(Xc[64 * hh:64 * hh + 64, 0:n], src[:, HT * 128 + t0:HT * 128 + t0 + n],
                                  reads=[C.OGo_res[hl + hh]], writes=[Xc])
                    else:
                        rk, hd = kc // 4, kc % 4
                        P.dma(X0[:, kc, 0:n], C.OG1o[i][rk, :, hd, :], reads=[C.OG1o_res[i]], writes=[X0.sub(kc)])
                        P.dma(Xc[:, 0:n], C.OG1o[NLB + i][rk, :, hd, :], reads=[C.OG1o_res[NLB + i]], writes=[Xc])
                    P.add(DVE, lambda e, n=n, kc=kc, Xc=Xc, X0=X0: e.copy_predicated(
                        out=X0[:, kc, 0:n], mask=rmk[:, 0:n], data=Xc[:, 0:n]), [Xc, rmk, X0.sub(kc)], [X0.sub(kc)])
                    yield

            def stW(i):
                t0, n = blocks[i]
                hs, X = hs2[i % 2], X0s[i % 2]
                for tl in range(n // 128):
                    for hf in range(2):
                        y = ybank()
                        _mm(P, y[:, :], [(X[:, kc, tl * 128:(tl + 1) * 128], Wo[:, kc, hf * 512:(hf + 1) * 512])
                                         for kc in range(8)], [X.sub(kc) for kc in range(8)] + [Wo], [y])
                        _tt(P, DVE, hs[tl][:, hf * 512:(hf + 1) * 512], hs[tl][:, hf * 512:(hf + 1) * 512], y[:, :], ALU.add,
                            [hs[tl], y], [hs[tl]])

            hns1 = [sbp(pre + "hn1_%d" % i, [128, D], BF16) for i in range(4)] if l == 0 else None

            def stNa(i):
                t0, n = blocks[i]
                norm_A(hs2[i % 2], hns, n // 128)

            def stNb(i):
                t0, n = blocks[i]
                norm_B(hns, hnT, n // 128, gcols["ffn%d" % l])

            def stCin(i, gA=None):
                t0, n = blocks[i]
                for j in range(NJ):
                    if gA is not None and j % 2 == 0:
                        next(gA, None)
                    w = S1.get(i * NJP + j // 2)
                    jj = j % 2
                    g, u = fbank(), fbank()
                    _mm(P, g[:, 0:n], [(w[:, jj, kc, 0:128], hnT[:, kc, 0:n]) for kc in range(8)], [w] + hnTr, [g])
                    _mm(P, u[:, 0:n], [(w[:, jj, kc, 128:256], hnT[:, kc, 0:n]) for kc in range(8)], [w] + hnTr, [u])
                    sl = sils[j % 2]
                    _act(P, sl[:, 0:n], g[:, 0:n], AF.Silu, [g], [sl])
                    _tt(P, DVE, actT[:, j, 0:n], sl[:, 0:n], u[:, 0:n], ALU.mult, [sl, u], [actT.sub(j)])

            def stCout(i):
                t0, n = blocks[i]
                nt = n // 128
                hs = hs2[i % 2]
                for j in range(NJ):
                    w = S2.get(i * NJP + j // 2)

                    def fn(e, j=j, w=w):
                        ins = None
                        for tl in range(nt):
                            for hf in range(2):
                                ins = e.matmul(B[tl * 2 + hf][:, :], lhsT=actT[:, j, tl * 128:(tl + 1) * 128],
                                               rhs=w[:, j % 2, hf * 512:(hf + 1) * 512], start=(j == 0), stop=(j == NJ - 1))
                        return ins
                    P.add(PE, fn, [w, actT.sub(j)], sum([bres(k) for k in range(2 * nt)], []))
                for k in sorted(range(2 * nt), key=lambda k: -k):
                    tl, hf = k // 2, k % 2
                    _tt(P, DVE, hs[tl][:, hf * 512:(hf + 1) * 512], hs[tl][:, hf * 512:(hf + 1) * 512],
                        B[k][:, :], ALU.add, [hs[tl]] + bres(k), [hs[tl]])

            def stDa(i):
                t0, n = blocks[i]
                nt = n // 128
                hs = hs2[i % 2]
                if l == 0:
                    for tl in range(nt):
                        P.dma(C.H1[t0 + tl * 128:t0 + (tl + 1) * 128, :], hs[tl][:], reads=[hs[tl]], writes=[C.H1_res[i]])
                    norm_A(hs, hns1, nt)
                else:
                    for tl in range(nt):
                        rstd = norm_rows(hs[tl])
                        ob = orow[oc[0] % 2]
                        oc[0] += 1
                        _stt(P, ob[:], hs[tl][:], rstd[:, 0:1], gfin[:], ALU.mult, ALU.mult, [hs[tl], rstd, gfin], [ob])
                        P.dma(C.out[t0 + tl * 128:t0 + (tl + 1) * 128, :], ob[:], reads=[ob])

            def stDb(i):
                if l != 0:
                    return
                t0, n = blocks[i]
                nt = n // 128
                norm_B(hns1, hnT1, nt, gcols["attn1"])
                h1r = [hnT1.sub(c) for c in range(8)]
                if i == 0:
                    _ts(P, DVE, hnT1[:, :, 0:NPAD], hnT1[:, :, 0:NPAD], rmf[:, 0:1], None, ALU.mult, None, h1r + [rmf], h1r)
                P.dma(C.HNi[i], hnT1[:, :, 0:n], reads=h1r, writes=[C.HNi_res[i]])
                ag(C.HNi_t[i], C.HNo_t[i], C.HNi_res[i], C.HNo_res[i])

            nb = len(blocks)
            for _ in stA(0):
                pass
            stW(0)
            stNa(0)
            stNb(0)
            for i in range(nb):
                gA = stA(i + 1) if i + 1 < nb else None
                stCin(i, gA)
                if gA is not None:
                    for _ in gA:
                        pass
                if i > 0:
                    stDb(i - 1)
                if i + 1 < nb:
                    stW(i + 1)
                    stNa(i + 1)
                stCout(i)
                if i + 1 < nb:
                    stNb(i + 1)
                stDa(i)
            stDb(nb - 1)
            P.barrier()

    def hgrn_phase():
        st, sbp = scope()
        with st:
            hnTs = [sbp("p2_hnT%d" % i, [128, 8, 512], BF16) for i in range(2)]
            X1s = [sbp("p2_X1_%d" % i, [128, NHG, 512], BF16) for i in range(2)]
            WHb = [sbp("p2_WHb%d" % i, [128, 8, 512], BF16) for i in range(4)]
            silq = sbp("p2_silq", [128, NHG, 512], BF16)
            sgT = sbp("p2_sgT", [128, NHG, 512], BF16)
            Vsb2 = [[sbp("p2_V%d_%d" % (j, i), [128, NHG * 128], BF16) for i in range(4)] for j in range(2)]
            NBUF = 4
            mkb = lambda nm, dt: [sbp("p2_%s%d" % (nm, i), [128, 512], dt) for i in range(NBUF)]
            qts, kts, khs = mkb("qt", BF16), mkb("kt", BF16), mkb("kh", BF16)
            khTs = [sbp("p2_khT%d" % i, [128, 4, 128], BF16) for i in range(NBUF)]
            ezs, kks, ebs = mkb("ez", F32), mkb("kk", F32), mkb("eb", F32)
            mk2 = lambda nm, dt: [sbp("p2_%s%d" % (nm, i), [128, 512], dt) for i in range(2)]
            bbs, enbs, lnfs, ebcs = mk2("bb", F32), mk2("enb", F32), mk2("lnf", F32), mk2("ebc", F32)
            Asb4s = [sbp("p2_Asb4%d" % i, [128, 4, 128], BF16) for i in range(NBUF)]
            Usbs = mkb("Usb", F32)
            Sst = sbp("p2_S", [128, NHG, 128], F32)
            P.add(DVE, lambda e: e.memset(Sst[:], 0.0), [], [Sst])
            osqs = [sbp("p2_osq%d" % i, [128, 512], BF16) for i in range(2)]
            ons = [sbp("p2_on%d" % i, [128, 512], F32) for i in range(2)]
            rh = []
            for gb in range(2 * NLB):
                rh.append((lambda b: b[:], C.WHV[0]))
                for hd in range(NHG):
                    rh.append((lambda b: b[:, :, 0:384], C.WHS[hd]))
            SH = Stream(P, WHb, rh, keep=1)
            items = [(gb, hd) for gb in range(2 * NLB) for hd in range(NHG)]

            def geo(gb):
                rk, lb = gb // NLB, gb % NLB
                t0, n = LBLK[lb]
                return rk, lb, n, n // 128

            def h_front(i):
                gb, hd = items[i]
                rk, lb, n, nt = geo(gb)
                hnT = hnTs[gb % 2]
                hr = [hnT.sub(c) for c in range(8)]
                Vs = Vsb2[gb % 2]
                if hd == 0:
                    P.dma(hnT[:, :, 0:n], C.HNo[lb][rk], reads=[C.HNo_res[lb]], writes=hr, q=POOL)
                    w = SH.get(gb * (NHG + 1))
                    for tl in range(nt):
                        y = ybank()
                        _mm(P, y[:, :], [(hnT[:, kc, tl * 128:(tl + 1) * 128], w[:, kc, 0:512]) for kc in range(8)], [w] + hr, [y])
                        _copy(P, ACT if tl % 2 == 0 else DVE, Vs[tl][:, :], y[:, :], [y], [Vs[tl]])
                w = SH.get(gb * (NHG + 1) + 1 + hd)
                k3 = i % NBUF
                ez, kk, eb = ezs[k3], kks[k3], ebs[k3]
                lnf, bb, enb, ebc = [x[i % 2] for x in (lnfs, bbs, enbs, ebcs)]
                qt, kt, kh, khT, Asb4, Usb = qts[k3], kts[k3], khs[k3], khTs[k3], Asb4s[k3], Usbs[k3]
                qp, gp = hbank(), hbank()
                _mm(P, qp[:, 0:n], [(w[:, kc, 0:128], hnT[:, kc, 0:n]) for kc in range(8)], [w] + hr, [qp])
                _mm(P, gp[:, 0:n], [(w[:, kc, 256:384], hnT[:, kc, 0:n]) for kc in range(8)], [w] + hr, [gp])
                _act(P, silq[:, hd, 0:n], qp[:, 0:n], AF.Silu, [qp], [silq.sub(hd)])
                yield
                _act(P, sgT[:, hd, 0:n], gp[:, 0:n], AF.Silu, [gp], [sgT.sub(hd)])
                yield
                zp = hbank()
                _mm(P, zp[:, 0:n], [(w[:, kc, 128:256], hnT[:, kc, 0:n]) for kc in range(8)], [w] + hr, [zp])
                _act(P, ez[:, 0:n], zp[:, 0:n], AF.Exp, [zp], [ez])
                yield
                _act(P, ez[:, 0:n], ez[:, 0:n], AF.Ln, [ez], [ez], bias=1.0)
                yield
                _act(P, kk[:, 0:n], ez[:, 0:n], AF.Exp, [ez, lnomlb], [kk], scale=-1.0, bias=lnomlb[:, hd:hd + 1])
                yield
                _act(P, lnf[:, 0:n], kk[:, 0:n], AF.Ln, [kk], [lnf], scale=-1.0, bias=1.0)
                yield
                for c in range(nt):
                    P.add(DVE, lambda e, c=c: e.tensor_tensor_scan(
                        out=bb[:, c * 128:(c + 1) * 128], data0=onesf[:, 0:128], data1=lnf[:, c * 128:(c + 1) * 128],
                        initial=0.0, op0=ALU.mult, op1=ALU.add), [onesf, lnf], [bb])
                _act(P, eb[:, 0:n], bb[:, 0:n], AF.Exp, [bb], [eb])
                yield
                _act(P, enb[:, 0:n], bb[:, 0:n], AF.Exp, [bb], [enb], scale=-1.0)
                yield
                for c in range(nt):
                    _act(P, ebc[:, c * 128:(c + 1) * 128], bb[:, c * 128:(c + 1) * 128], AF.Exp, [bb], [ebc], scale=-1.0,
                         bias=bb[:, c * 128 + 127:c * 128 + 128])
                _tt(P, DVE, qt[:, 0:n], silq[:, hd, 0:n], eb[:, 0:n], ALU.mult, [silq.sub(hd), eb], [qt])
                yield
                _tt(P, DVE, kt[:, 0:n], kk[:, 0:n], enb[:, 0:n], ALU.mult, [kk, enb], [kt])
                yield
                _tt(P, DVE, kh[:, 0:n], kk[:, 0:n], ebc[:, 0:n], ALU.mult, [kk, ebc], [kh])
                yield
                pT = Tt[0][:, 0:512]
                res = Tt[0].res

                def fnT(e):
                    ins = None
                    for c in range(nt):
                        ins = e.transpose(out=pT[:, c * 128:(c + 1) * 128], in_=kh[:, c * 128:(c + 1) * 128], identity=C.identb[:])
                    return ins
                P.add(PE, fnT, [kh, C.identb], [res])
                _copy(P, ACT, khT[:, 0:nt, :], pT[:, 0:n].rearrange("p (c s) -> p c s", s=128), [res], [khT])
                yield
                Ab, Ub = B[5], B[7]

                def fnA(e):
                    ins = None
                    for c in range(nt):
                        cs = slice(c * 128, (c + 1) * 128)
                        ins = e.matmul(Ab[:, cs], lhsT=kt[:, cs], rhs=qt[:, cs], start=True, stop=True)
                    return ins
                P.add(PE, fnA, [kt, qt], [Ab])
                P.add(DVE, lambda e: e.tensor_tensor(
                    out=Asb4[:, 0:nt, :], in0=Ab[:, 0:n].rearrange("p (c s) -> p c s", s=128),
                    in1=C.tri[:, :].unsqueeze(1).to_broadcast([128, nt, 128]), op=ALU.mult), [Ab, C.tri], [Asb4])

                def fnU(e):
                    ins = None
                    for c in range(nt):
                        cs = slice(c * 128, (c + 1) * 128)
                        ins = e.matmul(Ub[:, cs], lhsT=khT[:, c, :], rhs=Vs[c][:, hd * 128:(hd + 1) * 128], start=True, stop=True)
                    return ins
                P.add(PE, fnU, [khT] + list(Vs[:nt]), [Ub])
                _copy(P, ACT, Usb[:, 0:n], Ub[:, 0:n], [Ub], [Usb])
                yield

            Sbf4s = [sbp("p2_Sbf4%d" % i, [128, 4, 128], BF16) for i in range(NBUF)]

            def h_state(i):
                gb, hd = items[i]
                rk, lb, n, nt = geo(gb)
                k3 = i % NBUF
                eb, Usb, Sbf4 = ebs[k3], Usbs[k3], Sbf4s[k3]
                for c in range(nt):
                    cs = slice(c * 128, (c + 1) * 128)
                    _copy(P, DVE, Sbf4[:, c, :], Sst[:, hd, :], [Sst.sub(hd)], [Sbf4])
                    _stt(P, Sst[:, hd, :], Sst[:, hd, :], eb[:, c * 128 + 127:c * 128 + 128], Usb[:, cs], ALU.mult, ALU.add,
                         [Sst.sub(hd), eb, Usb], [Sst.sub(hd)])

            def h_back(i):
                gb, hd = items[i]
                rk, lb, n, nt = geo(gb)
                Vs = Vsb2[gb % 2]
                k3 = i % NBUF
                qt, Asb4, Sbf4 = qts[k3], Asb4s[k3], Sbf4s[k3]
                X1 = X1s[gb % 2]
                osq, on = osqs[i % 2], ons[i % 2]
                op = B[3 + hd % 2]

                def fnO(e):
                    ins = None
                    for c in range(nt):
                        cs = slice(c * 128, (c + 1) * 128)
                        e.matmul(op[:, cs], lhsT=Vs[c][:, hd * 128:(hd + 1) * 128], rhs=Asb4[:, c, :], start=True, stop=False)
                        ins = e.matmul(op[:, cs], lhsT=Sbf4[:, c, :], rhs=qt[:, cs], start=False, stop=True)
                    return ins
                P.add(PE, fnO, [Sbf4, qt, Asb4] + list(Vs[:nt]), [op])
                yield
                _act(P, osq[:, 0:n], op[:, 0:n], AF.Square, [op], [osq])
                yield
                sp = hbank()
                _mm(P, sp[:, 0:n], [(ones128b[:], osq[:, 0:n])], [ones128b, osq], [sp])
                lnv, rs = ezs[k3], kks[k3]
                _act(P, lnv[:, 0:n], sp[:, 0:n], AF.Ln, [sp], [lnv], scale=1.0 / 128, bias=EPS)
                yield
                _act(P, rs[:, 0:n], lnv[:, 0:n], AF.Exp, [lnv], [rs], scale=-0.5)
                yield
                _stt(P, on[:, 0:n], op[:, 0:n], gn[:, 0:1], rs[:, 0:n], ALU.mult, ALU.mult, [op, gn, rs], [on])
                yield
                _tt(P, DVE, X1[:, hd, 0:n], on[:, 0:n], sgT[:, hd, 0:n], ALU.mult, [on, sgT.sub(hd)], [X1])
                yield
                if hd == NHG - 1:
                    P.dma(C.OG1i[gb], X1[:, :, 0:n], reads=[X1], writes=[C.OG1i_res[gb]])
                    ag(C.OG1i_t[gb], C.OG1o_t[gb], C.OG1i_res[gb], C.OG1o_res[gb])

            def zipped(gens):
                gens = list(gens)
                if DBG == "seq":
                    for g in gens:
                        for _ in g:
                            pass
                    return
                while gens:
                    for g in list(gens):
                        try:
                            next(g)
                        except StopIteration:
                            gens.remove(g)

            N = len(items)
            npair = N // 2
            zipped([h_front(0), h_front(1)])
            h_state(0)
            h_state(1)
            for t in range(npair):
                gens = [h_back(2 * t), h_back(2 * t + 1)]
                if t + 1 < npair:
                    gens = [h_front(2 * t + 2), h_front(2 * t + 3)] + gens
                zipped(gens)
                if t + 1 < npair:
                    h_state(2 * t + 2)
                    h_state(2 * t + 3)
            P.barrier()

    ffn_phase(0)
    if C.s3parts >= 2:
        hgrn_phase()
    if C.s3parts >= 3:
        ffn_phase(1)
```

```python
import numpy as np
import concourse.bass as bass
import concourse.mybir as mybir
from concourse.bass_utils import run_bass_kernel_spmd

F32 = mybir.dt.float32
BF16 = mybir.dt.bfloat16
AF = mybir.ActivationFunctionType
ALU = mybir.AluOpType
AX = mybir.AxisListType

PE, ACT, DVE, POOL, SP = "pe", "act", "dve", "pool", "sp"
ENGS = (PE, ACT, DVE, POOL, SP)
NDMA_SEM = 12


class Res:
    __slots__ = ("w", "r")

    def __init__(self):
        self.w = None
        self.r = []


class Buf:
    def __init__(self, t, name=""):
        self.t = t
        self.name = name
        self.res = Res()
        self.subs = {}

    def __getitem__(self, k):
        return self.t[k]

    def sub(self, key):
        r = self.subs.get(key)
        if r is None:
            r = self.subs[key] = Res()
        return r


def _res(x):
    return x.res if isinstance(x, Buf) else x


class Op:
    __slots__ = ("eng", "fn", "raw", "oth", "dma", "sigval", "need", "dsem", "dval", "dprev", "pos")


class Prog:
    def __init__(self, nc):
        self.nc = nc
        self.ops = {e: [] for e in ENGS}
        self.ndma = {e: 0 for e in ENGS}

    def add(self, eng, fn, reads=(), writes=(), dma=False):
        op = Op()
        op.eng, op.fn, op.dma = eng, fn, dma
        op.raw, op.oth = set(), set()
        op.need = False
        op.sigval = None
        for r in reads:
            r = _res(r)
            if r.w is not None:
                op.raw.add(r.w)
        for w in writes:
            w = _res(w)
            if w.w is not None:
                op.oth.add(w.w)
            for x in w.r:
                op.oth.add(x)
        for r in reads:
            _res(r).r.append(op)
        for w in writes:
            w = _res(w)
            w.w = op
            w.r = []
        if dma:
            i = self.ndma[eng]
            self.ndma[eng] += 1
            op.dsem = i % NDMA_SEM
            op.dval = 16 * (i // NDMA_SEM + 1)
        self.ops[eng].append(op)
        return op

    def cc(self, fn, reads=(), writes=()):
        op = self.add(POOL, fn, reads, writes, dma=True)
        self.ndma[POOL] -= 1
        self.ncc = getattr(self, "ncc", 0) + 1
        op.dsem = "cc"
        op.dval = self.ncc
        return op

    def barrier(self):
        last = {}
        for e in ENGS:
            lc = None
            for o in reversed(self.ops[e]):
                if isinstance(o, Op) and not o.dma:
                    lc = o
                    break
            if lc is not None:
                lc.need = True
            last[e] = lc
        mark = ("barrier", last, dict(self.ndma), getattr(self, "ncc", 0))
        for e in ENGS:
            self.ops[e].append(mark)

    def dma(self, out, in_, reads=(), writes=(), q=SP, **kw):
        return self.add(q, lambda e: e.dma_start(out=out, in_=in_, **kw), reads, writes, dma=True)

    def _needed(self, o, d):
        if d.dma:
            return True
        if d.eng != o.eng:
            return True
        if o.dma:
            return True
        if o.eng == PE:
            return False
        return d in o.raw

    def _deps(self, o):
        best = {}
        out = []
        for d in list(o.raw) + list(o.oth):
            if not self._needed(o, d):
                continue
            if d.dma:
                out.append(d)
            else:
                b = best.get(d.eng)
                if b is None or d.pos > b.pos:
                    best[d.eng] = d
        return out + list(best.values())

    def emit(self):
        nc = self.nc
        for e in ENGS:
            for k, o in enumerate(self.ops[e]):
                if isinstance(o, Op):
                    o.pos = k
        for e in ENGS:
            for o in self.ops[e]:
                if not isinstance(o, Op):
                    continue
                for d in self._deps(o):
                    if not d.dma:
                        d.need = True
        for e in ENGS:
            c = 0
            for o in self.ops[e]:
                if isinstance(o, Op) and (not o.dma) and o.need:
                    c += 1
                    o.sigval = c
        import contextlib
        with contextlib.ExitStack() as st:
            esem = {e: st.enter_context(nc.semaphore("s_" + e)) for e in ENGS}
            dsem = {e: [st.enter_context(nc.semaphore("d_%s%d" % (e, i))) for i in range(NDMA_SEM)]
                    for e in ENGS if self.ndma[e] > 0}
            ccsem = st.enter_context(nc.semaphore("s_cc"))
            block = st.enter_context(nc.Block())

            def run(ename, eng):
                seen = {}

                def wait(sem, val):
                    k = id(sem)
                    if seen.get(k, 0) >= val:
                        return
                    seen[k] = val
                    eng.wait_ge(sem, val)

                for o in self.ops[ename]:
                    if not isinstance(o, Op):
                        _, last, nd, ncc = o
                        if ncc > 0:
                            wait(ccsem, ncc)
                        for e2 in ENGS:
                            if last[e2] is not None:
                                wait(esem[e2], last[e2].sigval)
                            n = nd[e2]
                            for i in range(NDMA_SEM):
                                cnt = (n - i + NDMA_SEM - 1) // NDMA_SEM if n > i else 0
                                if cnt > 0:
                                    wait(dsem[e2][i], 16 * cnt)
                        continue
                    for d in self._deps(o):
                        if d.dma and d.dsem == "cc":
                            wait(ccsem, d.dval)
                        elif d.dma:
                            wait(dsem[d.eng][d.dsem], d.dval)
                        else:
                            wait(esem[d.eng], d.sigval)
                    if o.dma and o.dsem == "cc":
                        if o.dval > 1:
                            wait(ccsem, o.dval - 1)
                        o.fn(eng).then_inc(ccsem)
                    elif o.dma:
                        s = dsem[ename][o.dsem]
                        if o.dval > 16:
                            wait(s, o.dval - 16)
                        o.fn(eng).then_inc(s, 16)
                    else:
                        ins = o.fn(eng)
                        if o.need:
                            ins.then_inc(esem[ename], 1)
                if ename == POOL and getattr(self, "ncc", 0) > 0:
                    wait(ccsem, self.ncc)
                if self.ndma[ename] > 0:
                    n = self.ndma[ename]
                    for i in range(NDMA_SEM):
                        cnt = (n - i + NDMA_SEM - 1) // NDMA_SEM if n > i else 0
                        if cnt > 0:
                            wait(dsem[ename][i], 16 * cnt)

            @block.tensor
            def _(eng):
                run(PE, eng)

            @block.scalar
            def _(eng):
                run(ACT, eng)

            @block.vector
            def _(eng):
                run(DVE, eng)

            @block.gpsimd
            def _(eng):
                run(POOL, eng)

            @block.sync
            def _(eng):
                run(SP, eng)
import contextlib
import ml_dtypes

D = 1024
H = 16
DH = 64
NMETA = 16
SEQ = 8192
NPAD = 112
LP = NPAD + NMETA + SEQ
NT = LP // 128
FF = 2816
NJ = FF // 128
EPS = 1e-6
BLOCKS = [(0, 128)] + [(128 + 512 * i, 512) for i in range(16)]
NB = len(BLOCKS)


import os
DBG = os.environ.get('KDBG', '')


class Ctx:
    pass


def _mm(P, out_ap, pairs, reads, writes, start=True, stop=True):
    def fn(e):
        n = len(pairs)
        ins = None
        for i, (l, r) in enumerate(pairs):
            ins = e.matmul(out_ap, lhsT=l, rhs=r, start=(start and i == 0), stop=(stop and i == n - 1))
        return ins
    return P.add(PE, fn, reads, writes)


def _act(P, out, in_, func, reads, writes, **kw):
    return P.add(ACT, lambda e: e.activation(out=out, in_=in_, func=func, **kw), reads, writes)


def _tt(P, eng, out, in0, in1, op, reads, writes):
    return P.add(eng, lambda e: e.tensor_tensor(out=out, in0=in0, in1=in1, op=op), reads, writes)


def _ts(P, eng, out, in0, s1, s2, op0, op1, reads, writes):
    if op1 is None:
        return P.add(eng, lambda e: e.tensor_scalar(out=out, in0=in0, scalar1=s1, scalar2=None, op0=op0), reads, writes)
    return P.add(eng, lambda e: e.tensor_scalar(out=out, in0=in0, scalar1=s1, scalar2=s2, op0=op0, op1=op1), reads, writes)


def _stt(P, out, in0, scalar, in1, op0, op1, reads, writes):
    return P.add(DVE, lambda e: e.scalar_tensor_tensor(out=out, in0=in0, scalar=scalar, in1=in1, op0=op0, op1=op1), reads, writes)


def _copy(P, eng, out, in_, reads, writes):
    if eng == ACT:
        return P.add(ACT, lambda e: e.copy(out=out, in_=in_), reads, writes)
    return P.add(eng, lambda e: e.tensor_copy(out=out, in_=in_), reads, writes)


def _rmsnorm_rows(P, C, ht, hn, gbc, tmp):
    junk, ssq, lnv, rstd = tmp
    _act(P, junk[:], ht[:], AF.Square, [ht], [junk, ssq], accum_out=ssq[:])
    _act(P, lnv[:], ssq[:], AF.Ln, [ssq], [lnv], scale=1.0 / D, bias=EPS)
    _act(P, rstd[:], lnv[:], AF.Exp, [lnv], [rstd], scale=-0.5)
    _stt(P, hn[:], ht[:], rstd[:, 0:1], gbc[:], ALU.mult, ALU.mult, [ht, rstd, gbc], [hn])


def _transpose_block(P, C, hns, nt, hnT, pTs, cnt0):
    n = nt * 128
    for c in range(8):
        pT = pTs[(cnt0 + c) % len(pTs)]

        def fn(e, c=c, pT=pT):
            ins = None
            for tl in range(nt):
                ins = e.transpose(out=pT[:, tl * 128:(tl + 1) * 128], in_=hns[tl][:, c * 128:(c + 1) * 128],
                                  identity=C.identb[:])
            return ins
        P.add(PE, fn, list(hns[:nt]) + [C.identb], [pT])
        _copy(P, ACT if c % 2 == 0 else DVE, hnT[:, c, 0:n], pT[:, 0:n], [pT], [hnT.sub(c)])


def stage1(P, nc, C, sb, ps, bounce=None):
    NHL = C.nheads
    NQ = NHL // 2
    WC = 4 * NHL * DH + NHL
    W = sb("s1_W", [128, 8, WC], BF16)
    wst = [sb("s1_wst%d" % i, [128, WC], F32) for i in range(2)]
    Wr = [W.sub(k) for k in range(8)]
    gbc = sb("s1_gbc", [128, D], F32)
    P.dma(gbc[:], C.attn_norm[0].partition_broadcast(128), writes=[gbc])
    gcol = sb("s1_gcol", [128, 2], F32)
    for hh in range(2):
        P.dma(gcol[64 * hh:64 * hh + 64, 0:1], C.fox_q_norm[0].rearrange("(p o) -> p o", o=1), writes=[gcol])
        P.dma(gcol[64 * hh:64 * hh + 64, 1:2], C.fox_k_norm[0].rearrange("(p o) -> p o", o=1), writes=[gcol])
    _ts(P, DVE, gcol[:, 0:1], gcol[:, 0:1], 0.125, None, ALU.mult, None, [gcol], [gcol])
    negbf = sb("s1_negbf", [NHL, 1], F32)
    P.dma(negbf[:], C.fox_b_f[0].rearrange("(p o) -> p o", o=1), writes=[negbf])
    _ts(P, DVE, negbf[:], negbf[:], -1.0, None, ALU.mult, None, [negbf], [negbf])
    ones16 = sb("s1_ones16", [NHL, 512], F32)
    P.add(DVE, lambda e: e.memset(ones16[:], 1.0), [], [ones16])
    zero16 = sb("s1_zero16", [NHL, 1], F32)
    P.add(DVE, lambda e: e.memset(zero16[:], 0.0), [], [zero16])

    hts = [sb("s1_ht%d" % i, [128, D], F32) for i in range(2)]
    hns = [sb("s1_hn%d" % i, [128, D], BF16) for i in range(8)]
    junk = sb("s1_junk", [128, D], BF16)
    ssqs = [[sb("s1_ssq%d_%d" % (i, j), [128, 1], F32) for j in range(3)] for i in range(2)]
    hnTs = [sb("s1_hnT%d" % i, [128, 8, 512], BF16) for i in range(2)]
    pTs = [ps("s1_pT%d" % i, [128, 512], BF16) for i in range(2)]
    psA = [ps("s1_psA%d" % i, [128, 512], F32) for i in range(4)]
    psSs = [ps("s1_psS%d" % i, [128, 512], F32) for i in range(2)]
    sqs = [sb("s1_sq%d" % i, [128, 512], BF16) for i in range(2)]
    lnvs = [sb("s1_lnv%d" % i, [128, 512], F32) for i in range(2)]
    rss = [sb("s1_rs%d" % i, [128, 512], F32) for i in range(2)]
    outs = [sb("s1_out%d" % i, [128, 512], BF16) for i in range(6)]
    vsts = [sb("s1_vst%d" % i, [128, NHL, 4, 65], BF16) for i in range(2)]
    for v in vsts:
        P.add(POOL, lambda e, v=v: e.memset(v[:, :, :, 64:65], 1.0), [], [v])
    ef = sb("s1_ef", [NHL, 512], F32)
    lf = sb("s1_lf", [NHL, 512], F32)
    cps = [sb("s1_cp%d" % i, [NHL, 512], F32) for i in range(2)]
    hsp = [[sb("s1_hsp%d_%d" % (i, j), [NHL, 512], BF16) for j in range(6)] for i in range(2)]
    r1 = sb("s1_r1", [NHL, 512], F32)
    r2 = sb("s1_r2", [NHL, 512], F32)
    onesb = sb("s1_onesb", [3, LP // 4], BF16)
    P.add(DVE, lambda e: e.memset(onesb[:], 1.0), [], [onesb])
    for h in range(NHL):
        for qd in range(4):
            cs_ = slice(qd * (LP // 4), (qd + 1) * (LP // 4))
            P.dma(C.KT_d[h, 67:70, cs_], onesb[0:3, :], reads=[onesb], q=POOL)
            P.dma(C.QT_d[h, 64:67, cs_], onesb[0:3, :], reads=[onesb], q=POOL)

    cA = 0
    cO = 0
    cT = 0
    cTh = [0]

    def fa_load(bi, tl):
        t0, n = BLOCKS[bi]
        if bi >= NB or tl >= n // 128:
            return
        ht = hts[tl % 2]
        P.dma(ht[:], C.h0[t0 + tl * 128:t0 + (tl + 1) * 128, :], writes=[ht])

    def fa_norm(bi, tl):
        t0, n = BLOCKS[bi]
        if bi >= NB or tl >= n // 128:
            return
        myhn = hns[(bi % 2) * 4:(bi % 2) * 4 + 4]
        tmp = [junk] + ssqs[cTh[0] % 2]
        cTh[0] += 1
        _rmsnorm_rows(P, C, hts[tl % 2], myhn[tl], gbc, tmp)

    def s1_front_a(bi):
        for tl in range(4):
            fa_load(bi, tl)
            fa_norm(bi, tl)

    def s1_front_b(bi):
        t0, n = BLOCKS[bi]
        myhn = hns[(bi % 2) * 4:(bi % 2) * 4 + 4]
        _transpose_block(P, C, myhn, n // 128, hnTs[bi % 2], pTs, 0)

    s1_front_a(0)
    for kc in range(8):
        P.dma(wst[kc % 2][:], C.fox_w_in[kc * 128:(kc + 1) * 128, :], writes=[wst[kc % 2]])
        _copy(P, ACT if kc % 2 == 0 else DVE, W[:, kc, :], wst[kc % 2][:], [wst[kc % 2]], [W.sub(kc)])
    s1_front_b(0)
    for bi, (t0, n) in enumerate(BLOCKS):
        nt = n // 128
        hnT = hnTs[bi % 2]
        if bi + 1 < NB:
            fa_load(bi + 1, 0)
            fa_load(bi + 1, 1)
        hr = [hnT.sub(c) for c in range(8)]
        pqs = {}

        def qk_a(c):
            nonlocal cA
            cols = c * 128
            pq = psA[cA % 4]
            cA += 1
            pqs[c] = pq
            _mm(P, pq[:, 0:n], [(W[:, kc, cols:cols + 128], hnT[:, kc, 0:n]) for kc in range(8)], Wr + hr, [pq])
            _act(P, sqs[c % 2][:, 0:n], pq[:, 0:n], AF.Square, [pq], [sqs[c % 2]])

        def qk_b(c):
            nonlocal cO
            pq = pqs[c]
            sq, lnv, rs = sqs[c % 2], lnvs[c % 2], rss[c % 2]
            psS = psSs[c % 2]
            _mm(P, psS[:, 0:n], [(C.bdones[:], sq[:, 0:n])], [C.bdones, sq], [psS])
            _act(P, lnv[:, 0:n], psS[:, 0:n], AF.Ln, [psS], [lnv], scale=1.0 / DH, bias=EPS)
            _act(P, rs[:, 0:n], lnv[:, 0:n], AF.Exp, [lnv], [rs], scale=-0.5)
            ob = outs[cO % 6]
            cO += 1
            gi = 0 if c < NQ else 1
            _stt(P, ob[:, 0:n], pq[:, 0:n], gcol[:, gi:gi + 1], rs[:, 0:n], ALU.mult, ALU.mult, [pq, gcol, rs], [ob])
            dst = C.QT_d if c < NQ else C.KT_d
            for hh in range(2):
                h = (c % NQ) * 2 + hh
                P.dma(dst[h, 0:64, t0:t0 + n], ob[64 * hh:64 * hh + 64, 0:n], reads=[ob], q=POOL)

        qk_a(0)
        for c in range(2 * NQ):
            if c + 1 < 2 * NQ:
                qk_a(c + 1)
            qk_b(c)
        if bi + 1 < NB:
            fa_norm(bi + 1, 0)
            fa_norm(bi + 1, 1)
            fa_load(bi + 1, 2)
            fa_load(bi + 1, 3)
        for c in range(NQ):
            cols = 3 * NHL * DH + c * 128
            pg = psA[cA % 4]
            cA += 1
            _mm(P, pg[:, 0:n], [(W[:, kc, cols:cols + 128], hnT[:, kc, 0:n]) for kc in range(8)], Wr + hr, [pg])
            lnv, rs = lnvs[c % 2], rss[c % 2]
            _act(P, rs[:, 0:n], pg[:, 0:n], AF.Exp, [pg], [rs], scale=-1.0)
            _act(P, lnv[:, 0:n], rs[:, 0:n], AF.Ln, [rs], [lnv], bias=1.0)
            ob = outs[cO % 6]
            cO += 1
            _act(P, ob[:, 0:n], lnv[:, 0:n], AF.Exp, [lnv], [ob], scale=-1.0)
            P.dma(C.SG_d[c * 128:(c + 1) * 128, t0:t0 + n], ob[:, 0:n], reads=[ob], q=POOL)
        pf = psA[cA % 4]
        cA += 1
        _mm(P, pf[0:NHL, 0:n], [(W[:, kc, 4 * NHL * DH:WC], hnT[:, kc, 0:n]) for kc in range(8)], Wr + hr, [pf])
        _act(P, ef[:, 0:n], pf[0:NHL, 0:n], AF.Exp, [pf], [ef], scale=-1.0, bias=negbf[:, 0:1])
        _act(P, lf[:, 0:n], ef[:, 0:n], AF.Ln, [ef], [lf], bias=1.0)
        cp = cps[bi % 2]
        if bi == 0:
            ref_ap, ref_b = zero16[:, 0:1], zero16
        else:
            pn = BLOCKS[bi - 1][1]
            ref_ap, ref_b = cps[(bi - 1) % 2][:, pn - 1:pn], cps[(bi - 1) % 2]
        P.add(DVE, lambda e, cp=cp, ref_ap=ref_ap, n=n: e.tensor_tensor_scan(
            out=cp[:, 0:n], data0=ones16[:, 0:n], data1=lf[:, 0:n], initial=ref_ap, op0=ALU.mult, op1=ALU.add),
            [ones16, lf, ref_b], [cp])
        hh = hsp[bi % 2]
        _copy(P, DVE, hh[0][:, 0:n], cp[:, 0:n], [cp], [hh[0]])
        _tt(P, DVE, r1[:, 0:n], cp[:, 0:n], hh[0][:, 0:n], ALU.subtract, [cp, hh[0]], [r1])
        _copy(P, DVE, hh[1][:, 0:n], r1[:, 0:n], [r1], [hh[1]])
        _tt(P, DVE, r2[:, 0:n], r1[:, 0:n], hh[1][:, 0:n], ALU.subtract, [r1, hh[1]], [r2])
        _copy(P, DVE, hh[2][:, 0:n], r2[:, 0:n], [r2], [hh[2]])
        for j in range(3):
            _ts(P, DVE, hh[3 + j][:, 0:n], hh[j][:, 0:n], -1.0, None, ALU.mult, None, [hh[j]], [hh[3 + j]])
            P.dma(C.KT_d[:, 64 + j, t0:t0 + n], hh[j][:, 0:n], reads=[hh[j]])
            P.dma(C.QT_d[:, 67 + j, t0:t0 + n], hh[3 + j][:, 0:n], reads=[hh[3 + j]])
        if bi + 1 < NB:
            fa_norm(bi + 1, 2)
            fa_norm(bi + 1, 3)
        vst = vsts[bi % 2]
        if bi == 0:
            P.add(POOL, lambda e, v=vst: e.memset(v[0:NPAD, :, 0:1, 64:65], 0.0), [], [vst])
        for tl in range(nt):
            for half in range(NHL // 8):
                pv = psA[cA % 4]
                cA += 1
                cols = 2 * NHL * DH + half * 512
                _mm(P, pv[:, :], [(hnT[:, kc, tl * 128:(tl + 1) * 128], W[:, kc, cols:cols + 512]) for kc in range(8)],
                    Wr + hr, [pv])
                _copy(P, DVE if half == 0 else ACT, vst[:, 8 * half:8 * half + 8, tl, 0:64],
                      pv[:, :].rearrange("p (h d) -> p h d", h=8), [pv], [vst])
        for h in range(NHL):
            P.dma(C.VA_d[h, :, t0 // 128:t0 // 128 + nt, :], vst[:, h, 0:nt, :], reads=[vst])
        if bi == 0:
            P.add(POOL, lambda e, v=vst: e.memset(v[0:NPAD, :, 0:1, 64:65], 1.0), [], [vst])
        if bi + 1 < NB:
            s1_front_b(bi + 1)


def stage2(P, nc, C, sb, ps, bounce=None):
    for kc in range(8):
        P.dma(C.Wo[0][:, kc, :], C.fox_w_out[kc * 128:(kc + 1) * 128, :], writes=[C.Wo[0]], q=POOL)
        P.dma(C.Wo[1][:, kc, :], C.hgrn_w_out[kc * 128:(kc + 1) * 128, :], writes=[C.Wo[1]], q=POOL)
    gate_t = sb("s2_gate", [128, 1], F32)
    gate = [Res()]
    conv = convert_weights(P, nc, C, bounce, gate) if bounce is not None else iter(())
    KTb = [sb("s2_KT%d" % i, [70, LP], BF16) for i in range(2)]
    VAb = [sb("s2_VA%d" % i, [128, NT, 65], BF16) for i in range(2)]
    Qbs = [sb("s2_Q%d" % i, [70, 512], BF16) for i in range(3)]
    sgs = [sb("s2_sg%d" % i, [64, 512], BF16) for i in range(3)]
    biases = [sb("s2_bias%d" % i, [128, NT], F32) for i in range(3)]
    Pts = [sb("s2_Pt%d" % i, [128, 512], BF16) for i in range(6)]
    Sb = [ps("s2_S%d" % i, [128, 512], F32) for i in range(5)]
    Ob = [ps("s2_O%d" % i, [128, 512], F32) for i in range(2)]
    Rps = ps("s2_R", [128, 512], F32)
    rds = [sb("s2_rd%d" % i, [128, 512], F32) for i in range(2)]
    rd2s = [sb("s2_rd2%d" % i, [128, 512], F32) for i in range(2)]
    pending = []
    onesf = sb("s2_onesf", [128, 64], F32)
    P.add(DVE, lambda e: e.memset(onesf[:], 1.0), [], [onesf])
    zt2 = sb("s2_zt", [64, 128], BF16)
    P.add(DVE, lambda e: e.memset(zt2[:], 0.0), [], [zt2])
    for h in range(C.nheads):
        P.dma(C.OGi[h][:, LP:LP + 128], zt2[:], reads=[zt2], writes=[C.OGi_res[h]])
    Osbs = [sb("s2_Osb%d" % i, [64, 512], F32) for i in range(2)]
    og1s = [sb("s2_og1%d" % i, [64, 512], F32) for i in range(2)]
    ogbs = [sb("s2_ogb%d" % i, [64, 512], BF16) for i in range(2)]

    items = []
    for h in range(C.nheads):
        for bi, (t0, n) in enumerate(BLOCKS):
            nkt = (t0 + n) // 128
            for kt in range(nkt):
                items.append((h, bi, kt, nkt))
    state = {}

    def prologue(h, bi):
        t0, n = BLOCKS[bi]
        g = h * NB + bi
        if g in state or g >= C.nheads * NB:
            return
        state[g] = True
        if g % 3 == 0:
            gate[0] = Res()
            P.add(DVE, lambda e: e.memset(gate_t[:], 0.0), [], [gate[0]])
            next(conv, None)
        if bi == 0:
            KTh, VAh = KTb[h % 2], VAb[h % 2]
            P.dma(KTh[:, :], C.KT_d[h], writes=[KTh])
            P.dma(VAh[:], C.VA_d[h], writes=[VAh])
        Qb, sg, bias = Qbs[g % 3], sgs[g % 3], biases[g % 3]
        P.dma(Qb[0:70, 0:n], C.QT_d[h, :, t0:t0 + n], writes=[Qb])
        P.dma(sg[0:64, 0:n], C.SG_d[64 * h:64 * h + 64, t0:t0 + n], writes=[sg])

    def qk(i):
        h, bi, kt, nkt = items[i]
        t0, n = BLOCKS[bi]
        g = h * NB + bi
        if kt == 0:
            prologue(h, bi)
        j = kt - t0 // 128
        c0 = 128 * j if j >= 0 else 0
        S = Sb[i % 5]
        _mm(P, S[:, c0:n], [(KTb[h % 2][0:70, kt * 128:(kt + 1) * 128], Qbs[g % 3][0:70, c0:n])],
            [KTb[h % 2], Qbs[g % 3]], [S])

    def rest(i):
        h, bi, kt, nkt = items[i]
        t0, n = BLOCKS[bi]
        g = h * NB + bi
        j = kt - t0 // 128
        c0 = 128 * j if j >= 0 else 0
        S, Pt, O = Sb[i % 5], Pts[i % 6], Ob[g % 2]
        bias = biases[g % 3]
        if kt == 0:
            while pending and pending[0][0] <= g - 2:
                epilogue2(pending.pop(0)[0])
            prologue((g + 1) // NB, (g + 1) % NB)
        _act(P, Pt[:, c0:n], S[:, c0:n], AF.Exp, [S], [Pt])
        if j >= 0:
            _tt(P, DVE, Pt[:, c0:c0 + 128], Pt[:, c0:c0 + 128], C.tri[:], ALU.mult, [Pt, C.tri], [Pt])
        _mm(P, O[0:65, c0:n], [(VAb[h % 2][:, kt, 0:65], Pt[:, c0:n])], [VAb[h % 2], Pt], [O],
            start=(kt == 0), stop=(kt == nkt - 1))
        if kt == nkt - 1:
            Osb = Osbs[g % 2]
            rd, rd2 = rds[g % 2], rd2s[g % 2]
            _ts(P, DVE, rd[64:65, 0:n], O[64:65, 0:n], 1e-30, None, ALU.max, None, [O], [rd])
            P.add(DVE, lambda e: e.reciprocal(out=rd2[64:65, 0:n], in_=rd[64:65, 0:n]), [rd], [rd2])
            _copy(P, ACT, Osb[:, 0:n], O[0:64, 0:n], [O], [Osb])
            pending.append((g, i))
        while pending and (i - pending[0][1] >= 5 or i == len(items) - 1):
            epilogue2(pending.pop(0)[0])

    def epilogue2(g):
        h, bi = g // NB, g % NB
        t0, n = BLOCKS[bi]
        Osb, og1, ogb, sg = Osbs[g % 2], og1s[g % 2], ogbs[g % 2], sgs[g % 3]
        rd2 = rd2s[g % 2]
        _mm(P, Rps[0:64, 0:n], [(onesf[64:65, 0:64], rd2[64:65, 0:n])], [onesf, rd2], [Rps])
        _tt(P, DVE, og1[:, 0:n], Osb[:, 0:n], Rps[0:64, 0:n], ALU.mult, [Osb, Rps], [og1])
        _tt(P, DVE, ogb[:, 0:n], og1[:, 0:n], sg[0:64, 0:n], ALU.mult, [og1, sg], [ogb])
        P.dma(C.OGi[h][:, t0:t0 + n], ogb[:, 0:n], reads=[ogb], writes=[C.OGi_res[h]])
        if bi == NB - 1:
            P.cc(lambda e, h=h: e.collective_compute("AllGather", ALU.bypass, replica_groups=C.RG,
                                                     ins=[C.OGi_t[h].ap().opt()], outs=[C.OGo_t[h].ap().opt()]),
                 reads=[C.OGi_res[h]], writes=[C.OGo_res[h]])

    LA = 3
    N = len(items)
    for i in range(N + LA):
        if i < N:
            qk(i)
        if i - LA >= 0:
            rest(i - LA)
    for _ in conv:
        pass


def build(stages=3, debug=False, nheads=H // 2, conv=None, s3parts=99, nblk=NB):
    nc = bass.Bass("TRN2", target_bir_lowering=False)
    C = Ctx()
    C.nheads = nheads
    C.s3parts = s3parts
    C.nblk = nblk
    if conv is None:
        conv = stages >= 3

    def din(name, shape, dt=F32):
        return nc.dram_tensor(name, list(shape), dt, kind="ExternalInput").ap()

    def dscr(name, shape, dt, out=False):
        return nc.dram_tensor(name, list(shape), dt, kind="ExternalOutput" if out else "Internal").ap()

    C.h0 = din("h0", [LP, D])
    C.attn_norm = din("attn_norm", [2, D])
    C.ffn_norm = din("ffn_norm", [2, D])
    C.final_norm = din("final_norm", [D])
    NHL = nheads
    C.fox_w_in = din("fox_w_in", [D, 4 * NHL * DH + NHL])
    C.fox_b_f = din("fox_b_f", [1, NHL])
    C.fox_q_norm = din("fox_q_norm", [1, DH])
    C.fox_k_norm = din("fox_k_norm", [1, DH])
    C.fox_w_out = din("fox_w_out", [D, D])
    C.hgrn_w_in = din("hgrn_w_in", [D, 4 * NHG * 128])
    C.lb = din("hgrn_lower_bounds", [2, NHG * 128])
    C.h0h = din("h0h", [HT * 128, D])
    C.rmask = din("rmask", [128, 8 * 512], mybir.dt.uint16)
    C.rmaskf = din("rmaskf", [128, 1])
    C.g_norm = din("hgrn_g_norm", [1, 128])
    C.hgrn_w_out = din("hgrn_w_out", [D, D])
    C.ffn_w_in = din("ffn_w_in", [2, D, 2 * FF])
    C.ffn_w_out = din("ffn_w_out", [2, FF, D])
    c_identb = din("c_identb", [128, 128], BF16)
    c_identf = din("c_identf", [128, 128], F32)
    c_bdones = din("c_bdones", [128, 128], BF16)
    c_tri = din("c_tri", [128, 128], BF16)
    d1 = debug and stages == 1
    C.QT_d = dscr("QT_d", [NHL, 70, LP], BF16, d1)
    C.KT_d = dscr("KT_d", [NHL, 70, LP], BF16, d1)
    C.VA_d = dscr("VA_d", [NHL, 128, NT, 65], BF16, d1)
    C.SG_d = dscr("SG_d", [NHL * DH, LP], BF16, d1)
    C.RG = [[0, 1], [2, 3], [4, 5], [6, 7]]
    C.OGi_t = [nc.dram_tensor("OGi%d" % h, [DH, LP + 128], BF16) for h in range(NHL)]
    C.OGo_t = [nc.dram_tensor("OGo%d" % h, [2 * DH, LP + 128], BF16) for h in range(NHL)]
    C.HNi_t = [nc.dram_tensor("HNi%d" % i, [128, 8, n], BF16) for i, (t0, n) in enumerate(LBLK)]
    C.HNo_t = [nc.dram_tensor("HNo%d" % i, [256, 8, n], BF16) for i, (t0, n) in enumerate(LBLK)]
    C.HNi = [t.ap() for t in C.HNi_t]
    C.HNo = [t.ap().rearrange("(r p) k n -> r p k n", r=2) for t in C.HNo_t]
    C.HNi_res = [Res() for _ in LBLK]
    C.HNo_res = [Res() for _ in LBLK]
    C.OG1i_t = [nc.dram_tensor("OG1i%d" % g, [128, NHG, LBLK[g % NLB][1]], BF16) for g in range(2 * NLB)]
    C.OG1o_t = [nc.dram_tensor("OG1o%d" % g, [256, NHG, LBLK[g % NLB][1]], BF16) for g in range(2 * NLB)]
    C.OG1i = [t.ap() for t in C.OG1i_t]
    C.OG1o = [t.ap().rearrange("(r p) k n -> r p k n", r=2) for t in C.OG1o_t]
    C.OG1i_res = [Res() for _ in range(2 * NLB)]
    C.OG1o_res = [Res() for _ in range(2 * NLB)]
    C.H1 = dscr("H1", [HT * 128, D], F32)
    C.H1_res = [Res() for _ in LBLK]
    C.OGi = [t.ap() for t in C.OGi_t]
    C.OGo = [t.ap() for t in C.OGo_t]
    C.OGi_res = [Res() for _ in range(NHL)]
    C.OGo_res = [Res() for _ in range(NHL)]
    C.W1S = dscr("W1S", [2, NJ, 128, 8, 256], BF16)
    C.W2S = dscr("W2S", [2, NJ, 128, D], BF16)
    C.WHS = dscr("WHS", [NHG, 128, 8, 384], BF16)
    C.WHV = dscr("WHV", [1, 128, 8, 512], BF16)
    C.out = nc.dram_tensor("out", [HT * 128, D], F32, kind="ExternalOutput").ap()
    C.dbg_x1 = nc.dram_tensor("dbg_x1", [nblk, 128, 8, 512], BF16, kind="ExternalOutput").ap() if (debug and stages == 3) else None

    with contextlib.ExitStack() as st0:
        P = Prog(nc)

        def mk(st):
            def sb(name, shape, dt):
                return Buf(st.enter_context(nc.sbuf_tensor(name, list(shape), dt)), name)

            def ps(name, shape, dt):
                return Buf(st.enter_context(nc.psum_tensor(name, list(shape), dt)), name)
            return sb, ps
        sb0, ps0 = mk(st0)
        C.identb = sb0("identb", [128, 128], BF16)
        C.identf = sb0("identf", [128, 128], F32)
        C.bdones = sb0("bdones", [128, 128], BF16)
        C.tri = sb0("tri", [128, 128], BF16)
        P.dma(C.identb[:], c_identb, writes=[C.identb])
        P.dma(C.identf[:], c_identf, writes=[C.identf])
        P.dma(C.bdones[:], c_bdones, writes=[C.bdones])
        P.dma(C.tri[:], c_tri, writes=[C.tri])
        C.Wo = [sb0("Wo0", [128, 8, D], BF16), sb0("Wo1", [128, 8, D], BF16)]
        if debug:
            cpc_o = nc.dram_tensor("CPC_o", [128, NT, NHL], F32, kind="ExternalOutput").ap()
            refb_o = nc.dram_tensor("REFB_o", [128, NB, NHL], F32, kind="ExternalOutput").ap()
        with contextlib.ExitStack() as stm:
            sbm, psm = mk(stm)
            C.CPC = sbm("CPC", [128, NT, NHL], F32)
            C.REFB = sbm("REFB", [128, NB, NHL], F32)
            bounce = [sbm("bounce%d" % i, [128, 2 * FF], BF16) for i in range(2)]
            with contextlib.ExitStack() as st:
                sb, ps = mk(st)
                stage1(P, nc, C, sb, ps, bounce if conv else None)
                if debug:
                    P.dma(cpc_o, C.CPC[:], reads=[C.CPC])
                    P.dma(refb_o, C.REFB[:], reads=[C.REFB])
                P.barrier()
            if stages >= 2:
                with contextlib.ExitStack() as st:
                    sb, ps = mk(st)
                    stage2(P, nc, C, sb, ps, bounce if conv else None)
                    P.barrier()
        if stages >= 3:
            with contextlib.ExitStack() as st:
                sb, ps = mk(st)
                stage3(P, nc, C, sb, ps)
                P.barrier()
        P.emit()
    return nc


def host_consts():
    bf = ml_dtypes.bfloat16
    idx = np.arange(128)
    return {
        "c_identb": np.eye(128, dtype=np.float32).astype(bf),
        "c_identf": np.eye(128, dtype=np.float32),
        "c_bdones": (idx[:, None] // 64 == idx[None, :] // 64).astype(np.float32).astype(bf),
        "c_tri": (idx[:, None] <= idx[None, :]).astype(np.float32).astype(bf),
    }


WNAMES = ["attn_norm", "ffn_norm", "final_norm", "fox_w_in", "fox_b_f", "fox_q_norm", "fox_k_norm", "fox_w_out",
          "hgrn_w_in", "hgrn_lower_bounds", "hgrn_g_norm", "hgrn_w_out", "ffn_w_in", "ffn_w_out"]


def make_in_maps(inputs, ncores=8):
    x = np.asarray(inputs["x"], dtype=np.float32)
    meta = np.asarray(inputs["meta_tokens"], dtype=np.float32)
    consts = host_consts()
    maps = []
    NHL = H // 2
    for c in range(ncores):
        b, r = c // 2, c % 2
        h0 = np.zeros((LP, D), np.float32)
        h0[NPAD:NPAD + NMETA] = meta
        h0[NPAD + NMETA:] = x[b]
        h0h = np.zeros((HT * 128, D), np.float32)
        seg = h0[r * HT * 128:(r + 1) * HT * 128]
        h0h[:seg.shape[0]] = seg
        m = {"h0": h0, "h0h": h0h, "rmask": np.full((128, 8 * 512), 0xFFFF if r else 0, np.uint16),
             "rmaskf": np.full((128, 1), float(r), np.float32)}
        for k in WNAMES:
            a = np.asarray(inputs[k], dtype=np.float32)
            if k in ("fox_w_in", "fox_w_out", "hgrn_w_in", "hgrn_w_out"):
                a = a.reshape(a.shape[-2], a.shape[-1])
            if k == "fox_w_in":
                w = NHL * DH
                a = np.concatenate([a[:, s0 + r * w:s0 + (r + 1) * w] for s0 in (0, D, 2 * D, 3 * D)]
                                   + [a[:, 4 * D + r * NHL:4 * D + (r + 1) * NHL]], axis=1)
            if k == "fox_b_f":
                a = a[:, r * NHL:(r + 1) * NHL]
            if k == "hgrn_w_in":
                w = NHG * 128
                a = np.concatenate([a[:, s0 + r * w:s0 + (r + 1) * w] for s0 in (0, D, 2 * D, 3 * D)], axis=1)
            if k == "hgrn_lower_bounds":
                a = a[:, r * NHG * 128:(r + 1) * NHG * 128]
            m[k] = np.ascontiguousarray(a)
        m.update(consts)
        maps.append(m)
    return maps


def kernel(**inputs):
    nc = build(stages=3)
    maps = make_in_maps(inputs, 8)
    res = run_bass_kernel_spmd(nc, maps, core_ids=list(range(8)))
    outs = []
    for b in range(4):
        o0 = np.asarray(res.results[2 * b]["out"], dtype=np.float32)
        o1 = np.asarray(res.results[2 * b + 1]["out"], dtype=np.float32)
        outs.append(np.concatenate([o0[NPAD + NMETA:], o1[:SEQ - (HT * 128 - NPAD - NMETA)]], axis=0))
    return np.stack(outs, axis=0)


class Stream:
    def __init__(self, P, bufs, reqs, q=SP, keep=0):
        self.P, self.bufs, self.reqs, self.q, self.keep = P, bufs, reqs, q, keep
        self.issued = 0

    def get(self, i):
        lim = min(len(self.reqs), i + len(self.bufs) - self.keep)
        while self.issued < lim:
            k = self.issued
            b = self.bufs[k % len(self.bufs)]
            dst, src = self.reqs[k]
            self.P.dma(dst(b), src, writes=[b], q=self.q)
            self.issued += 1
        return self.bufs[i % len(self.bufs)]


def convert_weights(P, nc, C, bounce, gate):
    k = [0]

    def ld(src, width):
        b = bounce[k[0] % 2]
        k[0] += 1
        P.dma(b[:, 0:width], src, reads=[gate[0]], writes=[b], q=POOL)
        return b
    for l in range(2):
        for kc in range(8):
            b = ld(C.ffn_w_in[l, kc * 128:(kc + 1) * 128, :], 2 * FF)
            for hf in range(2):
                P.dma(C.W1S[l, :, :, kc, hf * 128:(hf + 1) * 128].rearrange("j p c -> p j c"),
                      b[:, hf * FF:(hf + 1) * FF].rearrange("p (j c) -> p j c", c=128), reads=[b], q=POOL)
            yield
        wo = C.ffn_w_out[l].rearrange("(j p) c -> p j c", p=128)
        for j0 in range(0, NJ, 5):
            j1 = min(NJ, j0 + 5)
            b = ld(wo[:, j0:j1, :], (j1 - j0) * D)
            P.dma(C.W2S[l, j0:j1].rearrange("j p c -> p j c"),
                  b[:, 0:(j1 - j0) * D].rearrange("p (j c) -> p j c", c=D), reads=[b], q=POOL)
            yield
    GW = NHG * 128
    for kc in range(8):
        b = ld(C.hgrn_w_in[kc * 128:(kc + 1) * 128, :], 4 * GW)
        for gi, base in enumerate((0, GW, 3 * GW)):
            P.dma(C.WHS[:, :, kc, gi * 128:(gi + 1) * 128].rearrange("h p c -> p h c"),
                  b[:, base:base + GW].rearrange("p (h c) -> p h c", c=128), reads=[b], q=POOL)
        P.dma(C.WHV[0, :, kc, :], b[:, 2 * GW:3 * GW], reads=[b], q=POOL)
        yield


NHG = 4
HT = 33
LBLK = [(0, 128)] + [(128 + 512 * i, 512) for i in range(8)]
NLB = len(LBLK)


def stage3(P, nc, C, sb, ps):
    B = [ps("s3_B%d" % i, [128, 512], F32) for i in range(6)]
    Tt = [ps("s3_T%d" % i, [128, 1024], BF16) for i in range(2)]
    for i in range(2):
        v = Buf(Tt[i].t[:, :].bitcast(F32), "B%df" % (6 + i))
        v.res = Tt[i].res
        B.append(v)

    def bres(i):
        return [B[i].res]

    prow = sb("s3_prow", [32, 128], F32)
    P.dma(prow[0:8, :], C.lb.rearrange("r (h p) -> (r h) p", p=128), writes=[prow])
    P.dma(prow[8:16, :], C.attn_norm[1].rearrange("(c p) -> c p", p=128), writes=[prow])
    P.dma(prow[16:24, :], C.ffn_norm[0].rearrange("(c p) -> c p", p=128), writes=[prow])
    P.dma(prow[24:32, :], C.ffn_norm[1].rearrange("(c p) -> c p", p=128), writes=[prow])
    pcol = sb("s3_pcol", [128, 32], F32)
    P.add(PE, lambda e: e.transpose(out=B[5][:, 0:32], in_=prow[0:32, :], identity=C.identf[0:32, 0:32]),
          [prow, C.identf], bres(5))
    _copy(P, DVE, pcol[:], B[5][:, 0:32], bres(5), [pcol])
    omlb = sb("s3_omlb", [128, NHG], F32)
    _tt(P, DVE, omlb[:], pcol[:, 4:8], pcol[:, 0:4], ALU.subtract, [pcol], [omlb])
    _act(P, omlb[:], omlb[:], AF.Exp, [omlb], [omlb])
    _ts(P, DVE, omlb[:], omlb[:], 1.0, None, ALU.add, None, [omlb], [omlb])
    P.add(DVE, lambda e: e.reciprocal(out=omlb[:], in_=omlb[:]), [omlb], [omlb])
    lnomlb = sb("s3_lnomlb", [128, NHG], F32)
    _act(P, lnomlb[:], omlb[:], AF.Ln, [omlb], [lnomlb])
    gcols = {"attn1": pcol[:, 8:16], "ffn0": pcol[:, 16:24], "ffn1": pcol[:, 24:32]}
    gn = sb("s3_gn", [128, 1], F32)
    P.dma(gn[:], C.g_norm[0].rearrange("(p o) -> p o", o=1), writes=[gn])
    gfin = sb("s3_gfin", [128, D], F32)
    P.dma(gfin[:], C.final_norm.partition_broadcast(128), writes=[gfin])
    mh = sb("s3_mh", [128, 1], F32)
    P.add(POOL, lambda e: e.memset(mh[:], -0.5), [], [mh])
    ones128b = sb("s3_ones128b", [128, 128], BF16)
    P.add(DVE, lambda e: e.memset(ones128b[:], 1.0), [], [ones128b])
    onesf = sb("s3_onesf", [128, 128], F32)
    P.add(DVE, lambda e: e.memset(onesf[:], 1.0), [], [onesf])
    rmk = sb("s3_rmk", [128, 512], mybir.dt.uint16)
    P.dma(rmk[:], C.rmask[:, 0:512], writes=[rmk])
    rmf = sb("s3_rmf", [128, 1], F32)
    P.dma(rmf[:], C.rmaskf, writes=[rmf])
    junk = sb("s3_junk", [128, D], BF16)
    nrm = [[sb("s3_nrm%d_%d" % (i, j), [128, 1], F32) for j in range(3)] for i in range(4)]
    cnt = {"y": 0, "f": 0, "T": 0, "n": 0, "h": 0}

    def ybank():
        cnt["y"] += 1
        return B[cnt["y"] % 4]

    def fbank():
        cnt["f"] += 1
        return B[cnt["f"] % 6]

    def hbank():
        cnt["h"] += 1
        return B[cnt["h"] % 3]

    def norm_rows(ht):
        ssq, tt, rstd = nrm[cnt["n"] % 4]
        cnt["n"] += 1
        _act(P, junk[:], ht[:], AF.Square, [ht], [junk, ssq], accum_out=ssq[:])
        _ts(P, DVE, tt[:], ssq[:], 1.0 / D, EPS, ALU.mult, ALU.add, [ssq], [tt])
        _tt(P, POOL, rstd[:], tt[:], mh[:], ALU.pow, [tt, mh], [rstd])
        return rstd

    def norm_A(hs, hns, nt):
        for tl in range(nt):
            rstd = norm_rows(hs[tl])
            _ts(P, DVE, hns[tl][:], hs[tl][:], rstd[:, 0:1], None, ALU.mult, None, [hs[tl], rstd], [hns[tl]])

    def norm_B(hns, hnT, nt, gcol):
        n = nt * 128
        for c in range(8):
            half = cnt["T"] % 2
            cnt["T"] += 1
            pT = Tt[half][:, 0:512]
            res = Tt[half].res

            def fn(e, c=c, pT=pT):
                ins = None
                for tl in range(nt):
                    ins = e.transpose(out=pT[:, tl * 128:(tl + 1) * 128], in_=hns[tl][:, c * 128:(c + 1) * 128],
                                      identity=C.identb[:])
                return ins
            P.add(PE, fn, list(hns[:nt]) + [C.identb], [res])
            if c % 2 == 0:
                _ts(P, DVE, hnT[:, c, 0:n], pT[:, 0:n], gcol[:, c:c + 1], None, ALU.mult, None, [res, pcol], [hnT.sub(c)])
            else:
                P.add(ACT, lambda e, c=c, pT=pT: e.mul(out=hnT[:, c, 0:n], in_=pT[:, 0:n], mul=gcol[:, c:c + 1]),
                      [res, pcol], [hnT.sub(c)])

    def ag(in_t, out_t, rin, rout):
        P.cc(lambda e: e.collective_compute("AllGather", ALU.bypass, replica_groups=C.RG,
                                            ins=[in_t.ap().opt()], outs=[out_t.ap().opt()]),
             reads=[rin], writes=[rout])

    def scope():
        st = contextlib.ExitStack()
        return st, (lambda name, shape, dt: Buf(st.enter_context(nc.sbuf_tensor(name, list(shape), dt)), name))

    def ffn_phase(l):
        st, sbp = scope()
        with st:
            pre = "p%d_" % (1 + 2 * l)
            Wo = C.Wo[l]
            hs2 = [[sbp(pre + "h%d_%d" % (j, i), [128, D], F32) for i in range(4)] for j in range(2)]
            X0s = [sbp(pre + "X0_%d" % j, [128, 8, 512], BF16) for j in range(2)]
            Xcs = [sbp(pre + "Xc%d" % i, [128, 512], BF16) for i in range(2)]
            hns = [sbp(pre + "hn%d" % i, [128, D], BF16) for i in range(4)]
            hnT = sbp(pre + "hnT", [128, 8, 512], BF16)
            hnT1 = sbp(pre + "hnT1", [128, 8, 512], BF16) if l == 0 else None
            actT = sbp(pre + "actT", [128, NJ, 512], BF16)
            sils = [sbp(pre + "sil%d" % i, [128, 512], F32) for i in range(2)]
            W1b = [sbp(pre + "W1b%d" % i, [128, 2, 8, 256], BF16) for i in range(3)]
            W2b = [sbp(pre + "W2b%d" % i, [128, 2, D], BF16) for i in range(3)]
            orow = [sbp(pre + "orow%d" % i, [128, D], F32) for i in range(2)] if l == 1 else None
            NJP = NJ // 2
            S1 = Stream(P, W1b, [(lambda b: b[:], C.W1S[l, 2 * jp:2 * jp + 2].rearrange("j p k c -> p j k c"))
                                 for lb in range(NLB) for jp in range(NJP)])
            S2 = Stream(P, W2b, [(lambda b: b[:], C.W2S[l, 2 * jp:2 * jp + 2].rearrange("j p c -> p j c"))
                                 for lb in range(NLB) for jp in range(NJP)])
            S1.get(0)
            S2.get(0)
            hnTr = [hnT.sub(c) for c in range(8)]
            blocks = LBLK[:C.nblk]
            oc = [0]

            def stA(i):
                t0, n = blocks[i]
                nt = n // 128
                hs, X0 = hs2[i % 2], X0s[i % 2]
                for tl in range(nt):
                    if l == 0:
                        P.dma(hs[tl][:], C.h0h[t0 + tl * 128:t0 + (tl + 1) * 128, :], writes=[hs[tl]])
                    else:
                        P.dma(hs[tl][:], C.H1[t0 + tl * 128:t0 + (tl + 1) * 128, :], reads=[C.H1_res[i]], writes=[hs[tl]])
                yield
                for kc in range(8):
                    Xc = Xcs[kc % 2]
                    if l == 0:
                        rk, hl = kc // 4, 2 * (kc % 4)
                        for hh in range(2):
                            src = C.OGo[hl + hh][64 * rk:64 * rk + 64, :]
                            P.dma(X0[64 * hh:64 * hh + 64, kc, 0:n], src[:, t0:t0 + n], reads=[C.OGo_res[hl + hh]],
                                  writes=[X0.sub(kc)])
                            P.dma(Xc[64 * hh:64 * hh + 64, 0:n], src[:, HT * 128 + t0:HT * 128 + t0 + n],
                                  reads=[C.OGo_res[hl + hh]], writes=[Xc])
                    else:
                        rk, hd = kc // 4, kc % 4
                        P.dma(X0[:, kc, 0:n], C.OG1o[i][rk, :, hd, :], reads=[C.OG1o_res[i]], writes=[X0.sub(kc)])
                        P.dma(Xc[:, 0:n], C.OG1o[NLB + i][rk, :, hd, :], reads=[C.OG1o_res[NLB + i]], writes=[Xc])
                    P.add(DVE, lambda e, n=n, kc=kc, Xc=Xc, X0=X0: e.copy_predicated(
                        out=X0[:, kc, 0:n], mask=rmk[:, 0:n], data=Xc[:, 0:n]), [Xc, rmk, X0.sub(kc)], [X0.sub(kc)])
                    yield

            def stW(i):
                t0, n = blocks[i]
                hs, X = hs2[i % 2], X0s[i % 2]
                for tl in range(n // 128):
                    for hf in range(2):
                        y = ybank()
                        _mm(P, y[:, :], [(X[:, kc, tl * 128:(tl + 1) * 128], Wo[:, kc, hf * 512:(hf + 1) * 512])
                                         for kc in range(8)], [X.sub(kc) for kc in range(8)] + [Wo], [y])
                        _tt(P, DVE, hs[tl][:, hf * 512:(hf + 1) * 512], hs[tl][:, hf * 512:(hf + 1) * 512], y[:, :], ALU.add,
                            [hs[tl], y], [hs[tl]])

            hns1 = [sbp(pre + "hn1_%d" % i, [128, D], BF16) for i in range(4)] if l == 0 else None

            def stNa(i):
                t0, n = blocks[i]
                norm_A(hs2[i % 2], hns, n // 128)

            def stNb(i):
                t0, n = blocks[i]
                norm_B(hns, hnT, n // 128, gcols["ffn%d" % l])

            def stCin(i, gA=None):
                t0, n = blocks[i]
                for j in range(NJ):
                    if gA is not None and j % 2 == 0:
                        next(gA, None)
                    w = S1.get(i * NJP + j // 2)
                    jj = j % 2
                    g, u = fbank(), fbank()
                    _mm(P, g[:, 0:n], [(w[:, jj, kc, 0:128], hnT[:, kc, 0:n]) for kc in range(8)], [w] + hnTr, [g])
                    _mm(P, u[:, 0:n], [(w[:, jj, kc, 128:256], hnT[:, kc, 0:n]) for kc in range(8)], [w] + hnTr, [u])
                    sl = sils[j % 2]
                    _act(P, sl[:, 0:n], g[:, 0:n], AF.Silu, [g], [sl])
                    _tt(P, DVE, actT[:, j, 0:n], sl[:, 0:n], u[:, 0:n], ALU.mult, [sl, u], [actT.sub(j)])

            def stCout(i):
                t0, n = blocks[i]
                nt = n // 128
                hs = hs2[i % 2]
                for j in range(NJ):
                    w = S2.get(i * NJP + j // 2)

                    def fn(e, j=j, w=w):
                        ins = None
                        for tl in range(nt):
                            for hf in range(2):
                                ins = e.matmul(B[tl * 2 + hf][:, :], lhsT=actT[:, j, tl * 128:(tl + 1) * 128],
                                               rhs=w[:, j % 2, hf * 512:(hf + 1) * 512], start=(j == 0), stop=(j == NJ - 1))
                        return ins
                    P.add(PE, fn, [w, actT.sub(j)], sum([bres(k) for k in range(2 * nt)], []))
                for k in sorted(range(2 * nt), key=lambda k: -k):
                    tl, hf = k // 2, k % 2
                    _tt(P, DVE, hs[tl][:, hf * 512:(hf + 1) * 512], hs[tl][:, hf * 512:(hf + 1) * 512],
                        B[k][:, :], ALU.add, [hs[tl]] + bres(k), [hs[tl]])

            def stDa(i):
                t0, n = blocks[i]
                nt = n // 128
                hs = hs2[i % 2]
                if l == 0:
                    for tl in range(nt):
                        P.dma(C.H1[t0 + tl * 128:t0 + (tl + 1) * 128, :], hs[tl][:], reads=[hs[tl]], writes=[C.H1_res[i]])
                    norm_A(hs, hns1, nt)
                else:
                    for tl in range(nt):
                        rstd = norm_rows(hs[tl])
                        ob = orow[oc[0] % 2]
                        oc[0] += 1
                        _stt(P, ob[:], hs[tl][:], rstd[:, 0:1], gfin[:], ALU.mult, ALU.mult, [hs[tl], rstd, gfin], [ob])
                        P.dma(C.out[t0 + tl * 128:t0 + (tl + 1) * 128, :], ob[:], reads=[ob])

            def stDb(i):
                if l != 0:
                    return
                t0, n = blocks[i]
                nt = n // 128
                norm_B(hns1, hnT1, nt, gcols["attn1"])
                h1r = [hnT1.sub(c) for c in range(8)]
                if i == 0:
                    _ts(P, DVE, hnT1[:, :, 0:NPAD], hnT1[:, :, 0:NPAD], rmf[:, 0:1], None, ALU.mult, None, h1r + [rmf], h1r)
                P.dma(C.HNi[i], hnT1[:, :, 0:n], reads=h1r, writes=[C.HNi_res[i]])
                ag(C.HNi_t[i], C.HNo_t[i], C.HNi_res[i], C.HNo_res[i])

            nb = len(blocks)
            for _ in stA(0):
                pass
            stW(0)
            stNa(0)
            stNb(0)
            for i in range(nb):
                gA = stA(i + 1) if i + 1 < nb else None
                stCin(i, gA)
                if gA is not None:
                    for _ in gA:
                        pass
                if i > 0:
                    stDb(i - 1)
                if i + 1 < nb:
                    stW(i + 1)
                    stNa(i + 1)
                stCout(i)
                if i + 1 < nb:
                    stNb(i + 1)
                stDa(i)
            stDb(nb - 1)
            P.barrier()

    def hgrn_phase():
        st, sbp = scope()
        with st:
            hnTs = [sbp("p2_hnT%d" % i, [128, 8, 512], BF16) for i in range(2)]
            X1s = [sbp("p2_X1_%d" % i, [128, NHG, 512], BF16) for i in range(2)]
            WHb = [sbp("p2_WHb%d" % i, [128, 8, 512], BF16) for i in range(4)]
            silq = sbp("p2_silq", [128, NHG, 512], BF16)
            sgT = sbp("p2_sgT", [128, NHG, 512], BF16)
            Vsb2 = [[sbp("p2_V%d_%d" % (j, i), [128, NHG * 128], BF16) for i in range(4)] for j in range(2)]
            NBUF = 4
            mkb = lambda nm, dt: [sbp("p2_%s%d" % (nm, i), [128, 512], dt) for i in range(NBUF)]
            qts, kts, khs = mkb("qt", BF16), mkb("kt", BF16), mkb("kh", BF16)
            khTs = [sbp("p2_khT%d" % i, [128, 4, 128], BF16) for i in range(NBUF)]
            ezs, kks, ebs = mkb("ez", F32), mkb("kk", F32), mkb("eb", F32)
            mk2 = lambda nm, dt: [sbp("p2_%s%d" % (nm, i), [128, 512], dt) for i in range(2)]
            bbs, enbs, lnfs, ebcs = mk2("bb", F32), mk2("enb", F32), mk2("lnf", F32), mk2("ebc", F32)
            Asb4s = [sbp("p2_Asb4%d" % i, [128, 4, 128], BF16) for i in range(NBUF)]
            Usbs = mkb("Usb", F32)
            Sst = sbp("p2_S", [128, NHG, 128], F32)
            P.add(DVE, lambda e: e.memset(Sst[:], 0.0), [], [Sst])
            osqs = [sbp("p2_osq%d" % i, [128, 512], BF16) for i in range(2)]
            ons = [sbp("p2_on%d" % i, [128, 512], F32) for i in range(2)]
            rh = []
            for gb in range(2 * NLB):
                rh.append((lambda b: b[:], C.WHV[0]))
                for hd in range(NHG):
                    rh.append((lambda b: b[:, :, 0:384], C.WHS[hd]))
            SH = Stream(P, WHb, rh, keep=1)
            items = [(gb, hd) for gb in range(2 * NLB) for hd in range(NHG)]

            def geo(gb):
                rk, lb = gb // NLB, gb % NLB
                t0, n = LBLK[lb]
                return rk, lb, n, n // 128

            def h_front(i):
                gb, hd = items[i]
                rk, lb, n, nt = geo(gb)
                hnT = hnTs[gb % 2]
                hr = [hnT.sub(c) for c in range(8)]
                Vs = Vsb2[gb % 2]
                if hd == 0:
                    P.dma(hnT[:, :, 0:n], C.HNo[lb][rk], reads=[C.HNo_res[lb]], writes=hr, q=POOL)
                    w = SH.get(gb * (NHG + 1))
                    for tl in range(nt):
                        y = ybank()
                        _mm(P, y[:, :], [(hnT[:, kc, tl * 128:(tl + 1) * 128], w[:, kc, 0:512]) for kc in range(8)], [w] + hr, [y])
                        _copy(P, ACT if tl % 2 == 0 else DVE, Vs[tl][:, :], y[:, :], [y], [Vs[tl]])
                w = SH.get(gb * (NHG + 1) + 1 + hd)
                k3 = i % NBUF
                ez, kk, eb = ezs[k3], kks[k3], ebs[k3]
                lnf, bb, enb, ebc = [x[i % 2] for x in (lnfs, bbs, enbs, ebcs)]
                qt, kt, kh, khT, Asb4, Usb = qts[k3], kts[k3], khs[k3], khTs[k3], Asb4s[k3], Usbs[k3]
                qp, gp = hbank(), hbank()
                _mm(P, qp[:, 0:n], [(w[:, kc, 0:128], hnT[:, kc, 0:n]) for kc in range(8)], [w] + hr, [qp])
                _mm(P, gp[:, 0:n], [(w[:, kc, 256:384], hnT[:, kc, 0:n]) for kc in range(8)], [w] + hr, [gp])
                _act(P, silq[:, hd, 0:n], qp[:, 0:n], AF.Silu, [qp], [silq.sub(hd)])
                yield
                _act(P, sgT[:, hd, 0:n], gp[:, 0:n], AF.Silu, [gp], [sgT.sub(hd)])
                yield
                zp = hbank()
                _mm(P, zp[:, 0:n], [(w[:, kc, 128:256], hnT[:, kc, 0:n]) for kc in range(8)], [w] + hr, [zp])
                _act(P, ez[:, 0:n], zp[:, 0:n], AF.Exp, [zp], [ez])
                yield
                _act(P, ez[:, 0:n], ez[:, 0:n], AF.Ln, [ez], [ez], bias=1.0)
                yield
                _act(P, kk[:, 0:n], ez[:, 0:n], AF.Exp, [ez, lnomlb], [kk], scale=-1.0, bias=lnomlb[:, hd:hd + 1])
                yield
                _act(P, lnf[:, 0:n], kk[:, 0:n], AF.Ln, [kk], [lnf], scale=-1.0, bias=1.0)
                yield
                for c in range(nt):
                    P.add(DVE, lambda e, c=c: e.tensor_tensor_scan(
                        out=bb[:, c * 128:(c + 1) * 128], data0=onesf[:, 0:128], data1=lnf[:, c * 128:(c + 1) * 128],
                        initial=0.0, op0=ALU.mult, op1=ALU.add), [onesf, lnf], [bb])
                _act(P, eb[:, 0:n], bb[:, 0:n], AF.Exp, [bb], [eb])
                yield
                _act(P, enb[:, 0:n], bb[:, 0:n], AF.Exp, [bb], [enb], scale=-1.0)
                yield
                for c in range(nt):
                    _act(P, ebc[:, c * 128:(c + 1) * 128], bb[:, c * 128:(c + 1) * 128], AF.Exp, [bb], [ebc], scale=-1.0,
                         bias=bb[:, c * 128 + 127:c * 128 + 128])
                _tt(P, DVE, qt[:, 0:n], silq[:, hd, 0:n], eb[:, 0:n], ALU.mult, [silq.sub(hd), eb], [qt])
                yield
                _tt(P, DVE, kt[:, 0:n], kk[:, 0:n], enb[:, 0:n], ALU.mult, [kk, enb], [kt])
                yield
                _tt(P, DVE, kh[:, 0:n], kk[:, 0:n], ebc[:, 0:n], ALU.mult, [kk, ebc], [kh])
                yield
                pT = Tt[0][:, 0:512]
                res = Tt[0].res

                def fnT(e):
                    ins = None
                    for c in range(nt):
                        ins = e.transpose(out=pT[:, c * 128:(c + 1) * 128], in_=kh[:, c * 128:(c + 1) * 128], identity=C.identb[:])
                    return ins
                P.add(PE, fnT, [kh, C.identb], [res])
                _copy(P, ACT, khT[:, 0:nt, :], pT[:, 0:n].rearrange("p (c s) -> p c s", s=128), [res], [khT])
                yield
                Ab, Ub = B[5], B[7]

                def fnA(e):
                    ins = None
                    for c in range(nt):
                        cs = slice(c * 128, (c + 1) * 128)
                        ins = e.matmul(Ab[:, cs], lhsT=kt[:, cs], rhs=qt[:, cs], start=True, stop=True)
                    return ins
                P.add(PE, fnA, [kt, qt], [Ab])
                P.add(DVE, lambda e: e.tensor_tensor(
                    out=Asb4[:, 0:nt, :], in0=Ab[:, 0:n].rearrange("p (c s) -> p c s", s=128),
                    in1=C.tri[:, :].unsqueeze(1).to_broadcast([128, nt, 128]), op=ALU.mult), [Ab, C.tri], [Asb4])

                def fnU(e):
                    ins = None
                    for c in range(nt):
                        cs = slice(c * 128, (c + 1) * 128)
                        ins = e.matmul(Ub[:, cs], lhsT=khT[:, c, :], rhs=Vs[c][:, hd * 128:(hd + 1) * 128], start=True, stop=True)
                    return ins
                P.add(PE, fnU, [khT] + list(Vs[:nt]), [Ub])
                _copy(P, ACT, Usb[:, 0:n], Ub[:, 0:n], [Ub], [Usb])
                yield

            Sbf4s = [sbp("p2_Sbf4%d" % i, [128, 4, 128], BF16) for i in range(NBUF)]

            def h_state(i):
                gb, hd = items[i]
                rk, lb, n, nt = geo(gb)
                k3 = i % NBUF
                eb, Usb, Sbf4 = ebs[k3], Usbs[k3], Sbf4s[k3]
                for c in range(nt):
                    cs = slice(c * 128, (c + 1) * 128)
                    _copy(P, DVE, Sbf4[:, c, :], Sst[:, hd, :], [Sst.sub(hd)], [Sbf4])
                    _stt(P, Sst[:, hd, :], Sst[:, hd, :], eb[:, c * 128 + 127:c * 128 + 128], Usb[:, cs], ALU.mult, ALU.add,
                         [Sst.sub(hd), eb, Usb], [Sst.sub(hd)])

            def h_back(i):
                gb, hd = items[i]
                rk, lb, n, nt = geo(gb)
                Vs = Vsb2[gb % 2]
                k3 = i % NBUF
                qt, Asb4, Sbf4 = qts[k3], Asb4s[k3], Sbf4s[k3]
                X1 = X1s[gb % 2]
                osq, on = osqs[i % 2], ons[i % 2]
                op = B[3 + hd % 2]

                def fnO(e):
                    ins = None
                    for c in range(nt):
                        cs = slice(c * 128, (c + 1) * 128)
                        e.matmul(op[:, cs], lhsT=Vs[c][:, hd * 128:(hd + 1) * 128], rhs=Asb4[:, c, :], start=True, stop=False)
                        ins = e.matmul(op[:, cs], lhsT=Sbf4[:, c, :], rhs=qt[:, cs], start=False, stop=True)
                    return ins
                P.add(PE, fnO, [Sbf4, qt, Asb4] + list(Vs[:nt]), [op])
                yield
                _act(P, osq[:, 0:n], op[:, 0:n], AF.Square, [op], [osq])
                yield
                sp = hbank()
                _mm(P, sp[:, 0:n], [(ones128b[:], osq[:, 0:n])], [ones128b, osq], [sp])
                lnv, rs = ezs[k3], kks[k3]
                _act(P, lnv[:, 0:n], sp[:, 0:n], AF.Ln, [sp], [lnv], scale=1.0 / 128, bias=EPS)
                yield
                _act(P, rs[:, 0:n], lnv[:, 0:n], AF.Exp, [lnv], [rs], scale=-0.5)
                yield
                _stt(P, on[:, 0:n], op[:, 0:n], gn[:, 0:1], rs[:, 0:n], ALU.mult, ALU.mult, [op, gn, rs], [on])
                yield
                _tt(P, DVE, X1[:, hd, 0:n], on[:, 0:n], sgT[:, hd, 0:n], ALU.mult, [on, sgT.sub(hd)], [X1])
                yield
                if hd == NHG - 1:
                    P.dma(C.OG1i[gb], X1[:, :, 0:n], reads=[X1], writes=[C.OG1i_res[gb]])
                    ag(C.OG1i_t[gb], C.OG1o_t[gb], C.OG1i_res[gb], C.OG1o_res[gb])

            def zipped(gens):
                gens = list(gens)
                if DBG == "seq":
                    for g in gens:
                        for _ in g:
                            pass
                    return
                while gens:
                    for g in list(gens):
                        try:
                            next(g)
                        except StopIteration:
                            gens.remove(g)

            N = len(items)
            npair = N // 2
            zipped([h_front(0), h_front(1)])
            h_state(0)
            h_state(1)
            for t in range(npair):
                gens = [h_back(2 * t), h_back(2 * t + 1)]
                if t + 1 < npair:
                    gens = [h_front(2 * t + 2), h_front(2 * t + 3)] + gens
                zipped(gens)
                if t + 1 < npair:
                    h_state(2 * t + 2)
                    h_state(2 * t + 3)
            P.barrier()

    ffn_phase(0)
    if C.s3parts >= 2:
        hgrn_phase()
    if C.s3parts >= 3:
        ffn_phase(1)
```

```python
import numpy as np
import concourse.bass as bass
import concourse.mybir as mybir
from concourse.bass_utils import run_bass_kernel_spmd

F32 = mybir.dt.float32
BF16 = mybir.dt.bfloat16
AF = mybir.ActivationFunctionType
ALU = mybir.AluOpType
AX = mybir.AxisListType

PE, ACT, DVE, POOL, SP = "pe", "act", "dve", "pool", "sp"
ENGS = (PE, ACT, DVE, POOL, SP)
NDMA_SEM = 12


class Res:
    __slots__ = ("w", "r")

    def __init__(self):
        self.w = None
        self.r = []


class Buf:
    def __init__(self, t, name=""):
        self.t = t
        self.name = name
        self.res = Res()
        self.subs = {}

    def __getitem__(self, k):
        return self.t[k]

    def sub(self, key):
        r = self.subs.get(key)
        if r is None:
            r = self.subs[key] = Res()
        return r


def _res(x):
    return x.res if isinstance(x, Buf) else x


class Op:
    __slots__ = ("eng", "fn", "raw", "oth", "dma", "sigval", "need", "dsem", "dval", "dprev", "pos")


class Prog:
    def __init__(self, nc):
        self.nc = nc
        self.ops = {e: [] for e in ENGS}
        self.ndma = {e: 0 for e in ENGS}

    def add(self, eng, fn, reads=(), writes=(), dma=False):
        op = Op()
        op.eng, op.fn, op.dma = eng, fn, dma
        op.raw, op.oth = set(), set()
        op.need = False
        op.sigval = None
        for r in reads:
            r = _res(r)
            if r.w is not None:
                op.raw.add(r.w)
        for w in writes:
            w = _res(w)
            if w.w is not None:
                op.oth.add(w.w)
            for x in w.r:
                op.oth.add(x)
        for r in reads:
            _res(r).r.append(op)
        for w in writes:
            w = _res(w)
            w.w = op
            w.r = []
        if dma:
            i = self.ndma[eng]
            self.ndma[eng] += 1
            op.dsem = i % NDMA_SEM
            op.dval = 16 * (i // NDMA_SEM + 1)
        self.ops[eng].append(op)
        return op

    def cc(self, fn, reads=(), writes=()):
        op = self.add(POOL, fn, reads, writes, dma=True)
        self.ndma[POOL] -= 1
        self.ncc = getattr(self, "ncc", 0) + 1
        op.dsem = "cc"
        op.dval = self.ncc
        return op

    def barrier(self):
        last = {}
        for e in ENGS:
            lc = None
            for o in reversed(self.ops[e]):
                if isinstance(o, Op) and not o.dma:
                    lc = o
                    break
            if lc is not None:
                lc.need = True
            last[e] = lc
        mark = ("barrier", last, dict(self.ndma), getattr(self, "ncc", 0))
        for e in ENGS:
            self.ops[e].append(mark)

    def dma(self, out, in_, reads=(), writes=(), q=SP, **kw):
        return self.add(q, lambda e: e.dma_start(out=out, in_=in_, **kw), reads, writes, dma=True)

    def _needed(self, o, d):
        if d.dma:
            return True
        if d.eng != o.eng:
            return True
        if o.dma:
            return True
        if o.eng == PE:
            return False
        return d in o.raw

    def _deps(self, o):
        best = {}
        out = []
        for d in list(o.raw) + list(o.oth):
            if not self._needed(o, d):
                continue
            if d.dma:
                out.append(d)
            else:
                b = best.get(d.eng)
                if b is None or d.pos > b.pos:
                    best[d.eng] = d
        return out + list(best.values())

    def emit(self):
        nc = self.nc
        for e in ENGS:
            for k, o in enumerate(self.ops[e]):
                if isinstance(o, Op):
                    o.pos = k
        for e in ENGS:
            for o in self.ops[e]:
                if not isinstance(o, Op):
                    continue
                for d in self._deps(o):
                    if not d.dma:
                        d.need = True
        for e in ENGS:
            c = 0
            for o in self.ops[e]:
                if isinstance(o, Op) and (not o.dma) and o.need:
                    c += 1
                    o.sigval = c
        import contextlib
        with contextlib.ExitStack() as st:
            esem = {e: st.enter_context(nc.semaphore("s_" + e)) for e in ENGS}
            dsem = {e: [st.enter_context(nc.semaphore("d_%s%d" % (e, i))) for i in range(NDMA_SEM)]
                    for e in ENGS if self.ndma[e] > 0}
            ccsem = st.enter_context(nc.semaphore("s_cc"))
            block = st.enter_context(nc.Block())

            def run(ename, eng):
                seen = {}

                def wait(sem, val):
                    k = id(sem)
                    if seen.get(k, 0) >= val:
                        return
                    seen[k] = val
                    eng.wait_ge(sem, val)

                for o in self.ops[ename]:
                    if not isinstance(o, Op):
                        _, last, nd, ncc = o
                        if ncc > 0:
                            wait(ccsem, ncc)
                        for e2 in ENGS:
                            if last[e2] is not None:
                                wait(esem[e2], last[e2].sigval)
                            n = nd[e2]
                            for i in range(NDMA_SEM):
                                cnt = (n - i + NDMA_SEM - 1) // NDMA_SEM if n > i else 0
                                if cnt > 0:
                                    wait(dsem[e2][i], 16 * cnt)
                        continue
                    for d in self._deps(o):
                        if d.dma and d.dsem == "cc":
                            wait(ccsem, d.dval)
                        elif d.dma:
                            wait(dsem[d.eng][d.dsem], d.dval)
                        else:
                            wait(esem[d.eng], d.sigval)
                    if o.dma and o.dsem == "cc":
                        if o.dval > 1:
                            wait(ccsem, o.dval - 1)
                        o.fn(eng).then_inc(ccsem)
                    elif o.dma:
                        s = dsem[ename][o.dsem]
                        if o.dval > 16:
                            wait(s, o.dval - 16)
                        o.fn(eng).then_inc(s, 16)
                    else:
                        ins = o.fn(eng)
                        if o.need:
                            ins.then_inc(esem[ename], 1)
                if ename == POOL and getattr(self, "ncc", 0) > 0:
                    wait(ccsem, self.ncc)
                if self.ndma[ename] > 0:
                    n = self.ndma[ename]
                    for i in range(NDMA_SEM):
                        cnt = (n - i + NDMA_SEM - 1) // NDMA_SEM if n > i else 0
                        if cnt > 0:
                            wait(dsem[ename][i], 16 * cnt)

            @block.tensor
            def _(eng):
                run(PE, eng)

            @block.scalar
            def _(eng):
                run(ACT, eng)

            @block.vector
            def _(eng):
                run(DVE, eng)

            @block.gpsimd
            def _(eng):
                run(POOL, eng)

            @block.sync
            def _(eng):
                run(SP, eng)
import contextlib
import ml_dtypes

D = 1024
H = 16
DH = 64
NMETA = 16
SEQ = 8192
NPAD = 112
LP = NPAD + NMETA + SEQ
NT = LP // 128
FF = 2816
NJ = FF // 128
EPS = 1e-6
BLOCKS = [(0, 128)] + [(128 + 512 * i, 512) for i in range(16)]
NB = len(BLOCKS)


import os
DBG = os.environ.get('KDBG', '')


class Ctx:
    pass


def _mm(P, out_ap, pairs, reads, writes, start=True, stop=True):
    def fn(e):
        n = len(pairs)
        ins = None
        for i, (l, r) in enumerate(pairs):
            ins = e.matmul(out_ap, lhsT=l, rhs=r, start=(start and i == 0), stop=(stop and i == n - 1))
        return ins
    return P.add(PE, fn, reads, writes)


def _act(P, out, in_, func, reads, writes, **kw):
    return P.add(ACT, lambda e: e.activation(out=out, in_=in_, func=func, **kw), reads, writes)


def _tt(P, eng, out, in0, in1, op, reads, writes):
    return P.add(eng, lambda e: e.tensor_tensor(out=out, in0=in0, in1=in1, op=op), reads, writes)


def _ts(P, eng, out, in0, s1, s2, op0, op1, reads, writes):
    if op1 is None:
        return P.add(eng, lambda e: e.tensor_scalar(out=out, in0=in0, scalar1=s1, scalar2=None, op0=op0), reads, writes)
    return P.add(eng, lambda e: e.tensor_scalar(out=out, in0=in0, scalar1=s1, scalar2=s2, op0=op0, op1=op1), reads, writes)


def _stt(P, out, in0, scalar, in1, op0, op1, reads, writes):
    return P.add(DVE, lambda e: e.scalar_tensor_tensor(out=out, in0=in0, scalar=scalar, in1=in1, op0=op0, op1=op1), reads, writes)


def _copy(P, eng, out, in_, reads, writes):
    if eng == ACT:
        return P.add(ACT, lambda e: e.copy(out=out, in_=in_), reads, writes)
    return P.add(eng, lambda e: e.tensor_copy(out=out, in_=in_), reads, writes)


def _rmsnorm_rows(P, C, ht, hn, gbc, tmp):
    junk, ssq, lnv, rstd = tmp
    _act(P, junk[:], ht[:], AF.Square, [ht], [junk, ssq], accum_out=ssq[:])
    _act(P, lnv[:], ssq[:], AF.Ln, [ssq], [lnv], scale=1.0 / D, bias=EPS)
    _act(P, rstd[:], lnv[:], AF.Exp, [lnv], [rstd], scale=-0.5)
    _stt(P, hn[:], ht[:], rstd[:, 0:1], gbc[:], ALU.mult, ALU.mult, [ht, rstd, gbc], [hn])


def _transpose_block(P, C, hns, nt, hnT, pTs, cnt0):
    n = nt * 128
    for c in range(8):
        pT = pTs[(cnt0 + c) % len(pTs)]

        def fn(e, c=c, pT=pT):
            ins = None
            for tl in range(nt):
                ins = e.transpose(out=pT[:, tl * 128:(tl + 1) * 128], in_=hns[tl][:, c * 128:(c + 1) * 128],
                                  identity=C.identb[:])
            return ins
        P.add(PE, fn, list(hns[:nt]) + [C.identb], [pT])
        _copy(P, DVE if c % 4 != 0 else ACT, hnT[:, c, 0:n], pT[:, 0:n], [pT], [hnT.sub(c)])


def stage1(P, nc, C, sb, ps, bounce=None):
    NHL = C.nheads
    NQ = NHL // 2
    WC = 4 * NHL * DH + NHL
    W = sb("s1_W", [128, 8, WC], BF16)
    wst = [sb("s1_wst%d" % i, [128, WC], F32) for i in range(2)]
    Wr = [W.sub(k) for k in range(8)]
    gbc = sb("s1_gbc", [128, D], F32)
    P.dma(gbc[:], C.attn_norm[0].partition_broadcast(128), writes=[gbc])
    gcol = sb("s1_gcol", [128, 2], F32)
    for hh in range(2):
        P.dma(gcol[64 * hh:64 * hh + 64, 0:1], C.fox_q_norm[0].rearrange("(p o) -> p o", o=1), writes=[gcol])
        P.dma(gcol[64 * hh:64 * hh + 64, 1:2], C.fox_k_norm[0].rearrange("(p o) -> p o", o=1), writes=[gcol])
    _ts(P, DVE, gcol[:, 0:1], gcol[:, 0:1], 0.125, None, ALU.mult, None, [gcol], [gcol])
    negbf = sb("s1_negbf", [NHL, 1], F32)
    P.dma(negbf[:], C.fox_b_f[0].rearrange("(p o) -> p o", o=1), writes=[negbf])
    _ts(P, DVE, negbf[:], negbf[:], -1.0, None, ALU.mult, None, [negbf], [negbf])
    ones16 = sb("s1_ones16", [NHL, 512], F32)
    P.add(DVE, lambda e: e.memset(ones16[:], 1.0), [], [ones16])
    zero16 = sb("s1_zero16", [NHL, 1], F32)
    P.add(DVE, lambda e: e.memset(zero16[:], 0.0), [], [zero16])

    hts = [sb("s1_ht%d" % i, [128, D], F32) for i in range(2)]
    hns = [sb("s1_hn%d" % i, [128, D], BF16) for i in range(8)]
    junk = sb("s1_junk", [128, D], BF16)
    ssqs = [[sb("s1_ssq%d_%d" % (i, j), [128, 1], F32) for j in range(3)] for i in range(2)]
    hnTs = [sb("s1_hnT%d" % i, [128, 8, 512], BF16) for i in range(2)]
    pTs = [ps("s1_pT%d" % i, [128, 512], BF16) for i in range(2)]
    psA = [ps("s1_psA%d" % i, [128, 512], F32) for i in range(4)]
    psSs = [ps("s1_psS%d" % i, [128, 512], F32) for i in range(2)]
    sqs = [sb("s1_sq%d" % i, [128, 512], BF16) for i in range(2)]
    lnvs = [sb("s1_lnv%d" % i, [128, 512], F32) for i in range(2)]
    rss = [sb("s1_rs%d" % i, [128, 512], F32) for i in range(2)]
    outs = [sb("s1_out%d" % i, [128, 512], BF16) for i in range(6)]
    vsts = [sb("s1_vst%d" % i, [128, NHL, 4, 65], BF16) for i in range(2)]
    for v in vsts:
        P.add(POOL, lambda e, v=v: e.memset(v[:, :, :, 64:65], 1.0), [], [v])
    ef = sb("s1_ef", [NHL, 512], F32)
    lf = sb("s1_lf", [NHL, 512], F32)
    cps = [sb("s1_cp%d" % i, [NHL, 512], F32) for i in range(2)]
    hsp = [[sb("s1_hsp%d_%d" % (i, j), [NHL, 512], BF16) for j in range(6)] for i in range(2)]
    r1 = sb("s1_r1", [NHL, 512], F32)
    r2 = sb("s1_r2", [NHL, 512], F32)
    onesb = sb("s1_onesb", [3, LP // 4], BF16)
    P.add(DVE, lambda e: e.memset(onesb[:], 1.0), [], [onesb])
    for h in range(NHL):
        for qd in range(4):
            cs_ = slice(qd * (LP // 4), (qd + 1) * (LP // 4))
            P.dma(C.KT_d[h, 67:70, cs_], onesb[0:3, :], reads=[onesb], q=POOL)
            P.dma(C.QT_d[h, 64:67, cs_], onesb[0:3, :], reads=[onesb], q=POOL)

    cA = 0
    cO = 0
    cT = 0
    cTh = [0]

    def fa_load(bi, tl):
        t0, n = BLOCKS[bi]
        if bi >= NB or tl >= n // 128:
            return
        ht = hts[tl % 2]
        P.dma(ht[:], C.h0[t0 + tl * 128:t0 + (tl + 1) * 128, :], writes=[ht])

    def fa_norm(bi, tl):
        t0, n = BLOCKS[bi]
        if bi >= NB or tl >= n // 128:
            return
        myhn = hns[(bi % 2) * 4:(bi % 2) * 4 + 4]
        tmp = [junk] + ssqs[cTh[0] % 2]
        cTh[0] += 1
        _rmsnorm_rows(P, C, hts[tl % 2], myhn[tl], gbc, tmp)

    def s1_front_a(bi):
        for tl in range(4):
            fa_load(bi, tl)
            fa_norm(bi, tl)

    def s1_front_b(bi):
        t0, n = BLOCKS[bi]
        myhn = hns[(bi % 2) * 4:(bi % 2) * 4 + 4]
        _transpose_block(P, C, myhn, n // 128, hnTs[bi % 2], pTs, 0)

    s1_front_a(0)
    for kc in range(8):
        P.dma(wst[kc % 2][:], C.fox_w_in[kc * 128:(kc + 1) * 128, :], writes=[wst[kc % 2]])
        _copy(P, ACT if kc % 2 == 0 else DVE, W[:, kc, :], wst[kc % 2][:], [wst[kc % 2]], [W.sub(kc)])
    s1_front_b(0)
    for bi, (t0, n) in enumerate(BLOCKS):
        nt = n // 128
        hnT = hnTs[bi % 2]
        if bi + 1 < NB:
            fa_load(bi + 1, 0)
            fa_load(bi + 1, 1)
        hr = [hnT.sub(c) for c in range(8)]
        pqs = {}

        def qk_a(c):
            nonlocal cA
            cols = c * 128
            pq = psA[cA % 4]
            cA += 1
            pqs[c] = pq
            _mm(P, pq[:, 0:n], [(W[:, kc, cols:cols + 128], hnT[:, kc, 0:n]) for kc in range(8)], Wr + hr, [pq])
            _act(P, sqs[c % 2][:, 0:n], pq[:, 0:n], AF.Square, [pq], [sqs[c % 2]])

        def qk_b(c):
            nonlocal cO
            pq = pqs[c]
            sq, lnv, rs = sqs[c % 2], lnvs[c % 2], rss[c % 2]
            psS = psSs[c % 2]
            _mm(P, psS[:, 0:n], [(C.bdones[:], sq[:, 0:n])], [C.bdones, sq], [psS])
            _act(P, lnv[:, 0:n], psS[:, 0:n], AF.Ln, [psS], [lnv], scale=1.0 / DH, bias=EPS)
            _act(P, rs[:, 0:n], lnv[:, 0:n], AF.Exp, [lnv], [rs], scale=-0.5)
            ob = outs[cO % 6]
            cO += 1
            gi = 0 if c < NQ else 1
            _stt(P, ob[:, 0:n], pq[:, 0:n], gcol[:, gi:gi + 1], rs[:, 0:n], ALU.mult, ALU.mult, [pq, gcol, rs], [ob])
            dst = C.QT_d if c < NQ else C.KT_d
            for hh in range(2):
                h = (c % NQ) * 2 + hh
                P.dma(dst[h, 0:64, t0:t0 + n], ob[64 * hh:64 * hh + 64, 0:n], reads=[ob], q=POOL)

        qk_a(0)
        for c in range(2 * NQ):
            if c + 1 < 2 * NQ:
                qk_a(c + 1)
            qk_b(c)
        if bi + 1 < NB:
            fa_norm(bi + 1, 0)
            fa_norm(bi + 1, 1)
            fa_load(bi + 1, 2)
            fa_load(bi + 1, 3)
        for c in range(NQ):
            cols = 3 * NHL * DH + c * 128
            pg = psA[cA % 4]
            cA += 1
            _mm(P, pg[:, 0:n], [(W[:, kc, cols:cols + 128], hnT[:, kc, 0:n]) for kc in range(8)], Wr + hr, [pg])
            lnv, rs = lnvs[c % 2], rss[c % 2]
            _act(P, rs[:, 0:n], pg[:, 0:n], AF.Exp, [pg], [rs], scale=-1.0)
            _act(P, lnv[:, 0:n], rs[:, 0:n], AF.Ln, [rs], [lnv], bias=1.0)
            ob = outs[cO % 6]
            cO += 1
            _act(P, ob[:, 0:n], lnv[:, 0:n], AF.Exp, [lnv], [ob], scale=-1.0)
            P.dma(C.SG_d[c * 128:(c + 1) * 128, t0:t0 + n], ob[:, 0:n], reads=[ob], q=POOL)
        pf = psA[cA % 4]
        cA += 1
        _mm(P, pf[0:NHL, 0:n], [(W[:, kc, 4 * NHL * DH:WC], hnT[:, kc, 0:n]) for kc in range(8)], Wr + hr, [pf])
        _act(P, ef[:, 0:n], pf[0:NHL, 0:n], AF.Exp, [pf], [ef], scale=-1.0, bias=negbf[:, 0:1])
        _act(P, lf[:, 0:n], ef[:, 0:n], AF.Ln, [ef], [lf], bias=1.0)
        cp = cps[bi % 2]
        if bi == 0:
            ref_ap, ref_b = zero16[:, 0:1], zero16
        else:
            pn = BLOCKS[bi - 1][1]
            ref_ap, ref_b = cps[(bi - 1) % 2][:, pn - 1:pn], cps[(bi - 1) % 2]
        P.add(DVE, lambda e, cp=cp, ref_ap=ref_ap, n=n: e.tensor_tensor_scan(
            out=cp[:, 0:n], data0=ones16[:, 0:n], data1=lf[:, 0:n], initial=ref_ap, op0=ALU.mult, op1=ALU.add),
            [ones16, lf, ref_b], [cp])
        hh = hsp[bi % 2]
        _copy(P, DVE, hh[0][:, 0:n], cp[:, 0:n], [cp], [hh[0]])
        _tt(P, DVE, r1[:, 0:n], cp[:, 0:n], hh[0][:, 0:n], ALU.subtract, [cp, hh[0]], [r1])
        _copy(P, DVE, hh[1][:, 0:n], r1[:, 0:n], [r1], [hh[1]])
        _tt(P, DVE, r2[:, 0:n], r1[:, 0:n], hh[1][:, 0:n], ALU.subtract, [r1, hh[1]], [r2])
        _copy(P, DVE, hh[2][:, 0:n], r2[:, 0:n], [r2], [hh[2]])
        for j in range(3):
            _ts(P, DVE, hh[3 + j][:, 0:n], hh[j][:, 0:n], -1.0, None, ALU.mult, None, [hh[j]], [hh[3 + j]])
            P.dma(C.KT_d[:, 64 + j, t0:t0 + n], hh[j][:, 0:n], reads=[hh[j]])
            P.dma(C.QT_d[:, 67 + j, t0:t0 + n], hh[3 + j][:, 0:n], reads=[hh[3 + j]])
        if bi + 1 < NB:
            fa_norm(bi + 1, 2)
            fa_norm(bi + 1, 3)
        vst = vsts[bi % 2]
        if bi == 0:
            P.add(POOL, lambda e, v=vst: e.memset(v[0:NPAD, :, 0:1, 64:65], 0.0), [], [vst])
        for tl in range(nt):
            for half in range(NHL // 8):
                pv = psA[cA % 4]
                cA += 1
                cols = 2 * NHL * DH + half * 512
                _mm(P, pv[:, :], [(hnT[:, kc, tl * 128:(tl + 1) * 128], W[:, kc, cols:cols + 512]) for kc in range(8)],
                    Wr + hr, [pv])
                _copy(P, DVE if half == 0 else ACT, vst[:, 8 * half:8 * half + 8, tl, 0:64],
                      pv[:, :].rearrange("p (h d) -> p h d", h=8), [pv], [vst])
        for h in range(NHL):
            P.dma(C.VA_d[h, :, t0 // 128:t0 // 128 + nt, :], vst[:, h, 0:nt, :], reads=[vst])
        if bi == 0:
            P.add(POOL, lambda e, v=vst: e.memset(v[0:NPAD, :, 0:1, 64:65], 1.0), [], [vst])
        if bi + 1 < NB:
            s1_front_b(bi + 1)


def stage2(P, nc, C, sb, ps, bounce=None):
    for kc in range(8):
        P.dma(C.Wo[0][:, kc, :], C.fox_w_out[kc * 128:(kc + 1) * 128, :], writes=[C.Wo[0]], q=POOL)
        P.dma(C.Wo[1][:, kc, :], C.hgrn_w_out[kc * 128:(kc + 1) * 128, :], writes=[C.Wo[1]], q=POOL)
    gate_t = sb("s2_gate", [128, 1], F32)
    gate = [Res()]
    conv = convert_weights(P, nc, C, bounce, gate) if bounce is not None else iter(())
    KTb = [sb("s2_KT%d" % i, [70, LP], BF16) for i in range(2)]
    VAb = [sb("s2_VA%d" % i, [128, NT, 65], BF16) for i in range(2)]
    Qbs = [sb("s2_Q%d" % i, [70, 512], BF16) for i in range(3)]
    sgs = [sb("s2_sg%d" % i, [64, 512], BF16) for i in range(3)]
    biases = [sb("s2_bias%d" % i, [128, NT], F32) for i in range(3)]
    Pts = [sb("s2_Pt%d" % i, [128, 512], BF16) for i in range(6)]
    Sb = [ps("s2_S%d" % i, [128, 512], F32) for i in range(5)]
    Ob = [ps("s2_O%d" % i, [128, 512], F32) for i in range(2)]
    Rps = ps("s2_R", [128, 512], F32)
    rds = [sb("s2_rd%d" % i, [128, 512], F32) for i in range(2)]
    rd2s = [sb("s2_rd2%d" % i, [128, 512], F32) for i in range(2)]
    pending = []
    onesf = sb("s2_onesf", [128, 64], F32)
    P.add(DVE, lambda e: e.memset(onesf[:], 1.0), [], [onesf])
    zt2 = sb("s2_zt", [64, 128], BF16)
    P.add(DVE, lambda e: e.memset(zt2[:], 0.0), [], [zt2])
    for h in range(C.nheads):
        P.dma(C.OGi[h][:, LP:LP + 128], zt2[:], reads=[zt2], writes=[C.OGi_res[h]])
    Osbs = [sb("s2_Osb%d" % i, [64, 512], F32) for i in range(2)]
    og1s = [sb("s2_og1%d" % i, [64, 512], F32) for i in range(2)]
    ogbs = [sb("s2_ogb%d" % i, [64, 512], BF16) for i in range(2)]

    items = []
    for h in range(C.nheads):
        for bi, (t0, n) in enumerate(BLOCKS):
            nkt = (t0 + n) // 128
            for kt in range(nkt):
                items.append((h, bi, kt, nkt))
    state = {}

    def prologue(h, bi):
        t0, n = BLOCKS[bi]
        g = h * NB + bi
        if g in state or g >= C.nheads * NB:
            return
        state[g] = True
        if g % 3 == 0:
            gate[0] = Res()
            P.add(DVE, lambda e: e.memset(gate_t[:], 0.0), [], [gate[0]])
            next(conv, None)
        if bi == 0:
            KTh, VAh = KTb[h % 2], VAb[h % 2]
            P.dma(KTh[:, :], C.KT_d[h], writes=[KTh])
            P.dma(VAh[:], C.VA_d[h], writes=[VAh])
        Qb, sg, bias = Qbs[g % 3], sgs[g % 3], biases[g % 3]
        P.dma(Qb[0:70, 0:n], C.QT_d[h, :, t0:t0 + n], writes=[Qb])
        P.dma(sg[0:64, 0:n], C.SG_d[64 * h:64 * h + 64, t0:t0 + n], writes=[sg])

    def qk(i):
        h, bi, kt, nkt = items[i]
        t0, n = BLOCKS[bi]
        g = h * NB + bi
        if kt == 0:
            prologue(h, bi)
        j = kt - t0 // 128
        c0 = 128 * j if j >= 0 else 0
        S = Sb[i % 5]
        _mm(P, S[:, c0:n], [(KTb[h % 2][0:70, kt * 128:(kt + 1) * 128], Qbs[g % 3][0:70, c0:n])],
            [KTb[h % 2], Qbs[g % 3]], [S])

    def rest(i):
        h, bi, kt, nkt = items[i]
        t0, n = BLOCKS[bi]
        g = h * NB + bi
        j = kt - t0 // 128
        c0 = 128 * j if j >= 0 else 0
        S, Pt, O = Sb[i % 5], Pts[i % 6], Ob[g % 2]
        bias = biases[g % 3]
        if kt == 0:
            while pending and pending[0][0] <= g - 2:
                epilogue2(pending.pop(0)[0])
            prologue((g + 1) // NB, (g + 1) % NB)
        _act(P, Pt[:, c0:n], S[:, c0:n], AF.Exp, [S], [Pt])
        if j >= 0:
            _tt(P, DVE, Pt[:, c0:c0 + 128], Pt[:, c0:c0 + 128], C.tri[:], ALU.mult, [Pt, C.tri], [Pt])
        _mm(P, O[0:65, c0:n], [(VAb[h % 2][:, kt, 0:65], Pt[:, c0:n])], [VAb[h % 2], Pt], [O],
            start=(kt == 0), stop=(kt == nkt - 1))
        if kt == nkt - 1:
            Osb = Osbs[g % 2]
            rd, rd2 = rds[g % 2], rd2s[g % 2]
            _ts(P, DVE, rd[64:65, 0:n], O[64:65, 0:n], 1e-30, None, ALU.max, None, [O], [rd])
            P.add(DVE, lambda e: e.reciprocal(out=rd2[64:65, 0:n], in_=rd[64:65, 0:n]), [rd], [rd2])
            _copy(P, ACT, Osb[:, 0:n], O[0:64, 0:n], [O], [Osb])
            pending.append((g, i))
        while pending and (i - pending[0][1] >= 5 or i == len(items) - 1):
            epilogue2(pending.pop(0)[0])

    def epilogue2(g):
        h, bi = g // NB, g % NB
        t0, n = BLOCKS[bi]
        Osb, og1, ogb, sg = Osbs[g % 2], og1s[g % 2], ogbs[g % 2], sgs[g % 3]
        rd2 = rd2s[g % 2]
        _mm(P, Rps[0:64, 0:n], [(onesf[64:65, 0:64], rd2[64:65, 0:n])], [onesf, rd2], [Rps])
        _tt(P, DVE, og1[:, 0:n], Osb[:, 0:n], Rps[0:64, 0:n], ALU.mult, [Osb, Rps], [og1])
        _tt(P, DVE, ogb[:, 0:n], og1[:, 0:n], sg[0:64, 0:n], ALU.mult, [og1, sg], [ogb])
        P.dma(C.OGi[h][:, t0:t0 + n], ogb[:, 0:n], reads=[ogb], writes=[C.OGi_res[h]])
        if bi == NB - 1:
            P.cc(lambda e, h=h: e.collective_compute("AllGather", ALU.bypass, replica_groups=C.RG,
                                                     ins=[C.OGi_t[h].ap().opt()], outs=[C.OGo_t[h].ap().opt()]),
                 reads=[C.OGi_res[h]], writes=[C.OGo_res[h]])

    LA = 3
    N = len(items)
    for i in range(N + LA):
        if i < N:
            qk(i)
        if i - LA >= 0:
            rest(i - LA)
    for _ in conv:
        pass


def build(stages=3, debug=False, nheads=H // 2, conv=None, s3parts=99, nblk=NB):
    nc = bass.Bass("TRN2", target_bir_lowering=False)
    C = Ctx()
    C.nheads = nheads
    C.s3parts = s3parts
    C.nblk = nblk
    if conv is None:
        conv = stages >= 3

    def din(name, shape, dt=F32):
        return nc.dram_tensor(name, list(shape), dt, kind="ExternalInput").ap()

    def dscr(name, shape, dt, out=False):
        return nc.dram_tensor(name, list(shape), dt, kind="ExternalOutput" if out else "Internal").ap()

    C.h0 = din("h0", [LP, D])
    C.attn_norm = din("attn_norm", [2, D])
    C.ffn_norm = din("ffn_norm", [2, D])
    C.final_norm = din("final_norm", [D])
    NHL = nheads
    C.fox_w_in = din("fox_w_in", [D, 4 * NHL * DH + NHL])
    C.fox_b_f = din("fox_b_f", [1, NHL])
    C.fox_q_norm = din("fox_q_norm", [1, DH])
    C.fox_k_norm = din("fox_k_norm", [1, DH])
    C.fox_w_out = din("fox_w_out", [D, D])
    C.hgrn_w_in = din("hgrn_w_in", [D, 4 * NHG * 128])
    C.lb = din("hgrn_lower_bounds", [2, NHG * 128])
    C.h0h = din("h0h", [HT * 128, D])
    C.rmask = din("rmask", [128, 8 * 512], mybir.dt.uint16)
    C.rmaskf = din("rmaskf", [128, 1])
    C.g_norm = din("hgrn_g_norm", [1, 128])
    C.hgrn_w_out = din("hgrn_w_out", [D, D])
    C.ffn_w_in = din("ffn_w_in", [2, D, 2 * FF])
    C.ffn_w_out = din("ffn_w_out", [2, FF, D])
    c_identb = din("c_identb", [128, 128], BF16)
    c_identf = din("c_identf", [128, 128], F32)
    c_bdones = din("c_bdones", [128, 128], BF16)
    c_tri = din("c_tri", [128, 128], BF16)
    d1 = debug and stages == 1
    C.QT_d = dscr("QT_d", [NHL, 70, LP], BF16, d1)
    C.KT_d = dscr("KT_d", [NHL, 70, LP], BF16, d1)
    C.VA_d = dscr("VA_d", [NHL, 128, NT, 65], BF16, d1)
    C.SG_d = dscr("SG_d", [NHL * DH, LP], BF16, d1)
    C.RG = [[0, 1], [2, 3], [4, 5], [6, 7]]
    C.OGi_t = [nc.dram_tensor("OGi%d" % h, [DH, LP + 128], BF16) for h in range(NHL)]
    C.OGo_t = [nc.dram_tensor("OGo%d" % h, [2 * DH, LP + 128], BF16) for h in range(NHL)]
    C.HNi_t = [nc.dram_tensor("HNi%d" % i, [128, 8, n], BF16) for i, (t0, n) in enumerate(LBLK)]
    C.HNo_t = [nc.dram_tensor("HNo%d" % i, [256, 8, n], BF16) for i, (t0, n) in enumerate(LBLK)]
    C.HNi = [t.ap() for t in C.HNi_t]
    C.HNo = [t.ap().rearrange("(r p) k n -> r p k n", r=2) for t in C.HNo_t]
    C.HNi_res = [Res() for _ in LBLK]
    C.HNo_res = [Res() for _ in LBLK]
    C.OG1i_t = [nc.dram_tensor("OG1i%d" % g, [128, NHG, LBLK[g % NLB][1]], BF16) for g in range(2 * NLB)]
    C.OG1o_t = [nc.dram_tensor("OG1o%d" % g, [256, NHG, LBLK[g % NLB][1]], BF16) for g in range(2 * NLB)]
    C.OG1i = [t.ap() for t in C.OG1i_t]
    C.OG1o = [t.ap().rearrange("(r p) k n -> r p k n", r=2) for t in C.OG1o_t]
    C.OG1i_res = [Res() for _ in range(2 * NLB)]
    C.OG1o_res = [Res() for _ in range(2 * NLB)]
    C.H1 = dscr("H1", [HT * 128, D], F32)
    C.H1_res = [Res() for _ in LBLK]
    C.OGi = [t.ap() for t in C.OGi_t]
    C.OGo = [t.ap() for t in C.OGo_t]
    C.OGi_res = [Res() for _ in range(NHL)]
    C.OGo_res = [Res() for _ in range(NHL)]
    C.W1S = dscr("W1S", [2, NJ, 128, 8, 256], BF16)
    C.W2S = dscr("W2S", [2, NJ, 128, D], BF16)
    C.WHS = dscr("WHS", [NHG, 128, 8, 384], BF16)
    C.WHV = dscr("WHV", [1, 128, 8, 512], BF16)
    C.out = nc.dram_tensor("out", [HT * 128, D], F32, kind="ExternalOutput").ap()
    C.dbg_x1 = nc.dram_tensor("dbg_x1", [nblk, 128, 8, 512], BF16, kind="ExternalOutput").ap() if (debug and stages == 3) else None

    with contextlib.ExitStack() as st0:
        P = Prog(nc)

        def mk(st):
            def sb(name, shape, dt):
                return Buf(st.enter_context(nc.sbuf_tensor(name, list(shape), dt)), name)

            def ps(name, shape, dt):
                return Buf(st.enter_context(nc.psum_tensor(name, list(shape), dt)), name)
            return sb, ps
        sb0, ps0 = mk(st0)
        C.identb = sb0("identb", [128, 128], BF16)
        C.identf = sb0("identf", [128, 128], F32)
        C.bdones = sb0("bdones", [128, 128], BF16)
        C.tri = sb0("tri", [128, 128], BF16)
        P.dma(C.identb[:], c_identb, writes=[C.identb])
        P.dma(C.identf[:], c_identf, writes=[C.identf])
        P.dma(C.bdones[:], c_bdones, writes=[C.bdones])
        P.dma(C.tri[:], c_tri, writes=[C.tri])
        C.Wo = [sb0("Wo0", [128, 8, D], BF16), sb0("Wo1", [128, 8, D], BF16)]
        if debug:
            cpc_o = nc.dram_tensor("CPC_o", [128, NT, NHL], F32, kind="ExternalOutput").ap()
            refb_o = nc.dram_tensor("REFB_o", [128, NB, NHL], F32, kind="ExternalOutput").ap()
        with contextlib.ExitStack() as stm:
            sbm, psm = mk(stm)
            C.CPC = sbm("CPC", [128, NT, NHL], F32)
            C.REFB = sbm("REFB", [128, NB, NHL], F32)
            bounce = [sbm("bounce%d" % i, [128, 2 * FF], BF16) for i in range(2)]
            with contextlib.ExitStack() as st:
                sb, ps = mk(st)
                stage1(P, nc, C, sb, ps, bounce if conv else None)
                if debug:
                    P.dma(cpc_o, C.CPC[:], reads=[C.CPC])
                    P.dma(refb_o, C.REFB[:], reads=[C.REFB])
                P.barrier()
            if stages >= 2:
                with contextlib.ExitStack() as st:
                    sb, ps = mk(st)
                    stage2(P, nc, C, sb, ps, bounce if conv else None)
                    P.barrier()
        if stages >= 3:
            with contextlib.ExitStack() as st:
                sb, ps = mk(st)
                stage3(P, nc, C, sb, ps)
                P.barrier()
        P.emit()
    return nc


def host_consts():
    bf = ml_dtypes.bfloat16
    idx = np.arange(128)
    return {
        "c_identb": np.eye(128, dtype=np.float32).astype(bf),
        "c_identf": np.eye(128, dtype=np.float32),
        "c_bdones": (idx[:, None] // 64 == idx[None, :] // 64).astype(np.float32).astype(bf),
        "c_tri": (idx[:, None] <= idx[None, :]).astype(np.float32).astype(bf),
    }


WNAMES = ["attn_norm", "ffn_norm", "final_norm", "fox_w_in", "fox_b_f", "fox_q_norm", "fox_k_norm", "fox_w_out",
          "hgrn_w_in", "hgrn_lower_bounds", "hgrn_g_norm", "hgrn_w_out", "ffn_w_in", "ffn_w_out"]


def make_in_maps(inputs, ncores=8):
    x = np.asarray(inputs["x"], dtype=np.float32)
    meta = np.asarray(inputs["meta_tokens"], dtype=np.float32)
    consts = host_consts()
    maps = []
    NHL = H // 2
    for c in range(ncores):
        b, r = c // 2, c % 2
        h0 = np.zeros((LP, D), np.float32)
        h0[NPAD:NPAD + NMETA] = meta
        h0[NPAD + NMETA:] = x[b]
        h0h = np.zeros((HT * 128, D), np.float32)
        seg = h0[r * HT * 128:(r + 1) * HT * 128]
        h0h[:seg.shape[0]] = seg
        m = {"h0": h0, "h0h": h0h, "rmask": np.full((128, 8 * 512), 0xFFFF if r else 0, np.uint16),
             "rmaskf": np.full((128, 1), float(r), np.float32)}
        for k in WNAMES:
            a = np.asarray(inputs[k], dtype=np.float32)
            if k in ("fox_w_in", "fox_w_out", "hgrn_w_in", "hgrn_w_out"):
                a = a.reshape(a.shape[-2], a.shape[-1])
            if k == "fox_w_in":
                w = NHL * DH
                a = np.concatenate([a[:, s0 + r * w:s0 + (r + 1) * w] for s0 in (0, D, 2 * D, 3 * D)]
                                   + [a[:, 4 * D + r * NHL:4 * D + (r + 1) * NHL]], axis=1)
            if k == "fox_b_f":
                a = a[:, r * NHL:(r + 1) * NHL]
            if k == "hgrn_w_in":
                w = NHG * 128
                a = np.concatenate([a[:, s0 + r * w:s0 + (r + 1) * w] for s0 in (0, D, 2 * D, 3 * D)], axis=1)
            if k == "hgrn_lower_bounds":
                a = a[:, r * NHG * 128:(r + 1) * NHG * 128]
            m[k] = np.ascontiguousarray(a)
        m.update(consts)
        maps.append(m)
    return maps


def kernel(**inputs):
    nc = build(stages=3)
    maps = make_in_maps(inputs, 8)
    res = run_bass_kernel_spmd(nc, maps, core_ids=list(range(8)))
    outs = []
    for b in range(4):
        o0 = np.asarray(res.results[2 * b]["out"], dtype=np.float32)
        o1 = np.asarray(res.results[2 * b + 1]["out"], dtype=np.float32)
        outs.append(np.concatenate([o0[NPAD + NMETA:], o1[:SEQ - (HT * 128 - NPAD - NMETA)]], axis=0))
    return np.stack(outs, axis=0)


class Stream:
    def __init__(self, P, bufs, reqs, q=SP, keep=0):
        self.P, self.bufs, self.reqs, self.q, self.keep = P, bufs, reqs, q, keep
        self.issued = 0

    def get(self, i):
        lim = min(len(self.reqs), i + len(self.bufs) - self.keep)
        while self.issued < lim:
            k = self.issued
            b = self.bufs[k % len(self.bufs)]
            dst, src = self.reqs[k]
            self.P.dma(dst(b), src, writes=[b], q=self.q)
            self.issued += 1
        return self.bufs[i % len(self.bufs)]


def convert_weights(P, nc, C, bounce, gate):
    k = [0]

    def ld(src, width):
        b = bounce[k[0] % 2]
        k[0] += 1
        P.dma(b[:, 0:width], src, reads=[gate[0]], writes=[b], q=POOL)
        return b
    for l in range(2):
        for kc in range(8):
            b = ld(C.ffn_w_in[l, kc * 128:(kc + 1) * 128, :], 2 * FF)
            for hf in range(2):
                P.dma(C.W1S[l, :, :, kc, hf * 128:(hf + 1) * 128].rearrange("j p c -> p j c"),
                      b[:, hf * FF:(hf + 1) * FF].rearrange("p (j c) -> p j c", c=128), reads=[b], q=POOL)
            yield
        wo = C.ffn_w_out[l].rearrange("(j p) c -> p j c", p=128)
        for j0 in range(0, NJ, 5):
            j1 = min(NJ, j0 + 5)
            b = ld(wo[:, j0:j1, :], (j1 - j0) * D)
            P.dma(C.W2S[l, j0:j1].rearrange("j p c -> p j c"),
                  b[:, 0:(j1 - j0) * D].rearrange("p (j c) -> p j c", c=D), reads=[b], q=POOL)
            yield
    GW = NHG * 128
    for kc in range(8):
        b = ld(C.hgrn_w_in[kc * 128:(kc + 1) * 128, :], 4 * GW)
        for gi, base in enumerate((0, GW, 3 * GW)):
            P.dma(C.WHS[:, :, kc, gi * 128:(gi + 1) * 128].rearrange("h p c -> p h c"),
                  b[:, base:base + GW].rearrange("p (h c) -> p h c", c=128), reads=[b], q=POOL)
        P.dma(C.WHV[0, :, kc, :], b[:, 2 * GW:3 * GW], reads=[b], q=POOL)
        yield


NHG = 4
HT = 33
LBLK = [(0, 128)] + [(128 + 512 * i, 512) for i in range(8)]
NLB = len(LBLK)


def stage3(P, nc, C, sb, ps):
    B = [ps("s3_B%d" % i, [128, 512], F32) for i in range(6)]
    Tt = [ps("s3_T%d" % i, [128, 1024], BF16) for i in range(2)]
    for i in range(2):
        v = Buf(Tt[i].t[:, :].bitcast(F32), "B%df" % (6 + i))
        v.res = Tt[i].res
        B.append(v)

    def bres(i):
        return [B[i].res]

    prow = sb("s3_prow", [32, 128], F32)
    P.dma(prow[0:8, :], C.lb.rearrange("r (h p) -> (r h) p", p=128), writes=[prow])
    P.dma(prow[8:16, :], C.attn_norm[1].rearrange("(c p) -> c p", p=128), writes=[prow])
    P.dma(prow[16:24, :], C.ffn_norm[0].rearrange("(c p) -> c p", p=128), writes=[prow])
    P.dma(prow[24:32, :], C.ffn_norm[1].rearrange("(c p) -> c p", p=128), writes=[prow])
    pcol = sb("s3_pcol", [128, 32], F32)
    P.add(PE, lambda e: e.transpose(out=B[5][:, 0:32], in_=prow[0:32, :], identity=C.identf[0:32, 0:32]),
          [prow, C.identf], bres(5))
    _copy(P, DVE, pcol[:], B[5][:, 0:32], bres(5), [pcol])
    omlb = sb("s3_omlb", [128, NHG], F32)
    _tt(P, DVE, omlb[:], pcol[:, 4:8], pcol[:, 0:4], ALU.subtract, [pcol], [omlb])
    _act(P, omlb[:], omlb[:], AF.Exp, [omlb], [omlb])
    _ts(P, DVE, omlb[:], omlb[:], 1.0, None, ALU.add, None, [omlb], [omlb])
    P.add(DVE, lambda e: e.reciprocal(out=omlb[:], in_=omlb[:]), [omlb], [omlb])
    lnomlb = sb("s3_lnomlb", [128, NHG], F32)
    _act(P, lnomlb[:], omlb[:], AF.Ln, [omlb], [lnomlb])
    gcols = {"attn1": pcol[:, 8:16], "ffn0": pcol[:, 16:24], "ffn1": pcol[:, 24:32]}
    gn = sb("s3_gn", [128, 1], F32)
    P.dma(gn[:], C.g_norm[0].rearrange("(p o) -> p o", o=1), writes=[gn])
    gfin = sb("s3_gfin", [128, D], F32)
    P.dma(gfin[:], C.final_norm.partition_broadcast(128), writes=[gfin])
    mh = sb("s3_mh", [128, 1], F32)
    P.add(POOL, lambda e: e.memset(mh[:], -0.5), [], [mh])
    ones128b = sb("s3_ones128b", [128, 128], BF16)
    P.add(DVE, lambda e: e.memset(ones128b[:], 1.0), [], [ones128b])
    onesf = sb("s3_onesf", [128, 128], F32)
    P.add(DVE, lambda e: e.memset(onesf[:], 1.0), [], [onesf])
    rmk = sb("s3_rmk", [128, 512], mybir.dt.uint16)
    P.dma(rmk[:], C.rmask[:, 0:512], writes=[rmk])
    rmf = sb("s3_rmf", [128, 1], F32)
    P.dma(rmf[:], C.rmaskf, writes=[rmf])
    junk = sb("s3_junk", [128, D], BF16)
    nrm = [[sb("s3_nrm%d_%d" % (i, j), [128, 1], F32) for j in range(3)] for i in range(4)]
    cnt = {"y": 0, "f": 0, "T": 0, "n": 0, "h": 0}

    def ybank():
        cnt["y"] += 1
        return B[cnt["y"] % 4]

    def fbank():
        cnt["f"] += 1
        return B[cnt["f"] % 6]

    def hbank():
        cnt["h"] += 1
        return B[cnt["h"] % 3]

    def norm_rows(ht):
        ssq, tt, rstd = nrm[cnt["n"] % 4]
        cnt["n"] += 1
        _act(P, junk[:], ht[:], AF.Square, [ht], [junk, ssq], accum_out=ssq[:])
        _ts(P, DVE, tt[:], ssq[:], 1.0 / D, EPS, ALU.mult, ALU.add, [ssq], [tt])
        _tt(P, POOL, rstd[:], tt[:], mh[:], ALU.pow, [tt, mh], [rstd])
        return rstd

    def norm_A(hs, hns, nt):
        for tl in range(nt):
            rstd = norm_rows(hs[tl])
            _ts(P, DVE, hns[tl][:], hs[tl][:], rstd[:, 0:1], None, ALU.mult, None, [hs[tl], rstd], [hns[tl]])

    def norm_B(hns, hnT, nt, gcol):
        n = nt * 128
        for c in range(8):
            half = cnt["T"] % 2
            cnt["T"] += 1
            pT = Tt[half][:, 0:512]
            res = Tt[half].res

            def fn(e, c=c, pT=pT):
                ins = None
                for tl in range(nt):
                    ins = e.transpose(out=pT[:, tl * 128:(tl + 1) * 128], in_=hns[tl][:, c * 128:(c + 1) * 128],
                                      identity=C.identb[:])
                return ins
            P.add(PE, fn, list(hns[:nt]) + [C.identb], [res])
            if c % 2 == 0:
                _ts(P, DVE, hnT[:, c, 0:n], pT[:, 0:n], gcol[:, c:c + 1], None, ALU.mult, None, [res, pcol], [hnT.sub(c)])
            else:
                P.add(ACT, lambda e, c=c, pT=pT: e.mul(out=hnT[:, c, 0:n], in_=pT[:, 0:n], mul=gcol[:, c:c + 1]),
                      [res, pcol], [hnT.sub(c)])

    def ag(in_t, out_t, rin, rout):
        P.cc(lambda e: e.collective_compute("AllGather", ALU.bypass, replica_groups=C.RG,
                                            ins=[in_t.ap().opt()], outs=[out_t.ap().opt()]),
             reads=[rin], writes=[rout])

    def scope():
        st = contextlib.ExitStack()
        return st, (lambda name, shape, dt: Buf(st.enter_context(nc.sbuf_tensor(name, list(shape), dt)), name))

    def ffn_phase(l):
        st, sbp = scope()
        with st:
            pre = "p%d_" % (1 + 2 * l)
            Wo = C.Wo[l]
            hs2 = [[sbp(pre + "h%d_%d" % (j, i), [128, D], F32) for i in range(4)] for j in range(2)]
            X0s = [sbp(pre + "X0_%d" % j, [128, 8, 512], BF16) for j in range(2)]
            Xcs = [sbp(pre + "Xc%d" % i, [128, 512], BF16) for i in range(2)]
            hns = [sbp(pre + "hn%d" % i, [128, D], BF16) for i in range(4)]
            hnT = sbp(pre + "hnT", [128, 8, 512], BF16)
            hnT1 = sbp(pre + "hnT1", [128, 8, 512], BF16) if l == 0 else None
            actT = sbp(pre + "actT", [128, NJ, 512], BF16)
            sils = [sbp(pre + "sil%d" % i, [128, 512], F32) for i in range(2)]
            W1b = [sbp(pre + "W1b%d" % i, [128, 2, 8, 256], BF16) for i in range(3)]
            W2b = [sbp(pre + "W2b%d" % i, [128, 2, D], BF16) for i in range(3)]
            orow = [sbp(pre + "orow%d" % i, [128, D], F32) for i in range(2)] if l == 1 else None
            NJP = NJ // 2
            S1 = Stream(P, W1b, [(lambda b: b[:], C.W1S[l, 2 * jp:2 * jp + 2].rearrange("j p k c -> p j k c"))
                                 for lb in range(NLB) for jp in range(NJP)])
            S2 = Stream(P, W2b, [(lambda b: b[:], C.W2S[l, 2 * jp:2 * jp + 2].rearrange("j p c -> p j c"))
                                 for lb in range(NLB) for jp in range(NJP)])
            S1.get(0)
            S2.get(0)
            hnTr = [hnT.sub(c) for c in range(8)]
            blocks = LBLK[:C.nblk]
            oc = [0]

            def stA(i):
                t0, n = blocks[i]
                nt = n // 128
                hs, X0 = hs2[i % 2], X0s[i % 2]
                for tl in range(nt):
                    if l == 0:
                        P.dma(hs[tl][:], C.h0h[t0 + tl * 128:t0 + (tl + 1) * 128, :], writes=[hs[tl]])
                    else:
                        P.dma(hs[tl][:], C.H1[t0 + tl * 128:t0 + (tl + 1) * 128, :], reads=[C.H1_res[i]], writes=[hs[tl]])
                yield
                for kc in range(8):
                    Xc = Xcs[kc % 2]
                    if l == 0:
                        rk, hl = kc // 4, 2 * (kc % 4)
                        for hh in range(2):
                            src = C.OGo[hl + hh][64 * rk:64 * rk + 64, :]
                            P.dma(X0[64 * hh:64 * hh + 64, kc, 0:n], src[:, t0:t0 + n], reads=[C.OGo_res[hl + hh]],
                                  writes=[X0.sub(kc)])
                            P.dma(Xc[64 * hh:64 * hh + 64, 0:n], src[:, HT * 128 + t0:HT * 128 + t0 + n],
                                  reads=[C.OGo_res[hl + hh]], writes=[Xc])
                    else:
                        rk, hd = kc // 4, kc % 4
                        P.dma(X0[:, kc, 0:n], C.OG1o[i][rk, :, hd, :], reads=[C.OG1o_res[i]], writes=[X0.sub(kc)])
                        P.dma(Xc[:, 0:n], C.OG1o[NLB + i][rk, :, hd, :], reads=[C.OG1o_res[NLB + i]], writes=[Xc])
                    P.add(DVE, lambda e, n=n, kc=kc, Xc=Xc, X0=X0: e.copy_predicated(
                        out=X0[:, kc, 0:n], mask=rmk[:, 0:n], data=Xc[:, 0:n]), [Xc, rmk, X0.sub(kc)], [X0.sub(kc)])
                    yield

            def stW(i):
                t0, n = blocks[i]
                hs, X = hs2[i % 2], X0s[i % 2]
                for tl in range(n // 128):
                    for hf in range(2):
                        y = ybank()
                        _mm(P, y[:, :], [(X[:, kc, tl * 128:(tl + 1) * 128], Wo[:, kc, hf * 512:(hf + 1) * 512])
                                         for kc in range(8)], [X.sub(kc) for kc in range(8)] + [Wo], [y])
                        _tt(P, DVE, hs[tl][:, hf * 512:(hf + 1) * 512], hs[tl][:, hf * 512:(hf + 1) * 512], y[:, :], ALU.add,
                            [hs[tl], y], [hs[tl]])

            hns1 = [sbp(pre + "hn1_%d" % i, [128, D], BF16) for i in range(4)] if l == 0 else None

            def stNa(i):
                t0, n = blocks[i]
                norm_A(hs2[i % 2], hns, n // 128)

            def stNb(i):
                t0, n = blocks[i]
                norm_B(hns, hnT, n // 128, gcols["ffn%d" % l])

            def stCin(i, gA=None):
                t0, n = blocks[i]
                for j in range(NJ):
                    if gA is not None and j % 2 == 0:
                        next(gA, None)
                    w = S1.get(i * NJP + j // 2)
                    jj = j % 2
                    g, u = fbank(), fbank()
                    _mm(P, g[:, 0:n], [(w[:, jj, kc, 0:128], hnT[:, kc, 0:n]) for kc in range(8)], [w] + hnTr, [g])
                    _mm(P, u[:, 0:n], [(w[:, jj, kc, 128:256], hnT[:, kc, 0:n]) for kc in range(8)], [w] + hnTr, [u])
                    sl = sils[j % 2]
                    _act(P, sl[:, 0:n], g[:, 0:n], AF.Silu, [g], [sl])
                    _tt(P, DVE, actT[:, j, 0:n], sl[:, 0:n], u[:, 0:n], ALU.mult, [sl, u], [actT.sub(j)])

            def stCout(i):
                t0, n = blocks[i]
                nt = n // 128
                hs = hs2[i % 2]
                for j in range(NJ):
                    w = S2.get(i * NJP + j // 2)

                    def fn(e, j=j, w=w):
                        ins = None
                        for tl in range(nt):
                            for hf in range(2):
                                ins = e.matmul(B[tl * 2 + hf][:, :], lhsT=actT[:, j, tl * 128:(tl + 1) * 128],
                                               rhs=w[:, j % 2, hf * 512:(hf + 1) * 512], start=(j == 0), stop=(j == NJ - 1))
                        return ins
                    P.add(PE, fn, [w, actT.sub(j)], sum([bres(k) for k in range(2 * nt)], []))
                for k in sorted(range(2 * nt), key=lambda k: -k):
                    tl, hf = k // 2, k % 2
                    _tt(P, DVE, hs[tl][:, hf * 512:(hf + 1) * 512], hs[tl][:, hf * 512:(hf + 1) * 512],
                        B[k][:, :], ALU.add, [hs[tl]] + bres(k), [hs[tl]])

            def stDa(i):
                t0, n = blocks[i]
                nt = n // 128
                hs = hs2[i % 2]
                if l == 0:
                    for tl in range(nt):
                        P.dma(C.H1[t0 + tl * 128:t0 + (tl + 1) * 128, :], hs[tl][:], reads=[hs[tl]], writes=[C.H1_res[i]])
                    norm_A(hs, hns1, nt)
                else:
                    for tl in range(nt):
                        rstd = norm_rows(hs[tl])
                        ob = orow[oc[0] % 2]
                        oc[0] += 1
                        _stt(P, ob[:], hs[tl][:], rstd[:, 0:1], gfin[:], ALU.mult, ALU.mult, [hs[tl], rstd, gfin], [ob])
                        P.dma(C.out[t0 + tl * 128:t0 + (tl + 1) * 128, :], ob[:], reads=[ob])

            def stDb(i):
                if l != 0:
                    return
                t0, n = blocks[i]
                nt = n // 128
                norm_B(hns1, hnT1, nt, gcols["attn1"])
                h1r = [hnT1.sub(c) for c in range(8)]
                if i == 0:
                    _ts(P, DVE, hnT1[:, :, 0:NPAD], hnT1[:, :, 0:NPAD], rmf[:, 0:1], None, ALU.mult, None, h1r + [rmf], h1r)
                P.dma(C.HNi[i], hnT1[:, :, 0:n], reads=h1r, writes=[C.HNi_res[i]])
                ag(C.HNi_t[i], C.HNo_t[i], C.HNi_res[i], C.HNo_res[i])

            nb = len(blocks)
            for _ in stA(0):
                pass
            stW(0)
            stNa(0)
            stNb(0)
            for i in range(nb):
                gA = stA(i + 1) if i + 1 < nb else None
                stCin(i, gA)
                if gA is not None:
                    for _ in gA:
                        pass
                if i > 0:
                    stDb(i - 1)
                if i + 1 < nb:
                    stW(i + 1)
                    stNa(i + 1)
                stCout(i)
                if i + 1 < nb:
                    stNb(i + 1)
                stDa(i)
            stDb(nb - 1)
            P.barrier()

    def hgrn_phase():
        st, sbp = scope()
        with st:
            hnTs = [sbp("p2_hnT%d" % i, [128, 8, 512], BF16) for i in range(2)]
            X1s = [sbp("p2_X1_%d" % i, [128, NHG, 512], BF16) for i in range(2)]
            WHb = [sbp("p2_WHb%d" % i, [128, 8, 512], BF16) for i in range(4)]
            silq = sbp("p2_silq", [128, NHG, 512], BF16)
            sgT = sbp("p2_sgT", [128, NHG, 512], BF16)
            Vsb2 = [[sbp("p2_V%d_%d" % (j, i), [128, NHG * 128], BF16) for i in range(4)] for j in range(2)]
            NBUF = 4
            mkb = lambda nm, dt: [sbp("p2_%s%d" % (nm, i), [128, 512], dt) for i in range(NBUF)]
            qts, kts, khs = mkb("qt", BF16), mkb("kt", BF16), mkb("kh", BF16)
            khTs = [sbp("p2_khT%d" % i, [128, 4, 128], BF16) for i in range(NBUF)]
            ezs, kks, ebs = mkb("ez", F32), mkb("kk", F32), mkb("eb", F32)
            mk2 = lambda nm, dt: [sbp("p2_%s%d" % (nm, i), [128, 512], dt) for i in range(2)]
            bbs, enbs, lnfs, ebcs = mk2("bb", F32), mk2("enb", F32), mk2("lnf", F32), mk2("ebc", F32)
            Asb4s = [sbp("p2_Asb4%d" % i, [128, 4, 128], BF16) for i in range(NBUF)]
            Usbs = mkb("Usb", F32)
            Sst = sbp("p2_S", [128, NHG, 128], F32)
            P.add(DVE, lambda e: e.memset(Sst[:], 0.0), [], [Sst])
            osqs = [sbp("p2_osq%d" % i, [128, 512], BF16) for i in range(2)]
            ons = [sbp("p2_on%d" % i, [128, 512], F32) for i in range(2)]
            rh = []
            for gb in range(2 * NLB):
                rh.append((lambda b: b[:], C.WHV[0]))
                for hd in range(NHG):
                    rh.append((lambda b: b[:, :, 0:384], C.WHS[hd]))
            SH = Stream(P, WHb, rh, keep=1)
            items = [(gb, hd) for gb in range(2 * NLB) for hd in range(NHG)]

            def geo(gb):
                rk, lb = gb // NLB, gb % NLB
                t0, n = LBLK[lb]
                return rk, lb, n, n // 128

            def h_front(i):
                gb, hd = items[i]
                rk, lb, n, nt = geo(gb)
                hnT = hnTs[gb % 2]
                hr = [hnT.sub(c) for c in range(8)]
                Vs = Vsb2[gb % 2]
                if hd == 0:
                    P.dma(hnT[:, :, 0:n], C.HNo[lb][rk], reads=[C.HNo_res[lb]], writes=hr, q=POOL)
                    w = SH.get(gb * (NHG + 1))
                    for tl in range(nt):
                        y = ybank()
                        _mm(P, y[:, :], [(hnT[:, kc, tl * 128:(tl + 1) * 128], w[:, kc, 0:512]) for kc in range(8)], [w] + hr, [y])
                        _copy(P, ACT if tl % 2 == 0 else DVE, Vs[tl][:, :], y[:, :], [y], [Vs[tl]])
                w = SH.get(gb * (NHG + 1) + 1 + hd)
                k3 = i % NBUF
                ez, kk, eb = ezs[k3], kks[k3], ebs[k3]
                lnf, bb, enb, ebc = [x[i % 2] for x in (lnfs, bbs, enbs, ebcs)]
                qt, kt, kh, khT, Asb4, Usb = qts[k3], kts[k3], khs[k3], khTs[k3], Asb4s[k3], Usbs[k3]
                qp, gp = hbank(), hbank()
                _mm(P, qp[:, 0:n], [(w[:, kc, 0:128], hnT[:, kc, 0:n]) for kc in range(8)], [w] + hr, [qp])
                _mm(P, gp[:, 0:n], [(w[:, kc, 256:384], hnT[:, kc, 0:n]) for kc in range(8)], [w] + hr, [gp])
                _act(P, silq[:, hd, 0:n], qp[:, 0:n], AF.Silu, [qp], [silq.sub(hd)])
                yield
                _act(P, sgT[:, hd, 0:n], gp[:, 0:n], AF.Silu, [gp], [sgT.sub(hd)])
                yield
                zp = hbank()
                _mm(P, zp[:, 0:n], [(w[:, kc, 128:256], hnT[:, kc, 0:n]) for kc in range(8)], [w] + hr, [zp])
                _act(P, ez[:, 0:n], zp[:, 0:n], AF.Exp, [zp], [ez])
                yield
                _act(P, ez[:, 0:n], ez[:, 0:n], AF.Ln, [ez], [ez], bias=1.0)
                yield
                _act(P, kk[:, 0:n], ez[:, 0:n], AF.Exp, [ez, lnomlb], [kk], scale=-1.0, bias=lnomlb[:, hd:hd + 1])
                yield
                _act(P, lnf[:, 0:n], kk[:, 0:n], AF.Ln, [kk], [lnf], scale=-1.0, bias=1.0)
                yield
                for c in range(nt):
                    P.add(DVE, lambda e, c=c: e.tensor_tensor_scan(
                        out=bb[:, c * 128:(c + 1) * 128], data0=onesf[:, 0:128], data1=lnf[:, c * 128:(c + 1) * 128],
                        initial=0.0, op0=ALU.mult, op1=ALU.add), [onesf, lnf], [bb])
                _act(P, eb[:, 0:n], bb[:, 0:n], AF.Exp, [bb], [eb])
                yield
                _act(P, enb[:, 0:n], bb[:, 0:n], AF.Exp, [bb], [enb], scale=-1.0)
                yield
                for c in range(nt):
                    _act(P, ebc[:, c * 128:(c + 1) * 128], bb[:, c * 128:(c + 1) * 128], AF.Exp, [bb], [ebc], scale=-1.0,
                         bias=bb[:, c * 128 + 127:c * 128 + 128])
                _tt(P, DVE, qt[:, 0:n], silq[:, hd, 0:n], eb[:, 0:n], ALU.mult, [silq.sub(hd), eb], [qt])
                yield
                _tt(P, DVE, kt[:, 0:n], kk[:, 0:n], enb[:, 0:n], ALU.mult, [kk, enb], [kt])
                yield
                _tt(P, DVE, kh[:, 0:n], kk[:, 0:n], ebc[:, 0:n], ALU.mult, [kk, ebc], [kh])
                yield
                pT = Tt[0][:, 0:512]
                res = Tt[0].res

                def fnT(e):
                    ins = None
                    for c in range(nt):
                        ins = e.transpose(out=pT[:, c * 128:(c + 1) * 128], in_=kh[:, c * 128:(c + 1) * 128], identity=C.identb[:])
                    return ins
                P.add(PE, fnT, [kh, C.identb], [res])
                _copy(P, ACT, khT[:, 0:nt, :], pT[:, 0:n].rearrange("p (c s) -> p c s", s=128), [res], [khT])
                yield
                Ab, Ub = B[5], B[7]

                def fnA(e):
                    ins = None
                    for c in range(nt):
                        cs = slice(c * 128, (c + 1) * 128)
                        ins = e.matmul(Ab[:, cs], lhsT=kt[:, cs], rhs=qt[:, cs], start=True, stop=True)
                    return ins
                P.add(PE, fnA, [kt, qt], [Ab])
                P.add(DVE, lambda e: e.tensor_tensor(
                    out=Asb4[:, 0:nt, :], in0=Ab[:, 0:n].rearrange("p (c s) -> p c s", s=128),
                    in1=C.tri[:, :].unsqueeze(1).to_broadcast([128, nt, 128]), op=ALU.mult), [Ab, C.tri], [Asb4])

                def fnU(e):
                    ins = None
                    for c in range(nt):
                        cs = slice(c * 128, (c + 1) * 128)
                        ins = e.matmul(Ub[:, cs], lhsT=khT[:, c, :], rhs=Vs[c][:, hd * 128:(hd + 1) * 128], start=True, stop=True)
                    return ins
                P.add(PE, fnU, [khT] + list(Vs[:nt]), [Ub])
                _copy(P, ACT, Usb[:, 0:n], Ub[:, 0:n], [Ub], [Usb])
                yield

            Sbf4s = [sbp("p2_Sbf4%d" % i, [128, 4, 128], BF16) for i in range(NBUF)]

            def h_state(i):
                gb, hd = items[i]
                rk, lb, n, nt = geo(gb)
                k3 = i % NBUF
                eb, Usb, Sbf4 = ebs[k3], Usbs[k3], Sbf4s[k3]
                for c in range(nt):
                    cs = slice(c * 128, (c + 1) * 128)
                    _copy(P, DVE, Sbf4[:, c, :], Sst[:, hd, :], [Sst.sub(hd)], [Sbf4])
                    _stt(P, Sst[:, hd, :], Sst[:, hd, :], eb[:, c * 128 + 127:c * 128 + 128], Usb[:, cs], ALU.mult, ALU.add,
                         [Sst.sub(hd), eb, Usb], [Sst.sub(hd)])

            def h_back(i):
                gb, hd = items[i]
                rk, lb, n, nt = geo(gb)
                Vs = Vsb2[gb % 2]
                k3 = i % NBUF
                qt, Asb4, Sbf4 = qts[k3], Asb4s[k3], Sbf4s[k3]
                X1 = X1s[gb % 2]
                osq, on = osqs[i % 2], ons[i % 2]
                op = B[3 + hd % 2]

                def fnO(e):
                    ins = None
                    for c in range(nt):
                        cs = slice(c * 128, (c + 1) * 128)
                        e.matmul(op[:, cs], lhsT=Vs[c][:, hd * 128:(hd + 1) * 128], rhs=Asb4[:, c, :], start=True, stop=False)
                        ins = e.matmul(op[:, cs], lhsT=Sbf4[:, c, :], rhs=qt[:, cs], start=False, stop=True)
                    return ins
                P.add(PE, fnO, [Sbf4, qt, Asb4] + list(Vs[:nt]), [op])
                yield
                _act(P, osq[:, 0:n], op[:, 0:n], AF.Square, [op], [osq])
                yield
                sp = hbank()
                _mm(P, sp[:, 0:n], [(ones128b[:], osq[:, 0:n])], [ones128b, osq], [sp])
                lnv, rs = ezs[k3], kks[k3]
                _act(P, lnv[:, 0:n], sp[:, 0:n], AF.Ln, [sp], [lnv], scale=1.0 / 128, bias=EPS)
                yield
                _act(P, rs[:, 0:n], lnv[:, 0:n], AF.Exp, [lnv], [rs], scale=-0.5)
                yield
                _stt(P, on[:, 0:n], op[:, 0:n], gn[:, 0:1], rs[:, 0:n], ALU.mult, ALU.mult, [op, gn, rs], [on])
                yield
                _tt(P, DVE, X1[:, hd, 0:n], on[:, 0:n], sgT[:, hd, 0:n], ALU.mult, [on, sgT.sub(hd)], [X1])
                yield
                if hd == NHG - 1:
                    P.dma(C.OG1i[gb], X1[:, :, 0:n], reads=[X1], writes=[C.OG1i_res[gb]])
                    ag(C.OG1i_t[gb], C.OG1o_t[gb], C.OG1i_res[gb], C.OG1o_res[gb])

            def zipped(gens):
                gens = list(gens)
                if DBG == "seq":
                    for g in gens:
                        for _ in g:
                            pass
                    return
                while gens:
                    for g in list(gens):
                        try:
                            next(g)
                        except StopIteration:
                            gens.remove(g)

            N = len(items)
            npair = N // 2
            zipped([h_front(0), h_front(1)])
            h_state(0)
            h_state(1)
            for t in range(npair):
                gens = [h_back(2 * t), h_back(2 * t + 1)]
                if t + 1 < npair:
                    gens = [h_front(2 * t + 2), h_front(2 * t + 3)] + gens
                zipped(gens)
                if t + 1 < npair:
                    h_state(2 * t + 2)
                    h_state(2 * t + 3)
            P.barrier()

    ffn_phase(0)
    if C.s3parts >= 2:
        hgrn_phase()
    if C.s3parts >= 3:
        ffn_phase(1)
```

```python
import numpy as np
import concourse.bass as bass
import concourse.mybir as mybir
from concourse.bass_utils import run_bass_kernel_spmd

F32 = mybir.dt.float32
BF16 = mybir.dt.bfloat16
AF = mybir.ActivationFunctionType
ALU = mybir.AluOpType
AX = mybir.AxisListType

PE, ACT, DVE, POOL, SP = "pe", "act", "dve", "pool", "sp"
ENGS = (PE, ACT, DVE, POOL, SP)
NDMA_SEM = 12


class Res:
    __slots__ = ("w", "r")

    def __init__(self):
        self.w = None
        self.r = []


class Buf:
    def __init__(self, t, name=""):
        self.t = t
        self.name = name
        self.res = Res()
        self.subs = {}

    def __getitem__(self, k):
        return self.t[k]

    def sub(self, key):
        r = self.subs.get(key)
        if r is None:
            r = self.subs[key] = Res()
        return r


def _res(x):
    return x.res if isinstance(x, Buf) else x


class Op:
    __slots__ = ("eng", "fn", "raw", "oth", "dma", "sigval", "need", "dsem", "dval", "dprev", "pos")


class Prog:
    def __init__(self, nc):
        self.nc = nc
        self.ops = {e: [] for e in ENGS}
        self.ndma = {e: 0 for e in ENGS}

    def add(self, eng, fn, reads=(), writes=(), dma=False):
        op = Op()
        op.eng, op.fn, op.dma = eng, fn, dma
        op.raw, op.oth = set(), set()
        op.need = False
        op.sigval = None
        for r in reads:
            r = _res(r)
            if r.w is not None:
                op.raw.add(r.w)
        for w in writes:
            w = _res(w)
            if w.w is not None:
                op.oth.add(w.w)
            for x in w.r:
                op.oth.add(x)
        for r in reads:
            _res(r).r.append(op)
        for w in writes:
            w = _res(w)
            w.w = op
            w.r = []
        if dma:
            i = self.ndma[eng]
            self.ndma[eng] += 1
            op.dsem = i % NDMA_SEM
            op.dval = 16 * (i // NDMA_SEM + 1)
        self.ops[eng].append(op)
        return op

    def cc(self, fn, reads=(), writes=()):
        op = self.add(POOL, fn, reads, writes, dma=True)
        self.ndma[POOL] -= 1
        self.ncc = getattr(self, "ncc", 0) + 1
        op.dsem = "cc"
        op.dval = self.ncc
        return op

    def barrier(self):
        last = {}
        for e in ENGS:
            lc = None
            for o in reversed(self.ops[e]):
                if isinstance(o, Op) and not o.dma:
                    lc = o
                    break
            if lc is not None:
                lc.need = True
            last[e] = lc
        mark = ("barrier", last, dict(self.ndma), getattr(self, "ncc", 0))
        for e in ENGS:
            self.ops[e].append(mark)

    def dma(self, out, in_, reads=(), writes=(), q=SP, **kw):
        return self.add(q, lambda e: e.dma_start(out=out, in_=in_, **kw), reads, writes, dma=True)

    def _needed(self, o, d):
        if d.dma:
            return True
        if d.eng != o.eng:
            return True
        if o.dma:
            return True
        if o.eng == PE:
            return False
        return d in o.raw

    def _deps(self, o):
        best = {}
        out = []
        for d in list(o.raw) + list(o.oth):
            if not self._needed(o, d):
                continue
            if d.dma:
                out.append(d)
            else:
                b = best.get(d.eng)
                if b is None or d.pos > b.pos:
                    best[d.eng] = d
        return out + list(best.values())

    def emit(self):
        nc = self.nc
        for e in ENGS:
            for k, o in enumerate(self.ops[e]):
                if isinstance(o, Op):
                    o.pos = k
        for e in ENGS:
            for o in self.ops[e]:
                if not isinstance(o, Op):
                    continue
                for d in self._deps(o):
                    if not d.dma:
                        d.need = True
        for e in ENGS:
            c = 0
            for o in self.ops[e]:
                if isinstance(o, Op) and (not o.dma) and o.need:
                    c += 1
                    o.sigval = c
        import contextlib
        with contextlib.ExitStack() as st:
            esem = {e: st.enter_context(nc.semaphore("s_" + e)) for e in ENGS}
            dsem = {e: [st.enter_context(nc.semaphore("d_%s%d" % (e, i))) for i in range(NDMA_SEM)]
                    for e in ENGS if self.ndma[e] > 0}
            ccsem = st.enter_context(nc.semaphore("s_cc"))
            block = st.enter_context(nc.Block())

            def run(ename, eng):
                seen = {}

                def wait(sem, val):
                    k = id(sem)
                    if seen.get(k, 0) >= val:
                        return
                    seen[k] = val
                    eng.wait_ge(sem, val)

                for o in self.ops[ename]:
                    if not isinstance(o, Op):
                        _, last, nd, ncc = o
                        if ncc > 0:
                            wait(ccsem, ncc)
                        for e2 in ENGS:
                            if last[e2] is not None:
                                wait(esem[e2], last[e2].sigval)
                            n = nd[e2]
                            for i in range(NDMA_SEM):
                                cnt = (n - i + NDMA_SEM - 1) // NDMA_SEM if n > i else 0
                                if cnt > 0:
                                    wait(dsem[e2][i], 16 * cnt)
                        continue
                    for d in self._deps(o):
                        if d.dma and d.dsem == "cc":
                            wait(ccsem, d.dval)
                        elif d.dma:
                            wait(dsem[d.eng][d.dsem], d.dval)
                        else:
                            wait(esem[d.eng], d.sigval)
                    if o.dma and o.dsem == "cc":
                        if o.dval > 1:
                            wait(ccsem, o.dval - 1)
                        o.fn(eng).then_inc(ccsem)
                    elif o.dma:
                        s = dsem[ename][o.dsem]
                        if o.dval > 16:
                            wait(s, o.dval - 16)
                        o.fn(eng).then_inc(s, 16)
                    else:
                        ins = o.fn(eng)
                        if o.need:
                            ins.then_inc(esem[ename], 1)
                if ename == POOL and getattr(self, "ncc", 0) > 0:
                    wait(ccsem, self.ncc)
                if self.ndma[ename] > 0:
                    n = self.ndma[ename]
                    for i in range(NDMA_SEM):
                        cnt = (n - i + NDMA_SEM - 1) // NDMA_SEM if n > i else 0
                        if cnt > 0:
                            wait(dsem[ename][i], 16 * cnt)

            @block.tensor
            def _(eng):
                run(PE, eng)

            @block.scalar
            def _(eng):
                run(ACT, eng)

            @block.vector
            def _(eng):
                run(DVE, eng)

            @block.gpsimd
            def _(eng):
                run(POOL, eng)

            @block.sync
            def _(eng):
                run(SP, eng)
import contextlib
import ml_dtypes

D = 1024
H = 16
DH = 64
NMETA = 16
SEQ = 8192
NPAD = 112
LP = NPAD + NMETA + SEQ
NT = LP // 128
FF = 2816
NJ = FF // 128
EPS = 1e-6
BLOCKS = [(0, 128)] + [(128 + 512 * i, 512) for i in range(16)]
NB = len(BLOCKS)


import os
DBG = os.environ.get('KDBG', '')


class Ctx:
    pass


def _mm(P, out_ap, pairs, reads, writes, start=True, stop=True):
    def fn(e):
        n = len(pairs)
        ins = None
        for i, (l, r) in enumerate(pairs):
            ins = e.matmul(out_ap, lhsT=l, rhs=r, start=(start and i == 0), stop=(stop and i == n - 1))
        return ins
    return P.add(PE, fn, reads, writes)


def _act(P, out, in_, func, reads, writes, **kw):
    return P.add(ACT, lambda e: e.activation(out=out, in_=in_, func=func, **kw), reads, writes)


def _tt(P, eng, out, in0, in1, op, reads, writes):
    return P.add(eng, lambda e: e.tensor_tensor(out=out, in0=in0, in1=in1, op=op), reads, writes)


def _ts(P, eng, out, in0, s1, s2, op0, op1, reads, writes):
    if op1 is None:
        return P.add(eng, lambda e: e.tensor_scalar(out=out, in0=in0, scalar1=s1, scalar2=None, op0=op0), reads, writes)
    return P.add(eng, lambda e: e.tensor_scalar(out=out, in0=in0, scalar1=s1, scalar2=s2, op0=op0, op1=op1), reads, writes)


def _stt(P, out, in0, scalar, in1, op0, op1, reads, writes):
    return P.add(DVE, lambda e: e.scalar_tensor_tensor(out=out, in0=in0, scalar=scalar, in1=in1, op0=op0, op1=op1), reads, writes)


def _copy(P, eng, out, in_, reads, writes):
    if eng == ACT:
        return P.add(ACT, lambda e: e.copy(out=out, in_=in_), reads, writes)
    return P.add(eng, lambda e: e.tensor_copy(out=out, in_=in_), reads, writes)


def _rmsnorm_rows(P, C, ht, hn, gbc, tmp):
    junk, ssq, lnv, rstd = tmp
    _act(P, junk[:], ht[:], AF.Square, [ht], [junk, ssq], accum_out=ssq[:])
    _act(P, lnv[:], ssq[:], AF.Ln, [ssq], [lnv], scale=1.0 / D, bias=EPS)
    _act(P, rstd[:], lnv[:], AF.Exp, [lnv], [rstd], scale=-0.5)
    _stt(P, hn[:], ht[:], rstd[:, 0:1], gbc[:], ALU.mult, ALU.mult, [ht, rstd, gbc], [hn])


def _transpose_block(P, C, hns, nt, hnT, pTs, cnt0):
    n = nt * 128
    for c in range(8):
        pT = pTs[(cnt0 + c) % len(pTs)]

        def fn(e, c=c, pT=pT):
            ins = None
            for tl in range(nt):
                ins = e.transpose(out=pT[:, tl * 128:(tl + 1) * 128], in_=hns[tl][:, c * 128:(c + 1) * 128],
                                  identity=C.identb[:])
            return ins
        P.add(PE, fn, list(hns[:nt]) + [C.identb], [pT])
        _copy(P, ACT if c % 2 == 0 else DVE, hnT[:, c, 0:n], pT[:, 0:n], [pT], [hnT.sub(c)])


def stage1(P, nc, C, sb, ps, bounce=None):
    NHL = C.nheads
    NQ = NHL // 2
    WC = 4 * NHL * DH + NHL
    W = sb("s1_W", [128, 8, WC], BF16)
    wst = [sb("s1_wst%d" % i, [128, WC], F32) for i in range(2)]
    Wr = [W.sub(k) for k in range(8)]
    gbc = sb("s1_gbc", [128, D], F32)
    P.dma(gbc[:], C.attn_norm[0].partition_broadcast(128), writes=[gbc])
    gcol = sb("s1_gcol", [128, 2], F32)
    for hh in range(2):
        P.dma(gcol[64 * hh:64 * hh + 64, 0:1], C.fox_q_norm[0].rearrange("(p o) -> p o", o=1), writes=[gcol])
        P.dma(gcol[64 * hh:64 * hh + 64, 1:2], C.fox_k_norm[0].rearrange("(p o) -> p o", o=1), writes=[gcol])
    _ts(P, DVE, gcol[:, 0:1], gcol[:, 0:1], 0.125, None, ALU.mult, None, [gcol], [gcol])
    negbf = sb("s1_negbf", [NHL, 1], F32)
    P.dma(negbf[:], C.fox_b_f[0].rearrange("(p o) -> p o", o=1), writes=[negbf])
    _ts(P, DVE, negbf[:], negbf[:], -1.0, None, ALU.mult, None, [negbf], [negbf])
    ones16 = sb("s1_ones16", [NHL, 512], F32)
    P.add(DVE, lambda e: e.memset(ones16[:], 1.0), [], [ones16])
    zero16 = sb("s1_zero16", [NHL, 1], F32)
    P.add(DVE, lambda e: e.memset(zero16[:], 0.0), [], [zero16])

    hts = [sb("s1_ht%d" % i, [128, D], F32) for i in range(2)]
    hns = [sb("s1_hn%d" % i, [128, D], BF16) for i in range(8)]
    junk = sb("s1_junk", [128, D], BF16)
    ssqs = [[sb("s1_ssq%d_%d" % (i, j), [128, 1], F32) for j in range(3)] for i in range(2)]
    hnTs = [sb("s1_hnT%d" % i, [128, 8, 512], BF16) for i in range(2)]
    pTs = [ps("s1_pT%d" % i, [128, 512], BF16) for i in range(2)]
    psA = [ps("s1_psA%d" % i, [128, 512], F32) for i in range(4)]
    psSs = [ps("s1_psS%d" % i, [128, 512], F32) for i in range(2)]
    sqs = [sb("s1_sq%d" % i, [128, 512], BF16) for i in range(2)]
    lnvs = [sb("s1_lnv%d" % i, [128, 512], F32) for i in range(2)]
    rss = [sb("s1_rs%d" % i, [128, 512], F32) for i in range(2)]
    outs = [sb("s1_out%d" % i, [128, 512], BF16) for i in range(6)]
    vsts = [sb("s1_vst%d" % i, [128, NHL, 4, 65], BF16) for i in range(2)]
    for v in vsts:
        P.add(POOL, lambda e, v=v: e.memset(v[:, :, :, 64:65], 1.0), [], [v])
    ef = sb("s1_ef", [NHL, 512], F32)
    lf = sb("s1_lf", [NHL, 512], F32)
    cps = [sb("s1_cp%d" % i, [NHL, 512], F32) for i in range(2)]
    hsp = [[sb("s1_hsp%d_%d" % (i, j), [NHL, 512], BF16) for j in range(6)] for i in range(2)]
    r1 = sb("s1_r1", [NHL, 512], F32)
    r2 = sb("s1_r2", [NHL, 512], F32)
    onesb = sb("s1_onesb", [3, LP // 4], BF16)
    P.add(DVE, lambda e: e.memset(onesb[:], 1.0), [], [onesb])
    for h in range(NHL):
        for qd in range(4):
            cs_ = slice(qd * (LP // 4), (qd + 1) * (LP // 4))
            P.dma(C.KT_d[h, 67:70, cs_], onesb[0:3, :], reads=[onesb], q=POOL)
            P.dma(C.QT_d[h, 64:67, cs_], onesb[0:3, :], reads=[onesb], q=POOL)

    cA = 0
    cO = 0
    cT = 0
    cTh = [0]

    def fa_load(bi, tl):
        t0, n = BLOCKS[bi]
        if bi >= NB or tl >= n // 128:
            return
        ht = hts[tl % 2]
        P.dma(ht[:], C.h0[t0 + tl * 128:t0 + (tl + 1) * 128, :], writes=[ht])

    def fa_norm(bi, tl):
        t0, n = BLOCKS[bi]
        if bi >= NB or tl >= n // 128:
            return
        myhn = hns[(bi % 2) * 4:(bi % 2) * 4 + 4]
        tmp = [junk] + ssqs[cTh[0] % 2]
        cTh[0] += 1
        _rmsnorm_rows(P, C, hts[tl % 2], myhn[tl], gbc, tmp)

    def s1_front_a(bi):
        for tl in range(4):
            fa_load(bi, tl)
            fa_norm(bi, tl)

    def s1_front_b(bi):
        t0, n = BLOCKS[bi]
        myhn = hns[(bi % 2) * 4:(bi % 2) * 4 + 4]
        _transpose_block(P, C, myhn, n // 128, hnTs[bi % 2], pTs, 0)

    s1_front_a(0)
    for kc in range(8):
        P.dma(wst[kc % 2][:], C.fox_w_in[kc * 128:(kc + 1) * 128, :], writes=[wst[kc % 2]])
        _copy(P, ACT if kc % 2 == 0 else DVE, W[:, kc, :], wst[kc % 2][:], [wst[kc % 2]], [W.sub(kc)])
    s1_front_b(0)
    for bi, (t0, n) in enumerate(BLOCKS):
        nt = n // 128
        hnT = hnTs[bi % 2]
        if bi + 1 < NB:
            fa_load(bi + 1, 0)
            fa_load(bi + 1, 1)
        hr = [hnT.sub(c) for c in range(8)]
        pqs = {}

        def qk_a(c):
            nonlocal cA
            cols = c * 128
            pq = psA[cA % 4]
            cA += 1
            pqs[c] = pq
            _mm(P, pq[:, 0:n], [(W[:, kc, cols:cols + 128], hnT[:, kc, 0:n]) for kc in range(8)], Wr + hr, [pq])
            _act(P, sqs[c % 2][:, 0:n], pq[:, 0:n], AF.Square, [pq], [sqs[c % 2]])

        def qk_b(c):
            nonlocal cO
            pq = pqs[c]
            sq, lnv, rs = sqs[c % 2], lnvs[c % 2], rss[c % 2]
            psS = psSs[c % 2]
            _mm(P, psS[:, 0:n], [(C.bdones[:], sq[:, 0:n])], [C.bdones, sq], [psS])
            _act(P, lnv[:, 0:n], psS[:, 0:n], AF.Ln, [psS], [lnv], scale=1.0 / DH, bias=EPS)
            _act(P, rs[:, 0:n], lnv[:, 0:n], AF.Exp, [lnv], [rs], scale=-0.5)
            ob = outs[cO % 6]
            cO += 1
            gi = 0 if c < NQ else 1
            _stt(P, ob[:, 0:n], pq[:, 0:n], gcol[:, gi:gi + 1], rs[:, 0:n], ALU.mult, ALU.mult, [pq, gcol, rs], [ob])
            dst = C.QT_d if c < NQ else C.KT_d
            for hh in range(2):
                h = (c % NQ) * 2 + hh
                P.dma(dst[h, 0:64, t0:t0 + n], ob[64 * hh:64 * hh + 64, 0:n], reads=[ob], q=POOL)

        qk_a(0)
        for c in range(2 * NQ):
            if c + 1 < 2 * NQ:
                qk_a(c + 1)
            qk_b(c)
        if bi + 1 < NB:
            fa_norm(bi + 1, 0)
            fa_norm(bi + 1, 1)
            fa_load(bi + 1, 2)
            fa_load(bi + 1, 3)
        for c in range(NQ):
            cols = 3 * NHL * DH + c * 128
            pg = psA[cA % 4]
            cA += 1
            _mm(P, pg[:, 0:n], [(W[:, kc, cols:cols + 128], hnT[:, kc, 0:n]) for kc in range(8)], Wr + hr, [pg])
            lnv, rs = lnvs[c % 2], rss[c % 2]
            _act(P, rs[:, 0:n], pg[:, 0:n], AF.Exp, [pg], [rs], scale=-1.0)
            _act(P, lnv[:, 0:n], rs[:, 0:n], AF.Ln, [rs], [lnv], bias=1.0)
            ob = outs[cO % 6]
            cO += 1
            _act(P, ob[:, 0:n], lnv[:, 0:n], AF.Exp, [lnv], [ob], scale=-1.0)
            P.dma(C.SG_d[c * 128:(c + 1) * 128, t0:t0 + n], ob[:, 0:n], reads=[ob], q=POOL)
        pf = psA[cA % 4]
        cA += 1
        _mm(P, pf[0:NHL, 0:n], [(W[:, kc, 4 * NHL * DH:WC], hnT[:, kc, 0:n]) for kc in range(8)], Wr + hr, [pf])
        _act(P, ef[:, 0:n], pf[0:NHL, 0:n], AF.Exp, [pf], [ef], scale=-1.0, bias=negbf[:, 0:1])
        _act(P, lf[:, 0:n], ef[:, 0:n], AF.Ln, [ef], [lf], bias=1.0)
        cp = cps[bi % 2]
        if bi == 0:
            ref_ap, ref_b = zero16[:, 0:1], zero16
        else:
            pn = BLOCKS[bi - 1][1]
            ref_ap, ref_b = cps[(bi - 1) % 2][:, pn - 1:pn], cps[(bi - 1) % 2]
        P.add(DVE, lambda e, cp=cp, ref_ap=ref_ap, n=n: e.tensor_tensor_scan(
            out=cp[:, 0:n], data0=ones16[:, 0:n], data1=lf[:, 0:n], initial=ref_ap, op0=ALU.mult, op1=ALU.add),
            [ones16, lf, ref_b], [cp])
        hh = hsp[bi % 2]
        _copy(P, DVE, hh[0][:, 0:n], cp[:, 0:n], [cp], [hh[0]])
        _tt(P, DVE, r1[:, 0:n], cp[:, 0:n], hh[0][:, 0:n], ALU.subtract, [cp, hh[0]], [r1])
        _copy(P, DVE, hh[1][:, 0:n], r1[:, 0:n], [r1], [hh[1]])
        _tt(P, DVE, r2[:, 0:n], r1[:, 0:n], hh[1][:, 0:n], ALU.subtract, [r1, hh[1]], [r2])
        _copy(P, DVE, hh[2][:, 0:n], r2[:, 0:n], [r2], [hh[2]])
        for j in range(3):
            _ts(P, DVE, hh[3 + j][:, 0:n], hh[j][:, 0:n], -1.0, None, ALU.mult, None, [hh[j]], [hh[3 + j]])
            P.dma(C.KT_d[:, 64 + j, t0:t0 + n], hh[j][:, 0:n], reads=[hh[j]])
            P.dma(C.QT_d[:, 67 + j, t0:t0 + n], hh[3 + j][:, 0:n], reads=[hh[3 + j]])
        if bi + 1 < NB:
            fa_norm(bi + 1, 2)
            fa_norm(bi + 1, 3)
        vst = vsts[bi % 2]
        if bi == 0:
            P.add(POOL, lambda e, v=vst: e.memset(v[0:NPAD, :, 0:1, 64:65], 0.0), [], [vst])
        for tl in range(nt):
            for half in range(NHL // 8):
                pv = psA[cA % 4]
                cA += 1
                cols = 2 * NHL * DH + half * 512
                _mm(P, pv[:, :], [(hnT[:, kc, tl * 128:(tl + 1) * 128], W[:, kc, cols:cols + 512]) for kc in range(8)],
                    Wr + hr, [pv])
                _copy(P, DVE if half == 0 else ACT, vst[:, 8 * half:8 * half + 8, tl, 0:64],
                      pv[:, :].rearrange("p (h d) -> p h d", h=8), [pv], [vst])
        for h in range(NHL):
            P.dma(C.VA_d[h, :, t0 // 128:t0 // 128 + nt, :], vst[:, h, 0:nt, :], reads=[vst])
        if bi == 0:
            P.add(POOL, lambda e, v=vst: e.memset(v[0:NPAD, :, 0:1, 64:65], 1.0), [], [vst])
        if bi + 1 < NB:
            s1_front_b(bi + 1)


def stage2(P, nc, C, sb, ps, bounce=None):
    for kc in range(8):
        P.dma(C.Wo[0][:, kc, :], C.fox_w_out[kc * 128:(kc + 1) * 128, :], writes=[C.Wo[0]], q=POOL)
        P.dma(C.Wo[1][:, kc, :], C.hgrn_w_out[kc * 128:(kc + 1) * 128, :], writes=[C.Wo[1]], q=POOL)
    gate_t = sb("s2_gate", [128, 1], F32)
    gate = [Res()]
    conv = convert_weights(P, nc, C, bounce, gate) if bounce is not None else iter(())
    KTb = [sb("s2_KT%d" % i, [70, LP], BF16) for i in range(2)]
    VAb = [sb("s2_VA%d" % i, [128, NT, 65], BF16) for i in range(2)]
    Qbs = [sb("s2_Q%d" % i, [70, 512], BF16) for i in range(3)]
    sgs = [sb("s2_sg%d" % i, [64, 512], BF16) for i in range(3)]
    biases = [sb("s2_bias%d" % i, [128, NT], F32) for i in range(3)]
    Pts = [sb("s2_Pt%d" % i, [128, 512], BF16) for i in range(6)]
    Sb = [ps("s2_S%d" % i, [128, 512], F32) for i in range(5)]
    Ob = [ps("s2_O%d" % i, [128, 512], F32) for i in range(2)]
    Rps = ps("s2_R", [128, 512], F32)
    rds = [sb("s2_rd%d" % i, [128, 512], F32) for i in range(2)]
    rd2s = [sb("s2_rd2%d" % i, [128, 512], F32) for i in range(2)]
    pending = []
    onesf = sb("s2_onesf", [128, 64], F32)
    P.add(DVE, lambda e: e.memset(onesf[:], 1.0), [], [onesf])
    zt2 = sb("s2_zt", [64, 128], BF16)
    P.add(DVE, lambda e: e.memset(zt2[:], 0.0), [], [zt2])
    for h in range(C.nheads):
        P.dma(C.OGi[h][:, LP:LP + 128], zt2[:], reads=[zt2], writes=[C.OGi_res[h]])
    Osbs = [sb("s2_Osb%d" % i, [64, 512], F32) for i in range(2)]
    og1s = [sb("s2_og1%d" % i, [64, 512], F32) for i in range(2)]
    ogbs = [sb("s2_ogb%d" % i, [64, 512], BF16) for i in range(2)]

    items = []
    for h in range(C.nheads):
        for bi, (t0, n) in enumerate(BLOCKS):
            nkt = (t0 + n) // 128
            for kt in range(nkt):
                items.append((h, bi, kt, nkt))
    state = {}

    def prologue(h, bi):
        t0, n = BLOCKS[bi]
        g = h * NB + bi
        if g in state or g >= C.nheads * NB:
            return
        state[g] = True
        if g % 3 == 0:
            gate[0] = Res()
            P.add(DVE, lambda e: e.memset(gate_t[:], 0.0), [], [gate[0]])
            next(conv, None)
        if bi == 0:
            KTh, VAh = KTb[h % 2], VAb[h % 2]
            P.dma(KTh[:, :], C.KT_d[h], writes=[KTh])
            P.dma(VAh[:], C.VA_d[h], writes=[VAh])
        Qb, sg, bias = Qbs[g % 3], sgs[g % 3], biases[g % 3]
        P.dma(Qb[0:70, 0:n], C.QT_d[h, :, t0:t0 + n], writes=[Qb])
        P.dma(sg[0:64, 0:n], C.SG_d[64 * h:64 * h + 64, t0:t0 + n], writes=[sg])

    def qk(i):
        h, bi, kt, nkt = items[i]
        t0, n = BLOCKS[bi]
        g = h * NB + bi
        if kt == 0:
            prologue(h, bi)
        j = kt - t0 // 128
        c0 = 128 * j if j >= 0 else 0
        S = Sb[i % 5]
        _mm(P, S[:, c0:n], [(KTb[h % 2][0:70, kt * 128:(kt + 1) * 128], Qbs[g % 3][0:70, c0:n])],
            [KTb[h % 2], Qbs[g % 3]], [S])

    def rest(i):
        h, bi, kt, nkt = items[i]
        t0, n = BLOCKS[bi]
        g = h * NB + bi
        j = kt - t0 // 128
        c0 = 128 * j if j >= 0 else 0
        S, Pt, O = Sb[i % 5], Pts[i % 6], Ob[g % 2]
        bias = biases[g % 3]
        if kt == 0:
            while pending and pending[0][0] <= g - 2:
                epilogue2(pending.pop(0)[0])
            prologue((g + 1) // NB, (g + 1) % NB)
        _act(P, Pt[:, c0:n], S[:, c0:n], AF.Exp, [S], [Pt])
        if j >= 0:
            _tt(P, DVE, Pt[:, c0:c0 + 128], Pt[:, c0:c0 + 128], C.tri[:], ALU.mult, [Pt, C.tri], [Pt])
        _mm(P, O[0:65, c0:n], [(VAb[h % 2][:, kt, 0:65], Pt[:, c0:n])], [VAb[h % 2], Pt], [O],
            start=(kt == 0), stop=(kt == nkt - 1))
        if kt == nkt - 1:
            Osb = Osbs[g % 2]
            rd, rd2 = rds[g % 2], rd2s[g % 2]
            _ts(P, DVE, rd[64:65, 0:n], O[64:65, 0:n], 1e-30, None, ALU.max, None, [O], [rd])
            P.add(DVE, lambda e: e.reciprocal(out=rd2[64:65, 0:n], in_=rd[64:65, 0:n]), [rd], [rd2])
            _copy(P, ACT, Osb[:, 0:n], O[0:64, 0:n], [O], [Osb])
            pending.append((g, i))
        while pending and (i - pending[0][1] >= 5 or i == len(items) - 1):
            epilogue2(pending.pop(0)[0])

    def epilogue2(g):
        h, bi = g // NB, g % NB
        t0, n = BLOCKS[bi]
        Osb, og1, ogb, sg = Osbs[g % 2], og1s[g % 2], ogbs[g % 2], sgs[g % 3]
        rd2 = rd2s[g % 2]
        _mm(P, Rps[0:64, 0:n], [(onesf[64:65, 0:64], rd2[64:65, 0:n])], [onesf, rd2], [Rps])
        _tt(P, DVE, og1[:, 0:n], Osb[:, 0:n], Rps[0:64, 0:n], ALU.mult, [Osb, Rps], [og1])
        _tt(P, DVE, ogb[:, 0:n], og1[:, 0:n], sg[0:64, 0:n], ALU.mult, [og1, sg], [ogb])
        P.dma(C.OGi[h][:, t0:t0 + n], ogb[:, 0:n], reads=[ogb], writes=[C.OGi_res[h]])
        if bi == NB - 1:
            P.cc(lambda e, h=h: e.collective_compute("AllGather", ALU.bypass, replica_groups=C.RG,
                                                     ins=[C.OGi_t[h].ap().opt()], outs=[C.OGo_t[h].ap().opt()]),
                 reads=[C.OGi_res[h]], writes=[C.OGo_res[h]])

    LA = 3
    N = len(items)
    for i in range(N + LA):
        if i < N:
            qk(i)
        if i - LA >= 0:
            rest(i - LA)
    for _ in conv:
        pass


def build(stages=3, debug=False, nheads=H // 2, conv=None, s3parts=99, nblk=NB):
    nc = bass.Bass("TRN2", target_bir_lowering=False)
    C = Ctx()
    C.nheads = nheads
    C.s3parts = s3parts
    C.nblk = nblk
    if conv is None:
        conv = stages >= 3

    def din(name, shape, dt=F32):
        return nc.dram_tensor(name, list(shape), dt, kind="ExternalInput").ap()

    def dscr(name, shape, dt, out=False):
        return nc.dram_tensor(name, list(shape), dt, kind="ExternalOutput" if out else "Internal").ap()

    C.h0 = din("h0", [LP, D])
    C.attn_norm = din("attn_norm", [2, D])
    C.ffn_norm = din("ffn_norm", [2, D])
    C.final_norm = din("final_norm", [D])
    NHL = nheads
    C.fox_w_in = din("fox_w_in", [D, 4 * NHL * DH + NHL])
    C.fox_b_f = din("fox_b_f", [1, NHL])
    C.fox_q_norm = din("fox_q_norm", [1, DH])
    C.fox_k_norm = din("fox_k_norm", [1, DH])
    C.fox_w_out = din("fox_w_out", [D, D])
    C.hgrn_w_in = din("hgrn_w_in", [D, 4 * NHG * 128])
    C.lb = din("hgrn_lower_bounds", [2, NHG * 128])
    C.h0h = din("h0h", [HT * 128, D])
    C.rmask = din("rmask", [128, 8 * 512], mybir.dt.uint16)
    C.rmaskf = din("rmaskf", [128, 1])
    C.g_norm = din("hgrn_g_norm", [1, 128])
    C.hgrn_w_out = din("hgrn_w_out", [D, D])
    C.ffn_w_in = din("ffn_w_in", [2, D, 2 * FF])
    C.ffn_w_out = din("ffn_w_out", [2, FF, D])
    c_identb = din("c_identb", [128, 128], BF16)
    c_identf = din("c_identf", [128, 128], F32)
    c_bdones = din("c_bdones", [128, 128], BF16)
    c_tri = din("c_tri", [128, 128], BF16)
    d1 = debug and stages == 1
    C.QT_d = dscr("QT_d", [NHL, 70, LP], BF16, d1)
    C.KT_d = dscr("KT_d", [NHL, 70, LP], BF16, d1)
    C.VA_d = dscr("VA_d", [NHL, 128, NT, 65], BF16, d1)
    C.SG_d = dscr("SG_d", [NHL * DH, LP], BF16, d1)
    C.RG = [[0, 1], [2, 3], [4, 5], [6, 7]]
    C.OGi_t = [nc.dram_tensor("OGi%d" % h, [DH, LP + 128], BF16) for h in range(NHL)]
    C.OGo_t = [nc.dram_tensor("OGo%d" % h, [2 * DH, LP + 128], BF16) for h in range(NHL)]
    C.HNi_t = [nc.dram_tensor("HNi%d" % i, [128, 8, n], BF16) for i, (t0, n) in enumerate(LBLK)]
    C.HNo_t = [nc.dram_tensor("HNo%d" % i, [256, 8, n], BF16) for i, (t0, n) in enumerate(LBLK)]
    C.HNi = [t.ap() for t in C.HNi_t]
    C.HNo = [t.ap().rearrange("(r p) k n -> r p k n", r=2) for t in C.HNo_t]
    C.HNi_res = [Res() for _ in LBLK]
    C.HNo_res = [Res() for _ in LBLK]
    C.OG1i_t = [nc.dram_tensor("OG1i%d" % g, [128, NHG, LBLK[g % NLB][1]], BF16) for g in range(2 * NLB)]
    C.OG1o_t = [nc.dram_tensor("OG1o%d" % g, [256, NHG, LBLK[g % NLB][1]], BF16) for g in range(2 * NLB)]
    C.OG1i = [t.ap() for t in C.OG1i_t]
    C.OG1o = [t.ap().rearrange("(r p) k n -> r p k n", r=2) for t in C.OG1o_t]
    C.OG1i_res = [Res() for _ in range(2 * NLB)]
    C.OG1o_res = [Res() for _ in range(2 * NLB)]
    C.H1 = dscr("H1", [HT * 128, D], F32)
    C.H1_res = [Res() for _ in LBLK]
    C.OGi = [t.ap() for t in C.OGi_t]
    C.OGo = [t.ap() for t in C.OGo_t]
    C.OGi_res = [Res() for _ in range(NHL)]
    C.OGo_res = [Res() for _ in range(NHL)]
    C.W1S = dscr("W1S", [2, NJ, 128, 8, 256], BF16)
    C.W2S = dscr("W2S", [2, NJ, 128, D], BF16)
    C.WHS = dscr("WHS", [NHG, 128, 8, 384], BF16)
    C.WHV = dscr("WHV", [1, 128, 8, 512], BF16)
    C.out = nc.dram_tensor("out", [HT * 128, D], F32, kind="ExternalOutput").ap()
    C.dbg_x1 = nc.dram_tensor("dbg_x1", [nblk, 128, 8, 512], BF16, kind="ExternalOutput").ap() if (debug and stages == 3) else None

    with contextlib.ExitStack() as st0:
        P = Prog(nc)

        def mk(st):
            def sb(name, shape, dt):
                return Buf(st.enter_context(nc.sbuf_tensor(name, list(shape), dt)), name)

            def ps(name, shape, dt):
                return Buf(st.enter_context(nc.psum_tensor(name, list(shape), dt)), name)
            return sb, ps
        sb0, ps0 = mk(st0)
        C.identb = sb0("identb", [128, 128], BF16)
        C.identf = sb0("identf", [128, 128], F32)
        C.bdones = sb0("bdones", [128, 128], BF16)
        C.tri = sb0("tri", [128, 128], BF16)
        P.dma(C.identb[:], c_identb, writes=[C.identb])
        P.dma(C.identf[:], c_identf, writes=[C.identf])
        P.dma(C.bdones[:], c_bdones, writes=[C.bdones])
        P.dma(C.tri[:], c_tri, writes=[C.tri])
        C.Wo = [sb0("Wo0", [128, 8, D], BF16), sb0("Wo1", [128, 8, D], BF16)]
        if debug:
            cpc_o = nc.dram_tensor("CPC_o", [128, NT, NHL], F32, kind="ExternalOutput").ap()
            refb_o = nc.dram_tensor("REFB_o", [128, NB, NHL], F32, kind="ExternalOutput").ap()
        with contextlib.ExitStack() as stm:
            sbm, psm = mk(stm)
            C.CPC = sbm("CPC", [128, NT, NHL], F32)
            C.REFB = sbm("REFB", [128, NB, NHL], F32)
            bounce = [sbm("bounce%d" % i, [128, 2 * FF], BF16) for i in range(2)]
            with contextlib.ExitStack() as st:
                sb, ps = mk(st)
                stage1(P, nc, C, sb, ps, bounce if conv else None)
                if debug:
                    P.dma(cpc_o, C.CPC[:], reads=[C.CPC])
                    P.dma(refb_o, C.REFB[:], reads=[C.REFB])
                P.barrier()
            if stages >= 2:
                with contextlib.ExitStack() as st:
                    sb, ps = mk(st)
                    stage2(P, nc, C, sb, ps, bounce if conv else None)
                    P.barrier()
        if stages >= 3:
            with contextlib.ExitStack() as st:
                sb, ps = mk(st)
                stage3(P, nc, C, sb, ps)
                P.barrier()
        P.emit()
    return nc


def host_consts():
    bf = ml_dtypes.bfloat16
    idx = np.arange(128)
    return {
        "c_identb": np.eye(128, dtype=np.float32).astype(bf),
        "c_identf": np.eye(128, dtype=np.float32),
        "c_bdones": (idx[:, None] // 64 == idx[None, :] // 64).astype(np.float32).astype(bf),
        "c_tri": (idx[:, None] <= idx[None, :]).astype(np.float32).astype(bf),
    }


WNAMES = ["attn_norm", "ffn_norm", "final_norm", "fox_w_in", "fox_b_f", "fox_q_norm", "fox_k_norm", "fox_w_out",
          "hgrn_w_in", "hgrn_lower_bounds", "hgrn_g_norm", "hgrn_w_out", "ffn_w_in", "ffn_w_out"]


def make_in_maps(inputs, ncores=8):
    x = np.asarray(inputs["x"], dtype=np.float32)
    meta = np.asarray(inputs["meta_tokens"], dtype=np.float32)
    consts = host_consts()
    maps = []
    NHL = H // 2
    for c in range(ncores):
        b, r = c // 2, c % 2
        h0 = np.zeros((LP, D), np.float32)
        h0[NPAD:NPAD + NMETA] = meta
        h0[NPAD + NMETA:] = x[b]
        h0h = np.zeros((HT * 128, D), np.float32)
        seg = h0[r * HT * 128:(r + 1) * HT * 128]
        h0h[:seg.shape[0]] = seg
        m = {"h0": h0, "h0h": h0h, "rmask": np.full((128, 8 * 512), 0xFFFF if r else 0, np.uint16),
             "rmaskf": np.full((128, 1), float(r), np.float32)}
        for k in WNAMES:
            a = np.asarray(inputs[k], dtype=np.float32)
            if k in ("fox_w_in", "fox_w_out", "hgrn_w_in", "hgrn_w_out"):
                a = a.reshape(a.shape[-2], a.shape[-1])
            if k == "fox_w_in":
                w = NHL * DH
                a = np.concatenate([a[:, s0 + r * w:s0 + (r + 1) * w] for s0 in (0, D, 2 * D, 3 * D)]
                                   + [a[:, 4 * D + r * NHL:4 * D + (r + 1) * NHL]], axis=1)
            if k == "fox_b_f":
                a = a[:, r * NHL:(r + 1) * NHL]
            if k == "hgrn_w_in":
                w = NHG * 128
                a = np.concatenate([a[:, s0 + r * w:s0 + (r + 1) * w] for s0 in (0, D, 2 * D, 3 * D)], axis=1)
            if k == "hgrn_lower_bounds":
                a = a[:, r * NHG * 128:(r + 1) * NHG * 128]
            m[k] = np.ascontiguousarray(a)
        m.update(consts)
        maps.append(m)
    return maps


def kernel(**inputs):
    nc = build(stages=3)
    maps = make_in_maps(inputs, 8)
    res = run_bass_kernel_spmd(nc, maps, core_ids=list(range(8)))
    outs = []
    for b in range(4):
        o0 = np.asarray(res.results[2 * b]["out"], dtype=np.float32)
        o1 = np.asarray(res.results[2 * b + 1]["out"], dtype=np.float32)
        outs.append(np.concatenate([o0[NPAD + NMETA:], o1[:SEQ - (HT * 128 - NPAD - NMETA)]], axis=0))
    return np.stack(outs, axis=0)


class Stream:
    def __init__(self, P, bufs, reqs, q=SP, keep=0):
        self.P, self.bufs, self.reqs, self.q, self.keep = P, bufs, reqs, q, keep
        self.issued = 0

    def get(self, i):
        lim = min(len(self.reqs), i + len(self.bufs) - self.keep)
        while self.issued < lim:
            k = self.issued
            b = self.bufs[k % len(self.bufs)]
            dst, src = self.reqs[k]
            self.P.dma(dst(b), src, writes=[b], q=self.q)
            self.issued += 1
        return self.bufs[i % len(self.bufs)]


def convert_weights(P, nc, C, bounce, gate):
    k = [0]

    def ld(src, width):
        b = bounce[k[0] % 2]
        k[0] += 1
        P.dma(b[:, 0:width], src, reads=[gate[0]], writes=[b], q=POOL)
        return b
    for l in range(2):
        for kc in range(8):
            b = ld(C.ffn_w_in[l, kc * 128:(kc + 1) * 128, :], 2 * FF)
            for hf in range(2):
                P.dma(C.W1S[l, :, :, kc, hf * 128:(hf + 1) * 128].rearrange("j p c -> p j c"),
                      b[:, hf * FF:(hf + 1) * FF].rearrange("p (j c) -> p j c", c=128), reads=[b], q=POOL)
            yield
        wo = C.ffn_w_out[l].rearrange("(j p) c -> p j c", p=128)
        for j0 in range(0, NJ, 5):
            j1 = min(NJ, j0 + 5)
            b = ld(wo[:, j0:j1, :], (j1 - j0) * D)
            P.dma(C.W2S[l, j0:j1].rearrange("j p c -> p j c"),
                  b[:, 0:(j1 - j0) * D].rearrange("p (j c) -> p j c", c=D), reads=[b], q=POOL)
            yield
    GW = NHG * 128
    for kc in range(8):
        b = ld(C.hgrn_w_in[kc * 128:(kc + 1) * 128, :], 4 * GW)
        for gi, base in enumerate((0, GW, 3 * GW)):
            P.dma(C.WHS[:, :, kc, gi * 128:(gi + 1) * 128].rearrange("h p c -> p h c"),
                  b[:, base:base + GW].rearrange("p (h c) -> p h c", c=128), reads=[b], q=POOL)
        P.dma(C.WHV[0, :, kc, :], b[:, 2 * GW:3 * GW], reads=[b], q=POOL)
        yield


NHG = 4
HT = 33
LBLK = [(0, 128)] + [(128 + 512 * i, 512) for i in range(8)]
NLB = len(LBLK)


def stage3(P, nc, C, sb, ps):
    B = [ps("s3_B%d" % i, [128, 512], F32) for i in range(6)]
    Tt = [ps("s3_T%d" % i, [128, 1024], BF16) for i in range(2)]
    for i in range(2):
        v = Buf(Tt[i].t[:, :].bitcast(F32), "B%df" % (6 + i))
        v.res = Tt[i].res
        B.append(v)

    def bres(i):
        return [B[i].res]

    prow = sb("s3_prow", [32, 128], F32)
    P.dma(prow[0:8, :], C.lb.rearrange("r (h p) -> (r h) p", p=128), writes=[prow])
    P.dma(prow[8:16, :], C.attn_norm[1].rearrange("(c p) -> c p", p=128), writes=[prow])
    P.dma(prow[16:24, :], C.ffn_norm[0].rearrange("(c p) -> c p", p=128), writes=[prow])
    P.dma(prow[24:32, :], C.ffn_norm[1].rearrange("(c p) -> c p", p=128), writes=[prow])
    pcol = sb("s3_pcol", [128, 32], F32)
    P.add(PE, lambda e: e.transpose(out=B[5][:, 0:32], in_=prow[0:32, :], identity=C.identf[0:32, 0:32]),
          [prow, C.identf], bres(5))
    _copy(P, DVE, pcol[:], B[5][:, 0:32], bres(5), [pcol])
    omlb = sb("s3_omlb", [128, NHG], F32)
    _tt(P, DVE, omlb[:], pcol[:, 4:8], pcol[:, 0:4], ALU.subtract, [pcol], [omlb])
    _act(P, omlb[:], omlb[:], AF.Exp, [omlb], [omlb])
    _ts(P, DVE, omlb[:], omlb[:], 1.0, None, ALU.add, None, [omlb], [omlb])
    P.add(DVE, lambda e: e.reciprocal(out=omlb[:], in_=omlb[:]), [omlb], [omlb])
    lnomlb = sb("s3_lnomlb", [128, NHG], F32)
    _act(P, lnomlb[:], omlb[:], AF.Ln, [omlb], [lnomlb])
    gcols = {"attn1": pcol[:, 8:16], "ffn0": pcol[:, 16:24], "ffn1": pcol[:, 24:32]}
    gn = sb("s3_gn", [128, 1], F32)
    P.dma(gn[:], C.g_norm[0].rearrange("(p o) -> p o", o=1), writes=[gn])
    gfin = sb("s3_gfin", [128, D], F32)
    P.dma(gfin[:], C.final_norm.partition_broadcast(128), writes=[gfin])
    mh = sb("s3_mh", [128, 1], F32)
    P.add(POOL, lambda e: e.memset(mh[:], -0.5), [], [mh])
    ones128b = sb("s3_ones128b", [128, 128], BF16)
    P.add(DVE, lambda e: e.memset(ones128b[:], 1.0), [], [ones128b])
    onesf = sb("s3_onesf", [128, 128], F32)
    P.add(DVE, lambda e: e.memset(onesf[:], 1.0), [], [onesf])
    rmk = sb("s3_rmk", [128, 512], mybir.dt.uint16)
    P.dma(rmk[:], C.rmask[:, 0:512], writes=[rmk])
    rmf = sb("s3_rmf", [128, 1], F32)
    P.dma(rmf[:], C.rmaskf, writes=[rmf])
    junk = sb("s3_junk", [128, D], BF16)
    nrm = [[sb("s3_nrm%d_%d" % (i, j), [128, 1], F32) for j in range(3)] for i in range(4)]
    cnt = {"y": 0, "f": 0, "T": 0, "n": 0, "h": 0}

    def ybank():
        cnt["y"] += 1
        return B[cnt["y"] % 4]

    def fbank():
        cnt["f"] += 1
        return B[cnt["f"] % 6]

    def hbank():
        cnt["h"] += 1
        return B[cnt["h"] % 3]

    def norm_rows(ht):
        ssq, tt, rstd = nrm[cnt["n"] % 4]
        cnt["n"] += 1
        _act(P, junk[:], ht[:], AF.Square, [ht], [junk, ssq], accum_out=ssq[:])
        _ts(P, DVE, tt[:], ssq[:], 1.0 / D, EPS, ALU.mult, ALU.add, [ssq], [tt])
        _tt(P, POOL, rstd[:], tt[:], mh[:], ALU.pow, [tt, mh], [rstd])
        return rstd

    def norm_A(hs, hns, nt):
        for tl in range(nt):
            rstd = norm_rows(hs[tl])
            _ts(P, DVE, hns[tl][:], hs[tl][:], rstd[:, 0:1], None, ALU.mult, None, [hs[tl], rstd], [hns[tl]])

    def norm_B(hns, hnT, nt, gcol):
        n = nt * 128
        for c in range(8):
            half = cnt["T"] % 2
            cnt["T"] += 1
            pT = Tt[half][:, 0:512]
            res = Tt[half].res

            def fn(e, c=c, pT=pT):
                ins = None
                for tl in range(nt):
                    ins = e.transpose(out=pT[:, tl * 128:(tl + 1) * 128], in_=hns[tl][:, c * 128:(c + 1) * 128],
                                      identity=C.identb[:])
                return ins
            P.add(PE, fn, list(hns[:nt]) + [C.identb], [res])
            if c % 2 == 0:
                _ts(P, DVE, hnT[:, c, 0:n], pT[:, 0:n], gcol[:, c:c + 1], None, ALU.mult, None, [res, pcol], [hnT.sub(c)])
            else:
                P.add(ACT, lambda e, c=c, pT=pT: e.mul(out=hnT[:, c, 0:n], in_=pT[:, 0:n], mul=gcol[:, c:c + 1]),
                      [res, pcol], [hnT.sub(c)])

    def ag(in_t, out_t, rin, rout):
        P.cc(lambda e: e.collective_compute("AllGather", ALU.bypass, replica_groups=C.RG,
                                            ins=[in_t.ap().opt()], outs=[out_t.ap().opt()]),
             reads=[rin], writes=[rout])

    def scope():
        st = contextlib.ExitStack()
        return st, (lambda name, shape, dt: Buf(st.enter_context(nc.sbuf_tensor(name, list(shape), dt)), name))

    def ffn_phase(l):
        st, sbp = scope()
        with st:
            pre = "p%d_" % (1 + 2 * l)
            Wo = C.Wo[l]
            hs2 = [[sbp(pre + "h%d_%d" % (j, i), [128, D], F32) for i in range(4)] for j in range(2)]
            X0s = [sbp(pre + "X0_%d" % j, [128, 8, 512], BF16) for j in range(2)]
            Xcs = [sbp(pre + "Xc%d" % i, [128, 512], BF16) for i in range(2)]
            hns = [sbp(pre + "hn%d" % i, [128, D], BF16) for i in range(4)]
            hnT = sbp(pre + "hnT", [128, 8, 512], BF16)
            hnT1 = sbp(pre + "hnT1", [128, 8, 512], BF16) if l == 0 else None
            actT = sbp(pre + "actT", [128, NJ, 512], BF16)
            sils = [sbp(pre + "sil%d" % i, [128, 512], F32) for i in range(2)]
            W1b = [sbp(pre + "W1b%d" % i, [128, 2, 8, 256], BF16) for i in range(3)]
            W2b = [sbp(pre + "W2b%d" % i, [128, 2, D], BF16) for i in range(3)]
            orow = [sbp(pre + "orow%d" % i, [128, D], F32) for i in range(2)] if l == 1 else None
            NJP = NJ // 2
            S1 = Stream(P, W1b, [(lambda b: b[:], C.W1S[l, 2 * jp:2 * jp + 2].rearrange("j p k c -> p j k c"))
                                 for lb in range(NLB) for jp in range(NJP)])
            S2 = Stream(P, W2b, [(lambda b: b[:], C.W2S[l, 2 * jp:2 * jp + 2].rearrange("j p c -> p j c"))
                                 for lb in range(NLB) for jp in range(NJP)])
            S1.get(0)
            S2.get(0)
            hnTr = [hnT.sub(c) for c in range(8)]
            blocks = LBLK[:C.nblk]
            oc = [0]

            def stA(i):
                t0, n = blocks[i]
                nt = n // 128
                hs, X0 = hs2[i % 2], X0s[i % 2]
                for tl in range(nt):
                    if l == 0:
                        P.dma(hs[tl][:], C.h0h[t0 + tl * 128:t0 + (tl + 1) * 128, :], writes=[hs[tl]])
                    else:
                        P.dma(hs[tl][:], C.H1[t0 + tl * 128:t0 + (tl + 1) * 128, :], reads=[C.H1_res[i]], writes=[hs[tl]])
                yield
                for kc in range(8):
                    Xc = Xcs[kc % 2]
                    if l == 0:
                        rk, hl = kc // 4, 2 * (kc % 4)
                        for hh in range(2):
                            src = C.OGo[hl + hh][64 * rk:64 * rk + 64, :]
                            P.dma(X0[64 * hh:64 * hh + 64, kc, 0:n], src[:, t0:t0 + n], reads=[C.OGo_res[hl + hh]],
                                  writes=[X0.sub(kc)])
                            P.dma(Xc[64 * hh:64 * hh + 64, 0:n], src[:, HT * 128 + t0:HT * 128 + t0 + n],
                                  reads=[C.OGo_res[hl + hh]], writes=[Xc])
                    else:
                        rk, hd = kc // 4, kc % 4
                        P.dma(X0[:, kc, 0:n], C.OG1o[i][rk, :, hd, :], reads=[C.OG1o_res[i]], writes=[X0.sub(kc)])
                        P.dma(Xc[:, 0:n], C.OG1o[NLB + i][rk, :, hd, :], reads=[C.OG1o_res[NLB + i]], writes=[Xc])
                    P.add(DVE, lambda e, n=n, kc=kc, Xc=Xc, X0=X0: e.copy_predicated(
                        out=X0[:, kc, 0:n], mask=rmk[:, 0:n], data=Xc[:, 0:n]), [Xc, rmk, X0.sub(kc)], [X0.sub(kc)])
                    yield

            def stW(i):
                t0, n = blocks[i]
                hs, X = hs2[i % 2], X0s[i % 2]
                for tl in range(n // 128):
                    for hf in range(2):
                        y = ybank()
                        _mm(P, y[:, :], [(X[:, kc, tl * 128:(tl + 1) * 128], Wo[:, kc, hf * 512:(hf + 1) * 512])
                                         for kc in range(8)], [X.sub(kc) for kc in range(8)] + [Wo], [y])
                        _tt(P, DVE, hs[tl][:, hf * 512:(hf + 1) * 512], hs[tl][:, hf * 512:(hf + 1) * 512], y[:, :], ALU.add,
                            [hs[tl], y], [hs[tl]])

            hns1 = [sbp(pre + "hn1_%d" % i, [128, D], BF16) for i in range(4)] if l == 0 else None

            def stNa(i):
                t0, n = blocks[i]
                norm_A(hs2[i % 2], hns, n // 128)

            def stNb(i):
                t0, n = blocks[i]
                norm_B(hns, hnT, n // 128, gcols["ffn%d" % l])

            def stCin(i, gA=None):
                t0, n = blocks[i]
                for j in range(NJ):
                    if gA is not None and j % 2 == 0:
                        next(gA, None)
                    w = S1.get(i * NJP + j // 2)
                    jj = j % 2
                    g, u = fbank(), fbank()
                    _mm(P, g[:, 0:n], [(w[:, jj, kc, 0:128], hnT[:, kc, 0:n]) for kc in range(8)], [w] + hnTr, [g])
                    _mm(P, u[:, 0:n], [(w[:, jj, kc, 128:256], hnT[:, kc, 0:n]) for kc in range(8)], [w] + hnTr, [u])
                    sl = sils[j % 2]
                    _act(P, sl[:, 0:n], g[:, 0:n], AF.Silu, [g], [sl])
                    _tt(P, DVE, actT[:, j, 0:n], sl[:, 0:n], u[:, 0:n], ALU.mult, [sl, u], [actT.sub(j)])

            def stCout(i):
                t0, n = blocks[i]
                nt = n // 128
                hs = hs2[i % 2]
                for j in range(NJ):
                    w = S2.get(i * NJP + j // 2)

                    def fn(e, j=j, w=w):
                        ins = None
                        for tl in range(nt):
                            for hf in range(2):
                                ins = e.matmul(B[tl * 2 + hf][:, :], lhsT=actT[:, j, tl * 128:(tl + 1) * 128],
                                               rhs=w[:, j % 2, hf * 512:(hf + 1) * 512], start=(j == 0), stop=(j == NJ - 1))
                        return ins
                    P.add(PE, fn, [w, actT.sub(j)], sum([bres(k) for k in range(2 * nt)], []))
                for k in sorted(range(2 * nt), key=lambda k: -k):
                    tl, hf = k // 2, k % 2
                    _tt(P, DVE, hs[tl][:, hf * 512:(hf + 1) * 512], hs[tl][:, hf * 512:(hf + 1) * 512],
                        B[k][:, :], ALU.add, [hs[tl]] + bres(k), [hs[tl]])

            def stDa(i):
                t0, n = blocks[i]
                nt = n // 128
                hs = hs2[i % 2]
                if l == 0:
                    for tl in range(nt):
                        P.dma(C.H1[t0 + tl * 128:t0 + (tl + 1) * 128, :], hs[tl][:], reads=[hs[tl]], writes=[C.H1_res[i]])
                    norm_A(hs, hns1, nt)
                else:
                    for tl in range(nt):
                        rstd = norm_rows(hs[tl])
                        ob = orow[oc[0] % 2]
                        oc[0] += 1
                        _stt(P, ob[:], hs[tl][:], rstd[:, 0:1], gfin[:], ALU.mult, ALU.mult, [hs[tl], rstd, gfin], [ob])
                        P.dma(C.out[t0 + tl * 128:t0 + (tl + 1) * 128, :], ob[:], reads=[ob])

            def stDb(i):
                if l != 0:
                    return
                t0, n = blocks[i]
                nt = n // 128
                norm_B(hns1, hnT1, nt, gcols["attn1"])
                h1r = [hnT1.sub(c) for c in range(8)]
                if i == 0:
                    _ts(P, DVE, hnT1[:, :, 0:NPAD], hnT1[:, :, 0:NPAD], rmf[:, 0:1], None, ALU.mult, None, h1r + [rmf], h1r)
                P.dma(C.HNi[i], hnT1[:, :, 0:n], reads=h1r, writes=[C.HNi_res[i]])
                ag(C.HNi_t[i], C.HNo_t[i], C.HNi_res[i], C.HNo_res[i])

            nb = len(blocks)
            for _ in stA(0):
                pass
            stW(0)
            stNa(0)
            stNb(0)
            for i in range(nb):
                gA = stA(i + 1) if i + 1 < nb else None
                stCin(i, gA)
                if gA is not None:
                    for _ in gA:
                        pass
                if i > 0:
                    stDb(i - 1)
                if i + 1 < nb:
                    stW(i + 1)
                    stNa(i + 1)
                stCout(i)
                if i + 1 < nb:
                    stNb(i + 1)
                stDa(i)
            stDb(nb - 1)
            P.barrier()

    def hgrn_phase():
        st, sbp = scope()
        with st:
            hnTs = [sbp("p2_hnT%d" % i, [128, 8, 512], BF16) for i in range(2)]
            X1s = [sbp("p2_X1_%d" % i, [128, NHG, 512], BF16) for i in range(2)]
            WHb = [sbp("p2_WHb%d" % i, [128, 8, 512], BF16) for i in range(4)]
            silq = sbp("p2_silq", [128, NHG, 512], BF16)
            sgT = sbp("p2_sgT", [128, NHG, 512], BF16)
            Vsb2 = [[sbp("p2_V%d_%d" % (j, i), [128, NHG * 128], BF16) for i in range(4)] for j in range(2)]
            NBUF = 4
            mkb = lambda nm, dt: [sbp("p2_%s%d" % (nm, i), [128, 512], dt) for i in range(NBUF)]
            qts, kts, khs = mkb("qt", BF16), mkb("kt", BF16), mkb("kh", BF16)
            khTs = [sbp("p2_khT%d" % i, [128, 4, 128], BF16) for i in range(NBUF)]
            ezs, kks, ebs = mkb("ez", F32), mkb("kk", F32), mkb("eb", F32)
            mk2 = lambda nm, dt: [sbp("p2_%s%d" % (nm, i), [128, 512], dt) for i in range(2)]
            bbs, enbs, lnfs, ebcs = mk2("bb", F32), mk2("enb", F32), mk2("lnf", F32), mk2("ebc", F32)
            Asb4s = [sbp("p2_Asb4%d" % i, [128, 4, 128], BF16) for i in range(NBUF)]
            Usbs = mkb("Usb", F32)
            Sst = sbp("p2_S", [128, NHG, 128], F32)
            P.add(DVE, lambda e: e.memset(Sst[:], 0.0), [], [Sst])
            osqs = [sbp("p2_osq%d" % i, [128, 512], BF16) for i in range(2)]
            ons = [sbp("p2_on%d" % i, [128, 512], F32) for i in range(2)]
            rh = []
            for gb in range(2 * NLB):
                rh.append((lambda b: b[:], C.WHV[0]))
                for hd in range(NHG):
                    rh.append((lambda b: b[:, :, 0:384], C.WHS[hd]))
            SH = Stream(P, WHb, rh, keep=1)
            items = [(gb, hd) for gb in range(2 * NLB) for hd in range(NHG)]

            def geo(gb):
                rk, lb = gb // NLB, gb % NLB
                t0, n = LBLK[lb]
                return rk, lb, n, n // 128

            def h_front(i):
                gb, hd = items[i]
                rk, lb, n, nt = geo(gb)
                hnT = hnTs[gb % 2]
                hr = [hnT.sub(c) for c in range(8)]
                Vs = Vsb2[gb % 2]
                if hd == 0:
                    P.dma(hnT[:, :, 0:n], C.HNo[lb][rk], reads=[C.HNo_res[lb]], writes=hr, q=POOL)
                    w = SH.get(gb * (NHG + 1))
                    for tl in range(nt):
                        y = ybank()
                        _mm(P, y[:, :], [(hnT[:, kc, tl * 128:(tl + 1) * 128], w[:, kc, 0:512]) for kc in range(8)], [w] + hr, [y])
                        _copy(P, ACT if tl % 2 == 0 else DVE, Vs[tl][:, :], y[:, :], [y], [Vs[tl]])
                w = SH.get(gb * (NHG + 1) + 1 + hd)
                k3 = i % NBUF
                ez, kk, eb = ezs[k3], kks[k3], ebs[k3]
                lnf, bb, enb, ebc = [x[i % 2] for x in (lnfs, bbs, enbs, ebcs)]
                qt, kt, kh, khT, Asb4, Usb = qts[k3], kts[k3], khs[k3], khTs[k3], Asb4s[k3], Usbs[k3]
                qp, gp = hbank(), hbank()
                _mm(P, qp[:, 0:n], [(w[:, kc, 0:128], hnT[:, kc, 0:n]) for kc in range(8)], [w] + hr, [qp])
                _mm(P, gp[:, 0:n], [(w[:, kc, 256:384], hnT[:, kc, 0:n]) for kc in range(8)], [w] + hr, [gp])
                _act(P, silq[:, hd, 0:n], qp[:, 0:n], AF.Silu, [qp], [silq.sub(hd)])
                yield
                _act(P, sgT[:, hd, 0:n], gp[:, 0:n], AF.Silu, [gp], [sgT.sub(hd)])
                yield
                zp = hbank()
                _mm(P, zp[:, 0:n], [(w[:, kc, 128:256], hnT[:, kc, 0:n]) for kc in range(8)], [w] + hr, [zp])
                _act(P, ez[:, 0:n], zp[:, 0:n], AF.Exp, [zp], [ez])
                yield
                _act(P, ez[:, 0:n], ez[:, 0:n], AF.Ln, [ez], [ez], bias=1.0)
                yield
                _act(P, kk[:, 0:n], ez[:, 0:n], AF.Exp, [ez, lnomlb], [kk], scale=-1.0, bias=lnomlb[:, hd:hd + 1])
                yield
                _act(P, lnf[:, 0:n], kk[:, 0:n], AF.Ln, [kk], [lnf], scale=-1.0, bias=1.0)
                yield
                for c in range(nt):
                    P.add(DVE, lambda e, c=c: e.tensor_tensor_scan(
                        out=bb[:, c * 128:(c + 1) * 128], data0=onesf[:, 0:128], data1=lnf[:, c * 128:(c + 1) * 128],
                        initial=0.0, op0=ALU.mult, op1=ALU.add), [onesf, lnf], [bb])
                _act(P, eb[:, 0:n], bb[:, 0:n], AF.Exp, [bb], [eb])
                yield
                _act(P, enb[:, 0:n], bb[:, 0:n], AF.Exp, [bb], [enb], scale=-1.0)
                yield
                for c in range(nt):
                    _act(P, ebc[:, c * 128:(c + 1) * 128], bb[:, c * 128:(c + 1) * 128], AF.Exp, [bb], [ebc], scale=-1.0,
                         bias=bb[:, c * 128 + 127:c * 128 + 128])
                _tt(P, DVE, qt[:, 0:n], silq[:, hd, 0:n], eb[:, 0:n], ALU.mult, [silq.sub(hd), eb], [qt])
                yield
                _tt(P, DVE, kt[:, 0:n], kk[:, 0:n], enb[:, 0:n], ALU.mult, [kk, enb], [kt])
                yield
                _tt(P, DVE, kh[:, 0:n], kk[:, 0:n], ebc[:, 0:n], ALU.mult, [kk, ebc], [kh])
                yield
                pT = Tt[0][:, 0:512]
                res = Tt[0].res

                def fnT(e):
                    ins = None
                    for c in range(nt):
                        ins = e.transpose(out=pT[:, c * 128:(c + 1) * 128], in_=kh[:, c * 128:(c + 1) * 128], identity=C.identb[:])
                    return ins
                P.add(PE, fnT, [kh, C.identb], [res])
                _copy(P, ACT, khT[:, 0:nt, :], pT[:, 0:n].rearrange("p (c s) -> p c s", s=128), [res], [khT])
                yield
                Ab, Ub = B[5], B[7]

                def fnA(e):
                    ins = None
                    for c in range(nt):
                        cs = slice(c * 128, (c + 1) * 128)
                        ins = e.matmul(Ab[:, cs], lhsT=kt[:, cs], rhs=qt[:, cs], start=True, stop=True)
                    return ins
                P.add(PE, fnA, [kt, qt], [Ab])
                P.add(DVE, lambda e: e.tensor_tensor(
                    out=Asb4[:, 0:nt, :], in0=Ab[:, 0:n].rearrange("p (c s) -> p c s", s=128),
                    in1=C.tri[:, :].unsqueeze(1).to_broadcast([128, nt, 128]), op=ALU.mult), [Ab, C.tri], [Asb4])

                def fnU(e):
                    ins = None
                    for c in range(nt):
                        cs = slice(c * 128, (c + 1) * 128)
                        ins = e.matmul(Ub[:, cs], lhsT=khT[:, c, :], rhs=Vs[c][:, hd * 128:(hd + 1) * 128], start=True, stop=True)
                    return ins
                P.add(PE, fnU, [khT] + list(Vs[:nt]), [Ub])
                _copy(P, ACT, Usb[:, 0:n], Ub[:, 0:n], [Ub], [Usb])
                yield

            Sbf4s = [sbp("p2_Sbf4%d" % i, [128, 4, 128], BF16) for i in range(NBUF)]

            def h_state(i):
                gb, hd = items[i]
                rk, lb, n, nt = geo(gb)
                k3 = i % NBUF
                eb, Usb, Sbf4 = ebs[k3], Usbs[k3], Sbf4s[k3]
                for c in range(nt):
                    cs = slice(c * 128, (c + 1) * 128)
                    _copy(P, DVE, Sbf4[:, c, :], Sst[:, hd, :], [Sst.sub(hd)], [Sbf4])
                    _stt(P, Sst[:, hd, :], Sst[:, hd, :], eb[:, c * 128 + 127:c * 128 + 128], Usb[:, cs], ALU.mult, ALU.add,
                         [Sst.sub(hd), eb, Usb], [Sst.sub(hd)])

            def h_back(i):
                gb, hd = items[i]
                rk, lb, n, nt = geo(gb)
                Vs = Vsb2[gb % 2]
                k3 = i % NBUF
                qt, Asb4, Sbf4 = qts[k3], Asb4s[k3], Sbf4s[k3]
                X1 = X1s[gb % 2]
                osq, on = osqs[i % 2], ons[i % 2]
                op = B[3 + hd % 2]

                def fnO(e):
                    ins = None
                    for c in range(nt):
                        cs = slice(c * 128, (c + 1) * 128)
                        e.matmul(op[:, cs], lhsT=Vs[c][:, hd * 128:(hd + 1) * 128], rhs=Asb4[:, c, :], start=True, stop=False)
                        ins = e.matmul(op[:, cs], lhsT=Sbf4[:, c, :], rhs=qt[:, cs], start=False, stop=True)
                    return ins
                P.add(PE, fnO, [Sbf4, qt, Asb4] + list(Vs[:nt]), [op])
                yield
                _act(P, osq[:, 0:n], op[:, 0:n], AF.Square, [op], [osq])
                yield
                sp = hbank()
                _mm(P, sp[:, 0:n], [(ones128b[:], osq[:, 0:n])], [ones128b, osq], [sp])
                lnv, rs = ezs[k3], kks[k3]
                _act(P, lnv[:, 0:n], sp[:, 0:n], AF.Ln, [sp], [lnv], scale=1.0 / 128, bias=EPS)
                yield
                _act(P, rs[:, 0:n], lnv[:, 0:n], AF.Exp, [lnv], [rs], scale=-0.5)
                yield
                _stt(P, on[:, 0:n], op[:, 0:n], gn[:, 0:1], rs[:, 0:n], ALU.mult, ALU.mult, [op, gn, rs], [on])
                yield
                _tt(P, DVE, X1[:, hd, 0:n], on[:, 0:n], sgT[:, hd, 0:n], ALU.mult, [on, sgT.sub(hd)], [X1])
                yield
                if hd == NHG - 1:
                    P.dma(C.OG1i[gb], X1[:, :, 0:n], reads=[X1], writes=[C.OG1i_res[gb]])
                    ag(C.OG1i_t[gb], C.OG1o_t[gb], C.OG1i_res[gb], C.OG1o_res[gb])

            def zipped(gens, delays=None):
                gens = list(gens)
                delays = dict(zip(map(id, gens), delays or [0] * len(gens)))
                rnd = 0
                while gens:
                    for g in list(gens):
                        if delays[id(g)] > rnd:
                            continue
                        try:
                            next(g)
                        except StopIteration:
                            gens.remove(g)
                    rnd += 1

            N = len(items)
            npair = N // 2
            zipped([h_front(0), h_front(1)])
            h_state(0)
            h_state(1)
            for t in range(npair):
                gens = [h_back(2 * t), h_back(2 * t + 1)]
                dl = [0, 0]
                if t + 1 < npair:
                    gens = [h_front(2 * t + 2), h_front(2 * t + 3)] + gens
                    dl = [0, 0, 5, 5]
                zipped(gens, dl)
                if t + 1 < npair:
                    h_state(2 * t + 2)
                    h_state(2 * t + 3)
            P.barrier()

    ffn_phase(0)
    if C.s3parts >= 2:
        hgrn_phase()
    if C.s3parts >= 3:
        ffn_phase(1)
```

```python
import numpy as np
import concourse.bass as bass
import concourse.mybir as mybir
from concourse.bass_utils import run_bass_kernel_spmd

F32 = mybir.dt.float32
BF16 = mybir.dt.bfloat16
AF = mybir.ActivationFunctionType
ALU = mybir.AluOpType
AX = mybir.AxisListType

PE, ACT, DVE, POOL, SP = "pe", "act", "dve", "pool", "sp"
ENGS = (PE, ACT, DVE, POOL, SP)
NDMA_SEM = 12


class Res:
    __slots__ = ("w", "r")

    def __init__(self):
        self.w = None
        self.r = []


class Buf:
    def __init__(self, t, name=""):
        self.t = t
        self.name = name
        self.res = Res()
        self.subs = {}

    def __getitem__(self, k):
        return self.t[k]

    def sub(self, key):
        r = self.subs.get(key)
        if r is None:
            r = self.subs[key] = Res()
        return r


def _res(x):
    return x.res if isinstance(x, Buf) else x


class Op:
    __slots__ = ("eng", "fn", "raw", "oth", "dma", "sigval", "need", "dsem", "dval", "dprev", "pos")


class Prog:
    def __init__(self, nc):
        self.nc = nc
        self.ops = {e: [] for e in ENGS}
        self.ndma = {e: 0 for e in ENGS}

    def add(self, eng, fn, reads=(), writes=(), dma=False):
        op = Op()
        op.eng, op.fn, op.dma = eng, fn, dma
        op.raw, op.oth = set(), set()
        op.need = False
        op.sigval = None
        for r in reads:
            r = _res(r)
            if r.w is not None:
                op.raw.add(r.w)
        for w in writes:
            w = _res(w)
            if w.w is not None:
                op.oth.add(w.w)
            for x in w.r:
                op.oth.add(x)
        for r in reads:
            _res(r).r.append(op)
        for w in writes:
            w = _res(w)
            w.w = op
            w.r = []
        if dma:
            i = self.ndma[eng]
            self.ndma[eng] += 1
            op.dsem = i % NDMA_SEM
            op.dval = 16 * (i // NDMA_SEM + 1)
        self.ops[eng].append(op)
        return op

    def cc(self, fn, reads=(), writes=()):
        op = self.add(POOL, fn, reads, writes, dma=True)
        self.ndma[POOL] -= 1
        self.ncc = getattr(self, "ncc", 0) + 1
        op.dsem = "cc"
        op.dval = self.ncc
        return op

    def barrier(self):
        last = {}
        for e in ENGS:
            lc = None
            for o in reversed(self.ops[e]):
                if isinstance(o, Op) and not o.dma:
                    lc = o
                    break
            if lc is not None:
                lc.need = True
            last[e] = lc
        mark = ("barrier", last, dict(self.ndma), getattr(self, "ncc", 0))
        for e in ENGS:
            self.ops[e].append(mark)

    def dma(self, out, in_, reads=(), writes=(), q=SP, **kw):
        return self.add(q, lambda e: e.dma_start(out=out, in_=in_, **kw), reads, writes, dma=True)

    def _needed(self, o, d):
        if d.dma:
            return True
        if d.eng != o.eng:
            return True
        if o.dma:
            return True
        if o.eng == PE:
            return False
        return d in o.raw

    def _deps(self, o):
        best = {}
        out = []
        for d in list(o.raw) + list(o.oth):
            if not self._needed(o, d):
                continue
            if d.dma:
                out.append(d)
            else:
                b = best.get(d.eng)
                if b is None or d.pos > b.pos:
                    best[d.eng] = d
        return out + list(best.values())

    def emit(self):
        nc = self.nc
        for e in ENGS:
            for k, o in enumerate(self.ops[e]):
                if isinstance(o, Op):
                    o.pos = k
        for e in ENGS:
            for o in self.ops[e]:
                if not isinstance(o, Op):
                    continue
                for d in self._deps(o):
                    if not d.dma:
                        d.need = True
        for e in ENGS:
            c = 0
            for o in self.ops[e]:
                if isinstance(o, Op) and (not o.dma) and o.need:
                    c += 1
                    o.sigval = c
        import contextlib
        with contextlib.ExitStack() as st:
            esem = {e: st.enter_context(nc.semaphore("s_" + e)) for e in ENGS}
            dsem = {e: [st.enter_context(nc.semaphore("d_%s%d" % (e, i))) for i in range(NDMA_SEM)]
                    for e in ENGS if self.ndma[e] > 0}
            ccsem = st.enter_context(nc.semaphore("s_cc"))
            block = st.enter_context(nc.Block())

            def run(ename, eng):
                seen = {}

                def wait(sem, val):
                    k = id(sem)
                    if seen.get(k, 0) >= val:
                        return
                    seen[k] = val
                    eng.wait_ge(sem, val)

                for o in self.ops[ename]:
                    if not isinstance(o, Op):
                        _, last, nd, ncc = o
                        if ncc > 0:
                            wait(ccsem, ncc)
                        for e2 in ENGS:
                            if last[e2] is not None:
                                wait(esem[e2], last[e2].sigval)
                            n = nd[e2]
                            for i in range(NDMA_SEM):
                                cnt = (n - i + NDMA_SEM - 1) // NDMA_SEM if n > i else 0
                                if cnt > 0:
                                    wait(dsem[e2][i], 16 * cnt)
                        continue
                    for d in self._deps(o):
                        if d.dma and d.dsem == "cc":
                            wait(ccsem, d.dval)
                        elif d.dma:
                            wait(dsem[d.eng][d.dsem], d.dval)
                        else:
                            wait(esem[d.eng], d.sigval)
                    if o.dma and o.dsem == "cc":
                        if o.dval > 1:
                            wait(ccsem, o.dval - 1)
                        o.fn(eng).then_inc(ccsem)
                    elif o.dma:
                        s = dsem[ename][o.dsem]
                        if o.dval > 16:
                            wait(s, o.dval - 16)
                        o.fn(eng).then_inc(s, 16)
                    else:
                        ins = o.fn(eng)
                        if o.need:
                            ins.then_inc(esem[ename], 1)
                if ename == POOL and getattr(self, "ncc", 0) > 0:
                    wait(ccsem, self.ncc)
                if self.ndma[ename] > 0:
                    n = self.ndma[ename]
                    for i in range(NDMA_SEM):
                        cnt = (n - i + NDMA_SEM - 1) // NDMA_SEM if n > i else 0
                        if cnt > 0:
                            wait(dsem[ename][i], 16 * cnt)

            @block.tensor
            def _(eng):
                run(PE, eng)

            @block.scalar
            def _(eng):
                run(ACT, eng)

            @block.vector
            def _(eng):
                run(DVE, eng)

            @block.gpsimd
            def _(eng):
                run(POOL, eng)

            @block.sync
            def _(eng):
                run(SP, eng)
import contextlib
import ml_dtypes

D = 1024
H = 16
DH = 64
NMETA = 16
SEQ = 8192
NPAD = 112
LP = NPAD + NMETA + SEQ
NT = LP // 128
FF = 2816
NJ = FF // 128
EPS = 1e-6
BLOCKS = [(0, 128)] + [(128 + 512 * i, 512) for i in range(16)]
NB = len(BLOCKS)


import os
DBG = os.environ.get('KDBG', '')


class Ctx:
    pass


def _mm(P, out_ap, pairs, reads, writes, start=True, stop=True):
    def fn(e):
        n = len(pairs)
        ins = None
        for i, (l, r) in enumerate(pairs):
            ins = e.matmul(out_ap, lhsT=l, rhs=r, start=(start and i == 0), stop=(stop and i == n - 1))
        return ins
    return P.add(PE, fn, reads, writes)


def _act(P, out, in_, func, reads, writes, **kw):
    return P.add(ACT, lambda e: e.activation(out=out, in_=in_, func=func, **kw), reads, writes)


def _tt(P, eng, out, in0, in1, op, reads, writes):
    return P.add(eng, lambda e: e.tensor_tensor(out=out, in0=in0, in1=in1, op=op), reads, writes)


def _ts(P, eng, out, in0, s1, s2, op0, op1, reads, writes):
    if op1 is None:
        return P.add(eng, lambda e: e.tensor_scalar(out=out, in0=in0, scalar1=s1, scalar2=None, op0=op0), reads, writes)
    return P.add(eng, lambda e: e.tensor_scalar(out=out, in0=in0, scalar1=s1, scalar2=s2, op0=op0, op1=op1), reads, writes)


def _stt(P, out, in0, scalar, in1, op0, op1, reads, writes):
    return P.add(DVE, lambda e: e.scalar_tensor_tensor(out=out, in0=in0, scalar=scalar, in1=in1, op0=op0, op1=op1), reads, writes)


def _copy(P, eng, out, in_, reads, writes):
    if eng == ACT:
        return P.add(ACT, lambda e: e.copy(out=out, in_=in_), reads, writes)
    return P.add(eng, lambda e: e.tensor_copy(out=out, in_=in_), reads, writes)


def _rmsnorm_rows(P, C, ht, hn, gbc, tmp):
    junk, ssq, lnv, rstd = tmp
    _act(P, junk[:], ht[:], AF.Square, [ht], [junk, ssq], accum_out=ssq[:])
    _act(P, lnv[:], ssq[:], AF.Ln, [ssq], [lnv], scale=1.0 / D, bias=EPS)
    _act(P, rstd[:], lnv[:], AF.Exp, [lnv], [rstd], scale=-0.5)
    _stt(P, hn[:], ht[:], rstd[:, 0:1], gbc[:], ALU.mult, ALU.mult, [ht, rstd, gbc], [hn])


def _transpose_block(P, C, hns, nt, hnT, pTs, cnt0):
    n = nt * 128
    for c in range(8):
        pT = pTs[(cnt0 + c) % len(pTs)]

        def fn(e, c=c, pT=pT):
            ins = None
            for tl in range(nt):
                ins = e.transpose(out=pT[:, tl * 128:(tl + 1) * 128], in_=hns[tl][:, c * 128:(c + 1) * 128],
                                  identity=C.identb[:])
            return ins
        P.add(PE, fn, list(hns[:nt]) + [C.identb], [pT])
        _copy(P, ACT if c % 2 == 0 else DVE, hnT[:, c, 0:n], pT[:, 0:n], [pT], [hnT.sub(c)])


def stage1(P, nc, C, sb, ps, bounce=None):
    NHL = C.nheads
    NQ = NHL // 2
    WC = 4 * NHL * DH + NHL
    W = sb("s1_W", [128, 8, WC], BF16)
    wst = [sb("s1_wst%d" % i, [128, WC], F32) for i in range(2)]
    Wr = [W.sub(k) for k in range(8)]
    gbc = sb("s1_gbc", [128, D], F32)
    P.dma(gbc[:], C.attn_norm[0].partition_broadcast(128), writes=[gbc])
    gcol = sb("s1_gcol", [128, 2], F32)
    for hh in range(2):
        P.dma(gcol[64 * hh:64 * hh + 64, 0:1], C.fox_q_norm[0].rearrange("(p o) -> p o", o=1), writes=[gcol])
        P.dma(gcol[64 * hh:64 * hh + 64, 1:2], C.fox_k_norm[0].rearrange("(p o) -> p o", o=1), writes=[gcol])
    _ts(P, DVE, gcol[:, 0:1], gcol[:, 0:1], 0.125, None, ALU.mult, None, [gcol], [gcol])
    negbf = sb("s1_negbf", [NHL, 1], F32)
    P.dma(negbf[:], C.fox_b_f[0].rearrange("(p o) -> p o", o=1), writes=[negbf])
    _ts(P, DVE, negbf[:], negbf[:], -1.0, None, ALU.mult, None, [negbf], [negbf])
    ones16 = sb("s1_ones16", [NHL, 512], F32)
    P.add(DVE, lambda e: e.memset(ones16[:], 1.0), [], [ones16])
    zero16 = sb("s1_zero16", [NHL, 1], F32)
    P.add(DVE, lambda e: e.memset(zero16[:], 0.0), [], [zero16])

    hts = [sb("s1_ht%d" % i, [128, D], F32) for i in range(2)]
    hns = [sb("s1_hn%d" % i, [128, D], BF16) for i in range(8)]
    junk = sb("s1_junk", [128, D], BF16)
    ssqs = [[sb("s1_ssq%d_%d" % (i, j), [128, 1], F32) for j in range(3)] for i in range(2)]
    hnTs = [sb("s1_hnT%d" % i, [128, 8, 512], BF16) for i in range(2)]
    pTs = [ps("s1_pT%d" % i, [128, 512], BF16) for i in range(2)]
    psA = [ps("s1_psA%d" % i, [128, 512], F32) for i in range(4)]
    psSs = [ps("s1_psS%d" % i, [128, 512], F32) for i in range(2)]
    sqs = [sb("s1_sq%d" % i, [128, 512], BF16) for i in range(2)]
    lnvs = [sb("s1_lnv%d" % i, [128, 512], F32) for i in range(2)]
    rss = [sb("s1_rs%d" % i, [128, 512], F32) for i in range(2)]
    outs = [sb("s1_out%d" % i, [128, 512], BF16) for i in range(6)]
    vsts = [sb("s1_vst%d" % i, [128, NHL, 4, 65], BF16) for i in range(2)]
    for v in vsts:
        P.add(POOL, lambda e, v=v: e.memset(v[:, :, :, 64:65], 1.0), [], [v])
    ef = sb("s1_ef", [NHL, 512], F32)
    lf = sb("s1_lf", [NHL, 512], F32)
    cps = [sb("s1_cp%d" % i, [NHL, 512], F32) for i in range(2)]
    hsp = [[sb("s1_hsp%d_%d" % (i, j), [NHL, 512], BF16) for j in range(6)] for i in range(2)]
    r1 = sb("s1_r1", [NHL, 512], F32)
    r2 = sb("s1_r2", [NHL, 512], F32)
    onesb = sb("s1_onesb", [3, LP // 4], BF16)
    P.add(DVE, lambda e: e.memset(onesb[:], 1.0), [], [onesb])
    for h in range(NHL):
        for qd in range(4):
            cs_ = slice(qd * (LP // 4), (qd + 1) * (LP // 4))
            P.dma(C.KT_d[h, 67:70, cs_], onesb[0:3, :], reads=[onesb], q=POOL)
            P.dma(C.QT_d[h, 64:67, cs_], onesb[0:3, :], reads=[onesb], q=POOL)

    cA = 0
    cO = 0
    cT = 0
    cTh = [0]

    def fa_load(bi, tl):
        t0, n = BLOCKS[bi]
        if bi >= NB or tl >= n // 128:
            return
        ht = hts[tl % 2]
        P.dma(ht[:], C.h0[t0 + tl * 128:t0 + (tl + 1) * 128, :], writes=[ht])

    def fa_norm(bi, tl):
        t0, n = BLOCKS[bi]
        if bi >= NB or tl >= n // 128:
            return
        myhn = hns[(bi % 2) * 4:(bi % 2) * 4 + 4]
        tmp = [junk] + ssqs[cTh[0] % 2]
        cTh[0] += 1
        _rmsnorm_rows(P, C, hts[tl % 2], myhn[tl], gbc, tmp)

    def s1_front_a(bi):
        for tl in range(4):
            fa_load(bi, tl)
            fa_norm(bi, tl)

    def s1_front_b(bi):
        t0, n = BLOCKS[bi]
        myhn = hns[(bi % 2) * 4:(bi % 2) * 4 + 4]
        _transpose_block(P, C, myhn, n // 128, hnTs[bi % 2], pTs, 0)

    s1_front_a(0)
    for kc in range(8):
        P.dma(wst[kc % 2][:], C.fox_w_in[kc * 128:(kc + 1) * 128, :], writes=[wst[kc % 2]])
        _copy(P, ACT if kc % 2 == 0 else DVE, W[:, kc, :], wst[kc % 2][:], [wst[kc % 2]], [W.sub(kc)])
    s1_front_b(0)
    for bi, (t0, n) in enumerate(BLOCKS):
        nt = n // 128
        hnT = hnTs[bi % 2]
        if bi + 1 < NB:
            fa_load(bi + 1, 0)
            fa_load(bi + 1, 1)
        hr = [hnT.sub(c) for c in range(8)]
        pqs = {}

        def qk_a(c):
            nonlocal cA
            cols = c * 128
            pq = psA[cA % 4]
            cA += 1
            pqs[c] = pq
            _mm(P, pq[:, 0:n], [(W[:, kc, cols:cols + 128], hnT[:, kc, 0:n]) for kc in range(8)], Wr + hr, [pq])
            _act(P, sqs[c % 2][:, 0:n], pq[:, 0:n], AF.Square, [pq], [sqs[c % 2]])

        def qk_b(c):
            nonlocal cO
            pq = pqs[c]
            sq, lnv, rs = sqs[c % 2], lnvs[c % 2], rss[c % 2]
            psS = psSs[c % 2]
            _mm(P, psS[:, 0:n], [(C.bdones[:], sq[:, 0:n])], [C.bdones, sq], [psS])
            _act(P, lnv[:, 0:n], psS[:, 0:n], AF.Ln, [psS], [lnv], scale=1.0 / DH, bias=EPS)
            _act(P, rs[:, 0:n], lnv[:, 0:n], AF.Exp, [lnv], [rs], scale=-0.5)
            ob = outs[cO % 6]
            cO += 1
            gi = 0 if c < NQ else 1
            _stt(P, ob[:, 0:n], pq[:, 0:n], gcol[:, gi:gi + 1], rs[:, 0:n], ALU.mult, ALU.mult, [pq, gcol, rs], [ob])
            dst = C.QT_d if c < NQ else C.KT_d
            for hh in range(2):
                h = (c % NQ) * 2 + hh
                P.dma(dst[h, 0:64, t0:t0 + n], ob[64 * hh:64 * hh + 64, 0:n], reads=[ob], q=POOL)

        qk_a(0)
        for c in range(2 * NQ):
            if c + 1 < 2 * NQ:
                qk_a(c + 1)
            qk_b(c)
        if bi + 1 < NB:
            fa_norm(bi + 1, 0)
            fa_norm(bi + 1, 1)
            fa_load(bi + 1, 2)
            fa_load(bi + 1, 3)
        for c in range(NQ):
            cols = 3 * NHL * DH + c * 128
            pg = psA[cA % 4]
            cA += 1
            _mm(P, pg[:, 0:n], [(W[:, kc, cols:cols + 128], hnT[:, kc, 0:n]) for kc in range(8)], Wr + hr, [pg])
            lnv, rs = lnvs[c % 2], rss[c % 2]
            _act(P, rs[:, 0:n], pg[:, 0:n], AF.Exp, [pg], [rs], scale=-1.0)
            _act(P, lnv[:, 0:n], rs[:, 0:n], AF.Ln, [rs], [lnv], bias=1.0)
            ob = outs[cO % 6]
            cO += 1
            _act(P, ob[:, 0:n], lnv[:, 0:n], AF.Exp, [lnv], [ob], scale=-1.0)
            P.dma(C.SG_d[c * 128:(c + 1) * 128, t0:t0 + n], ob[:, 0:n], reads=[ob], q=POOL)
        pf = psA[cA % 4]
        cA += 1
        _mm(P, pf[0:NHL, 0:n], [(W[:, kc, 4 * NHL * DH:WC], hnT[:, kc, 0:n]) for kc in range(8)], Wr + hr, [pf])
        _act(P, ef[:, 0:n], pf[0:NHL, 0:n], AF.Exp, [pf], [ef], scale=-1.0, bias=negbf[:, 0:1])
        _act(P, lf[:, 0:n], ef[:, 0:n], AF.Ln, [ef], [lf], bias=1.0)
        cp = cps[bi % 2]
        if bi == 0:
            ref_ap, ref_b = zero16[:, 0:1], zero16
        else:
            pn = BLOCKS[bi - 1][1]
            ref_ap, ref_b = cps[(bi - 1) % 2][:, pn - 1:pn], cps[(bi - 1) % 2]
        P.add(DVE, lambda e, cp=cp, ref_ap=ref_ap, n=n: e.tensor_tensor_scan(
            out=cp[:, 0:n], data0=ones16[:, 0:n], data1=lf[:, 0:n], initial=ref_ap, op0=ALU.mult, op1=ALU.add),
            [ones16, lf, ref_b], [cp])
        hh = hsp[bi % 2]
        _copy(P, DVE, hh[0][:, 0:n], cp[:, 0:n], [cp], [hh[0]])
        _tt(P, DVE, r1[:, 0:n], cp[:, 0:n], hh[0][:, 0:n], ALU.subtract, [cp, hh[0]], [r1])
        _copy(P, DVE, hh[1][:, 0:n], r1[:, 0:n], [r1], [hh[1]])
        _tt(P, DVE, r2[:, 0:n], r1[:, 0:n], hh[1][:, 0:n], ALU.subtract, [r1, hh[1]], [r2])
        _copy(P, DVE, hh[2][:, 0:n], r2[:, 0:n], [r2], [hh[2]])
        for j in range(3):
            _ts(P, DVE, hh[3 + j][:, 0:n], hh[j][:, 0:n], -1.0, None, ALU.mult, None, [hh[j]], [hh[3 + j]])
            P.dma(C.KT_d[:, 64 + j, t0:t0 + n], hh[j][:, 0:n], reads=[hh[j]])
            P.dma(C.QT_d[:, 67 + j, t0:t0 + n], hh[3 + j][:, 0:n], reads=[hh[3 + j]])
        if bi + 1 < NB:
            fa_norm(bi + 1, 2)
            fa_norm(bi + 1, 3)
        vst = vsts[bi % 2]
        if bi == 0:
            P.add(POOL, lambda e, v=vst: e.memset(v[0:NPAD, :, 0:1, 64:65], 0.0), [], [vst])
        for tl in range(nt):
            for half in range(NHL // 8):
                pv = psA[cA % 4]
                cA += 1
                cols = 2 * NHL * DH + half * 512
                _mm(P, pv[:, :], [(hnT[:, kc, tl * 128:(tl + 1) * 128], W[:, kc, cols:cols + 512]) for kc in range(8)],
                    Wr + hr, [pv])
                _copy(P, DVE if half == 0 else ACT, vst[:, 8 * half:8 * half + 8, tl, 0:64],
                      pv[:, :].rearrange("p (h d) -> p h d", h=8), [pv], [vst])
        for h in range(NHL):
            P.dma(C.VA_d[h, :, t0 // 128:t0 // 128 + nt, :], vst[:, h, 0:nt, :], reads=[vst])
        if bi == 0:
            P.add(POOL, lambda e, v=vst: e.memset(v[0:NPAD, :, 0:1, 64:65], 1.0), [], [vst])
        if bi + 1 < NB:
            s1_front_b(bi + 1)


def stage2(P, nc, C, sb, ps, bounce=None):
    for kc in range(8):
        P.dma(C.Wo[0][:, kc, :], C.fox_w_out[kc * 128:(kc + 1) * 128, :], writes=[C.Wo[0]], q=POOL)
        P.dma(C.Wo[1][:, kc, :], C.hgrn_w_out[kc * 128:(kc + 1) * 128, :], writes=[C.Wo[1]], q=POOL)
    gate_t = sb("s2_gate", [128, 1], F32)
    gate = [Res()]
    conv = convert_weights(P, nc, C, bounce, gate) if bounce is not None else iter(())
    KTb = [sb("s2_KT%d" % i, [70, LP], BF16) for i in range(2)]
    VAb = [sb("s2_VA%d" % i, [128, NT, 65], BF16) for i in range(2)]
    Qbs = [sb("s2_Q%d" % i, [70, 512], BF16) for i in range(3)]
    sgs = [sb("s2_sg%d" % i, [64, 512], BF16) for i in range(3)]
    biases = [sb("s2_bias%d" % i, [128, NT], F32) for i in range(3)]
    Pts = [sb("s2_Pt%d" % i, [128, 512], BF16) for i in range(6)]
    Sb = [ps("s2_S%d" % i, [128, 512], F32) for i in range(5)]
    Ob = [ps("s2_O%d" % i, [128, 512], F32) for i in range(2)]
    Rps = ps("s2_R", [128, 512], F32)
    rds = [sb("s2_rd%d" % i, [128, 512], F32) for i in range(2)]
    rd2s = [sb("s2_rd2%d" % i, [128, 512], F32) for i in range(2)]
    pending = []
    onesf = sb("s2_onesf", [128, 64], F32)
    P.add(DVE, lambda e: e.memset(onesf[:], 1.0), [], [onesf])
    zt2 = sb("s2_zt", [64, 128], BF16)
    P.add(DVE, lambda e: e.memset(zt2[:], 0.0), [], [zt2])
    for h in range(C.nheads):
        P.dma(C.OGi[h][:, LP:LP + 128], zt2[:], reads=[zt2], writes=[C.OGi_res[h]])
    Osbs = [sb("s2_Osb%d" % i, [64, 512], F32) for i in range(2)]
    og1s = [sb("s2_og1%d" % i, [64, 512], F32) for i in range(2)]
    ogbs = [sb("s2_ogb%d" % i, [64, 512], BF16) for i in range(2)]

    items = []
    for h in range(C.nheads):
        for bi, (t0, n) in enumerate(BLOCKS):
            nkt = (t0 + n) // 128
            for kt in range(nkt):
                items.append((h, bi, kt, nkt))
    state = {}

    def prologue(h, bi):
        t0, n = BLOCKS[bi]
        g = h * NB + bi
        if g in state or g >= C.nheads * NB:
            return
        state[g] = True
        if g % 3 == 0:
            gate[0] = Res()
            P.add(DVE, lambda e: e.memset(gate_t[:], 0.0), [], [gate[0]])
            next(conv, None)
        if bi == 0:
            KTh, VAh = KTb[h % 2], VAb[h % 2]
            P.dma(KTh[:, :], C.KT_d[h], writes=[KTh])
            P.dma(VAh[:], C.VA_d[h], writes=[VAh])
        Qb, sg, bias = Qbs[g % 3], sgs[g % 3], biases[g % 3]
        P.dma(Qb[0:70, 0:n], C.QT_d[h, :, t0:t0 + n], writes=[Qb])
        P.dma(sg[0:64, 0:n], C.SG_d[64 * h:64 * h + 64, t0:t0 + n], writes=[sg])

    def qk(i):
        h, bi, kt, nkt = items[i]
        t0, n = BLOCKS[bi]
        g = h * NB + bi
        if kt == 0:
            prologue(h, bi)
        j = kt - t0 // 128
        c0 = 128 * j if j >= 0 else 0
        S = Sb[i % 5]
        _mm(P, S[:, c0:n], [(KTb[h % 2][0:70, kt * 128:(kt + 1) * 128], Qbs[g % 3][0:70, c0:n])],
            [KTb[h % 2], Qbs[g % 3]], [S])

    def rest(i):
        h, bi, kt, nkt = items[i]
        t0, n = BLOCKS[bi]
        g = h * NB + bi
        j = kt - t0 // 128
        c0 = 128 * j if j >= 0 else 0
        S, Pt, O = Sb[i % 5], Pts[i % 6], Ob[g % 2]
        bias = biases[g % 3]
        if kt == 0:
            while pending and pending[0][0] <= g - 2:
                epilogue2(pending.pop(0)[0])
            prologue((g + 1) // NB, (g + 1) % NB)
        _act(P, Pt[:, c0:n], S[:, c0:n], AF.Exp, [S], [Pt])
        if j >= 0:
            _tt(P, DVE, Pt[:, c0:c0 + 128], Pt[:, c0:c0 + 128], C.tri[:], ALU.mult, [Pt, C.tri], [Pt])
        _mm(P, O[0:65, c0:n], [(VAb[h % 2][:, kt, 0:65], Pt[:, c0:n])], [VAb[h % 2], Pt], [O],
            start=(kt == 0), stop=(kt == nkt - 1))
        if kt == nkt - 1:
            Osb = Osbs[g % 2]
            rd, rd2 = rds[g % 2], rd2s[g % 2]
            _ts(P, DVE, rd[64:65, 0:n], O[64:65, 0:n], 1e-30, None, ALU.max, None, [O], [rd])
            P.add(DVE, lambda e: e.reciprocal(out=rd2[64:65, 0:n], in_=rd[64:65, 0:n]), [rd], [rd2])
            _copy(P, ACT, Osb[:, 0:n], O[0:64, 0:n], [O], [Osb])
            pending.append((g, i))
        while pending and (i - pending[0][1] >= 5 or i == len(items) - 1):
            epilogue2(pending.pop(0)[0])

    def epilogue2(g):
        h, bi = g // NB, g % NB
        t0, n = BLOCKS[bi]
        Osb, og1, ogb, sg = Osbs[g % 2], og1s[g % 2], ogbs[g % 2], sgs[g % 3]
        rd2 = rd2s[g % 2]
        _mm(P, Rps[0:64, 0:n], [(onesf[64:65, 0:64], rd2[64:65, 0:n])], [onesf, rd2], [Rps])
        _tt(P, DVE, og1[:, 0:n], Osb[:, 0:n], Rps[0:64, 0:n], ALU.mult, [Osb, Rps], [og1])
        _tt(P, DVE, ogb[:, 0:n], og1[:, 0:n], sg[0:64, 0:n], ALU.mult, [og1, sg], [ogb])
        P.dma(C.OGi[h][:, t0:t0 + n], ogb[:, 0:n], reads=[ogb], writes=[C.OGi_res[h]])
        if bi == NB - 1:
            P.cc(lambda e, h=h: e.collective_compute("AllGather", ALU.bypass, replica_groups=C.RG,
                                                     ins=[C.OGi_t[h].ap().opt()], outs=[C.OGo_t[h].ap().opt()]),
                 reads=[C.OGi_res[h]], writes=[C.OGo_res[h]])

    LA = 3
    N = len(items)
    for i in range(N + LA):
        if i < N:
            qk(i)
        if i - LA >= 0:
            rest(i - LA)
    for _ in conv:
        pass


def build(stages=3, debug=False, nheads=H // 2, conv=None, s3parts=99, nblk=NB):
    nc = bass.Bass("TRN2", target_bir_lowering=False)
    C = Ctx()
    C.nheads = nheads
    C.s3parts = s3parts
    C.nblk = nblk
    if conv is None:
        conv = stages >= 3

    def din(name, shape, dt=F32):
        return nc.dram_tensor(name, list(shape), dt, kind="ExternalInput").ap()

    def dscr(name, shape, dt, out=False):
        return nc.dram_tensor(name, list(shape), dt, kind="ExternalOutput" if out else "Internal").ap()

    C.h0 = din("h0", [LP, D])
    C.attn_norm = din("attn_norm", [2, D])
    C.ffn_norm = din("ffn_norm", [2, D])
    C.final_norm = din("final_norm", [D])
    NHL = nheads
    C.fox_w_in = din("fox_w_in", [D, 4 * NHL * DH + NHL])
    C.fox_b_f = din("fox_b_f", [1, NHL])
    C.fox_q_norm = din("fox_q_norm", [1, DH])
    C.fox_k_norm = din("fox_k_norm", [1, DH])
    C.fox_w_out = din("fox_w_out", [D, D])
    C.hgrn_w_in = din("hgrn_w_in", [D, 4 * NHG * 128])
    C.lb = din("hgrn_lower_bounds", [2, NHG * 128])
    C.h0h = din("h0h", [HT * 128, D])
    C.rmask = din("rmask", [128, 8 * 512], mybir.dt.uint16)
    C.rmaskf = din("rmaskf", [128, 1])
    C.g_norm = din("hgrn_g_norm", [1, 128])
    C.hgrn_w_out = din("hgrn_w_out", [D, D])
    C.ffn_w_in = din("ffn_w_in", [2, D, 2 * FF])
    C.ffn_w_out = din("ffn_w_out", [2, FF, D])
    c_identb = din("c_identb", [128, 128], BF16)
    c_identf = din("c_identf", [128, 128], F32)
    c_bdones = din("c_bdones", [128, 128], BF16)
    c_tri = din("c_tri", [128, 128], BF16)
    d1 = debug and stages == 1
    C.QT_d = dscr("QT_d", [NHL, 70, LP], BF16, d1)
    C.KT_d = dscr("KT_d", [NHL, 70, LP], BF16, d1)
    C.VA_d = dscr("VA_d", [NHL, 128, NT, 65], BF16, d1)
    C.SG_d = dscr("SG_d", [NHL * DH, LP], BF16, d1)
    C.RG = [[0, 1], [2, 3], [4, 5], [6, 7]]
    C.OGi_t = [nc.dram_tensor("OGi%d" % h, [DH, LP + 128], BF16) for h in range(NHL)]
    C.OGo_t = [nc.dram_tensor("OGo%d" % h, [2 * DH, LP + 128], BF16) for h in range(NHL)]
    C.HNi_t = [nc.dram_tensor("HNi%d" % i, [128, 8, n], BF16) for i, (t0, n) in enumerate(LBLK)]
    C.HNo_t = [nc.dram_tensor("HNo%d" % i, [256, 8, n], BF16) for i, (t0, n) in enumerate(LBLK)]
    C.HNi = [t.ap() for t in C.HNi_t]
    C.HNo = [t.ap().rearrange("(r p) k n -> r p k n", r=2) for t in C.HNo_t]
    C.HNi_res = [Res() for _ in LBLK]
    C.HNo_res = [Res() for _ in LBLK]
    C.OG1i_t = [nc.dram_tensor("OG1i%d" % g, [128, NHG, LBLK[g % NLB][1]], BF16) for g in range(2 * NLB)]
    C.OG1o_t = [nc.dram_tensor("OG1o%d" % g, [256, NHG, LBLK[g % NLB][1]], BF16) for g in range(2 * NLB)]
    C.OG1i = [t.ap() for t in C.OG1i_t]
    C.OG1o = [t.ap().rearrange("(r p) k n -> r p k n", r=2) for t in C.OG1o_t]
    C.OG1i_res = [Res() for _ in range(2 * NLB)]
    C.OG1o_res = [Res() for _ in range(2 * NLB)]
    C.H1 = dscr("H1", [HT * 128, D], F32)
    C.H1_res = [Res() for _ in LBLK]
    C.OGi = [t.ap() for t in C.OGi_t]
    C.OGo = [t.ap() for t in C.OGo_t]
    C.OGi_res = [Res() for _ in range(NHL)]
    C.OGo_res = [Res() for _ in range(NHL)]
    C.W1S = dscr("W1S", [2, NJ, 128, 8, 256], BF16)
    C.W2S = dscr("W2S", [2, NJ, 128, D], BF16)
    C.WHS = dscr("WHS", [NHG, 128, 8, 384], BF16)
    C.WHV = dscr("WHV", [1, 128, 8, 512], BF16)
    C.out = nc.dram_tensor("out", [HT * 128, D], F32, kind="ExternalOutput").ap()
    C.dbg_x1 = nc.dram_tensor("dbg_x1", [nblk, 128, 8, 512], BF16, kind="ExternalOutput").ap() if (debug and stages == 3) else None

    with contextlib.ExitStack() as st0:
        P = Prog(nc)

        def mk(st):
            def sb(name, shape, dt):
                return Buf(st.enter_context(nc.sbuf_tensor(name, list(shape), dt)), name)

            def ps(name, shape, dt):
                return Buf(st.enter_context(nc.psum_tensor(name, list(shape), dt)), name)
            return sb, ps
        sb0, ps0 = mk(st0)
        C.identb = sb0("identb", [128, 128], BF16)
        C.identf = sb0("identf", [128, 128], F32)
        C.bdones = sb0("bdones", [128, 128], BF16)
        C.tri = sb0("tri", [128, 128], BF16)
        P.dma(C.identb[:], c_identb, writes=[C.identb])
        P.dma(C.identf[:], c_identf, writes=[C.identf])
        P.dma(C.bdones[:], c_bdones, writes=[C.bdones])
        P.dma(C.tri[:], c_tri, writes=[C.tri])
        C.Wo = [sb0("Wo0", [128, 8, D], BF16), sb0("Wo1", [128, 8, D], BF16)]
        if debug:
            cpc_o = nc.dram_tensor("CPC_o", [128, NT, NHL], F32, kind="ExternalOutput").ap()
            refb_o = nc.dram_tensor("REFB_o", [128, NB, NHL], F32, kind="ExternalOutput").ap()
        with contextlib.ExitStack() as stm:
            sbm, psm = mk(stm)
            C.CPC = sbm("CPC", [128, NT, NHL], F32)
            C.REFB = sbm("REFB", [128, NB, NHL], F32)
            bounce = [sbm("bounce%d" % i, [128, 2 * FF], BF16) for i in range(2)]
            with contextlib.ExitStack() as st:
                sb, ps = mk(st)
                stage1(P, nc, C, sb, ps, bounce if conv else None)
                if debug:
                    P.dma(cpc_o, C.CPC[:], reads=[C.CPC])
                    P.dma(refb_o, C.REFB[:], reads=[C.REFB])
                P.barrier()
            if stages >= 2:
                with contextlib.ExitStack() as st:
                    sb, ps = mk(st)
                    stage2(P, nc, C, sb, ps, bounce if conv else None)
                    P.barrier()
        if stages >= 3:
            with contextlib.ExitStack() as st:
                sb, ps = mk(st)
                stage3(P, nc, C, sb, ps)
                P.barrier()
        P.emit()
    return nc


def host_consts():
    bf = ml_dtypes.bfloat16
    idx = np.arange(128)
    return {
        "c_identb": np.eye(128, dtype=np.float32).astype(bf),
        "c_identf": np.eye(128, dtype=np.float32),
        "c_bdones": (idx[:, None] // 64 == idx[None, :] // 64).astype(np.float32).astype(bf),
        "c_tri": (idx[:, None] <= idx[None, :]).astype(np.float32).astype(bf),
    }


WNAMES = ["attn_norm", "ffn_norm", "final_norm", "fox_w_in", "fox_b_f", "fox_q_norm", "fox_k_norm", "fox_w_out",
          "hgrn_w_in", "hgrn_lower_bounds", "hgrn_g_norm", "hgrn_w_out", "ffn_w_in", "ffn_w_out"]


def make_in_maps(inputs, ncores=8):
    x = np.asarray(inputs["x"], dtype=np.float32)
    meta = np.asarray(inputs["meta_tokens"], dtype=np.float32)
    consts = host_consts()
    maps = []
    NHL = H // 2
    for c in range(ncores):
        b, r = c // 2, c % 2
        h0 = np.zeros((LP, D), np.float32)
        h0[NPAD:NPAD + NMETA] = meta
        h0[NPAD + NMETA:] = x[b]
        h0h = np.zeros((HT * 128, D), np.float32)
        seg = h0[r * HT * 128:(r + 1) * HT * 128]
        h0h[:seg.shape[0]] = seg
        m = {"h0": h0, "h0h": h0h, "rmask": np.full((128, 8 * 512), 0xFFFF if r else 0, np.uint16),
             "rmaskf": np.full((128, 1), float(r), np.float32)}
        for k in WNAMES:
            a = np.asarray(inputs[k], dtype=np.float32)
            if k in ("fox_w_in", "fox_w_out", "hgrn_w_in", "hgrn_w_out"):
                a = a.reshape(a.shape[-2], a.shape[-1])
            if k == "fox_w_in":
                w = NHL * DH
                a = np.concatenate([a[:, s0 + r * w:s0 + (r + 1) * w] for s0 in (0, D, 2 * D, 3 * D)]
                                   + [a[:, 4 * D + r * NHL:4 * D + (r + 1) * NHL]], axis=1)
            if k == "fox_b_f":
                a = a[:, r * NHL:(r + 1) * NHL]
            if k == "hgrn_w_in":
                w = NHG * 128
                a = np.concatenate([a[:, s0 + r * w:s0 + (r + 1) * w] for s0 in (0, D, 2 * D, 3 * D)], axis=1)
            if k == "hgrn_lower_bounds":
                a = a[:, r * NHG * 128:(r + 1) * NHG * 128]
            m[k] = np.ascontiguousarray(a)
        m.update(consts)
        maps.append(m)
    return maps


def kernel(**inputs):
    nc = build(stages=3)
    maps = make_in_maps(inputs, 8)
    res = run_bass_kernel_spmd(nc, maps, core_ids=list(range(8)))
    outs = []
    for b in range(4):
        o0 = np.asarray(res.results[2 * b]["out"], dtype=np.float32)
        o1 = np.asarray(res.results[2 * b + 1]["out"], dtype=np.float32)
        outs.append(np.concatenate([o0[NPAD + NMETA:], o1[:SEQ - (HT * 128 - NPAD - NMETA)]], axis=0))
    return np.stack(outs, axis=0)


class Stream:
    def __init__(self, P, bufs, reqs, q=SP, keep=0):
        self.P, self.bufs, self.reqs, self.q, self.keep = P, bufs, reqs, q, keep
        self.issued = 0

    def get(self, i):
        lim = min(len(self.reqs), i + len(self.bufs) - self.keep)
        while self.issued < lim:
            k = self.issued
            b = self.bufs[k % len(self.bufs)]
            dst, src = self.reqs[k]
            self.P.dma(dst(b), src, writes=[b], q=self.q)
            self.issued += 1
        return self.bufs[i % len(self.bufs)]


def convert_weights(P, nc, C, bounce, gate):
    k = [0]

    def ld(src, width):
        b = bounce[k[0] % 2]
        k[0] += 1
        P.dma(b[:, 0:width], src, reads=[gate[0]], writes=[b], q=POOL)
        return b
    for l in range(2):
        for kc in range(8):
            b = ld(C.ffn_w_in[l, kc * 128:(kc + 1) * 128, :], 2 * FF)
            for hf in range(2):
                P.dma(C.W1S[l, :, :, kc, hf * 128:(hf + 1) * 128].rearrange("j p c -> p j c"),
                      b[:, hf * FF:(hf + 1) * FF].rearrange("p (j c) -> p j c", c=128), reads=[b], q=POOL)
            yield
        wo = C.ffn_w_out[l].rearrange("(j p) c -> p j c", p=128)
        for j0 in range(0, NJ, 5):
            j1 = min(NJ, j0 + 5)
            b = ld(wo[:, j0:j1, :], (j1 - j0) * D)
            P.dma(C.W2S[l, j0:j1].rearrange("j p c -> p j c"),
                  b[:, 0:(j1 - j0) * D].rearrange("p (j c) -> p j c", c=D), reads=[b], q=POOL)
            yield
    GW = NHG * 128
    for kc in range(8):
        b = ld(C.hgrn_w_in[kc * 128:(kc + 1) * 128, :], 4 * GW)
        for gi, base in enumerate((0, GW, 3 * GW)):
            P.dma(C.WHS[:, :, kc, gi * 128:(gi + 1) * 128].rearrange("h p c -> p h c"),
                  b[:, base:base + GW].rearrange("p (h c) -> p h c", c=128), reads=[b], q=POOL)
        P.dma(C.WHV[0, :, kc, :], b[:, 2 * GW:3 * GW], reads=[b], q=POOL)
        yield


NHG = 4
HT = 33
LBLK = [(0, 128)] + [(128 + 512 * i, 512) for i in range(8)]
NLB = len(LBLK)


def stage3(P, nc, C, sb, ps):
    B = [ps("s3_B%d" % i, [128, 512], F32) for i in range(6)]
    Tt = [ps("s3_T%d" % i, [128, 1024], BF16) for i in range(2)]
    for i in range(2):
        v = Buf(Tt[i].t[:, :].bitcast(F32), "B%df" % (6 + i))
        v.res = Tt[i].res
        B.append(v)

    def bres(i):
        return [B[i].res]

    prow = sb("s3_prow", [32, 128], F32)
    P.dma(prow[0:8, :], C.lb.rearrange("r (h p) -> (r h) p", p=128), writes=[prow])
    P.dma(prow[8:16, :], C.attn_norm[1].rearrange("(c p) -> c p", p=128), writes=[prow])
    P.dma(prow[16:24, :], C.ffn_norm[0].rearrange("(c p) -> c p", p=128), writes=[prow])
    P.dma(prow[24:32, :], C.ffn_norm[1].rearrange("(c p) -> c p", p=128), writes=[prow])
    pcol = sb("s3_pcol", [128, 32], F32)
    P.add(PE, lambda e: e.transpose(out=B[5][:, 0:32], in_=prow[0:32, :], identity=C.identf[0:32, 0:32]),
          [prow, C.identf], bres(5))
    _copy(P, DVE, pcol[:], B[5][:, 0:32], bres(5), [pcol])
    omlb = sb("s3_omlb", [128, NHG], F32)
    _tt(P, DVE, omlb[:], pcol[:, 4:8], pcol[:, 0:4], ALU.subtract, [pcol], [omlb])
    _act(P, omlb[:], omlb[:], AF.Exp, [omlb], [omlb])
    _ts(P, DVE, omlb[:], omlb[:], 1.0, None, ALU.add, None, [omlb], [omlb])
    P.add(DVE, lambda e: e.reciprocal(out=omlb[:], in_=omlb[:]), [omlb], [omlb])
    lnomlb = sb("s3_lnomlb", [128, NHG], F32)
    _act(P, lnomlb[:], omlb[:], AF.Ln, [omlb], [lnomlb])
    gcols = {"attn1": pcol[:, 8:16], "ffn0": pcol[:, 16:24], "ffn1": pcol[:, 24:32]}
    gn = sb("s3_gn", [128, 1], F32)
    P.dma(gn[:], C.g_norm[0].rearrange("(p o) -> p o", o=1), writes=[gn])
    gfin = sb("s3_gfin", [128, D], F32)
    P.dma(gfin[:], C.final_norm.partition_broadcast(128), writes=[gfin])
    mh = sb("s3_mh", [128, 1], F32)
    P.add(POOL, lambda e: e.memset(mh[:], -0.5), [], [mh])
    ones128b = sb("s3_ones128b", [128, 128], BF16)
    P.add(DVE, lambda e: e.memset(ones128b[:], 1.0), [], [ones128b])
    onesf = sb("s3_onesf", [128, 128], F32)
    P.add(DVE, lambda e: e.memset(onesf[:], 1.0), [], [onesf])
    rmk = sb("s3_rmk", [128, 512], mybir.dt.uint16)
    P.dma(rmk[:], C.rmask[:, 0:512], writes=[rmk])
    rmf = sb("s3_rmf", [128, 1], F32)
    P.dma(rmf[:], C.rmaskf, writes=[rmf])
    junk = sb("s3_junk", [128, D], BF16)
    nrm = [[sb("s3_nrm%d_%d" % (i, j), [128, 1], F32) for j in range(3)] for i in range(4)]
    cnt = {"y": 0, "f": 0, "T": 0, "n": 0, "h": 0}

    def ybank():
        cnt["y"] += 1
        return B[cnt["y"] % 4]

    def fbank():
        cnt["f"] += 1
        return B[cnt["f"] % 6]

    def hbank():
        cnt["h"] += 1
        return B[cnt["h"] % 3]

    def norm_rows(ht):
        ssq, tt, rstd = nrm[cnt["n"] % 4]
        cnt["n"] += 1
        _act(P, junk[:], ht[:], AF.Square, [ht], [junk, ssq], accum_out=ssq[:])
        _ts(P, DVE, tt[:], ssq[:], 1.0 / D, EPS, ALU.mult, ALU.add, [ssq], [tt])
        _tt(P, POOL, rstd[:], tt[:], mh[:], ALU.pow, [tt, mh], [rstd])
        return rstd

    def norm_A(hs, hns, nt):
        for tl in range(nt):
            rstd = norm_rows(hs[tl])
            _ts(P, DVE, hns[tl][:], hs[tl][:], rstd[:, 0:1], None, ALU.mult, None, [hs[tl], rstd], [hns[tl]])

    def norm_B(hns, hnT, nt, gcol):
        n = nt * 128
        for c in range(8):
            half = cnt["T"] % 2
            cnt["T"] += 1
            pT = Tt[half][:, 0:512]
            res = Tt[half].res

            def fn(e, c=c, pT=pT):
                ins = None
                for tl in range(nt):
                    ins = e.transpose(out=pT[:, tl * 128:(tl + 1) * 128], in_=hns[tl][:, c * 128:(c + 1) * 128],
                                      identity=C.identb[:])
                return ins
            P.add(PE, fn, list(hns[:nt]) + [C.identb], [res])
            if c % 4 == 0:
                _ts(P, DVE, hnT[:, c, 0:n], pT[:, 0:n], gcol[:, c:c + 1], None, ALU.mult, None, [res, pcol], [hnT.sub(c)])
            else:
                P.add(ACT, lambda e, c=c, pT=pT: e.mul(out=hnT[:, c, 0:n], in_=pT[:, 0:n], mul=gcol[:, c:c + 1]),
                      [res, pcol], [hnT.sub(c)])

    def ag(in_t, out_t, rin, rout):
        P.cc(lambda e: e.collective_compute("AllGather", ALU.bypass, replica_groups=C.RG,
                                            ins=[in_t.ap().opt()], outs=[out_t.ap().opt()]),
             reads=[rin], writes=[rout])

    def scope():
        st = contextlib.ExitStack()
        return st, (lambda name, shape, dt: Buf(st.enter_context(nc.sbuf_tensor(name, list(shape), dt)), name))

    def ffn_phase(l):
        st, sbp = scope()
        with st:
            pre = "p%d_" % (1 + 2 * l)
            Wo = C.Wo[l]
            hs2 = [[sbp(pre + "h%d_%d" % (j, i), [128, D], F32) for i in range(4)] for j in range(2)]
            X0s = [sbp(pre + "X0_%d" % j, [128, 8, 512], BF16) for j in range(2)]
            Xcs = [sbp(pre + "Xc%d" % i, [128, 512], BF16) for i in range(2)]
            hns = [sbp(pre + "hn%d" % i, [128, D], BF16) for i in range(4)]
            hnT = sbp(pre + "hnT", [128, 8, 512], BF16)
            hnT1 = sbp(pre + "hnT1", [128, 8, 512], BF16) if l == 0 else None
            actT = sbp(pre + "actT", [128, NJ, 512], BF16)
            sils = [sbp(pre + "sil%d" % i, [128, 512], F32) for i in range(2)]
            W1b = [sbp(pre + "W1b%d" % i, [128, 2, 8, 256], BF16) for i in range(3)]
            W2b = [sbp(pre + "W2b%d" % i, [128, 2, D], BF16) for i in range(3)]
            orow = [sbp(pre + "orow%d" % i, [128, D], F32) for i in range(2)] if l == 1 else None
            NJP = NJ // 2
            S1 = Stream(P, W1b, [(lambda b: b[:], C.W1S[l, 2 * jp:2 * jp + 2].rearrange("j p k c -> p j k c"))
                                 for lb in range(NLB) for jp in range(NJP)])
            S2 = Stream(P, W2b, [(lambda b: b[:], C.W2S[l, 2 * jp:2 * jp + 2].rearrange("j p c -> p j c"))
                                 for lb in range(NLB) for jp in range(NJP)])
            S1.get(0)
            S2.get(0)
            hnTr = [hnT.sub(c) for c in range(8)]
            blocks = LBLK[:C.nblk]
            oc = [0]

            def stA(i):
                t0, n = blocks[i]
                nt = n // 128
                hs, X0 = hs2[i % 2], X0s[i % 2]
                for tl in range(nt):
                    if l == 0:
                        P.dma(hs[tl][:], C.h0h[t0 + tl * 128:t0 + (tl + 1) * 128, :], writes=[hs[tl]])
                    else:
                        P.dma(hs[tl][:], C.H1[t0 + tl * 128:t0 + (tl + 1) * 128, :], reads=[C.H1_res[i]], writes=[hs[tl]])
                yield
                for kc in range(8):
                    Xc = Xcs[kc % 2]
                    if l == 0:
                        rk, hl = kc // 4, 2 * (kc % 4)
                        for hh in range(2):
                            src = C.OGo[hl + hh][64 * rk:64 * rk + 64, :]
                            P.dma(X0[64 * hh:64 * hh + 64, kc, 0:n], src[:, t0:t0 + n], reads=[C.OGo_res[hl + hh]],
                                  writes=[X0.sub(kc)])
                            P.dma(Xc[64 * hh:64 * hh + 64, 0:n], src[:, HT * 128 + t0:HT * 128 + t0 + n],
                                  reads=[C.OGo_res[hl + hh]], writes=[Xc])
                    else:
                        rk, hd = kc // 4, kc % 4
                        P.dma(X0[:, kc, 0:n], C.OG1o[i][rk, :, hd, :], reads=[C.OG1o_res[i]], writes=[X0.sub(kc)])
                        P.dma(Xc[:, 0:n], C.OG1o[NLB + i][rk, :, hd, :], reads=[C.OG1o_res[NLB + i]], writes=[Xc])
                    P.add(DVE, lambda e, n=n, kc=kc, Xc=Xc, X0=X0: e.copy_predicated(
                        out=X0[:, kc, 0:n], mask=rmk[:, 0:n], data=Xc[:, 0:n]), [Xc, rmk, X0.sub(kc)], [X0.sub(kc)])
                    yield

            def stW(i):
                t0, n = blocks[i]
                hs, X = hs2[i % 2], X0s[i % 2]
                for tl in range(n // 128):
                    for hf in range(2):
                        y = ybank()
                        _mm(P, y[:, :], [(X[:, kc, tl * 128:(tl + 1) * 128], Wo[:, kc, hf * 512:(hf + 1) * 512])
                                         for kc in range(8)], [X.sub(kc) for kc in range(8)] + [Wo], [y])
                        _tt(P, DVE, hs[tl][:, hf * 512:(hf + 1) * 512], hs[tl][:, hf * 512:(hf + 1) * 512], y[:, :], ALU.add,
                            [hs[tl], y], [hs[tl]])

            hns1 = [sbp(pre + "hn1_%d" % i, [128, D], BF16) for i in range(4)] if l == 0 else None

            def stNa(i):
                t0, n = blocks[i]
                norm_A(hs2[i % 2], hns, n // 128)

            def stNb(i):
                t0, n = blocks[i]
                norm_B(hns, hnT, n // 128, gcols["ffn%d" % l])

            def stCin(i, gA=None):
                t0, n = blocks[i]
                for j in range(NJ):
                    if gA is not None and j % 2 == 0:
                        next(gA, None)
                    w = S1.get(i * NJP + j // 2)
                    jj = j % 2
                    g, u = fbank(), fbank()
                    _mm(P, g[:, 0:n], [(w[:, jj, kc, 0:128], hnT[:, kc, 0:n]) for kc in range(8)], [w] + hnTr, [g])
                    _mm(P, u[:, 0:n], [(w[:, jj, kc, 128:256], hnT[:, kc, 0:n]) for kc in range(8)], [w] + hnTr, [u])
                    sl = sils[j % 2]
                    _act(P, sl[:, 0:n], g[:, 0:n], AF.Silu, [g], [sl])
                    _tt(P, DVE, actT[:, j, 0:n], sl[:, 0:n], u[:, 0:n], ALU.mult, [sl, u], [actT.sub(j)])

            def stCout(i):
                t0, n = blocks[i]
                nt = n // 128
                hs = hs2[i % 2]
                for j in range(NJ):
                    w = S2.get(i * NJP + j // 2)

                    def fn(e, j=j, w=w):
                        ins = None
                        for tl in range(nt):
                            for hf in range(2):
                                ins = e.matmul(B[tl * 2 + hf][:, :], lhsT=actT[:, j, tl * 128:(tl + 1) * 128],
                                               rhs=w[:, j % 2, hf * 512:(hf + 1) * 512], start=(j == 0), stop=(j == NJ - 1))
                        return ins
                    P.add(PE, fn, [w, actT.sub(j)], sum([bres(k) for k in range(2 * nt)], []))
                for k in sorted(range(2 * nt), key=lambda k: -k):
                    tl, hf = k // 2, k % 2
                    _tt(P, DVE, hs[tl][:, hf * 512:(hf + 1) * 512], hs[tl][:, hf * 512:(hf + 1) * 512],
                        B[k][:, :], ALU.add, [hs[tl]] + bres(k), [hs[tl]])

            def stDa(i):
                t0, n = blocks[i]
                nt = n // 128
                hs = hs2[i % 2]
                if l == 0:
                    for tl in range(nt):
                        P.dma(C.H1[t0 + tl * 128:t0 + (tl + 1) * 128, :], hs[tl][:], reads=[hs[tl]], writes=[C.H1_res[i]])
                    norm_A(hs, hns1, nt)
                else:
                    for tl in range(nt):
                        rstd = norm_rows(hs[tl])
                        ob = orow[oc[0] % 2]
                        oc[0] += 1
                        _stt(P, ob[:], hs[tl][:], rstd[:, 0:1], gfin[:], ALU.mult, ALU.mult, [hs[tl], rstd, gfin], [ob])
                        P.dma(C.out[t0 + tl * 128:t0 + (tl + 1) * 128, :], ob[:], reads=[ob])

            def stDb(i):
                if l != 0:
                    return
                t0, n = blocks[i]
                nt = n // 128
                norm_B(hns1, hnT1, nt, gcols["attn1"])
                h1r = [hnT1.sub(c) for c in range(8)]
                if i == 0:
                    _ts(P, DVE, hnT1[:, :, 0:NPAD], hnT1[:, :, 0:NPAD], rmf[:, 0:1], None, ALU.mult, None, h1r + [rmf], h1r)
                P.dma(C.HNi[i], hnT1[:, :, 0:n], reads=h1r, writes=[C.HNi_res[i]])
                ag(C.HNi_t[i], C.HNo_t[i], C.HNi_res[i], C.HNo_res[i])

            nb = len(blocks)
            for _ in stA(0):
                pass
            stW(0)
            stNa(0)
            stNb(0)
            for i in range(nb):
                gA = stA(i + 1) if i + 1 < nb else None
                stCin(i, gA)
                if gA is not None:
                    for _ in gA:
                        pass
                if i > 0:
                    stDb(i - 1)
                if i + 1 < nb:
                    stW(i + 1)
                    stNa(i + 1)
                stCout(i)
                if i + 1 < nb:
                    stNb(i + 1)
                stDa(i)
            stDb(nb - 1)
            P.barrier()

    def hgrn_phase():
        st, sbp = scope()
        with st:
            hnTs = [sbp("p2_hnT%d" % i, [128, 8, 512], BF16) for i in range(2)]
            X1s = [sbp("p2_X1_%d" % i, [128, NHG, 512], BF16) for i in range(2)]
            WHb = [sbp("p2_WHb%d" % i, [128, 8, 512], BF16) for i in range(4)]
            silq = sbp("p2_silq", [128, NHG, 512], BF16)
            sgT = sbp("p2_sgT", [128, NHG, 512], BF16)
            Vsb2 = [[sbp("p2_V%d_%d" % (j, i), [128, NHG * 128], BF16) for i in range(4)] for j in range(2)]
            NBUF = 4
            mkb = lambda nm, dt: [sbp("p2_%s%d" % (nm, i), [128, 512], dt) for i in range(NBUF)]
            qts, kts, khs = mkb("qt", BF16), mkb("kt", BF16), mkb("kh", BF16)
            khTs = [sbp("p2_khT%d" % i, [128, 4, 128], BF16) for i in range(NBUF)]
            ezs, kks, ebs = mkb("ez", F32), mkb("kk", F32), mkb("eb", F32)
            mk2 = lambda nm, dt: [sbp("p2_%s%d" % (nm, i), [128, 512], dt) for i in range(2)]
            bbs, enbs, lnfs, ebcs = mk2("bb", F32), mk2("enb", F32), mk2("lnf", F32), mk2("ebc", F32)
            Asb4s = [sbp("p2_Asb4%d" % i, [128, 4, 128], BF16) for i in range(NBUF)]
            Usbs = mkb("Usb", F32)
            Sst = sbp("p2_S", [128, NHG, 128], F32)
            P.add(DVE, lambda e: e.memset(Sst[:], 0.0), [], [Sst])
            osqs = [sbp("p2_osq%d" % i, [128, 512], BF16) for i in range(2)]
            ons = [sbp("p2_on%d" % i, [128, 512], F32) for i in range(2)]
            rh = []
            for gb in range(2 * NLB):
                rh.append((lambda b: b[:], C.WHV[0]))
                for hd in range(NHG):
                    rh.append((lambda b: b[:, :, 0:384], C.WHS[hd]))
            SH = Stream(P, WHb, rh, keep=1)
            items = [(gb, hd) for gb in range(2 * NLB) for hd in range(NHG)]

            def geo(gb):
                rk, lb = gb // NLB, gb % NLB
                t0, n = LBLK[lb]
                return rk, lb, n, n // 128

            def h_front(i):
                gb, hd = items[i]
                rk, lb, n, nt = geo(gb)
                hnT = hnTs[gb % 2]
                hr = [hnT.sub(c) for c in range(8)]
                Vs = Vsb2[gb % 2]
                if hd == 0:
                    P.dma(hnT[:, :, 0:n], C.HNo[lb][rk], reads=[C.HNo_res[lb]], writes=hr, q=POOL)
                    w = SH.get(gb * (NHG + 1))
                    for tl in range(nt):
                        y = ybank()
                        _mm(P, y[:, :], [(hnT[:, kc, tl * 128:(tl + 1) * 128], w[:, kc, 0:512]) for kc in range(8)], [w] + hr, [y])
                        _copy(P, ACT if tl % 2 == 0 else DVE, Vs[tl][:, :], y[:, :], [y], [Vs[tl]])
                w = SH.get(gb * (NHG + 1) + 1 + hd)
                k3 = i % NBUF
                ez, kk, eb = ezs[k3], kks[k3], ebs[k3]
                lnf, bb, enb, ebc = [x[i % 2] for x in (lnfs, bbs, enbs, ebcs)]
                qt, kt, kh, khT, Asb4, Usb = qts[k3], kts[k3], khs[k3], khTs[k3], Asb4s[k3], Usbs[k3]
                qp, gp = hbank(), hbank()
                _mm(P, qp[:, 0:n], [(w[:, kc, 0:128], hnT[:, kc, 0:n]) for kc in range(8)], [w] + hr, [qp])
                _mm(P, gp[:, 0:n], [(w[:, kc, 256:384], hnT[:, kc, 0:n]) for kc in range(8)], [w] + hr, [gp])
                _act(P, silq[:, hd, 0:n], qp[:, 0:n], AF.Silu, [qp], [silq.sub(hd)])
                yield
                _act(P, sgT[:, hd, 0:n], gp[:, 0:n], AF.Silu, [gp], [sgT.sub(hd)])
                yield
                zp = hbank()
                _mm(P, zp[:, 0:n], [(w[:, kc, 128:256], hnT[:, kc, 0:n]) for kc in range(8)], [w] + hr, [zp])
                _act(P, ez[:, 0:n], zp[:, 0:n], AF.Exp, [zp], [ez])
                yield
                _act(P, ez[:, 0:n], ez[:, 0:n], AF.Ln, [ez], [ez], bias=1.0)
                yield
                _act(P, kk[:, 0:n], ez[:, 0:n], AF.Exp, [ez, lnomlb], [kk], scale=-1.0, bias=lnomlb[:, hd:hd + 1])
                yield
                _act(P, lnf[:, 0:n], kk[:, 0:n], AF.Ln, [kk], [lnf], scale=-1.0, bias=1.0)
                yield
                for c in range(nt):
                    P.add(DVE, lambda e, c=c: e.tensor_tensor_scan(
                        out=bb[:, c * 128:(c + 1) * 128], data0=onesf[:, 0:128], data1=lnf[:, c * 128:(c + 1) * 128],
                        initial=0.0, op0=ALU.mult, op1=ALU.add), [onesf, lnf], [bb])
                _act(P, eb[:, 0:n], bb[:, 0:n], AF.Exp, [bb], [eb])
                yield
                _act(P, enb[:, 0:n], bb[:, 0:n], AF.Exp, [bb], [enb], scale=-1.0)
                yield
                for c in range(nt):
                    _act(P, ebc[:, c * 128:(c + 1) * 128], bb[:, c * 128:(c + 1) * 128], AF.Exp, [bb], [ebc], scale=-1.0,
                         bias=bb[:, c * 128 + 127:c * 128 + 128])
                _tt(P, DVE, qt[:, 0:n], silq[:, hd, 0:n], eb[:, 0:n], ALU.mult, [silq.sub(hd), eb], [qt])
                yield
                _tt(P, DVE, kt[:, 0:n], kk[:, 0:n], enb[:, 0:n], ALU.mult, [kk, enb], [kt])
                yield
                _tt(P, DVE, kh[:, 0:n], kk[:, 0:n], ebc[:, 0:n], ALU.mult, [kk, ebc], [kh])
                yield
                pT = Tt[0][:, 0:512]
                res = Tt[0].res

                def fnT(e):
                    ins = None
                    for c in range(nt):
                        ins = e.transpose(out=pT[:, c * 128:(c + 1) * 128], in_=kh[:, c * 128:(c + 1) * 128], identity=C.identb[:])
                    return ins
                P.add(PE, fnT, [kh, C.identb], [res])
                _copy(P, ACT, khT[:, 0:nt, :], pT[:, 0:n].rearrange("p (c s) -> p c s", s=128), [res], [khT])
                yield
                Ab, Ub = B[5], B[7]

                def fnA(e):
                    ins = None
                    for c in range(nt):
                        cs = slice(c * 128, (c + 1) * 128)
                        ins = e.matmul(Ab[:, cs], lhsT=kt[:, cs], rhs=qt[:, cs], start=True, stop=True)
                    return ins
                P.add(PE, fnA, [kt, qt], [Ab])
                P.add(DVE, lambda e: e.tensor_tensor(
                    out=Asb4[:, 0:nt, :], in0=Ab[:, 0:n].rearrange("p (c s) -> p c s", s=128),
                    in1=C.tri[:, :].unsqueeze(1).to_broadcast([128, nt, 128]), op=ALU.mult), [Ab, C.tri], [Asb4])

                def fnU(e):
                    ins = None
                    for c in range(nt):
                        cs = slice(c * 128, (c + 1) * 128)
                        ins = e.matmul(Ub[:, cs], lhsT=khT[:, c, :], rhs=Vs[c][:, hd * 128:(hd + 1) * 128], start=True, stop=True)
                    return ins
                P.add(PE, fnU, [khT] + list(Vs[:nt]), [Ub])
                _copy(P, ACT, Usb[:, 0:n], Ub[:, 0:n], [Ub], [Usb])
                yield

            Sbf4s = [sbp("p2_Sbf4%d" % i, [128, 4, 128], BF16) for i in range(NBUF)]

            def h_state(i):
                gb, hd = items[i]
                rk, lb, n, nt = geo(gb)
                k3 = i % NBUF
                eb, Usb, Sbf4 = ebs[k3], Usbs[k3], Sbf4s[k3]
                for c in range(nt):
                    cs = slice(c * 128, (c + 1) * 128)
                    _copy(P, DVE, Sbf4[:, c, :], Sst[:, hd, :], [Sst.sub(hd)], [Sbf4])
                    _stt(P, Sst[:, hd, :], Sst[:, hd, :], eb[:, c * 128 + 127:c * 128 + 128], Usb[:, cs], ALU.mult, ALU.add,
                         [Sst.sub(hd), eb, Usb], [Sst.sub(hd)])

            def h_back(i):
                gb, hd = items[i]
                rk, lb, n, nt = geo(gb)
                Vs = Vsb2[gb % 2]
                k3 = i % NBUF
                qt, Asb4, Sbf4 = qts[k3], Asb4s[k3], Sbf4s[k3]
                X1 = X1s[gb % 2]
                osq, on = osqs[i % 2], ons[i % 2]
                op = B[3 + hd % 2]

                def fnO(e):
                    ins = None
                    for c in range(nt):
                        cs = slice(c * 128, (c + 1) * 128)
                        e.matmul(op[:, cs], lhsT=Vs[c][:, hd * 128:(hd + 1) * 128], rhs=Asb4[:, c, :], start=True, stop=False)
                        ins = e.matmul(op[:, cs], lhsT=Sbf4[:, c, :], rhs=qt[:, cs], start=False, stop=True)
                    return ins
                P.add(PE, fnO, [Sbf4, qt, Asb4] + list(Vs[:nt]), [op])
                yield
                _act(P, osq[:, 0:n], op[:, 0:n], AF.Square, [op], [osq])
                yield
                sp = hbank()
                _mm(P, sp[:, 0:n], [(ones128b[:], osq[:, 0:n])], [ones128b, osq], [sp])
                lnv, rs = ezs[k3], kks[k3]
                _act(P, lnv[:, 0:n], sp[:, 0:n], AF.Ln, [sp], [lnv], scale=1.0 / 128, bias=EPS)
                yield
                _act(P, rs[:, 0:n], lnv[:, 0:n], AF.Exp, [lnv], [rs], scale=-0.5)
                yield
                _stt(P, on[:, 0:n], op[:, 0:n], gn[:, 0:1], rs[:, 0:n], ALU.mult, ALU.mult, [op, gn, rs], [on])
                yield
                _tt(P, DVE, X1[:, hd, 0:n], on[:, 0:n], sgT[:, hd, 0:n], ALU.mult, [on, sgT.sub(hd)], [X1])
                yield
                if hd == NHG - 1:
                    P.dma(C.OG1i[gb], X1[:, :, 0:n], reads=[X1], writes=[C.OG1i_res[gb]])
                    ag(C.OG1i_t[gb], C.OG1o_t[gb], C.OG1i_res[gb], C.OG1o_res[gb])

            def zipped(gens):
                gens = list(gens)
                if DBG == "seq":
                    for g in gens:
                        for _ in g:
                            pass
                    return
                while gens:
                    for g in list(gens):
                        try:
                            next(g)
                        except StopIteration:
                            gens.remove(g)

            N = len(items)
            npair = N // 2
            zipped([h_front(0), h_front(1)])
            h_state(0)
            h_state(1)
            for t in range(npair):
                gens = [h_back(2 * t), h_back(2 * t + 1)]
                if t + 1 < npair:
                    gens = [h_front(2 * t + 2), h_front(2 * t + 3)] + gens
                zipped(gens)
                if t + 1 < npair:
                    h_state(2 * t + 2)
                    h_state(2 * t + 3)
            P.barrier()

    ffn_phase(0)
    if C.s3parts >= 2:
        hgrn_phase()
    if C.s3parts >= 3:
        ffn_phase(1)
```
